# Optimizing a Trainium2 kernel written in Bass

```python
import jax, jax.numpy as jnp
from jax import lax
import numpy as np

D_MODEL = 1024
BATCH = 2
SEQ = 8192
DEPTH = 4
DEC_BATCH = 16
DEC_SEQ = 2048
PAST_LEN = 128

GRID_W = 64
ROPE_THETA = 10000.0
NORM_EPS = 1e-6

ATTN_HEADS = 8
ATTN_KV_HEADS = 2
ATTN_GROUP = ATTN_HEADS // ATTN_KV_HEADS
ATTN_HEAD_DIM = 64
ATTN_Q_WIDTH = ATTN_HEADS * ATTN_HEAD_DIM
ATTN_KV_WIDTH = ATTN_KV_HEADS * ATTN_HEAD_DIM
QUERY_BLOCK = 128

MLSTM_HEADS = 4
MLSTM_HEAD_DIM = 128
MLSTM_WIDTH = MLSTM_HEADS * MLSTM_HEAD_DIM
MLSTM_CHUNK = 128
MLSTM_GATES = 4 * MLSTM_HEADS

AB_SIZES = (ATTN_Q_WIDTH, ATTN_KV_WIDTH, ATTN_KV_WIDTH,
            MLSTM_WIDTH, MLSTM_WIDTH, MLSTM_WIDTH, MLSTM_WIDTH, MLSTM_GATES)
AB_IN_COLS = sum(AB_SIZES)
AB_SPLIT_POINTS = tuple(int(p) for p in np.cumsum(AB_SIZES)[:-1])
AB_MIX_WIDTH = ATTN_Q_WIDTH + MLSTM_WIDTH

RET_HEADS = 4
RET_QK_DIM = 256
RET_V_DIM = 512
RET_QK_WIDTH = RET_HEADS * RET_QK_DIM
RET_V_WIDTH = RET_HEADS * RET_V_DIM
RET_CHUNK = 128
RET_SIZES = (RET_QK_WIDTH, RET_QK_WIDTH, RET_V_WIDTH, RET_V_WIDTH)
RET_IN_COLS = sum(RET_SIZES)
RET_SPLIT_POINTS = tuple(int(p) for p in np.cumsum(RET_SIZES)[:-1])

D_FF = 2816
CONV_WIDTH = 3

N_AB_LAYERS = (DEPTH + 1) // 2
N_RET_LAYERS = DEPTH // 2

kernel_name = 'hybrid_bidir_attn_mlstm_retention_convffn'


def rms_norm(x, gain):
    xf = x.astype(jnp.float32)
    y = xf * lax.rsqrt(jnp.mean(xf * xf, axis=-1, keepdims=True) + NORM_EPS)
    return (y * gain.astype(jnp.float32)).astype(x.dtype)


def head_rms_norm(x, gain):
    xf = x.astype(jnp.float32)
    return xf * lax.rsqrt(jnp.mean(xf * xf, axis=-1, keepdims=True) + NORM_EPS) * gain.astype(jnp.float32)


def flip_seq(a):
    return jnp.flip(a, axis=1)


def axial_rope_tables(seq_len, head_dim):
    rows = seq_len // GRID_W
    row_idx = jnp.repeat(jnp.arange(rows, dtype=jnp.float32), GRID_W)
    col_idx = jnp.tile(jnp.arange(GRID_W, dtype=jnp.float32), rows)
    axis_dim = head_dim // 2
    inv_freq = ROPE_THETA ** (-jnp.arange(0, axis_dim, 2, dtype=jnp.float32) / axis_dim)
    ang = jnp.concatenate([row_idx[:, None] * inv_freq, col_idx[:, None] * inv_freq], axis=-1)
    return jnp.cos(ang), jnp.sin(ang)


def apply_rope(x, cos, sin):
    half = x.shape[-1] // 2
    x1, x2 = x[..., :half], x[..., half:]
    c = cos[None, :, None, :]
    s = sin[None, :, None, :]
    return jnp.concatenate([x1 * c - x2 * s, x2 * c + x1 * s], axis=-1)


def bidirectional_gqa(q, k, v):
    B, S, _, d = q.shape
    nblk = S // QUERY_BLOCK
    qb = q.reshape(B, nblk, QUERY_BLOCK, ATTN_KV_HEADS, ATTN_GROUP, d).transpose(1, 0, 2, 3, 4, 5) * (d ** -0.5)

    def block(q_blk):
        s = jnp.einsum('bqhgd,bkhd->bhgqk', q_blk, k)
        p = jax.nn.softmax(s, axis=-1)
        return jnp.einsum('bhgqk,bkhd->bqhgd', p, v)

    o = lax.map(block, qb)
    return o.transpose(1, 0, 2, 3, 4, 5).reshape(B, S, ATTN_Q_WIDTH)


def mlstm_causal(q, k, v, log_i, log_f):
    B, S, H, dk = q.shape
    dv = v.shape[-1]
    L = MLSTM_CHUNK
    n_chunks = S // L

    def chunks(a):
        return jnp.moveaxis(a.reshape((B, n_chunks, L) + a.shape[2:]), 1, 0)

    mask = jnp.tril(jnp.ones((L, L), dtype=bool))

    def step(carry, xs):
        C, nv, m = carry
        qc, kc, vc, ic, fc = xs
        b = jnp.cumsum(fc, axis=1).transpose(0, 2, 1)
        ig = ic.transpose(0, 2, 1)
        dlog = jnp.where(mask, b[..., :, None] - b[..., None, :] + ig[..., None, :], -jnp.inf)
        inter = b + m[..., None]
        m_row = jnp.maximum(inter, jnp.max(dlog, axis=-1))
        w = jnp.exp(dlog - m_row[..., None]) * jnp.einsum('blhd,bkhd->bhlk', qc, kc)
        inter_w = jnp.exp(inter - m_row)
        num = (jnp.einsum('bhlk,bkhe->blhe', w, vc)
               + inter_w.transpose(0, 2, 1)[..., None] * jnp.einsum('blhd,bhde->blhe', qc, C))
        den = jnp.sum(w, axis=-1) + inter_w * jnp.einsum('blhd,bhd->bhl', qc, nv)
        h = num / jnp.maximum(jnp.abs(den), jnp.exp(-m_row)).transpose(0, 2, 1)[..., None]
        b_last = b[..., -1]
        wlog = b_last[..., None] - b + ig
        m_new = jnp.maximum(b_last + m, jnp.max(wlog, axis=-1))
        keep = jnp.exp(b_last + m - m_new)
        wk = jnp.exp(wlog - m_new[..., None])
        C_new = keep[..., None, None] * C + jnp.einsum('bhl,blhd,blhe->bhde', wk, kc, vc)
        n_new = keep[..., None] * nv + jnp.einsum('bhl,blhd->bhd', wk, kc)
        return (C_new, n_new, m_new), h

    init = (jnp.zeros((B, H, dk, dv), jnp.float32),
            jnp.zeros((B, H, dk), jnp.float32),
            jnp.zeros((B, H), jnp.float32))
    _, hs = lax.scan(step, init, (chunks(q), chunks(k), chunks(v), chunks(log_i), chunks(log_f)))
    return jnp.moveaxis(hs, 0, 1).reshape(B, S, H, dv)


def retention_causal(q, k, v, log_gamma, strict):
    B, S, H, dk = q.shape
    dv = v.shape[-1]
    L = RET_CHUNK
    n_chunks = S // L
    pos = jnp.arange(L, dtype=jnp.float32)
    diff = pos[:, None] - pos[None, :]
    mask = (diff > 0) if strict else (diff >= 0)
    intra_decay = jnp.where(mask[None], jnp.exp(jnp.maximum(diff, 0.0)[None] * log_gamma[:, None, None]), 0.0)
    query_decay = jnp.exp((pos + 1.0)[None, :] * log_gamma[:, None]).T
    key_decay = jnp.exp((L - 1.0 - pos)[None, :] * log_gamma[:, None])
    chunk_decay = jnp.exp(L * log_gamma)

    def chunks(a):
        return jnp.moveaxis(a.reshape((B, n_chunks, L) + a.shape[2:]), 1, 0)

    def step(state, xs):
        qc, kc, vc = xs
        scores = jnp.einsum('blhd,bkhd->bhlk', qc, kc) * intra_decay
        out = (jnp.einsum('bhlk,bkhe->blhe', scores, vc)
               + jnp.einsum('blhd,bhde->blhe', qc, state) * query_decay[None, :, :, None])
        state_new = chunk_decay[None, :, None, None] * state + jnp.einsum('hl,blhd,blhe->bhde', key_decay, kc, vc)
        return state_new, out

    init = jnp.zeros((B, H, dk, dv), jnp.float32)
    _, outs = lax.scan(step, init, (chunks(q), chunks(k), chunks(v)))
    return jnp.moveaxis(outs, 0, 1).reshape(B, S, H, dv)


def attn_mlstm_mixer(h, w_in, gate_bias, q_norm, k_norm, out_norm, w_out, cos_a, sin_a):
    B, S, _ = h.shape
    f32 = jnp.float32
    aq, ak, av, mq, mk, mv, mo, gates = jnp.split(h @ w_in, AB_SPLIT_POINTS, axis=-1)
    aq = apply_rope(head_rms_norm(aq.reshape(B, S, ATTN_HEADS, ATTN_HEAD_DIM), q_norm), cos_a, sin_a)
    ak = apply_rope(head_rms_norm(ak.reshape(B, S, ATTN_KV_HEADS, ATTN_HEAD_DIM), k_norm), cos_a, sin_a)
    av = av.reshape(B, S, ATTN_KV_HEADS, ATTN_HEAD_DIM).astype(f32)
    attn_out = bidirectional_gqa(aq, ak, av)
    mq = mq.reshape(B, S, MLSTM_HEADS, MLSTM_HEAD_DIM).astype(f32)
    mk = mk.reshape(B, S, MLSTM_HEADS, MLSTM_HEAD_DIM).astype(f32) * (MLSTM_HEAD_DIM ** -0.5)
    mv = mv.reshape(B, S, MLSTM_HEADS, MLSTM_HEAD_DIM).astype(f32)
    g = (gates.astype(f32) + gate_bias.astype(f32)).reshape(B, S, 4, MLSTM_HEADS)
    i_fwd, f_fwd = g[:, :, 0], jax.nn.log_sigmoid(g[:, :, 1])
    i_bwd, f_bwd = g[:, :, 2], jax.nn.log_sigmoid(g[:, :, 3])
    h_fwd = mlstm_causal(mq, mk, mv, i_fwd, f_fwd)
    h_bwd = flip_seq(mlstm_causal(flip_seq(mq), flip_seq(mk), flip_seq(mv), flip_seq(i_bwd), flip_seq(f_bwd)))
    m_out = head_rms_norm(h_fwd + h_bwd, out_norm.reshape(MLSTM_HEADS, MLSTM_HEAD_DIM)).reshape(B, S, MLSTM_WIDTH)
    m_out = m_out * jax.nn.sigmoid(mo.astype(f32))
    mixed = jnp.concatenate([attn_out, m_out], axis=-1).astype(h.dtype)
    return mixed @ w_out


def retention_mixer(h, w_in, decay_logit, out_norm, w_out, cos_r, sin_r):
    B, S, _ = h.shape
    f32 = jnp.float32
    rq, rk, rv, rg = jnp.split(h @ w_in, RET_SPLIT_POINTS, axis=-1)
    rq = apply_rope(rq.reshape(B, S, RET_HEADS, RET_QK_DIM).astype(f32), cos_r, sin_r)
    rk = apply_rope(rk.reshape(B, S, RET_HEADS, RET_QK_DIM).astype(f32), cos_r, sin_r) * (RET_QK_DIM ** -0.5)
    rv = rv.reshape(B, S, RET_HEADS, RET_V_DIM).astype(f32)
    log_gamma = jax.nn.log_sigmoid(decay_logit.astype(f32))
    y = (retention_causal(rq, rk, rv, log_gamma[0], False)
         + flip_seq(retention_causal(flip_seq(rq), flip_seq(rk), flip_seq(rv), log_gamma[1], True)))
    y = head_rms_norm(y, out_norm.reshape(RET_HEADS, RET_V_DIM)).reshape(B, S, RET_V_WIDTH)
    y = y * jax.nn.silu(rg.astype(f32))
    return y.astype(h.dtype) @ w_out


def conv_ffn(h, w_up, conv_w, conv_b, w_down):
    u, g = jnp.split(h @ w_up, 2, axis=-1)
    gp = jnp.pad(g, ((0, 0), (1, 1), (0, 0)))
    g = gp[:, :-2] * conv_w[0] + gp[:, 1:-1] * conv_w[1] + gp[:, 2:] * conv_w[2] + conv_b
    return (jax.nn.gelu(g, approximate=False) * u) @ w_down


def trunk(x, norm_mix, norm_ffn, norm_final, ab_w_in, ab_gate_bias, attn_q_norm, attn_k_norm,
          mlstm_out_norm, ab_w_out, ret_w_in, ret_decay_logit, ret_out_norm, ret_w_out,
          ffn_w_up, ffn_conv_w, ffn_conv_b, ffn_w_down):
    S = x.shape[1]
    cos_a, sin_a = axial_rope_tables(S, ATTN_HEAD_DIM)
    cos_r, sin_r = axial_rope_tables(S, RET_QK_DIM)
    for layer in range(DEPTH):
        j = layer // 2
        h = rms_norm(x, norm_mix[layer])
        if layer % 2 == 0:
            x = x + attn_mlstm_mixer(h, ab_w_in[j], ab_gate_bias[j], attn_q_norm[j], attn_k_norm[j],
                                     mlstm_out_norm[j], ab_w_out[j], cos_a, sin_a)
        else:
            x = x + retention_mixer(h, ret_w_in[j], ret_decay_logit[j], ret_out_norm[j], ret_w_out[j],
                                    cos_r, sin_r)
        x = x + conv_ffn(rms_norm(x, norm_ffn[layer]), ffn_w_up[layer], ffn_conv_w[layer],
                         ffn_conv_b[layer], ffn_w_down[layer])
    return rms_norm(x, norm_final)


def setup_inputs(seed: int = 0) -> dict:
    key = jax.random.key(seed)
    ks = jax.random.split(key, 20)
    nrm = jax.random.normal
    out_scale = (2.0 * DEPTH) ** -0.5
    f_bias = jnp.linspace(3.0, 6.0, MLSTM_HEADS, dtype=jnp.float32)
    zeros_h = jnp.zeros((MLSTM_HEADS,), jnp.float32)
    gate_base = jnp.concatenate([zeros_h, f_bias, zeros_h, f_bias])
    a = 5.0 + jnp.arange(RET_HEADS, dtype=jnp.float32)
    decay_base = jnp.log(2.0 ** a - 1.0)
    return {
        'x_prompt': nrm(ks[0], (BATCH, SEQ, D_MODEL), jnp.float32),
        'x_sample': nrm(ks[1], (DEC_BATCH, DEC_SEQ, D_MODEL), jnp.float32),
        'norm_mix': 1.0 + 0.05 * nrm(ks[2], (DEPTH, D_MODEL), jnp.float32),
        'norm_ffn': 1.0 + 0.05 * nrm(ks[3], (DEPTH, D_MODEL), jnp.float32),
        'norm_final': 1.0 + 0.05 * nrm(ks[4], (D_MODEL,), jnp.float32),
        'ab_w_in': nrm(ks[5], (N_AB_LAYERS, D_MODEL, AB_IN_COLS), jnp.float32) * D_MODEL ** -0.5,
        'ab_gate_bias': gate_base + 0.1 * nrm(ks[6], (N_AB_LAYERS, MLSTM_GATES), jnp.float32),
        'attn_q_norm': 1.0 + 0.05 * nrm(ks[7], (N_AB_LAYERS, ATTN_HEAD_DIM), jnp.float32),
        'attn_k_norm': 1.0 + 0.05 * nrm(ks[8], (N_AB_LAYERS, ATTN_HEAD_DIM), jnp.float32),
        'mlstm_out_norm': 1.0 + 0.05 * nrm(ks[9], (N_AB_LAYERS, MLSTM_WIDTH), jnp.float32),
        'ab_w_out': nrm(ks[10], (N_AB_LAYERS, AB_MIX_WIDTH, D_MODEL), jnp.float32) * AB_MIX_WIDTH ** -0.5 * out_scale,
        'ret_w_in': nrm(ks[11], (N_RET_LAYERS, D_MODEL, RET_IN_COLS), jnp.float32) * D_MODEL ** -0.5,
        'ret_decay_logit': decay_base + 0.05 * nrm(ks[12], (N_RET_LAYERS, 2, RET_HEADS), jnp.float32),
        'ret_out_norm': 1.0 + 0.05 * nrm(ks[13], (N_RET_LAYERS, RET_V_WIDTH), jnp.float32),
        'ret_w_out': nrm(ks[14], (N_RET_LAYERS, RET_V_WIDTH, D_MODEL), jnp.float32) * RET_V_WIDTH ** -0.5 * out_scale,
        'ffn_w_up': nrm(ks[15], (DEPTH, D_MODEL, 2 * D_FF), jnp.float32) * D_MODEL ** -0.5,
        'ffn_conv_w': nrm(ks[16], (DEPTH, CONV_WIDTH, D_FF), jnp.float32) * CONV_WIDTH ** -0.5,
        'ffn_conv_b': 0.02 * nrm(ks[17], (DEPTH, D_FF), jnp.float32),
        'ffn_w_down': nrm(ks[18], (DEPTH, D_FF, D_MODEL), jnp.float32) * D_FF ** -0.5 * out_scale,
    }


def reference(x_prompt, x_sample, norm_mix, norm_ffn, norm_final, ab_w_in, ab_gate_bias, attn_q_norm,
              attn_k_norm, mlstm_out_norm, ab_w_out, ret_w_in, ret_decay_logit, ret_out_norm, ret_w_out,
              ffn_w_up, ffn_conv_w, ffn_conv_b, ffn_w_down):
    y_prompt = trunk(x_prompt, norm_mix, norm_ffn, norm_final, ab_w_in, ab_gate_bias, attn_q_norm,
                     attn_k_norm, mlstm_out_norm, ab_w_out, ret_w_in, ret_decay_logit, ret_out_norm,
                     ret_w_out, ffn_w_up, ffn_conv_w, ffn_conv_b, ffn_w_down)
    y_sample = trunk(x_sample, norm_mix, norm_ffn, norm_final, ab_w_in, ab_gate_bias, attn_q_norm,
                     attn_k_norm, mlstm_out_norm, ab_w_out, ret_w_in, ret_decay_logit, ret_out_norm,
                     ret_w_out, ffn_w_up, ffn_conv_w, ffn_conv_b, ffn_w_down)
    return (y_prompt, y_sample)
```

```python
import contextlib
import numpy as np
import concourse.bass as bass
import concourse.mybir as mybir
from concourse.bass_utils import run_bass_kernel_spmd

F32 = mybir.dt.float32
BF16 = mybir.dt.bfloat16
AF = mybir.ActivationFunctionType
ALU = mybir.AluOpType
AX = mybir.AxisListType
EPS = 1e-6


class _Op:
    __slots__ = ("eng", "fn", "waits", "signal", "idx", "chan", "cidx", "sigval")

    def __init__(self, eng, fn, chan):
        self.eng = eng
        self.fn = fn
        self.chan = chan
        self.waits = []
        self.signal = False
        self.idx = -1
        self.cidx = -1
        self.sigval = 0


class Sched:
    ENG = ("pe", "act", "dve", "pool", "sp")

    def __init__(self, nc):
        self.nc = nc
        self.ops = {e: [] for e in self.ENG}
        self.res = {}
        self.waited = {e: {} for e in self.ENG}
        self.chan_last = {}
        self.chan_n = {}
        self.chan_phase = {}
        self.phase_id = 0

    def _wait(self, op, d):
        eng = op.eng
        if d is op:
            return
        if d.chan is not None:
            ek, val = ("c", d.chan), d.cidx
        else:
            if d.eng == eng and eng == "pe":
                return
            ek, val = d.eng, d.idx
        w = self.waited[eng]
        if w.get(ek, -1) >= val:
            return
        w[ek] = val
        op.waits.append(d)
        d.signal = True

    def add(self, eng, fn, reads=(), writes=(), chan=None):
        if chan is not None:
            chan = (self.phase_id, chan)
        op = _Op(eng, fn, chan)
        deps = []
        res = self.res
        for k in reads:
            st = res.get(k)
            if st is not None and st[0] is not None:
                deps.append(st[0])
        for k in writes:
            st = res.get(k)
            if st is not None:
                if st[0] is not None:
                    deps.append(st[0])
                deps.extend(st[1])
        if chan is not None:
            prev = self.chan_last.get(chan)
            if prev is not None:
                deps.append(prev)
            op.cidx = self.chan_n.get(chan, 0)
            if op.cidx == 0:
                self.chan_phase[chan] = self.phase_id
            assert self.chan_phase[chan] == self.phase_id, chan
            self.chan_n[chan] = op.cidx + 1
            self.chan_last[chan] = op
            op.signal = True
        op.idx = len(self.ops[eng])
        for d in deps:
            self._wait(op, d)
        self.ops[eng].append(op)
        for k in writes:
            res[k] = [op, []]
        for k in reads:
            st = res.get(k)
            if st is None:
                res[k] = [None, [op]]
            elif st[0] is not op:
                st[1].append(op)
        return op

    def barrier(self):
        lasts = []
        for e in self.ENG:
            for o in reversed(self.ops[e]):
                if o.fn is not None and o.chan is None:
                    lasts.append(o)
                    break
        lasts.extend(self.chan_last.values())
        for e in self.ENG:
            op = _Op(e, None, None)
            op.idx = len(self.ops[e])
            for d in lasts:
                self._wait(op, d)
            self.ops[e].append(op)
        self.res = {}
        self.phase_id += 1

    def pe(self, fn, reads=(), writes=()):
        return self.add("pe", fn, reads, writes)

    def act(self, fn, reads=(), writes=()):
        return self.add("act", fn, reads, writes)

    def dve(self, fn, reads=(), writes=()):
        return self.add("dve", fn, reads, writes)

    def pool(self, fn, reads=(), writes=()):
        return self.add("pool", fn, reads, writes)

    def dma(self, out, in_, reads=(), writes=(), chan=None, eng="sp", **kw):
        assert chan is not None
        return self.add(eng, lambda e: e.dma_start(out=out, in_=in_, **kw), reads, writes, chan=chan)

    def emit(self, stack):
        nc = self.nc
        esem = {}
        for e in self.ENG:
            if e != "sp":
                esem[e] = stack.enter_context(nc.semaphore("s_" + e))
        csem = {}
        cbase = {}
        pool = []
        by_phase = {}
        for c, ph in self.chan_phase.items():
            by_phase.setdefault(ph, []).append(c)
        nsem = 0
        for ph in sorted(by_phase):
            used = []
            for c in by_phase[ph]:
                if pool:
                    sv = pool.pop()
                else:
                    sv = [stack.enter_context(nc.semaphore("c%d" % nsem)), 0]
                    nsem += 1
                csem[c] = sv[0]
                cbase[c] = sv[1]
                sv[1] += 16 * self.chan_n[c]
                used.append(sv)
            pool.extend(used)
        self.nsem = nsem
        for e in self.ENG:
            cnt = 0
            for op in self.ops[e]:
                if op.chan is not None:
                    op.sigval = cbase[op.chan] + 16 * (op.cidx + 1)
                elif op.signal:
                    cnt += 1
                    op.sigval = cnt

        def run(e, eng):
            for op in self.ops[e]:
                for d in op.waits:
                    sem = csem[d.chan] if d.chan is not None else esem[d.eng]
                    eng.wait_ge(sem, d.sigval)
                if op.fn is None:
                    continue
                ins = op.fn(eng)
                if op.chan is not None:
                    ins.then_inc(csem[op.chan], 16)
                elif op.signal:
                    ins.then_inc(esem[e], 1)

        with nc.Block() as block:
            @block.sync
            def _(eng):
                run("sp", eng)

            @block.tensor
            def _(eng):
                run("pe", eng)

            @block.scalar
            def _(eng):
                run("act", eng)

            @block.vector
            def _(eng):
                run("dve", eng)

            @block.gpsimd
            def _(eng):
                run("pool", eng)


def mmg(items):
    n = len(items)

    def f(e):
        ins = None
        for i, (o, l, r) in enumerate(items):
            ins = e.matmul(o, lhsT=l, rhs=r, start=(i == 0), stop=(i == n - 1))
        return ins
    return f


def mm1(o, l, r):
    return lambda e: e.matmul(o, lhsT=l, rhs=r, start=True, stop=True)


def tr(o, i, ident):
    return lambda e: e.transpose(o, i, ident)


def actf(o, i, func, bias=None, scale=None, accum=None):
    kw = {}
    if bias is not None:
        kw["bias"] = bias
    if scale is not None:
        kw["scale"] = scale
    if accum is not None:
        kw["accum_out"] = accum
    return lambda e: e.activation(out=o, in_=i, func=func, **kw)


def tt(o, a, b, op):
    return lambda e: e.tensor_tensor(out=o, in0=a, in1=b, op=op)


def ts(o, a, s1, op0, s2=None, op1=None):
    if op1 is None:
        return lambda e: e.tensor_scalar(out=o, in0=a, scalar1=s1, scalar2=None, op0=op0)
    return lambda e: e.tensor_scalar(out=o, in0=a, scalar1=s1, scalar2=s2, op0=op0, op1=op1)


def stt(o, a, s, b, op0, op1):
    return lambda e: e.scalar_tensor_tensor(out=o, in0=a, scalar=s, in1=b, op0=op0, op1=op1)


def cp(o, i):
    return lambda e: e.tensor_copy(out=o, in_=i)


def recip(o, i):
    return lambda e: e.reciprocal(out=o, in_=i)


def mset(o, v):
    return lambda e: e.memset(o, v)


C_ID, C_ONES, C_BD32, C_TRIU, C_TRIL, C_SL, C_DFW, C_DBW, C_QDF, C_QDB, C_SEL, C_KD = range(12)
NCM = 12


def host_cmat():
    s = np.arange(128)[:, None].astype(np.float32)
    l = np.arange(128)[None, :].astype(np.float32)
    m = np.zeros((NCM, 128, 128), np.float32)
    m[C_ID] = np.eye(128)
    m[C_ONES] = 1.0
    m[C_BD32] = (np.arange(128)[:, None] // 32 == np.arange(128)[None, :] // 32)
    m[C_TRIU] = (s <= l)
    m[C_TRIL] = (s >= l)
    m[C_SL] = (s > l)
    m[C_DFW] = np.maximum(l - s, 0)
    m[C_DBW] = np.maximum(s - l, 0)
    m[C_QDF] = np.broadcast_to(l + 1.0, (128, 128))
    m[C_QDB] = np.broadcast_to(128.0 - l, (128, 128))
    m[C_SEL][64, :] = 1.0
    m[C_KD][:, 0] = 127.0 - np.arange(128)
    m[C_KD][:, 1] = np.arange(128)
    m[C_KD][:, 2] = 128.0
    return np.ascontiguousarray(m.transpose(1, 0, 2))


class Ring:
    def __init__(self, items):
        self.items = items
        self.i = 0

    def next(self):
        it = self.items[self.i % len(self.items)]
        self.i += 1
        return it


class MK:
    W_SPECS = [
        ("norm_mix", (4, 1024)), ("norm_ffn", (4, 1024)), ("norm_final", (1024,)),
        ("ab_w_in", (2, 1024, 2832)), ("ab_gate_bias", (2, 16)), ("attn_q_norm", (2, 64)),
        ("attn_k_norm", (2, 64)), ("mlstm_out_norm", (2, 512)), ("ab_w_out", (2, 1024, 1024)),
        ("ret_w_in", (2, 1024, 6144)), ("ret_decay_logit", (2, 2, 4)), ("ret_out_norm", (2, 2048)),
        ("ret_w_out", (2, 2048, 1024)), ("ffn_w_up", (4, 1024, 5632)), ("ffn_conv_w", (4, 3, 2816)),
        ("ffn_conv_b", (4, 2816)), ("ffn_w_down", (4, 2816, 1024)),
    ]

    def __init__(self, T, steps):
        self.T = T
        self.NT = T // 512
        self.NCH = T // 128
        self.steps = steps
        self.nc = nc = bass.Bass("TRN2", target_bir_lowering=False)
        self.S = Sched(nc)
        self._uid = 0
        NT, NCH = self.NT, self.NCH
        di = lambda n, s, dt=F32: nc.dram_tensor(n, list(s), dt, kind="ExternalInput").ap()
        ds = lambda n, s, dt: nc.dram_tensor(n, list(s), dt, kind="Internal").ap()
        self.xT = di("xT", (1024, T))
        self.w = {n: di(n, s) for n, s in self.W_SPECS}
        self.cmat_d = di("cmat", (128, NCM, 128))
        self.ropeA = di("ropeA", (2, 128, T))
        self.ropeR = di("ropeR", (2, 128, T))
        self.mb_d = di("maskb", (128, NT * NT))
        self.carry_d = di("carry", (128, 2 * NCH))
        self.flb_d = di("flb", (128, NT * 2))
        self.yT = nc.dram_tensor("yT", [1024, T], F32, kind="ExternalOutput").ap()
        self.XM = ds("XM", (1024, T), F32)
        self.XR = ds("XR", (1024, T), F32)
        self.ACTS = ds("ACTS", (2816, T), BF16)
        self.MIX = ds("MIX", (2048, T), BF16)
        self.QA = ds("QA", (8, 64, T), BF16)
        self.KA = ds("KA", (128, T), BF16)
        self.VA = ds("VA", (T, 130), BF16)
        self.MQ = ds("MQ", (4, 128, T), BF16)
        self.MKs = ds("MKs", (4, 128, T), BF16)
        self.MV = ds("MV", (T, 516), BF16)
        self.SG = ds("SG", (T, 512), F32)
        self.GT = ds("GT", (T, 16), F32)
        self.HF = ds("HF", (T, 512), F32)
        self.HB = ds("HB", (T, 512), F32)
        self.YB = ds("YB", (T, 2048), F32)
        self.RQ = ds("RQ", (8, 128, T), BF16)
        self.RK = ds("RK", (8, 128, T), BF16)
        self.RV = ds("RV", (T, 2048), BF16)
        self.RG = ds("RG", (T, 2048), F32)
        self.YF = ds("YF", (T, 2048), F32)

    def un(self, n):
        self._uid += 1
        return "%s_%d" % (n, self._uid)

    def sb(self, name, shape, dt):
        return self._ph.enter_context(self.nc.sbuf_tensor(self.un(name), list(shape), dt))

    @contextlib.contextmanager
    def phase(self, psum=True):
        self.S.barrier()
        with contextlib.ExitStack() as ph:
            self._ph = ph
            if psum:
                nc = self.nc
                self.ps = [ph.enter_context(nc.psum_tensor(self.un("ps%d" % i), [128, 512], F32)) for i in range(6)]
                self.psb = [ph.enter_context(nc.psum_tensor(self.un("psb%d" % i), [128, 1024], BF16)) for i in range(2)]
            yield
        self._ph = self._gl

    def ring(self, name, n, shape, dt):
        items = []
        for i in range(n):
            t = self.sb(name, shape, dt)
            items.append((t, self.un(name)))
        return Ring(items)

    def pipeline(self, n, stage1, stage2, la):
        for i in range(n + la):
            if i < n:
                stage1(i)
            if i >= la:
                stage2(i - la)

    def rr(self, engs):
        self._rr = getattr(self, "_rr", 0) + 1
        return engs[self._rr % len(engs)]

    def build(self):
        nc, S = self.nc, self.S
        with contextlib.ExitStack() as gl:
            self._gl = gl
            self._ph = gl
            self.pk = ["ps%d" % i for i in range(6)]
            self.cm = self.sb("cm", [128, NCM, 128], F32)
            self.cmb = self.sb("cmb", [128, 3, 128], BF16)
            S.dma(self.cm[:], self.cmat_d, writes=["cm"], chan="cm")
            S.dve(cp(self.cmb[:], self.cm[:, 0:3, :]), reads=["cm"], writes=["cmb"])
            self.ident_f = self.cm[:, C_ID, :]
            self.ones_f = self.cm[:, C_ONES, :]
            self.ident_b = self.cmb[:, C_ID, :]
            self.ones_b = self.cmb[:, C_ONES, :]
            self.bd32_b = self.cmb[:, C_BD32, :]
            self.carry = self.sb("carry", [128, 2, self.NCH], F32)
            S.dma(self.carry[:], self.carry_d.rearrange("p (d c) -> p d c", d=2), writes=["carry"], chan="carry")
            self.mb = self.sb("mb", [128, self.NT, self.NT], F32)
            S.dma(self.mb[:], self.mb_d.rearrange("p (a b) -> p a b", a=self.NT), writes=["mb"], chan="mb")
            self.flb = self.sb("flb", [128, self.NT * 2], F32)
            S.dma(self.flb[:], self.flb_d, writes=["flb"], chan="flb")
            cur = self.xT
            for st in self.steps:
                kind = st[0]
                if kind == "ab":
                    j, L = st[1], st[2]
                    self.ab_inproj(cur, j, L)
                    self.attention()
                    self.mlstm(j)
                    self.proj_resid(self.MIX, 8, self.w["ab_w_out"][j], cur, self.XM)
                    cur = self.XM
                elif kind == "ret":
                    j, L = st[1], st[2]
                    self.ret_inproj(cur, j, L)
                    if not DBG.get("noscan"):
                        self.retention(j)
                    self.proj_resid(self.MIX, 16, self.w["ret_w_out"][j], cur, self.XM)
                    cur = self.XM
                elif kind == "ffn":
                    L = st[1]
                    self.ffn_up(cur, L)
                    self.proj_resid(self.ACTS, 22, self.w["ffn_w_down"][L], cur, self.XR)
                    cur = self.XR
                elif kind == "final":
                    self.final_norm(cur)
                    cur = None
                elif kind == "copy":
                    self.copy_out(cur)
                    cur = None
            S.barrier()
            S.emit(gl)
        return nc

    def x3(self, X):
        return X.rearrange("(c p) t -> p c t", p=128)

    def load_gain(self, vec_ap, name):
        g = self.sb(name, [128, 8], F32)
        self.S.dma(g[:], vec_ap.rearrange("(c p) -> p c", p=128), writes=[name], chan=name, allow_slow_non_contiguous=True)
        return g

    def prep_weight(self, dst, dkey, src, KC, blocks, gain=None, gkey=None):
        S = self.S
        stg = self.ring("wstg", 2, [128, 1024], F32)
        blocks2 = []
        for (d0, nd, s0, ns, vf) in blocks:
            o = 0
            while o < ns:
                n_ = min(1024, ns - o)
                blocks2.append((d0 + o, n_, s0 + o, n_, None))
                o += n_
        for c in range(KC):
            for (d0, nd, s0, ns, vf) in blocks2:
                t, k = stg.next()
                S.dma(t[:, 0:ns], src[c * 128:(c + 1) * 128, s0:s0 + ns], writes=[k], chan=k)
                iv = t[:, 0:ns]
                ov = dst[:, c, d0:d0 + nd]
                eng = self.rr(["dve", "pool", "act"])
                rd = [k] + ([gkey] if gain is not None else [])
                if gain is None:
                    if eng == "act":
                        S.act(actf(ov, iv, AF.Copy), reads=rd, writes=[dkey])
                    else:
                        S.add(eng, cp(ov, iv), reads=rd, writes=[dkey])
                else:
                    if eng == "act":
                        S.act(actf(ov, iv, AF.Copy, scale=gain[:, c:c + 1]), reads=rd, writes=[dkey])
                    else:
                        S.add(eng, ts(ov, iv, gain[:, c:c + 1], ALU.mult), reads=rd, writes=[dkey])

    def norm_load(self, src3, t0, n, ent):
        xt, kx = ent
        self.S.dma(xt[:, :, 0:n], src3[:, :, t0:t0 + n], writes=[kx], chan=kx)

    def norm_tile(self, src3, t0, n, bufs, D_feat=1024, load=True):
        S = self.S
        xt, kx = bufs["xt"]
        sq, ksq = bufs["sq"]
        rs, krs = bufs["rs"]
        hT, kh = bufs["hT"]
        ssp, kss = bufs["ss"]
        if load:
            S.dma(xt[:, :, 0:n], src3[:, :, t0:t0 + n], writes=[kx], chan=kx)
        S.act(actf(sq[:, :, 0:n], xt[:, :, 0:n], AF.Square), reads=[kx], writes=[ksq])
        S.pe(mmg([(ssp[:, 0:n], self.ones_b, sq[:, c, 0:n]) for c in range(8)]), reads=[ksq, "cmb"], writes=[kss])
        S.act(actf(rs[:, 0:n], ssp[:, 0:n], AF.Sqrt, bias=EPS, scale=1.0 / D_feat), reads=[kss], writes=[krs])
        S.dve(recip(rs[:, 0:n], rs[:, 0:n]), reads=[krs], writes=[krs])
        S.dve(tt(hT[:, 0:4, 0:n], xt[:, 0:4, 0:n], rs[:, 0:n].unsqueeze(1).to_broadcast([128, 4, n]), ALU.mult),
              reads=[kx, krs], writes=[kh + "a"])
        S.pool(tt(hT[:, 4:8, 0:n], xt[:, 4:8, 0:n], rs[:, 0:n].unsqueeze(1).to_broadcast([128, 4, n]), ALU.mult),
               reads=[kx, krs], writes=[kh + "b"])
        return [kh + "a", kh + "b"]

    def norm_bufs(self, nbuf=1, n=512):
        xt = self.ring("xt", nbuf, [128, 8, n], F32)
        sq = self.ring("sq", 1, [128, 8, n], BF16)
        rs = self.ring("rs", 2, [128, n], F32)
        hT = self.ring("hT", 2, [128, 8, n], BF16)
        return xt, sq, rs, hT

    def proj_resid(self, A, KC, w_d, Xin, Xout):
        S, NT = self.S, self.NT
        with self.phase():
            w = self.sb("wpr", [128, KC, 1024], BF16)
            self.prep_weight(w, "wpr", w_d, KC, [(0, 1024, 0, 1024, None)])
            ar = self.ring("a", 2, [128, KC, 512], BF16)
            xr = self.ring("x", 2, [128, 8, 512], F32)
            A3 = A[0:KC * 128, :].rearrange("(c p) t -> p c t", p=128)
            Xi3, Xo3 = self.x3(Xin), self.x3(Xout)
            psr = Ring([(self.ps[i], self.pk[i]) for i in range(4)])
            def pr_load(j):
                at, ka = ar.next()
                xt, kx = xr.next()
                S.dma(at[:], A3[:, :, j * 512:(j + 1) * 512], writes=[ka], chan=ka)
                S.dma(xt[:], Xi3[:, :, j * 512:(j + 1) * 512], writes=[kx], chan=kx)
                return at, ka, xt, kx
            nxt = pr_load(0)
            for j in range(NT):
                t0 = j * 512
                at, ka, xt, kx = nxt
                if j + 1 < NT:
                    nxt = pr_load(j + 1)
                for d in range(8):
                    p, kp = psr.next()
                    S.pe(mmg([(p[:], w[:, c, d * 128:(d + 1) * 128], at[:, c, :]) for c in range(KC)]),
                         reads=[ka, "wpr"], writes=[kp])
                    S.dve(tt(xt[:, d, :], p[:], xt[:, d, :], ALU.add), reads=[kp, kx], writes=[kx])
                S.dma(Xo3[:, :, t0:t0 + 512], xt[:], reads=[kx], writes=["Xo"], chan=kx + "s")

    def final_norm(self, X):
        S, NT = self.S, self.NT
        with self.phase():
            g = self.load_gain(self.w["norm_final"], "gfin")
            xt, sq, rs, hT = self.norm_bufs(nbuf=2)
            yr = self.ring("y", 2, [128, 8, 512], F32)
            X3, Y3 = self.x3(X), self.x3(self.yT)
            nxt = xt.next()
            self.norm_load(X3, 0, 512, nxt)
            for j in range(NT):
                t0 = j * 512
                x_, kx = nxt
                if j + 1 < NT:
                    nxt = xt.next()
                    self.norm_load(X3, t0 + 512, 512, nxt)
                q_, kq = sq.next()
                r_, kr = rs.next()
                y_, ky = yr.next()
                S.act(actf(q_[:], x_[:], AF.Square), reads=[kx], writes=[kq])
                S.pe(mmg([(self.ps[0][:], self.ones_b, q_[:, c, :]) for c in range(8)]), reads=[kq, "cmb"], writes=["ps0"])
                S.act(actf(r_[:], self.ps[0][:], AF.Sqrt, bias=EPS, scale=1.0 / 1024), reads=["ps0"], writes=[kr])
                S.dve(recip(r_[:], r_[:]), reads=[kr], writes=[kr])
                for c in range(8):
                    S.dve(stt(y_[:, c, :], x_[:, c, :], g[:, c:c + 1], r_[:], ALU.mult, ALU.mult),
                          reads=[kx, kr, "gfin"], writes=[ky])
                S.dma(Y3[:, :, t0:t0 + 512], y_[:], reads=[ky], writes=["Y"], chan=ky + "s")

    def copy_out(self, X):
        S, NT = self.S, self.NT
        with self.phase():
            xr = self.ring("x", 2, [128, 8, 512], F32)
            X3, Y3 = self.x3(X), self.x3(self.yT)
            for j in range(NT):
                x_, kx = xr.next()
                S.dma(x_[:], X3[:, :, j * 512:(j + 1) * 512], writes=[kx], chan=kx)
                S.dma(Y3[:, :, j * 512:(j + 1) * 512], x_[:], reads=[kx], writes=["Y"], chan=kx + "s")

    def ffn_up(self, X, L):
        S, NT = self.S, self.NT
        nc = self.nc
        NB = 2 * (NT - 1)
        with self.phase():
            w = self.sb("wup", [128, 8, 5632], BF16)
            g = self.load_gain(self.w["norm_ffn"][L], "gffn")
            cw = self.sb("cw", [128, 22, 3], F32)
            cb = self.sb("cb", [128, 22], F32)
            for k3 in range(3):
                S.dma(cw[:, :, k3], self.w["ffn_conv_w"][L, k3].rearrange("(f p) -> p f", p=128), writes=["cw"], chan="cw", allow_slow_non_contiguous=True)
            S.dma(cb[:], self.w["ffn_conv_b"][L].rearrange("(f p) -> p f", p=128), writes=["cb"], chan="cb", allow_slow_non_contiguous=True)
            self.prep_weight(w, "wup", self.w["ffn_w_up"][L], 8,
                             [(i * 2048, min(2048, 5632 - i * 2048), i * 2048, min(2048, 5632 - i * 2048), None) for i in range(3)],
                             gain=g, gkey="gffn")
            xt, sq, rs, hT = self.norm_bufs()
            X3 = self.x3(X)
            gh = self.sb("gh", [128, 22, NT, 2], F32)
            S.pool(mset(gh[:], 0.0), writes=["gh"])
            if NB > 0:
                xb = self.sb("xb", [128, 8, NT - 1, 2], F32)
                sqb = self.sb("sqb", [128, 8, NB], BF16)
                rsb = self.sb("rsb", [128, NB], F32)
                hb = self.sb("hb", [128, 8, NB], BF16)
                for c in range(8):
                    src = X3[:, c, 511:511 + 512 * (NT - 1)].rearrange("p (b r) -> p b r", r=512)[:, :, 0:2]
                    S.dma(xb[:, c, :, :], src, writes=["xb%d" % c], chan="xb", allow_slow_non_contiguous=True)
                xbf = xb[:].rearrange("p c b e -> p c (b e)")
                xk = ["xb%d" % c for c in range(8)]
                S.act(actf(sqb[:], xbf, AF.Square), reads=xk, writes=["sqb"])
                S.pe(mmg([(self.ps[0][:, 0:NB], self.ones_b, sqb[:, c, :]) for c in range(8)]), reads=["sqb", "cmb"], writes=["ps0"])
                S.act(actf(rsb[:], self.ps[0][:, 0:NB], AF.Sqrt, bias=EPS, scale=1.0 / 1024), reads=["ps0"], writes=["rsb"])
                S.dve(recip(rsb[:], rsb[:]), reads=["rsb"], writes=["rsb"])
                S.dve(tt(hb[:], xbf, rsb[:].unsqueeze(1).to_broadcast([128, 8, NB]), ALU.mult), reads=xk + ["rsb"], writes=["hb"])
                pr = Ring([(self.ps[1], "ps1"), (self.ps[2], "ps2")])
                for f in range(22):
                    p, kp = pr.next()
                    S.pe(mmg([(p[:, 0:NB], w[:, c, 2816 + f * 128:2816 + (f + 1) * 128], hb[:, c, :]) for c in range(8)]),
                         reads=["hb", "wup"], writes=[kp])
                    pv = p[:, 0:NB].rearrange("p (b e) -> p b e", e=2)
                    S.dve(cp(gh[:, f, 1:NT, 0], pv[:, :, 0]), reads=[kp], writes=["gh"])
                    S.dve(cp(gh[:, f, 0:NT - 1, 1], pv[:, :, 1]), reads=[kp], writes=["gh"])
            ghf = gh[:].rearrange("p f j e -> p f (j e)")
            S.dve(tt(ghf, ghf, self.flb[:].unsqueeze(1).to_broadcast([128, 22, NT * 2]), ALU.mult), reads=["gh", "flb"], writes=["gh"])
            actr = self.ring("act", 2, [128, 22, 512], BF16)
            gbr = self.ring("gb", 2, [128, 514], F32)
            tr_ = self.ring("tc", 2, [128, 512], F32)
            pu = Ring([(self.ps[1], "ps1"), (self.ps[2], "ps2")])
            pg = Ring([(self.ps[3], "ps3"), (self.ps[4], "ps4"), (self.ps[5], "ps5")])
            A3 = self.ACTS.rearrange("(f p) t -> p f t", p=128)
            nxt = xt.next()
            self.norm_load(X3, 0, 512, nxt)
            for j in range(NT):
                t0 = j * 512
                bufs = {"xt": nxt, "sq": sq.next(), "rs": rs.next(), "hT": hT.next(), "ss": (self.ps[0], "ps0")}
                hk = self.norm_tile(X3, t0, 512, bufs, load=False)
                if j + 1 < NT:
                    nxt = xt.next()
                    self.norm_load(X3, t0 + 512, 512, nxt)
                h_ = bufs["hT"][0]
                a_, ka = actr.next()
                for f in range(22):
                    u, ku = pu.next()
                    gp, kg = pg.next()
                    S.pe(mmg([(u[:], w[:, c, f * 128:(f + 1) * 128], h_[:, c, :]) for c in range(8)]), reads=hk + ["wup"], writes=[ku])
                    S.pe(mmg([(gp[:], w[:, c, 2816 + f * 128:2816 + (f + 1) * 128], h_[:, c, :]) for c in range(8)]), reads=hk + ["wup"], writes=[kg])
                    gb, kb = gbr.next()
                    tc, kt = tr_.next()
                    S.act(actf(gb[:, 1:513], gp[:], AF.Copy), reads=[kg], writes=[kb + "m"])
                    S.pool(cp(gb[:, 0:514:513], gh[:, f, j, :]), reads=["gh"], writes=[kb + "h"])
                    S.pool(ts(tc[:], gb[:, 1:513], cw[:, f, 1:2], ALU.mult, cb[:, f:f + 1], ALU.add), reads=[kb + "m", "cw", "cb"], writes=[kt])
                    S.dve(stt(tc[:], gb[:, 0:512], cw[:, f, 0:1], tc[:], ALU.mult, ALU.add), reads=[kb + "m", kb + "h", kt, "cw"], writes=[kt])
                    S.dve(stt(tc[:], gb[:, 2:514], cw[:, f, 2:3], tc[:], ALU.mult, ALU.add), reads=[kb + "m", kb + "h", kt, "cw"], writes=[kt])
                    S.act(actf(tc[:], tc[:], AF.Gelu), reads=[kt], writes=[kt])
                    S.dve(tt(a_[:, f, :], tc[:], u[:], ALU.mult), reads=[kt, ku], writes=[ka])
                S.dma(A3[:, :, t0:t0 + 512], a_[:], reads=[ka], writes=["ACTS"], chan=ka + "s")

    def ab_inproj(self, X, j, L):
        S, NT, nc = self.S, self.NT, self.nc
        with self.phase():
            w = self.sb("wab", [128, 8, 2832], BF16)
            g = self.load_gain(self.w["norm_mix"][L], "gmix")
            stg = self.ring("wstg", 2, [128, 2832], F32)
            src = self.w["ab_w_in"][j]

            def hsplit(ap, nh, half):
                return ap.rearrange("p (h d) -> p h d", d=64)[:, :, half * 32:(half + 1) * 32]

            def h32(ap, nh):
                return ap.rearrange("p (h d) -> p h d", d=32)

            for c in range(8):
                t, k = stg.next()
                S.dma(t[:], src[c * 128:(c + 1) * 128, :], writes=[k], chan=k)
                gc = g[:, c:c + 1]
                engs = ["dve", "pool"]
                for gi in range(2):
                    for half in range(2):
                        S.add(self.rr(engs), ts(h32(w[:, c, gi * 256 + half * 128: gi * 256 + half * 128 + 128], 4),
                                                hsplit(t[:, gi * 256:(gi + 1) * 256], 4, half), gc, ALU.mult),
                              reads=[k, "gmix"], writes=["wab"])
                for half in range(2):
                    S.add(self.rr(engs), ts(h32(w[:, c, 512 + half * 64: 512 + half * 64 + 64], 2),
                                            hsplit(t[:, 512:640], 2, half), gc, ALU.mult), reads=[k, "gmix"], writes=["wab"])
                for (d0, s0, n) in [(1664, 640, 128), (640, 768, 1024), (1808, 1792, 1024), (1792, 2816, 16)]:
                    S.add(self.rr(engs), ts(w[:, c, d0:d0 + n], t[:, s0:s0 + n], gc, ALU.mult), reads=[k, "gmix"], writes=["wab"])
            gq = self.sb("gq", [128, 2], F32)
            gk = self.sb("gk", [128, 2], F32)
            for r in range(4):
                for half in range(2):
                    S.dma(gq[r * 32:(r + 1) * 32, half:half + 1], self.w["attn_q_norm"][j, half * 32:(half + 1) * 32].unsqueeze(1),
                          writes=["gq%d%d" % (r, half)], chan="gqld", allow_slow_non_contiguous=True)
                    S.dma(gk[r * 32:(r + 1) * 32, half:half + 1], self.w["attn_k_norm"][j, half * 32:(half + 1) * 32].unsqueeze(1),
                          writes=["gk%d%d" % (r, half)], chan="gkld", allow_slow_non_contiguous=True)
            gqk = ["gq%d%d" % (r, h) for r in range(4) for h in range(2)]
            gkk = ["gk%d%d" % (r, h) for r in range(4) for h in range(2)]
            S.dve(ts(gq[:], gq[:], 0.125, ALU.mult), reads=gqk, writes=["gq"])
            gbias = self.sb("gbias", [128, 16], F32)
            S.dma(gbias[:], self.w["ab_gate_bias"][j:j + 1, :].to_broadcast([128, 16]), writes=["gbias"], chan="gbias")
            xt, sq, rs, hT = self.norm_bufs()
            X3 = self.x3(X)
            cs = self.ring("cs", 2, [128, 2, 512], F32)
            sqa = self.ring("sqa", 2, [128, 2, 512], BF16)
            rq = self.ring("rq", 2, [128, 512], F32)
            an = self.ring("an", 2, [128, 2, 512], F32)
            t4 = self.ring("t4", 1, [128, 4, 512], F32)
            o12 = self.ring("o12", 3, [128, 2, 512], BF16)
            mqs = self.ring("mqs", 1, [128, 4, 512], BF16)
            mks = self.ring("mks", 1, [128, 4, 512], BF16)
            vts = self.ring("vt", 2, [128, 4, 2, 65], BF16)
            mvs = self.ring("mvt", 2, [128, 4, 4, 129], BF16)
            sgs = self.ring("sg", 1, [128, 4, 512], F32)
            gts = self.ring("gt", 2, [128, 4, 16], F32)
            for (t, k) in vts.items:
                S.pool(mset(t[:], 1.0), writes=[k])
            for (t, k) in mvs.items:
                S.pool(mset(t[:], 1.0), writes=[k])
            pf = Ring([(self.ps[i], self.pk[i]) for i in (1, 2, 3)])
            ptm = Ring([(self.ps[4], "ps4"), (self.ps[5], "ps5")])
            def cs_load(jt_):
                c_, kc = cs.next()
                S.dma(c_[:], self.ropeA[:, :, jt_ * 512:(jt_ + 1) * 512].rearrange("a p t -> p a t"), writes=[kc], chan=kc)
                return c_, kc
            nxt = xt.next()
            self.norm_load(X3, 0, 512, nxt)
            cnxt = cs_load(0)
            for jt in range(NT):
                t0 = jt * 512
                bufs = {"xt": nxt, "sq": sq.next(), "rs": rs.next(), "hT": hT.next(), "ss": (self.ps[0], "ps0")}
                hk = self.norm_tile(X3, t0, 512, bufs, load=False)
                h_ = bufs["hT"][0]
                c_, kc = cnxt
                if jt + 1 < NT:
                    nxt = xt.next()
                    self.norm_load(X3, t0 + 512, 512, nxt)
                    cnxt = cs_load(jt + 1)

                def fm(col0, M):
                    p, kp = pf.next()
                    S.pe(mmg([(p[0:M, :], w[:, c, col0:col0 + M], h_[:, c, :]) for c in range(8)]), reads=hk + ["wab"], writes=[kp])
                    return p, kp

                for grp in range(3):
                    M = 128 if grp < 2 else 64
                    colA = grp * 256 if grp < 2 else 512
                    colB = colA + M
                    gg, ggk = (gq, ["gq"]) if grp < 2 else (gk, gkk)
                    pa, kpa = fm(colA, M)
                    pb, kpb = fm(colB, M)
                    s_, ks = sqa.next()
                    S.act(actf(s_[0:M, 0, :], pa[0:M, :], AF.Square), reads=[kpa], writes=[ks + "a"])
                    S.act(actf(s_[0:M, 1, :], pb[0:M, :], AF.Square), reads=[kpb], writes=[ks + "b"])
                    S.pe(mmg([(self.ps[0][0:M, :], self.bd32_b[0:M, 0:M], s_[0:M, 0, :]),
                              (self.ps[0][0:M, :], self.bd32_b[0:M, 0:M], s_[0:M, 1, :])]), reads=[ks + "a", ks + "b", "cmb"], writes=["ps0"])
                    r_, kr = rq.next()
                    S.act(actf(r_[0:M, :], self.ps[0][0:M, :], AF.Sqrt, bias=EPS, scale=1.0 / 64), reads=["ps0"], writes=[kr])
                    S.dve(recip(r_[0:M, :], r_[0:M, :]), reads=[kr], writes=[kr])
                    a_, kan = an.next()
                    S.dve(stt(a_[0:M, 0, :], pa[0:M, :], gg[0:M, 0:1], r_[0:M, :], ALU.mult, ALU.mult), reads=[kpa, kr] + ggk, writes=[kan + "a"])
                    S.dve(stt(a_[0:M, 1, :], pb[0:M, :], gg[0:M, 1:2], r_[0:M, :], ALU.mult, ALU.mult), reads=[kpb, kr] + ggk, writes=[kan + "b"])
                    t_, kt = t4.next()
                    S.pool(tt(t_[0:M, 0, :], a_[0:M, 0, :], c_[0:M, 0, :], ALU.mult), reads=[kan + "a", kc], writes=[kt + "0"])
                    S.pool(tt(t_[0:M, 1, :], a_[0:M, 1, :], c_[0:M, 1, :], ALU.mult), reads=[kan + "b", kc], writes=[kt + "1"])
                    S.dve(tt(t_[0:M, 2, :], a_[0:M, 1, :], c_[0:M, 0, :], ALU.mult), reads=[kan + "b", kc], writes=[kt + "2"])
                    S.pool(tt(t_[0:M, 3, :], a_[0:M, 0, :], c_[0:M, 1, :], ALU.mult), reads=[kan + "a", kc], writes=[kt + "3"])
                    o_, ko = o12.next()
                    S.pool(tt(o_[0:M, 0, :], t_[0:M, 0, :], t_[0:M, 1, :], ALU.subtract), reads=[kt + "0", kt + "1"], writes=[ko + "0"])
                    S.dve(tt(o_[0:M, 1, :], t_[0:M, 2, :], t_[0:M, 3, :], ALU.add), reads=[kt + "2", kt + "3"], writes=[ko + "1"])
                    for hl in range(M // 32):
                        for half in range(2):
                            if grp < 2:
                                dst = self.QA[grp * 4 + hl, half * 32:(half + 1) * 32, t0:t0 + 512]
                            else:
                                dst = self.KA[hl * 64 + half * 32: hl * 64 + (half + 1) * 32, t0:t0 + 512]
                            S.dma(dst, o_[hl * 32:(hl + 1) * 32, half, :], reads=[ko + str(half)], writes=["QK"], chan=ko + "s%d%d" % (hl, half))
                mq_, kmq = mqs.next()
                mk_, kmk = mks.next()
                for h in range(4):
                    p, kp = fm(640 + h * 128, 128)
                    S.act(actf(mq_[:, h, :], p[:], AF.Copy), reads=[kp], writes=[kmq])
                for h in range(4):
                    p, kp = fm(1152 + h * 128, 128)
                    S.act(actf(mk_[:, h, :], p[:], AF.Copy, scale=float(128 ** -0.5)), reads=[kp], writes=[kmk])
                S.dma(self.MQ[:, :, t0:t0 + 512].rearrange("h d t -> d h t"), mq_[:], reads=[kmq], writes=["MQ"], chan=kmq + "s")
                S.dma(self.MKs[:, :, t0:t0 + 512].rearrange("h d t -> d h t"), mk_[:], reads=[kmk], writes=["MK"], chan=kmk + "s")
                vt, kv = vts.next()
                mv, kmv = mvs.next()
                sg, ksg = sgs.next()
                gt, kgt = gts.next()
                for sub in range(4):
                    hs = lambda c: h_[:, c, sub * 128:(sub + 1) * 128]
                    p1, k1 = ptm.next()
                    S.pe(mmg([(p1[:, 0:144], hs(c), w[:, c, 1664:1808]) for c in range(8)]), reads=hk + ["wab"], writes=[k1])
                    S.act(actf(vt[:, sub, :, 0:64], p1[:, 0:128].rearrange("p (h d) -> p h d", d=64), AF.Copy), reads=[k1], writes=[kv])
                    S.dve(tt(gt[:, sub, :], p1[:, 128:144], gbias[:], ALU.add), reads=[k1, "gbias"], writes=[kgt])
                    p2, k2 = ptm.next()
                    S.pe(mmg([(p2[:], hs(c), w[:, c, 1808:2320]) for c in range(8)]), reads=hk + ["wab"], writes=[k2])
                    S.dve(cp(mv[:, sub, :, 0:128], p2[:].rearrange("p (h d) -> p h d", d=128)), reads=[k2], writes=[kmv])
                    p3, k3 = ptm.next()
                    S.pe(mmg([(p3[:], hs(c), w[:, c, 2320:2832]) for c in range(8)]), reads=hk + ["wab"], writes=[k3])
                    S.act(actf(sg[:, sub, :], p3[:], AF.Sigmoid), reads=[k3], writes=[ksg])
                S.dma(self.VA[t0:t0 + 512, :].rearrange("(s p) c -> p s c", p=128), vt[:].rearrange("p s h d -> p s (h d)"), reads=[kv], writes=["VA"], chan=kv + "s")
                S.dma(self.MV[t0:t0 + 512, :].rearrange("(s p) c -> p s c", p=128), mv[:].rearrange("p s h d -> p s (h d)"), reads=[kmv], writes=["MV"], chan=kmv + "s")
                S.dma(self.SG[t0:t0 + 512, :].rearrange("(s p) c -> p s c", p=128), sg[:], reads=[ksg], writes=["SG"], chan=ksg + "s")
                S.dma(self.GT[t0:t0 + 512, :].rearrange("(s p) c -> p s c", p=128), gt[:], reads=[kgt], writes=["GT"], chan=kgt + "s")

    def attention(self):
        S, NT, NCH, T, nc = self.S, self.NT, self.NCH, self.T, self.nc
        with self.phase(psum=False):
            ph = self._ph
            ppr = Ring([(ph.enter_context(nc.psum_tensor(self.un("pp"), [128, 1024], F32)), "pp%d" % i) for i in range(2)])
            poa = (ph.enter_context(nc.psum_tensor(self.un("poa"), [128, 512], F32)), "poa")
            pob = (ph.enter_context(nc.psum_tensor(self.un("pob"), [128, 512], F32)), "pob")
            ptb = ph.enter_context(nc.psum_tensor(self.un("ptb"), [128, 1024], BF16))
            K = self.sb("Kall", [128, T], BF16)
            V = self.sb("Vall", [128, NCH, 130], BF16)
            S.dma(K[:], self.KA, writes=["K"], chan="K")
            S.dma(V[:], self.VA.rearrange("(b p) c -> p b c", p=128), writes=["V"], chan="V")
            qr = self.ring("q", 2, [128, 4, 512], BF16)
            pr = self.ring("p", 3, [128, 1024], BF16)
            rcr = self.ring("rc", 2, [128, 4], F32)
            otr = self.ring("ot", 2, [128, 4, 8, 64], BF16)
            ob = self.ring("ob", 2, [128, 4, 512], BF16)
            items = [(jq, hp, kb) for jq in range(NT) for hp in range(4) for kb in range(NCH)]
            tctx, ictx = {}, {}

            def stage1(i):
                jq, hp, kb = items[i]
                t0 = jq * 512
                if hp == 0 and kb == 0:
                    q_, kq = qr.next()
                    for kvh in range(2):
                        S.dma(q_[kvh * 64:(kvh + 1) * 64, :, :], self.QA[kvh * 4:(kvh + 1) * 4, :, t0:t0 + 512].rearrange("h d t -> d h t"),
                              writes=[kq + str(kvh)], chan=kq + str(kvh))
                    tctx[jq] = (q_, kq) + otr.next()
                q_, kq, ot, kot = tctx[jq]
                pb, kpb = ppr.next()

                def st2(e, pb=pb, q_=q_, hp=hp, kb=kb):
                    e.matmul(pb[:, 0:512], lhsT=K[0:64, kb * 128:(kb + 1) * 128], rhs=q_[0:64, hp, :], start=True, stop=True)
                    return e.matmul(pb[:, 512:1024], lhsT=K[64:128, kb * 128:(kb + 1) * 128], rhs=q_[64:128, hp, :], start=True, stop=True)
                S.pe(st2, reads=["K", kq + "0", kq + "1"], writes=[kpb])
                p_, kp_ = pr.next()
                S.act(actf(p_[:], pb[:], AF.Exp, bias=self.mb[:, jq, (kb // 4):(kb // 4) + 1]), reads=[kpb, "mb"], writes=[kp_])
                ictx[i] = (p_, kp_)

            def stage2(i):
                jq, hp, kb = items[i]
                t0 = jq * 512
                q_, kq, ot, kot = tctx[jq]
                p_, kp_ = ictx.pop(i)

                def pv8(e, p_=p_, kb=kb):
                    ins = None
                    for hh, (po, _) in enumerate((poa, pob)):
                        for sub in range(4):
                            ins = e.matmul(po[:, sub * 65:(sub + 1) * 65], lhsT=p_[:, hh * 512 + sub * 128: hh * 512 + (sub + 1) * 128],
                                           rhs=V[:, kb, hh * 65:(hh + 1) * 65], start=(kb == 0 and sub == 0), stop=(kb == NCH - 1 and sub == 3))
                    return ins
                S.pe(pv8, reads=["V", kp_], writes=["poa", "pob"])
                if kb != NCH - 1:
                    return
                for hh, (po, kpo) in enumerate((poa, pob)):
                    h = hh * 4 + hp
                    pv = po[:, 0:260].rearrange("p (s c) -> p s c", c=65)
                    r_, kr = rcr.next()
                    S.dve(recip(r_[:], pv[:, :, 64]), reads=[kpo], writes=[kr])
                    S.dve(tt(ot[:, :, h, :], pv[:, :, 0:64], r_[:].unsqueeze(2).to_broadcast([128, 4, 64]), ALU.mult), reads=[kpo, kr], writes=[kot])
                if hp != 3:
                    return
                o_, ko = ob.next()
                for sp in range(2):
                    def tr8(e, ot=ot, sp=sp):
                        ins = None
                        for hp2 in range(4):
                            for s_ in range(2):
                                slot = hp2 * 2 + s_
                                ins = e.transpose(ptb[:, slot * 128:(slot + 1) * 128],
                                                  ot[:, sp * 2 + s_, 2 * hp2:2 * hp2 + 2, :].rearrange("p h d -> p (h d)"), self.ident_b)
                        return ins
                    S.pe(tr8, reads=[kot, "cmb"], writes=["ptb"])
                    S.act(actf(o_[:, :, sp * 256:(sp + 1) * 256].rearrange("p a (s q) -> p a s q", q=128),
                               ptb[:].rearrange("p (a s q) -> p a s q", a=4, s=2), AF.Copy), reads=["ptb"], writes=[ko])
                S.dma(self.MIX[0:512, t0:t0 + 512].rearrange("(a p) t -> p a t", p=128), o_[:], reads=[ko], writes=["MIXa"], chan=ko + "s")

            self.pipeline(len(items), stage1, stage2, 1)

    def mlstm(self, j):
        S, NT, NCH, T, nc = self.S, self.NT, self.NCH, self.T, self.nc
        NG = NCH * 4
        with self.phase():
            G = self.sb("G", [128, NCH, 16], F32)
            S.dma(G[:], self.GT.rearrange("(c p) g -> p c g", p=128), writes=["G"], chan="G")
            G5 = G[:].rearrange("p c (d k h) -> p d c k h", d=2, k=2, h=4)
            gi, gf = G5[:, :, :, 0, :], G5[:, :, :, 1, :]
            shp = [128, 2, NCH, 4]
            mk = lambda n: self.sb(n, shp, F32)
            FL, Bc, A_, AMr, BLr, Mt, mt, WK, THR, KEEP = [mk(n) for n in ("FL", "Bc", "Aa", "AMr", "BLr", "Mt", "mt", "WK", "THR", "KEEP")]
            tmp = mk("tmp")
            fl2 = lambda t, d: t[:, d, :, :].rearrange("p c h -> p (c h)")
            S.act(actf(tmp[:], gf, AF.Abs), reads=["G"], writes=["tmp"])
            S.act(actf(tmp[:], tmp[:], AF.Exp, scale=-1.0), reads=["tmp"], writes=["tmp"])
            S.act(actf(tmp[:], tmp[:], AF.Ln, bias=1.0), reads=["tmp"], writes=["tmp"])
            S.dve(ts(FL[:], gf, 0.0, ALU.min), reads=["G"], writes=["FL"])
            S.dve(tt(FL[:], FL[:], tmp[:], ALU.subtract), reads=["FL", "tmp"], writes=["FL"])
            bw = min(128, NG)
            nblk = NG // bw
            am = self.sb("am", [128, 2, nblk], F32)
            dg = self.sb("dg", [128, 128], F32)
            for d in range(2):
                tri = self.cm[:, C_TRIU, :] if d == 0 else self.cm[:, C_TRIL, :]
                S.pe(mm1(self.ps[0][:, 0:NG], tri, fl2(FL, d)), reads=["FL", "cm"], writes=["ps0"])
                S.act(actf(fl2(Bc, d), self.ps[0][:, 0:NG], AF.Copy), reads=["ps0"], writes=["Bc"])
                S.pe(mm1(self.ps[1][:, 0:NG], self.ones_f, fl2(FL, d)), reads=["FL", "cm"], writes=["ps1"])
                S.act(actf(fl2(BLr, d), self.ps[1][:, 0:NG], AF.Copy), reads=["ps1"], writes=["BLr"])
                S.dve(tt(A_[:, d], gi[:, d], Bc[:, d], ALU.subtract), reads=["G", "Bc"], writes=["Aa"])
                for b in range(nblk):
                    S.pe(tr(self.ps[2][0:bw, 0:128], fl2(A_, d)[:, b * bw:(b + 1) * bw], self.ident_f), reads=["Aa", "cm"], writes=["ps2"])
                    S.dve(lambda e, d=d, b=b, ps2=self.ps[2]: e.tensor_reduce(out=am[0:bw, d, b:b + 1], in_=ps2[0:bw, 0:128], axis=AX.X, op=ALU.max),
                          reads=["ps2"], writes=["am"])
                    S.dve(ts(dg[0:bw, 0:bw], self.ident_f[0:bw, 0:bw], am[0:bw, d, b:b + 1], ALU.mult), reads=["am", "cm"], writes=["dg"])
                    S.pe(mm1(self.ps[3][:, 0:bw], self.ones_f[0:bw, :], dg[0:bw, 0:bw]), reads=["dg", "cm"], writes=["ps3"])
                    S.act(actf(fl2(AMr, d)[:, b * bw:(b + 1) * bw], self.ps[3][:, 0:bw], AF.Copy), reads=["ps3"], writes=["AMr"])
            mrun = self.sb("mrun", [128, 2, 4], F32)
            S.dve(mset(mrun[:], 0.0), writes=["mrun0", "mrun1"])
            for k in range(NCH):
                for d in range(2):
                    eng = "dve"
                    kk = "rec%d" % d
                    c = k if d == 0 else NCH - 1 - k
                    S.add(eng, ts(mt[:, d, c, :], mrun[:, d, :], self.carry[:, d, c:c + 1], ALU.mult), reads=["mrun%d" % d, "carry"], writes=[kk + "mt"])
                    S.add(eng, tt(Mt[:, d, c, :], mt[:, d, c, :], AMr[:, d, c, :], ALU.max), reads=[kk + "mt", "AMr"], writes=[kk + "Mt"])
                    S.add(eng, tt(mrun[:, d, :], Mt[:, d, c, :], BLr[:, d, c, :], ALU.add), reads=[kk + "Mt", "BLr"], writes=["mrun%d" % d])
            rk = ["rec0mt", "rec1mt", "rec0Mt", "rec1Mt"]
            S.dve(tt(KEEP[:], mt[:], Mt[:], ALU.subtract), reads=rk, writes=["KEEP"])
            S.act(actf(KEEP[:], KEEP[:], AF.Exp), reads=["KEEP"], writes=["KEEP"])
            S.dve(tt(KEEP[:].rearrange("p d c h -> p (d c) h"), KEEP[:].rearrange("p d c h -> p (d c) h"),
                     self.carry[:].rearrange("p d c -> p (d c)").unsqueeze(2).to_broadcast([128, 2 * NCH, 4]), ALU.mult), reads=["KEEP", "carry"], writes=["KEEP"])
            S.dve(tt(WK[:], A_[:], Mt[:], ALU.subtract), reads=["Aa"] + rk, writes=["WK"])
            S.act(actf(WK[:], WK[:], AF.Exp), reads=["WK"], writes=["WK"])
            S.dve(tt(THR[:], Bc[:], Mt[:], ALU.add), reads=["Bc"] + rk, writes=["THR"])
            S.act(actf(THR[:], THR[:], AF.Exp, scale=-1.0), reads=["THR"], writes=["THR"])
            maskf = self.cm[:, C_TRIU, :]
            maskb = self.cm[:, C_TRIL, :]
            qr = [self.ring("mq", 2, [128, 4, 512], BF16) for d in range(2)]
            kr = [self.ring("mk", 2, [128, 4, 512], BF16) for d in range(2)]
            vr = [self.ring("mv", 2, [128, 4, 516], BF16) for d in range(2)]
            Cst = [[self.sb("Cst", [128, 129], F32) for h in range(4)] for d in range(2)]
            Cbf = [[self.sb("Cbf", [128, 129], BF16) for h in range(4)] for d in range(2)]
            for d in range(2):
                for h in range(4):
                    S.pool(mset(Cst[d][h][:], 0.0), writes=["C%d%d" % (d, h)])
            atr = self.ring("at", 6, [128, 128], BF16)
            kwr = self.ring("kw", 6, [128, 128], BF16)
            rr_ = self.ring("r", 6, [128, 2], F32)
            hst = [self.ring("hst", 2, [128, 512], F32) for d in range(2)]
            pS = Ring([(self.ps[0], "ps0"), (self.ps[1], "ps1")])
            pH = Ring([(self.ps[2], "ps2"), (self.ps[3], "ps3")])
            pU = Ring([(self.ps[4], "ps4"), (self.ps[5], "ps5")])
            pT = Ring([(self.psb[0], "psb0"), (self.psb[1], "psb1")])
            MQ3 = self.MQ.rearrange("h d t -> d h t")
            MK3 = self.MKs.rearrange("h d t -> d h t")
            HO = [self.HF, self.HB]
            items = [(k, h, d) for k in range(NCH) for h in range(4) for d in range(2)]
            tctx, cctx, ictx = {}, {}, {}

            def geom(it):
                k, h, d = it
                c = k if d == 0 else NCH - 1 - k
                return d, c // 4, c % 4, h, c, (c // 4) * 512

            def ensure(d, jt):
                if (d, jt) in tctx or jt < 0 or jt >= NT:
                    return
                t0 = jt * 512
                q_, kq = qr[d].next()
                k_, kk = kr[d].next()
                v_, kv = vr[d].next()
                S.dma(q_[:], MQ3[:, :, t0:t0 + 512], writes=[kq], chan=kq)
                S.dma(k_[:], MK3[:, :, t0:t0 + 512], writes=[kk], chan=kk)
                S.dma(v_[:], self.MV[t0:t0 + 512, :].rearrange("(s p) c -> p s c", p=128), writes=[kv], chan=kv)
                tctx[(d, jt)] = (q_, kq, k_, kk, v_, kv)

            def stage1(i):
                d, jt, sub, h, c, t0 = geom(items[i])
                mask = maskf if d == 0 else maskb
                if h == 0:
                    ensure(d, jt)
                    if items[i][0] % 4 == 1:
                        ensure(d, jt + (1 if d == 0 else -1))
                    cctx[(d, c)] = hst[d].next()
                q_, kq, k_, kk, v_, kv = tctx[(d, jt)]
                ck = "C%d%d" % (d, h)
                qs = q_[:, h, sub * 128:(sub + 1) * 128]
                ks_ = k_[:, h, sub * 128:(sub + 1) * 128]
                wkc = WK[:, d, c, h:h + 1]
                st_, kst = pS.next()
                S.pe(mm1(st_[:, 0:128], ks_, qs), reads=[kk, kq], writes=[kst])
                at, kat = atr.next()
                S.dve(stt(at[:], st_[:, 0:128], wkc, mask, ALU.mult, ALU.mult), reads=[kst, "WK", "cm"], writes=[kat])
                ptr, kptr = pT.next()
                S.pe(tr(ptr[:, 0:128], ks_, self.ident_b), reads=[kk, "cmb"], writes=[kptr])
                kw, kkw = kwr.next()
                S.act(actf(kw[:], ptr[:, 0:128], AF.Copy, scale=wkc), reads=[kptr, "WK"], writes=[kkw])
                S.pool(ts(Cst[d][h][:], Cst[d][h][:], KEEP[:, d, c, h:h + 1], ALU.mult), reads=[ck, "KEEP"], writes=[ck])
                S.pool(cp(Cbf[d][h][:], Cst[d][h][:]), reads=[ck], writes=[ck + "b"])
                ictx[i] = (at, kat, kw, kkw)

            def stage2(i):
                d, jt, sub, h, c, t0 = geom(items[i])
                q_, kq, k_, kk, v_, kv = tctx[(d, jt)]
                hs_, khs = cctx[(d, c)]
                at, kat, kw, kkw = ictx.pop(i)
                ck = "C%d%d" % (d, h)
                tc0 = t0 + sub * 128
                qs = q_[:, h, sub * 128:(sub + 1) * 128]
                vs = v_[:, sub, h * 129:(h + 1) * 129]
                ph, kph = pH.next()
                S.pe(mmg([(ph[:, 0:129], at[:], vs), (ph[:, 0:129], qs, Cbf[d][h][:])]), reads=[kat, kv, kq, ck + "b"], writes=[kph])
                pu, kpu = pU.next()
                S.pe(mm1(pu[:, 0:129], kw[:], vs), reads=[kkw, kv], writes=[kpu])
                S.dve(tt(Cst[d][h][:], Cst[d][h][:], pu[:, 0:129], ALU.add), reads=[ck, kpu], writes=[ck])
                r_, kr_ = rr_.next()
                S.dve(ts(r_[:, 0:1], ph[:, 128:129], -1.0, ALU.mult, THR[:, d, c, h:h + 1], ALU.max), reads=[kph, "THR"], writes=[kr_])
                S.dve(tt(r_[:, 0:1], r_[:, 0:1], ph[:, 128:129], ALU.max), reads=[kr_, kph], writes=[kr_])
                S.dve(recip(r_[:, 1:2], r_[:, 0:1]), reads=[kr_], writes=[kr_])
                S.act(actf(hs_[:, h * 128:(h + 1) * 128], ph[:, 0:128], AF.Copy, scale=r_[:, 1:2]), reads=[kph, kr_], writes=[khs])
                if h == 3:
                    S.dma(HO[d][tc0:tc0 + 128, :], hs_[:], reads=[khs], writes=["HO"], chan=khs + "s")

            self.pipeline(len(items), stage1, stage2, 2)

        with self.phase():
            gain = self.sb("ogain", [128, 512], F32)
            S.dma(gain[:], self.w["mlstm_out_norm"][j:j + 1, :].to_broadcast([128, 512]), writes=["ogain"], chan="ogain")
            hfr = self.ring("hf", 3, [128, 512], F32)
            hbr = self.ring("hb", 3, [128, 512], F32)
            sgr = self.ring("sgl", 3, [128, 512], F32)
            ssq = self.ring("ssq", 3, [128, 8], F32)
            junk = self.ring("junk", 2, [128, 128], F32)
            ymr = self.ring("ym", 3, [128, 512], BF16)
            mxs = self.ring("mxs", 2, [128, 4, 512], BF16)
            pT = Ring([(self.psb[0], "psb0"), (self.psb[1], "psb1")])

            def ld(c):
                hf, khf = hfr.next()
                hb, khb = hbr.next()
                sg, ksg = sgr.next()
                S.dma(hf[:], self.HF[c * 128:(c + 1) * 128, :], writes=[khf], chan=khf)
                S.dma(hb[:], self.HB[c * 128:(c + 1) * 128, :], writes=[khb], chan=khb)
                S.dma(sg[:], self.SG[c * 128:(c + 1) * 128, :], writes=[ksg], chan=ksg)
                return hf, khf, hb, khb, sg, ksg
            pend = [ld(0), ld(1)] if NCH > 1 else [ld(0)]
            for c in range(NCH):
                hf, khf, hb, khb, sg, ksg = pend.pop(0)
                if c + 2 < NCH:
                    pend.append(ld(c + 2))
                sub = c % 4
                if sub == 0:
                    mx, kmx = mxs.next()
                S.dve(tt(hf[:], hf[:], hb[:], ALU.add), reads=[khf, khb], writes=[khf])
                sq_, ksq = ssq.next()
                jk, kjk = junk.next()
                S.dve(mset(sq_[:], 0.0), writes=[ksq + "a", ksq + "b"])
                for hh in range(4):
                    S.act(actf(jk[:], hf[:, hh * 128:(hh + 1) * 128], AF.Square, accum=sq_[:, hh:hh + 1]), reads=[khf], writes=[kjk, ksq + "a"])
                S.act(actf(sq_[:, 4:8], sq_[:, 0:4], AF.Sqrt, bias=EPS, scale=1.0 / 128), reads=[ksq + "a"], writes=[ksq + "b"])
                S.dve(recip(sq_[:, 4:8], sq_[:, 4:8]), reads=[ksq + "b"], writes=[ksq + "b"])
                S.dve(tt(hf[:].rearrange("p (h e) -> p h e", e=128), hf[:].rearrange("p (h e) -> p h e", e=128),
                         sq_[:, 4:8].unsqueeze(2).to_broadcast([128, 4, 128]), ALU.mult), reads=[khf, ksq + "b"], writes=[khf])
                S.pool(tt(sg[:], sg[:], gain[:], ALU.mult), reads=[ksg, "ogain"], writes=[ksg])
                ym, kym = ymr.next()
                S.pool(tt(ym[:], hf[:], sg[:], ALU.mult), reads=[khf, ksg], writes=[kym])
                ptr, kptr = pT.next()

                def tr4(e, ptr=ptr, ym=ym):
                    ins = None
                    for hh in range(4):
                        ins = e.transpose(ptr[:, hh * 128:(hh + 1) * 128], ym[:, hh * 128:(hh + 1) * 128], self.ident_b)
                    return ins
                S.pe(tr4, reads=[kym, "cmb"], writes=[kptr])
                S.act(actf(mx[:, :, sub * 128:(sub + 1) * 128], ptr[:, 0:512].rearrange("p (h e) -> p h e", e=128), AF.Copy), reads=[kptr], writes=[kmx])
                if sub == 3:
                    t0 = (c // 4) * 512
                    S.dma(self.MIX[512:1024, t0:t0 + 512].rearrange("(h p) t -> p h t", p=128), mx[:], reads=[kmx], writes=["MIXb"], chan=kmx + "s")

    def ret_inproj(self, X, j, L):
        S, NT, nc = self.S, self.NT, self.nc
        with self.phase():
            w = self.sb("wret", [128, 8, 6144], BF16)
            g = self.load_gain(self.w["norm_mix"][L], "gmix")
            self.prep_weight(w, "wret", self.w["ret_w_in"][j], 8, [(i * 2048, 2048, i * 2048, 2048, None) for i in range(3)], gain=g, gkey="gmix")
            xt, sq, rs, hT = self.norm_bufs()
            X3 = self.x3(X)
            cs = self.ring("cs", 1, [128, 4, 512], F32)
            t4 = self.ring("t4", 1, [128, 4, 512], F32)
            o12 = self.ring("o12", 2, [128, 2, 512], BF16)
            vts = self.ring("rv", 1, [128, 4, 2048], BF16)
            gts = self.ring("rg", 1, [128, 2048], F32)
            pf = Ring([(self.ps[i], self.pk[i]) for i in (1, 2, 3)])
            ptm = Ring([(self.ps[4], "ps4"), (self.ps[5], "ps5")])
            def cs_load(jt_):
                c_, kc = cs.next()
                S.dma(c_[:, 0:2, :], self.ropeR[:, :, jt_ * 512:(jt_ + 1) * 512].rearrange("a p t -> p a t"), writes=[kc, kc + "k"], chan=kc)
                return c_, kc
            nxt = xt.next()
            self.norm_load(X3, 0, 512, nxt)
            cnxt = cs_load(0)
            for jt in range(NT):
                t0 = jt * 512
                bufs = {"xt": nxt, "sq": sq.next(), "rs": rs.next(), "hT": hT.next(), "ss": (self.ps[0], "ps0")}
                hk = self.norm_tile(X3, t0, 512, bufs, load=False)
                if jt + 1 < NT:
                    nxt = xt.next()
                    self.norm_load(X3, t0 + 512, 512, nxt)
                h_ = bufs["hT"][0]
                c_, kc = cnxt
                S.act(actf(c_[:, 2:4, :], c_[:, 0:2, :], AF.Copy, scale=1.0 / 16), reads=[kc], writes=[kc + "k"])
                for qk in range(2):
                    co = 0 if qk == 0 else 2
                    ck_ = [kc] if qk == 0 else [kc + "k"]
                    dstT = self.RQ if qk == 0 else self.RK
                    for h in range(4):
                        col = qk * 1024 + h * 256
                        pa, kpa = pf.next()
                        S.pe(mmg([(pa[:], w[:, c, col:col + 128], h_[:, c, :]) for c in range(8)]), reads=hk + ["wret"], writes=[kpa])
                        pb, kpb = pf.next()
                        S.pe(mmg([(pb[:], w[:, c, col + 128:col + 256], h_[:, c, :]) for c in range(8)]), reads=hk + ["wret"], writes=[kpb])
                        t_, kt = t4.next()
                        S.dve(tt(t_[:, 0, :], pa[:], c_[:, co, :], ALU.mult), reads=[kpa] + ck_, writes=[kt + "0"])
                        S.dve(tt(t_[:, 1, :], pb[:], c_[:, co + 1, :], ALU.mult), reads=[kpb] + ck_, writes=[kt + "1"])
                        S.dve(tt(t_[:, 2, :], pb[:], c_[:, co, :], ALU.mult), reads=[kpb] + ck_, writes=[kt + "2"])
                        S.dve(tt(t_[:, 3, :], pa[:], c_[:, co + 1, :], ALU.mult), reads=[kpa] + ck_, writes=[kt + "3"])
                        o_, ko = o12.next()
                        S.pool(tt(o_[:, 0, :], t_[:, 0, :], t_[:, 1, :], ALU.subtract), reads=[kt + "0", kt + "1"], writes=[ko])
                        S.pool(tt(o_[:, 1, :], t_[:, 2, :], t_[:, 3, :], ALU.add), reads=[kt + "2", kt + "3"], writes=[ko])
                        S.dma(dstT[2 * h:2 * h + 2, :, t0:t0 + 512].rearrange("a p t -> p a t"), o_[:], reads=[ko], writes=["RQK"], chan=ko + "s")
                if jt + 1 < NT:
                    cnxt = cs_load(jt + 1)
                vt, kv = vts.next()
                for sub in range(4):
                    hs = lambda c: h_[:, c, sub * 128:(sub + 1) * 128]
                    for blk in range(4):
                        p1, k1 = ptm.next()
                        S.pe(mmg([(p1[:], hs(c), w[:, c, 2048 + blk * 512:2048 + (blk + 1) * 512]) for c in range(8)]), reads=hk + ["wret"], writes=[k1])
                        S.act(actf(vt[:, sub, blk * 512:(blk + 1) * 512], p1[:], AF.Copy), reads=[k1], writes=[kv])
                    gt, kg = gts.next()
                    for blk in range(4):
                        p1, k1 = ptm.next()
                        S.pe(mmg([(p1[:], hs(c), w[:, c, 4096 + blk * 512:4096 + (blk + 1) * 512]) for c in range(8)]), reads=hk + ["wret"], writes=[k1])
                        S.act(actf(gt[:, blk * 512:(blk + 1) * 512], p1[:], AF.Silu), reads=[k1], writes=[kg])
                    S.dma(self.RG[t0 + sub * 128:t0 + (sub + 1) * 128, :], gt[:], reads=[kg], writes=["RG"], chan=kg + "s")
                S.dma(self.RV[t0:t0 + 512, :].rearrange("(s p) c -> p s c", p=128), vt[:], reads=[kv], writes=["RV"], chan=kv + "s")

    def retention(self, j):
        S, NT, NCH, T, nc = self.S, self.NT, self.NCH, self.T, self.nc
        with self.phase():
            lg = self.sb("lg", [128, 8], F32)
            tmp = self.sb("lgt", [128, 8], F32)
            S.dma(lg[:], self.w["ret_decay_logit"][j:j + 1].rearrange("a d h -> a (d h)").to_broadcast([128, 8]), writes=["lg"], chan="lg")
            S.act(actf(tmp[:], lg[:], AF.Abs), reads=["lg"], writes=["lgt"])
            S.act(actf(tmp[:], tmp[:], AF.Exp, scale=-1.0), reads=["lgt"], writes=["lgt"])
            S.act(actf(tmp[:], tmp[:], AF.Ln, bias=1.0), reads=["lgt"], writes=["lgt"])
            S.dve(ts(lg[:], lg[:], 0.0, ALU.min), reads=["lg"], writes=["lg"])
            S.dve(tt(lg[:], lg[:], tmp[:], ALU.subtract), reads=["lg", "lgt"], writes=["lg"])
            DT = self.sb("DT", [128, 8, 128], F32)
            QD = self.sb("QD", [128, 8, 128], F32)
            QDb = self.sb("QDb", [128, 8, 128], BF16)
            KD = self.sb("KD", [128, 8], F32)
            CDC = self.sb("CDC", [128, 8, NCH], F32)
            cdv = self.sb("cdv", [128, 8], F32)
            for hd in range(8):
                d = hd // 4
                S.act(actf(DT[:, hd, :], self.cm[:, C_DFW if d == 0 else C_DBW, :], AF.Exp, scale=lg[:, hd:hd + 1]), reads=["lg", "cm"], writes=["DT"])
                S.dve(tt(DT[:, hd, :], DT[:, hd, :], self.cm[:, C_TRIU if d == 0 else C_SL, :], ALU.mult), reads=["DT", "cm"], writes=["DT"])
                S.act(actf(QD[:, hd, :], self.cm[:, C_QDF if d == 0 else C_QDB, :], AF.Exp, scale=lg[:, hd:hd + 1]), reads=["lg", "cm"], writes=["QD"])
                S.dve(cp(QDb[:, hd, :], QD[:, hd, :]), reads=["QD"], writes=["QDb"])
                S.act(actf(KD[:, hd:hd + 1], self.cm[:, C_KD, d:d + 1], AF.Exp, scale=lg[:, hd:hd + 1]), reads=["lg", "cm"], writes=["KD"])
                S.act(actf(cdv[:, hd:hd + 1], self.cm[:, C_KD, 2:3], AF.Exp, scale=lg[:, hd:hd + 1]), reads=["lg", "cm"], writes=["cdv"])
                S.dve(ts(CDC[:, hd, :], self.carry[:, d, :], cdv[:, hd:hd + 1], ALU.mult), reads=["cdv", "carry"], writes=["CDC"])
            qr = [self.ring("rq", 2, [128, 8, 512], BF16) for d in range(2)]
            kr = [self.ring("rk", 2, [128, 8, 512], BF16) for d in range(2)]
            vr = [self.ring("rv", 3, [128, 2048], BF16) for d in range(2)]
            St = [[[self.sb("St", [128, 512], F32) for c in range(2)] for h in range(4)] for d in range(2)]
            Sb = [[[self.sb("Sb", [128, 512], BF16) for c in range(2)] for h in range(4)] for d in range(2)]
            for d in range(2):
                for h in range(4):
                    for c in range(2):
                        S.pool(mset(St[d][h][c][:], 0.0), writes=["S%d%d%d" % (d, h, c)])
            atr = self.ring("at", 6, [128, 128], BF16)
            kwr = self.ring("kw", 6, [128, 2, 128], BF16)
            qdr = self.ring("qd", 6, [128, 2, 128], BF16)
            yst = [self.ring("yst", 2, [128, 2048], F32) for d in range(2)]
            pS = Ring([(self.ps[0][:, 0:128], "ps0"), (self.ps[5][:, 0:128], "ps5")])
            pO = Ring([(self.ps[1], "ps1"), (self.ps[2], "ps2")])
            pU = Ring([(self.ps[3], "ps3"), (self.ps[4], "ps4")])
            pT = Ring([(self.psb[0], "psb0"), (self.psb[1], "psb1")])
            RQ3 = self.RQ.rearrange("a p t -> p a t")
            RK3 = self.RK.rearrange("a p t -> p a t")
            YO = [self.YF, self.YB]
            items = [(k, h, d) for k in range(NCH) for h in range(4) for d in range(2)]
            tctx, cctx, ictx = {}, {}, {}

            def geom(it):
                k, h, d = it
                c = k if d == 0 else NCH - 1 - k
                return d, c // 4, c % 4, h, c, (c // 4) * 512

            def ensure(d, jt):
                if (d, jt) in tctx or jt < 0 or jt >= NT:
                    return
                t0 = jt * 512
                q_, kq = qr[d].next()
                k_, kk = kr[d].next()
                S.dma(q_[:], RQ3[:, :, t0:t0 + 512], writes=[kq], chan=kq)
                S.dma(k_[:], RK3[:, :, t0:t0 + 512], writes=[kk], chan=kk)
                tctx[(d, jt)] = (q_, kq, k_, kk)

            def ensure_v(d, c):
                if (d, c) in cctx or c < 0 or c >= NCH:
                    return
                v_, kv = vr[d].next()
                S.dma(v_[:], self.RV[c * 128:(c + 1) * 128, :], writes=[kv], chan=kv)
                cctx[(d, c)] = (v_, kv) + yst[d].next()

            def stage1(i):
                d, jt, sub, h, c, t0 = geom(items[i])
                if h == 0:
                    ensure(d, jt)
                    if items[i][0] % 4 == 1:
                        ensure(d, jt + (1 if d == 0 else -1))
                    ensure_v(d, c)
                    ensure_v(d, c + (1 if d == 0 else -1))
                q_, kq, k_, kk = tctx[(d, jt)]
                sl = slice(sub * 128, (sub + 1) * 128)
                hd = d * 4 + h
                st_, kst = pS.next()
                S.pe(mmg([(st_, k_[:, 2 * h + cc, sl], q_[:, 2 * h + cc, sl]) for cc in range(2)]), reads=[kk, kq], writes=[kst])
                at, kat = atr.next()
                S.dve(tt(at[:], st_, DT[:, hd, :], ALU.mult), reads=[kst, "DT"], writes=[kat])
                kw, kkw = kwr.next()
                qd, kqd = qdr.next()
                ptr, kptr = pT.next()

                def tr2(e, ptr=ptr, k_=k_, h=h, sl=sl):
                    ins = None
                    for cc in range(2):
                        ins = e.transpose(ptr[:, cc * 128:(cc + 1) * 128], k_[:, 2 * h + cc, sl], self.ident_b)
                    return ins
                S.pe(tr2, reads=[kk, "cmb"], writes=[kptr])
                S.act(actf(kw[:], ptr[:, 0:256].rearrange("p (c e) -> p c e", e=128), AF.Copy, scale=KD[:, hd:hd + 1]), reads=[kptr, "KD"], writes=[kkw])
                S.pool(tt(qd[:], q_[:, 2 * h:2 * h + 2, sl], QDb[:, hd, :].unsqueeze(1).to_broadcast([128, 2, 128]), ALU.mult), reads=[kq, "QDb"], writes=[kqd])
                for cc in range(2):
                    sk = "S%d%d%d" % (d, h, cc)
                    S.act(actf(Sb[d][h][cc][:], St[d][h][cc][:], AF.Copy, scale=self.carry[:, d, c:c + 1]), reads=[sk, "carry"], writes=[sk + "b"])
                ictx[i] = (at, kat, kw, kkw, qd, kqd)

            def stage2(i):
                d, jt, sub, h, c, t0 = geom(items[i])
                v_, kv, ys, kys = cctx[(d, c)]
                at, kat, kw, kkw, qd, kqd = ictx.pop(i)
                hd = d * 4 + h
                vs = v_[:, h * 512:(h + 1) * 512]
                po, kpo = pO.next()
                S.pe(mmg([(po[:], at[:], vs), (po[:], qd[:, 0, :], Sb[d][h][0][:]), (po[:], qd[:, 1, :], Sb[d][h][1][:])]),
                     reads=[kat, kv, kqd, "S%d%d0b" % (d, h), "S%d%d1b" % (d, h)], writes=[kpo])
                for cc in range(2):
                    pu, kpu = pU.next()
                    sk = "S%d%d%d" % (d, h, cc)
                    S.pe(mm1(pu[:], kw[:, cc, :], vs), reads=[kkw, kv], writes=[kpu])
                    S.dve(stt(St[d][h][cc][:], St[d][h][cc][:], CDC[:, hd, c:c + 1], pu[:], ALU.mult, ALU.add), reads=[sk, kpu, "CDC"], writes=[sk])
                S.act(actf(ys[:, h * 512:(h + 1) * 512], po[:], AF.Copy), reads=[kpo], writes=[kys])
                if h == 3:
                    S.dma(YO[d][c * 128:(c + 1) * 128, :], ys[:], reads=[kys], writes=["YO"], chan=kys + "s")

            self.pipeline(len(items), stage1, stage2, 2)

        with self.phase():
            gain = self.sb("rgain", [128, 2048], F32)
            S.dma(gain[:], self.w["ret_out_norm"][j:j + 1, :].to_broadcast([128, 2048]), writes=["rgain"], chan="rgain")
            yfr = self.ring("yf", 2, [128, 2048], F32)
            ybr = self.ring("yb", 2, [128, 2048], F32)
            rgr = self.ring("rgl", 2, [128, 2048], F32)
            ssq = self.ring("ssq", 3, [128, 8], F32)
            junk = self.ring("junk", 2, [128, 512], F32)
            ymr = self.ring("ym", 2, [128, 2048], BF16)
            mxs = self.ring("mxs", 2, [128, 16, 512], BF16)
            pT = Ring([(self.psb[0], "psb0"), (self.psb[1], "psb1")])

            def ld(c):
                yf, kyf = yfr.next()
                yb, kyb = ybr.next()
                rg, krg = rgr.next()
                S.dma(yf[:], self.YF[c * 128:(c + 1) * 128, :], writes=[kyf], chan=kyf)
                S.dma(yb[:], self.YB[c * 128:(c + 1) * 128, :], writes=[kyb], chan=kyb)
                S.dma(rg[:], self.RG[c * 128:(c + 1) * 128, :], writes=[krg], chan=krg)
                return yf, kyf, yb, kyb, rg, krg
            nxt = ld(0)
            for c in range(NCH):
                yf, kyf, yb, kyb, rg, krg = nxt
                if c + 1 < NCH:
                    nxt = ld(c + 1)
                sub = c % 4
                sl = slice(sub * 128, (sub + 1) * 128)
                if sub == 0:
                    mx, kmx = mxs.next()
                S.pool(tt(yf[:], yf[:], yb[:], ALU.add), reads=[kyf, kyb], writes=[kyf])
                sq_, ksq = ssq.next()
                jk, kjk = junk.next()
                S.dve(mset(sq_[:], 0.0), writes=[ksq + "a", ksq + "b"])
                for hh in range(4):
                    S.act(actf(jk[:], yf[:, hh * 512:(hh + 1) * 512], AF.Square, accum=sq_[:, hh:hh + 1]), reads=[kyf], writes=[kjk, ksq + "a"])
                S.act(actf(sq_[:, 4:8], sq_[:, 0:4], AF.Sqrt, bias=EPS, scale=1.0 / 512), reads=[ksq + "a"], writes=[ksq + "b"])
                S.dve(recip(sq_[:, 4:8], sq_[:, 4:8]), reads=[ksq + "b"], writes=[ksq + "b"])
                S.dve(tt(yf[:].rearrange("p (h e) -> p h e", e=512), yf[:].rearrange("p (h e) -> p h e", e=512),
                         sq_[:, 4:8].unsqueeze(2).to_broadcast([128, 4, 512]), ALU.mult), reads=[kyf, ksq + "b"], writes=[kyf])
                S.pool(tt(rg[:], rg[:], gain[:], ALU.mult), reads=[krg, "rgain"], writes=[krg])
                ym, kym = ymr.next()
                S.dve(tt(ym[:], yf[:], rg[:], ALU.mult), reads=[kyf, krg], writes=[kym])
                for bg in range(2):
                    ptr, kptr = pT.next()

                    def tr8(e, ptr=ptr, ym=ym, bg=bg):
                        ins = None
                        for b_ in range(8):
                            ins = e.transpose(ptr[:, b_ * 128:(b_ + 1) * 128], ym[:, (bg * 8 + b_) * 128:(bg * 8 + b_ + 1) * 128], self.ident_b)
                        return ins
                    S.pe(tr8, reads=[kym, "cmb"], writes=[kptr])
                    S.act(actf(mx[:, bg * 8:(bg + 1) * 8, sl], ptr[:].rearrange("p (b e) -> p b e", e=128), AF.Copy), reads=[kptr], writes=[kmx])
                if sub == 3:
                    t0 = (c // 4) * 512
                    S.dma(self.MIX[0:2048, t0:t0 + 512].rearrange("(b p) t -> p b t", p=128), mx[:], reads=[kmx], writes=["MIXr"], chan=kmx + "s")


def rope_tables(seglen, head_dim, nseg, reps):
    rows = seglen // 64
    row_idx = np.repeat(np.arange(rows, dtype=np.float32), 64)
    col_idx = np.tile(np.arange(64, dtype=np.float32), rows)
    axis_dim = head_dim // 2
    inv_freq = (np.float32(10000.0) ** (-np.arange(0, axis_dim, 2, dtype=np.float32) / np.float32(axis_dim))).astype(np.float32)
    ang = np.concatenate([row_idx[:, None] * inv_freq, col_idx[:, None] * inv_freq], axis=-1).astype(np.float32)
    cs = np.stack([np.cos(ang), np.sin(ang)], 0).astype(np.float32)
    cs = np.tile(cs, (1, nseg, 1))
    cs = cs.transpose(0, 2, 1)
    return np.ascontiguousarray(np.tile(cs, (1, reps, 1)))


def core_tables(T, nseg):
    NT, NCH = T // 512, T // 128
    seglen = T // nseg
    tps = NT // nseg
    cps = NCH // nseg
    seg_t = np.arange(NT) // tps
    mb = np.where(seg_t[:, None] == seg_t[None, :], 0.0, -30000.0).astype(np.float32)
    carry = np.ones((2, NCH), np.float32)
    carry[0, np.arange(NCH) % cps == 0] = 0.0
    carry[1, np.arange(NCH) % cps == cps - 1] = 0.0
    flb = np.ones((NT, 2), np.float32)
    flb[np.arange(NT) % tps == 0, 0] = 0.0
    flb[np.arange(NT) % tps == tps - 1, 1] = 0.0
    rep = lambda a: np.ascontiguousarray(np.broadcast_to(a.reshape(1, -1), (128, a.size))).astype(np.float32)
    return {
        "maskb": rep(mb), "carry": rep(carry), "flb": rep(flb),
        "ropeA": rope_tables(seglen, 64, nseg, 4), "ropeR": rope_tables(seglen, 256, nseg, 1),
    }


DBG = {}
FULL_STEPS = [("ab", 0, 0), ("ffn", 0), ("ret", 0, 1), ("ffn", 1), ("ab", 1, 2), ("ffn", 2), ("ret", 1, 3), ("ffn", 3), ("final",)]
_CACHE = {}


def run_cores(T, steps, core_x, core_nseg, weights):
    key = (T, tuple(steps))
    if key not in _CACHE:
        _CACHE[key] = MK(T, steps).build()
    nc = _CACHE[key]
    cm = host_cmat()
    tabs = {}
    in_maps = []
    for x, ns in zip(core_x, core_nseg):
        if ns not in tabs:
            tabs[ns] = core_tables(T, ns)
        m = {"xT": np.ascontiguousarray(np.asarray(x, np.float32).T), "cmat": cm}
        m.update(tabs[ns])
        for n, _ in MK.W_SPECS:
            m[n] = weights[n]
        in_maps.append(m)
    res = run_bass_kernel_spmd(nc, in_maps, core_ids=list(range(len(in_maps))))
    return [np.ascontiguousarray(r["yT"].T) for r in res.results]


def kernel(x_prompt, x_sample, **weights):
    weights = {k: np.ascontiguousarray(np.asarray(v, np.float32)) for k, v in weights.items()}
    xp = np.asarray(x_prompt, np.float32)
    xs = np.asarray(x_sample, np.float32)
    T = 8192
    core_x = [xp[0], xp[1]] + [xs[4 * i:4 * i + 4].reshape(T, 1024) for i in range(4)]
    nseg = [1, 1, 4, 4, 4, 4]
    core_x += [core_x[5], core_x[5]]
    nseg += [4, 4]
    outs = run_cores(T, FULL_STEPS, core_x, nseg, weights)
    y_prompt = np.stack([outs[0], outs[1]], 0)
    y_sample = np.concatenate([outs[2 + i].reshape(4, 2048, 1024) for i in range(4)], 0)
    return (y_prompt, y_sample)
```

```python
import contextlib
import numpy as np
import concourse.bass as bass
import concourse.mybir as mybir
from concourse.bass_utils import run_bass_kernel_spmd

F32 = mybir.dt.float32
BF16 = mybir.dt.bfloat16
AF = mybir.ActivationFunctionType
ALU = mybir.AluOpType
AX = mybir.AxisListType
EPS = 1e-6


class _Op:
    __slots__ = ("eng", "fn", "waits", "signal", "idx", "chan", "cidx", "sigval")

    def __init__(self, eng, fn, chan):
        self.eng = eng
        self.fn = fn
        self.chan = chan
        self.waits = []
        self.signal = False
        self.idx = -1
        self.cidx = -1
        self.sigval = 0


class Sched:
    ENG = ("pe", "act", "dve", "pool", "sp")

    def __init__(self, nc):
        self.nc = nc
        self.ops = {e: [] for e in self.ENG}
        self.res = {}
        self.waited = {e: {} for e in self.ENG}
        self.chan_last = {}
        self.chan_n = {}
        self.chan_phase = {}
        self.phase_id = 0

    def _wait(self, op, d):
        eng = op.eng
        if d is op:
            return
        if d.chan is not None:
            ek, val = ("c", d.chan), d.cidx
        else:
            if d.eng == eng and eng == "pe":
                return
            ek, val = d.eng, d.idx
        w = self.waited[eng]
        if w.get(ek, -1) >= val:
            return
        w[ek] = val
        op.waits.append(d)
        d.signal = True

    def add(self, eng, fn, reads=(), writes=(), chan=None):
        if chan is not None:
            chan = (self.phase_id, chan)
        op = _Op(eng, fn, chan)
        deps = []
        res = self.res
        for k in reads:
            st = res.get(k)
            if st is not None and st[0] is not None:
                deps.append(st[0])
        for k in writes:
            st = res.get(k)
            if st is not None:
                if st[0] is not None:
                    deps.append(st[0])
                deps.extend(st[1])
        if chan is not None:
            prev = self.chan_last.get(chan)
            if prev is not None:
                deps.append(prev)
            op.cidx = self.chan_n.get(chan, 0)
            if op.cidx == 0:
                self.chan_phase[chan] = self.phase_id
            assert self.chan_phase[chan] == self.phase_id, chan
            self.chan_n[chan] = op.cidx + 1
            self.chan_last[chan] = op
            op.signal = True
        op.idx = len(self.ops[eng])
        for d in deps:
            self._wait(op, d)
        self.ops[eng].append(op)
        for k in writes:
            res[k] = [op, []]
        for k in reads:
            st = res.get(k)
            if st is None:
                res[k] = [None, [op]]
            elif st[0] is not op:
                st[1].append(op)
        return op

    def barrier(self):
        lasts = []
        for e in self.ENG:
            for o in reversed(self.ops[e]):
                if o.fn is not None and o.chan is None:
                    lasts.append(o)
                    break
        lasts.extend(self.chan_last.values())
        for e in self.ENG:
            op = _Op(e, None, None)
            op.idx = len(self.ops[e])
            for d in lasts:
                self._wait(op, d)
            self.ops[e].append(op)
        self.res = {}
        self.phase_id += 1

    def pe(self, fn, reads=(), writes=()):
        return self.add("pe", fn, reads, writes)

    def act(self, fn, reads=(), writes=()):
        return self.add("act", fn, reads, writes)

    def dve(self, fn, reads=(), writes=()):
        return self.add("dve", fn, reads, writes)

    def pool(self, fn, reads=(), writes=()):
        return self.add("pool", fn, reads, writes)

    def dma(self, out, in_, reads=(), writes=(), chan=None, eng="sp", **kw):
        assert chan is not None
        return self.add(eng, lambda e: e.dma_start(out=out, in_=in_, **kw), reads, writes, chan=chan)

    def emit(self, stack):
        nc = self.nc
        esem = {}
        for e in self.ENG:
            if e != "sp":
                esem[e] = stack.enter_context(nc.semaphore("s_" + e))
        csem = {}
        cbase = {}
        pool = []
        by_phase = {}
        for c, ph in self.chan_phase.items():
            by_phase.setdefault(ph, []).append(c)
        nsem = 0
        for ph in sorted(by_phase):
            used = []
            for c in by_phase[ph]:
                if pool:
                    sv = pool.pop()
                else:
                    sv = [stack.enter_context(nc.semaphore("c%d" % nsem)), 0]
                    nsem += 1
                csem[c] = sv[0]
                cbase[c] = sv[1]
                sv[1] += 16 * self.chan_n[c]
                used.append(sv)
            pool.extend(used)
        self.nsem = nsem
        for e in self.ENG:
            cnt = 0
            for op in self.ops[e]:
                if op.chan is not None:
                    op.sigval = cbase[op.chan] + 16 * (op.cidx + 1)
                elif op.signal:
                    cnt += 1
                    op.sigval = cnt

        def run(e, eng):
            for op in self.ops[e]:
                for d in op.waits:
                    sem = csem[d.chan] if d.chan is not None else esem[d.eng]
                    eng.wait_ge(sem, d.sigval)
                if op.fn is None:
                    continue
                ins = op.fn(eng)
                if op.chan is not None:
                    ins.then_inc(csem[op.chan], 16)
                elif op.signal:
                    ins.then_inc(esem[e], 1)

        with nc.Block() as block:
            @block.sync
            def _(eng):
                run("sp", eng)

            @block.tensor
            def _(eng):
                run("pe", eng)

            @block.scalar
            def _(eng):
                run("act", eng)

            @block.vector
            def _(eng):
                run("dve", eng)

            @block.gpsimd
            def _(eng):
                run("pool", eng)


def mmg(items):
    n = len(items)

    def f(e):
        ins = None
        for i, (o, l, r) in enumerate(items):
            ins = e.matmul(o, lhsT=l, rhs=r, start=(i == 0), stop=(i == n - 1))
        return ins
    return f


def mm1(o, l, r):
    return lambda e: e.matmul(o, lhsT=l, rhs=r, start=True, stop=True)


def tr(o, i, ident):
    return lambda e: e.transpose(o, i, ident)


def actf(o, i, func, bias=None, scale=None, accum=None):
    kw = {}
    if bias is not None:
        kw["bias"] = bias
    if scale is not None:
        kw["scale"] = scale
    if accum is not None:
        kw["accum_out"] = accum
    return lambda e: e.activation(out=o, in_=i, func=func, **kw)


def tt(o, a, b, op):
    return lambda e: e.tensor_tensor(out=o, in0=a, in1=b, op=op)


def ts(o, a, s1, op0, s2=None, op1=None):
    if op1 is None:
        return lambda e: e.tensor_scalar(out=o, in0=a, scalar1=s1, scalar2=None, op0=op0)
    return lambda e: e.tensor_scalar(out=o, in0=a, scalar1=s1, scalar2=s2, op0=op0, op1=op1)


def stt(o, a, s, b, op0, op1):
    return lambda e: e.scalar_tensor_tensor(out=o, in0=a, scalar=s, in1=b, op0=op0, op1=op1)


def cp(o, i):
    return lambda e: e.tensor_copy(out=o, in_=i)


def recip(o, i):
    return lambda e: e.reciprocal(out=o, in_=i)


def mset(o, v):
    return lambda e: e.memset(o, v)


C_ID, C_ONES, C_BD32, C_TRIU, C_TRIL, C_SL, C_DFW, C_DBW, C_QDF, C_QDB, C_SEL, C_KD = range(12)
NCM = 12


def host_cmat():
    s = np.arange(128)[:, None].astype(np.float32)
    l = np.arange(128)[None, :].astype(np.float32)
    m = np.zeros((NCM, 128, 128), np.float32)
    m[C_ID] = np.eye(128)
    m[C_ONES] = 1.0
    m[C_BD32] = (np.arange(128)[:, None] // 32 == np.arange(128)[None, :] // 32)
    m[C_TRIU] = (s <= l)
    m[C_TRIL] = (s >= l)
    m[C_SL] = (s > l)
    m[C_DFW] = np.maximum(l - s, 0)
    m[C_DBW] = np.maximum(s - l, 0)
    m[C_QDF] = np.broadcast_to(l + 1.0, (128, 128))
    m[C_QDB] = np.broadcast_to(128.0 - l, (128, 128))
    m[C_SEL][64, :] = 1.0
    m[C_KD][:, 0] = 127.0 - np.arange(128)
    m[C_KD][:, 1] = np.arange(128)
    m[C_KD][:, 2] = 128.0
    return np.ascontiguousarray(m.transpose(1, 0, 2))


class Ring:
    def __init__(self, items):
        self.items = items
        self.i = 0

    def next(self):
        it = self.items[self.i % len(self.items)]
        self.i += 1
        return it


class MK:
    W_SPECS = [
        ("norm_mix", (4, 1024)), ("norm_ffn", (4, 1024)), ("norm_final", (1024,)),
        ("ab_w_in", (2, 1024, 2832)), ("ab_gate_bias", (2, 16)), ("attn_q_norm", (2, 64)),
        ("attn_k_norm", (2, 64)), ("mlstm_out_norm", (2, 512)), ("ab_w_out", (2, 1024, 1024)),
        ("ret_w_in", (2, 1024, 6144)), ("ret_decay_logit", (2, 2, 4)), ("ret_out_norm", (2, 2048)),
        ("ret_w_out", (2, 2048, 1024)), ("ffn_w_up", (4, 1024, 5632)), ("ffn_conv_w", (4, 3, 2816)),
        ("ffn_conv_b", (4, 2816)), ("ffn_w_down", (4, 2816, 1024)),
    ]

    def __init__(self, T, steps):
        self.T = T
        self.NT = T // 512
        self.NCH = T // 128
        self.steps = steps
        self.nc = nc = bass.Bass("TRN2", target_bir_lowering=False)
        self.S = Sched(nc)
        self._uid = 0
        NT, NCH = self.NT, self.NCH
        di = lambda n, s, dt=F32: nc.dram_tensor(n, list(s), dt, kind="ExternalInput").ap()
        ds = lambda n, s, dt: nc.dram_tensor(n, list(s), dt, kind="Internal").ap()
        self.xT = di("xT", (1024, T))
        self.w = {n: di(n, s) for n, s in self.W_SPECS}
        self.cmat_d = di("cmat", (128, NCM, 128))
        self.ropeA = di("ropeA", (2, 128, T))
        self.ropeR = di("ropeR", (2, 128, T))
        self.mb_d = di("maskb", (128, NT * NT))
        self.carry_d = di("carry", (128, 2 * NCH))
        self.flb_d = di("flb", (128, NT * 2))
        self.yT = nc.dram_tensor("yT", [1024, T], F32, kind="ExternalOutput").ap()
        self.XM = ds("XM", (1024, T), F32)
        self.XR = ds("XR", (1024, T), F32)
        self.ACTS = ds("ACTS", (2816, T), BF16)
        self.MIX = ds("MIX", (2048, T), BF16)
        self.QA = ds("QA", (8, 64, T), BF16)
        self.KA = ds("KA", (128, T), BF16)
        self.VA = ds("VA", (T, 130), BF16)
        self.MQ = ds("MQ", (4, 128, T), BF16)
        self.MKs = ds("MKs", (4, 128, T), BF16)
        self.MV = ds("MV", (T, 516), BF16)
        self.SG = ds("SG", (T, 512), F32)
        self.GT = ds("GT", (T, 16), F32)
        self.HF = ds("HF", (T, 512), F32)
        self.HB = ds("HB", (T, 512), F32)
        self.YB = ds("YB", (T, 2048), F32)
        self.RQ = ds("RQ", (8, 128, T), BF16)
        self.RK = ds("RK", (8, 128, T), BF16)
        self.RV = ds("RV", (T, 2048), BF16)
        self.RG = ds("RG", (T, 2048), F32)
        self.YF = ds("YF", (T, 2048), F32)

    def un(self, n):
        self._uid += 1
        return "%s_%d" % (n, self._uid)

    def sb(self, name, shape, dt):
        return self._ph.enter_context(self.nc.sbuf_tensor(self.un(name), list(shape), dt))

    @contextlib.contextmanager
    def phase(self, psum=True):
        self.S.barrier()
        with contextlib.ExitStack() as ph:
            self._ph = ph
            if psum:
                nc = self.nc
                self.ps = [ph.enter_context(nc.psum_tensor(self.un("ps%d" % i), [128, 512], F32)) for i in range(6)]
                self.psb = [ph.enter_context(nc.psum_tensor(self.un("psb%d" % i), [128, 1024], BF16)) for i in range(2)]
            yield
        self._ph = self._gl

    def ring(self, name, n, shape, dt):
        items = []
        for i in range(n):
            t = self.sb(name, shape, dt)
            items.append((t, self.un(name)))
        return Ring(items)

    def pipeline(self, n, stage1, stage2, la):
        for i in range(n + la):
            if i < n:
                stage1(i)
            if i >= la:
                stage2(i - la)

    def rr(self, engs):
        self._rr = getattr(self, "_rr", 0) + 1
        return engs[self._rr % len(engs)]

    def build(self):
        nc, S = self.nc, self.S
        with contextlib.ExitStack() as gl:
            self._gl = gl
            self._ph = gl
            self.pk = ["ps%d" % i for i in range(6)]
            self.cm = self.sb("cm", [128, NCM, 128], F32)
            self.cmb = self.sb("cmb", [128, 3, 128], BF16)
            S.dma(self.cm[:], self.cmat_d, writes=["cm"], chan="cm")
            S.dve(cp(self.cmb[:], self.cm[:, 0:3, :]), reads=["cm"], writes=["cmb"])
            self.ident_f = self.cm[:, C_ID, :]
            self.ones_f = self.cm[:, C_ONES, :]
            self.ident_b = self.cmb[:, C_ID, :]
            self.ones_b = self.cmb[:, C_ONES, :]
            self.bd32_b = self.cmb[:, C_BD32, :]
            self.carry = self.sb("carry", [128, 2, self.NCH], F32)
            S.dma(self.carry[:], self.carry_d.rearrange("p (d c) -> p d c", d=2), writes=["carry"], chan="carry")
            self.mb = self.sb("mb", [128, self.NT, self.NT], F32)
            S.dma(self.mb[:], self.mb_d.rearrange("p (a b) -> p a b", a=self.NT), writes=["mb"], chan="mb")
            self.flb = self.sb("flb", [128, self.NT * 2], F32)
            S.dma(self.flb[:], self.flb_d, writes=["flb"], chan="flb")
            cur = self.xT
            for st in self.steps:
                kind = st[0]
                if kind == "ab":
                    j, L = st[1], st[2]
                    self.ab_inproj(cur, j, L)
                    self.attention()
                    self.mlstm(j)
                    self.proj_resid(self.MIX, 8, self.w["ab_w_out"][j], cur, self.XM)
                    cur = self.XM
                elif kind == "ret":
                    j, L = st[1], st[2]
                    self.ret_inproj(cur, j, L)
                    if not DBG.get("noscan"):
                        self.retention(j)
                    self.proj_resid(self.MIX, 16, self.w["ret_w_out"][j], cur, self.XM)
                    cur = self.XM
                elif kind == "ffn":
                    L = st[1]
                    self.ffn_up(cur, L)
                    self.proj_resid(self.ACTS, 22, self.w["ffn_w_down"][L], cur, self.XR)
                    cur = self.XR
                elif kind == "final":
                    self.final_norm(cur)
                    cur = None
                elif kind == "copy":
                    self.copy_out(cur)
                    cur = None
            S.barrier()
            S.emit(gl)
        return nc

    def x3(self, X):
        return X.rearrange("(c p) t -> p c t", p=128)

    def load_gain(self, vec_ap, name):
        g = self.sb(name, [128, 8], F32)
        self.S.dma(g[:], vec_ap.rearrange("(c p) -> p c", p=128), writes=[name], chan=name, allow_slow_non_contiguous=True)
        return g

    def wkeys(self, dkey, col0, n, bw=1024):
        return ["%s#%d" % (dkey, b_) for b_ in range(col0 // bw, (col0 + n - 1) // bw + 1)]

    def prep_weight(self, dst, dkey, src, KC, blocks, gain=None, gkey=None, bw=1024, order=None):
        S = self.S
        stg = self.ring("wstg", 2, [128, 1024], F32)
        pw = min(bw, 1024)
        pieces = []
        for (d0, nd, s0, ns, vf) in blocks:
            o = 0
            while o < ns:
                n_ = min(pw - (d0 + o) % pw, ns - o)
                pieces.append((d0 + o, n_, s0 + o))
                o += n_
        if order is not None:
            pieces.sort(key=lambda p: (order.index(p[0] // bw) if (p[0] // bw) in order else 999, p[0]))
        for (d0, n_, s0) in pieces:
            bk = "%s#%d" % (dkey, d0 // bw)
            for c in range(KC):
                t, k = stg.next()
                S.dma(t[:, 0:n_], src[c * 128:(c + 1) * 128, s0:s0 + n_], writes=[k], chan=k)
                iv = t[:, 0:n_]
                ov = dst[:, c, d0:d0 + n_]
                eng = self.rr(["dve", "act", "dve", "act", "pool"])
                rd = [k] + ([gkey] if gain is not None else [])
                if gain is None:
                    if eng == "act":
                        S.act(actf(ov, iv, AF.Copy), reads=rd, writes=[bk])
                    else:
                        S.add(eng, cp(ov, iv), reads=rd, writes=[bk])
                else:
                    if eng == "act":
                        S.act(actf(ov, iv, AF.Copy, scale=gain[:, c:c + 1]), reads=rd, writes=[bk])
                    else:
                        S.add(eng, ts(ov, iv, gain[:, c:c + 1], ALU.mult), reads=rd, writes=[bk])

    def norm_load(self, src3, t0, n, ent):
        xt, kx = ent
        self.S.dma(xt[:, :, 0:n], src3[:, :, t0:t0 + n], writes=[kx], chan=kx)

    def norm_tile(self, src3, t0, n, bufs, D_feat=1024, load=True):
        S = self.S
        xt, kx = bufs["xt"]
        sq, ksq = bufs["sq"]
        rs, krs = bufs["rs"]
        hT, kh = bufs["hT"]
        ssp, kss = bufs["ss"]
        if load:
            S.dma(xt[:, :, 0:n], src3[:, :, t0:t0 + n], writes=[kx], chan=kx)
        S.act(actf(sq[:, :, 0:n], xt[:, :, 0:n], AF.Square), reads=[kx], writes=[ksq])
        S.pe(mmg([(ssp[:, 0:n], self.ones_b, sq[:, c, 0:n]) for c in range(8)]), reads=[ksq, "cmb"], writes=[kss])
        S.act(actf(rs[:, 0:n], ssp[:, 0:n], AF.Sqrt, bias=EPS, scale=1.0 / D_feat), reads=[kss], writes=[krs])
        S.dve(recip(rs[:, 0:n], rs[:, 0:n]), reads=[krs], writes=[krs])
        S.dve(tt(hT[:, 0:4, 0:n], xt[:, 0:4, 0:n], rs[:, 0:n].unsqueeze(1).to_broadcast([128, 4, n]), ALU.mult),
              reads=[kx, krs], writes=[kh + "a"])
        S.pool(tt(hT[:, 4:8, 0:n], xt[:, 4:8, 0:n], rs[:, 0:n].unsqueeze(1).to_broadcast([128, 4, n]), ALU.mult),
               reads=[kx, krs], writes=[kh + "b"])
        return [kh + "a", kh + "b"]

    def norm_bufs(self, nbuf=1, n=512):
        xt = self.ring("xt", nbuf, [128, 8, n], F32)
        sq = self.ring("sq", 1, [128, 8, n], BF16)
        rs = self.ring("rs", 2, [128, n], F32)
        hT = self.ring("hT", 2, [128, 8, n], BF16)
        return xt, sq, rs, hT

    def proj_resid(self, A, KC, w_d, Xin, Xout):
        S, NT = self.S, self.NT
        with self.phase():
            w = self.sb("wpr", [128, KC, 1024], BF16)
            self.prep_weight(w, "wpr", w_d, KC, [(0, 1024, 0, 1024, None)], bw=256)
            ar = self.ring("a", 2, [128, KC, 512], BF16)
            xr = self.ring("x", 2, [128, 8, 512], F32)
            A3 = A[0:KC * 128, :].rearrange("(c p) t -> p c t", p=128)
            Xi3, Xo3 = self.x3(Xin), self.x3(Xout)
            psr = Ring([(self.ps[i], self.pk[i]) for i in range(4)])
            def pr_load(j):
                at, ka = ar.next()
                xt, kx = xr.next()
                S.dma(at[:], A3[:, :, j * 512:(j + 1) * 512], writes=[ka], chan=ka)
                S.dma(xt[:], Xi3[:, :, j * 512:(j + 1) * 512], writes=[kx], chan=kx)
                return at, ka, xt, kx
            nxt = pr_load(0)
            for j in range(NT):
                t0 = j * 512
                at, ka, xt, kx = nxt
                if j + 1 < NT:
                    nxt = pr_load(j + 1)
                for d in range(8):
                    p, kp = psr.next()
                    S.pe(mmg([(p[:], w[:, c, d * 128:(d + 1) * 128], at[:, c, :]) for c in range(KC)]),
                         reads=[ka] + self.wkeys("wpr", d * 128, 128, 256), writes=[kp])
                    S.dve(tt(xt[:, d, :], p[:], xt[:, d, :], ALU.add), reads=[kp, kx], writes=[kx])
                S.dma(Xo3[:, :, t0:t0 + 512], xt[:], reads=[kx], writes=["Xo"], chan=kx + "s")

    def final_norm(self, X):
        S, NT = self.S, self.NT
        with self.phase():
            g = self.load_gain(self.w["norm_final"], "gfin")
            xt, sq, rs, hT = self.norm_bufs(nbuf=2)
            yr = self.ring("y", 2, [128, 8, 512], F32)
            X3, Y3 = self.x3(X), self.x3(self.yT)
            nxt = xt.next()
            self.norm_load(X3, 0, 512, nxt)
            for j in range(NT):
                t0 = j * 512
                x_, kx = nxt
                if j + 1 < NT:
                    nxt = xt.next()
                    self.norm_load(X3, t0 + 512, 512, nxt)
                q_, kq = sq.next()
                r_, kr = rs.next()
                y_, ky = yr.next()
                S.act(actf(q_[:], x_[:], AF.Square), reads=[kx], writes=[kq])
                S.pe(mmg([(self.ps[0][:], self.ones_b, q_[:, c, :]) for c in range(8)]), reads=[kq, "cmb"], writes=["ps0"])
                S.act(actf(r_[:], self.ps[0][:], AF.Sqrt, bias=EPS, scale=1.0 / 1024), reads=["ps0"], writes=[kr])
                S.dve(recip(r_[:], r_[:]), reads=[kr], writes=[kr])
                for c in range(8):
                    S.dve(stt(y_[:, c, :], x_[:, c, :], g[:, c:c + 1], r_[:], ALU.mult, ALU.mult),
                          reads=[kx, kr, "gfin"], writes=[ky])
                S.dma(Y3[:, :, t0:t0 + 512], y_[:], reads=[ky], writes=["Y"], chan=ky + "s")

    def copy_out(self, X):
        S, NT = self.S, self.NT
        with self.phase():
            xr = self.ring("x", 2, [128, 8, 512], F32)
            X3, Y3 = self.x3(X), self.x3(self.yT)
            for j in range(NT):
                x_, kx = xr.next()
                S.dma(x_[:], X3[:, :, j * 512:(j + 1) * 512], writes=[kx], chan=kx)
                S.dma(Y3[:, :, j * 512:(j + 1) * 512], x_[:], reads=[kx], writes=["Y"], chan=kx + "s")

    def ffn_up(self, X, L):
        S, NT = self.S, self.NT
        nc = self.nc
        NB = 2 * (NT - 1)
        with self.phase():
            w = self.sb("wup", [128, 8, 5632], BF16)
            g = self.load_gain(self.w["norm_ffn"][L], "gffn")
            cw = self.sb("cw", [128, 22, 3], F32)
            cb = self.sb("cb", [128, 22], F32)
            for k3 in range(3):
                S.dma(cw[:, :, k3], self.w["ffn_conv_w"][L, k3].rearrange("(f p) -> p f", p=128), writes=["cw"], chan="cw", allow_slow_non_contiguous=True)
            S.dma(cb[:], self.w["ffn_conv_b"][L].rearrange("(f p) -> p f", p=128), writes=["cb"], chan="cb", allow_slow_non_contiguous=True)
            self.prep_weight(w, "wup", self.w["ffn_w_up"][L], 8,
                             [(i * 2048, min(2048, 5632 - i * 2048), i * 2048, min(2048, 5632 - i * 2048), None) for i in range(3)],
                             gain=g, gkey="gffn", order=[0, 2, 3, 1, 4, 5])
            xt, sq, rs, hT = self.norm_bufs()
            X3 = self.x3(X)
            gh = self.sb("gh", [128, 22, NT, 2], F32)
            S.pool(mset(gh[:], 0.0), writes=["gh"])
            if NB > 0:
                xb = self.sb("xb", [128, 8, NT - 1, 2], F32)
                sqb = self.sb("sqb", [128, 8, NB], BF16)
                rsb = self.sb("rsb", [128, NB], F32)
                hb = self.sb("hb", [128, 8, NB], BF16)
                for c in range(8):
                    src = X3[:, c, 511:511 + 512 * (NT - 1)].rearrange("p (b r) -> p b r", r=512)[:, :, 0:2]
                    S.dma(xb[:, c, :, :], src, writes=["xb%d" % c], chan="xb", allow_slow_non_contiguous=True)
                xbf = xb[:].rearrange("p c b e -> p c (b e)")
                xk = ["xb%d" % c for c in range(8)]
                S.act(actf(sqb[:], xbf, AF.Square), reads=xk, writes=["sqb"])
                S.pe(mmg([(self.ps[0][:, 0:NB], self.ones_b, sqb[:, c, :]) for c in range(8)]), reads=["sqb", "cmb"], writes=["ps0"])
                S.act(actf(rsb[:], self.ps[0][:, 0:NB], AF.Sqrt, bias=EPS, scale=1.0 / 1024), reads=["ps0"], writes=["rsb"])
                S.dve(recip(rsb[:], rsb[:]), reads=["rsb"], writes=["rsb"])
                S.dve(tt(hb[:], xbf, rsb[:].unsqueeze(1).to_broadcast([128, 8, NB]), ALU.mult), reads=xk + ["rsb"], writes=["hb"])
                pr = Ring([(self.ps[1], "ps1"), (self.ps[2], "ps2")])
                for f in range(22):
                    p, kp = pr.next()
                    S.pe(mmg([(p[:, 0:NB], w[:, c, 2816 + f * 128:2816 + (f + 1) * 128], hb[:, c, :]) for c in range(8)]),
                         reads=["hb"] + self.wkeys("wup", 2816 + f * 128, 128), writes=[kp])
                    pv = p[:, 0:NB].rearrange("p (b e) -> p b e", e=2)
                    S.dve(cp(gh[:, f, 1:NT, 0], pv[:, :, 0]), reads=[kp], writes=["gh"])
                    S.dve(cp(gh[:, f, 0:NT - 1, 1], pv[:, :, 1]), reads=[kp], writes=["gh"])
            ghf = gh[:].rearrange("p f j e -> p f (j e)")
            S.dve(tt(ghf, ghf, self.flb[:].unsqueeze(1).to_broadcast([128, 22, NT * 2]), ALU.mult), reads=["gh", "flb"], writes=["gh"])
            actr = self.ring("act", 2, [128, 22, 512], BF16)
            gbr = self.ring("gb", 2, [128, 514], F32)
            tr_ = self.ring("tc", 2, [128, 512], F32)
            pu = Ring([(self.ps[1], "ps1"), (self.ps[2], "ps2")])
            pg = Ring([(self.ps[3], "ps3"), (self.ps[4], "ps4"), (self.ps[5], "ps5")])
            A3 = self.ACTS.rearrange("(f p) t -> p f t", p=128)
            nxt = xt.next()
            self.norm_load(X3, 0, 512, nxt)
            for j in range(NT):
                t0 = j * 512
                bufs = {"xt": nxt, "sq": sq.next(), "rs": rs.next(), "hT": hT.next(), "ss": (self.ps[0], "ps0")}
                hk = self.norm_tile(X3, t0, 512, bufs, load=False)
                if j + 1 < NT:
                    nxt = xt.next()
                    self.norm_load(X3, t0 + 512, 512, nxt)
                h_ = bufs["hT"][0]
                a_, ka = actr.next()
                for f in range(22):
                    u, ku = pu.next()
                    gp, kg = pg.next()
                    S.pe(mmg([(u[:], w[:, c, f * 128:(f + 1) * 128], h_[:, c, :]) for c in range(8)]), reads=hk + self.wkeys("wup", f * 128, 128), writes=[ku])
                    S.pe(mmg([(gp[:], w[:, c, 2816 + f * 128:2816 + (f + 1) * 128], h_[:, c, :]) for c in range(8)]), reads=hk + self.wkeys("wup", 2816 + f * 128, 128), writes=[kg])
                    gb, kb = gbr.next()
                    tc, kt = tr_.next()
                    S.act(actf(gb[:, 1:513], gp[:], AF.Copy), reads=[kg], writes=[kb + "m"])
                    S.dve(cp(gb[:, 0:514:513], gh[:, f, j, :]), reads=["gh"], writes=[kb + "h"])
                    S.act(actf(tc[:], gp[:], AF.Identity, bias=cb[:, f:f + 1], scale=cw[:, f, 1:2]), reads=[kg, "cw", "cb"], writes=[kt])
                    S.dve(stt(tc[:], gb[:, 0:512], cw[:, f, 0:1], tc[:], ALU.mult, ALU.add), reads=[kb + "m", kb + "h", kt, "cw"], writes=[kt])
                    S.dve(stt(tc[:], gb[:, 2:514], cw[:, f, 2:3], tc[:], ALU.mult, ALU.add), reads=[kb + "m", kb + "h", kt, "cw"], writes=[kt])
                    S.act(actf(tc[:], tc[:], AF.Gelu), reads=[kt], writes=[kt])
                    S.dve(tt(a_[:, f, :], tc[:], u[:], ALU.mult), reads=[kt, ku], writes=[ka])
                S.dma(A3[:, :, t0:t0 + 512], a_[:], reads=[ka], writes=["ACTS"], chan=ka + "s")

    def ab_inproj(self, X, j, L):
        S, NT, nc = self.S, self.NT, self.nc
        with self.phase():
            w = self.sb("wab", [128, 8, 2832], BF16)
            g = self.load_gain(self.w["norm_mix"][L], "gmix")
            stg = self.ring("wstg", 2, [128, 2832], F32)
            src = self.w["ab_w_in"][j]

            def hsplit(ap, nh, half):
                return ap.rearrange("p (h d) -> p h d", d=64)[:, :, half * 32:(half + 1) * 32]

            def h32(ap, nh):
                return ap.rearrange("p (h d) -> p h d", d=32)

            for c in range(8):
                t, k = stg.next()
                S.dma(t[:], src[c * 128:(c + 1) * 128, :], writes=[k], chan=k)
                gc = g[:, c:c + 1]
                engs = ["dve", "pool"]
                for gi in range(2):
                    for half in range(2):
                        S.add(self.rr(engs), ts(h32(w[:, c, gi * 256 + half * 128: gi * 256 + half * 128 + 128], 4),
                                                hsplit(t[:, gi * 256:(gi + 1) * 256], 4, half), gc, ALU.mult),
                              reads=[k, "gmix"], writes=["wab"])
                for half in range(2):
                    S.add(self.rr(engs), ts(h32(w[:, c, 512 + half * 64: 512 + half * 64 + 64], 2),
                                            hsplit(t[:, 512:640], 2, half), gc, ALU.mult), reads=[k, "gmix"], writes=["wab"])
                for (d0, s0, n) in [(1664, 640, 128), (640, 768, 1024), (1808, 1792, 1024), (1792, 2816, 16)]:
                    S.add(self.rr(engs), ts(w[:, c, d0:d0 + n], t[:, s0:s0 + n], gc, ALU.mult), reads=[k, "gmix"], writes=["wab"])
            gq = self.sb("gq", [128, 2], F32)
            gk = self.sb("gk", [128, 2], F32)
            for r in range(4):
                for half in range(2):
                    S.dma(gq[r * 32:(r + 1) * 32, half:half + 1], self.w["attn_q_norm"][j, half * 32:(half + 1) * 32].unsqueeze(1),
                          writes=["gq%d%d" % (r, half)], chan="gqld", allow_slow_non_contiguous=True)
                    S.dma(gk[r * 32:(r + 1) * 32, half:half + 1], self.w["attn_k_norm"][j, half * 32:(half + 1) * 32].unsqueeze(1),
                          writes=["gk%d%d" % (r, half)], chan="gkld", allow_slow_non_contiguous=True)
            gqk = ["gq%d%d" % (r, h) for r in range(4) for h in range(2)]
            gkk = ["gk%d%d" % (r, h) for r in range(4) for h in range(2)]
            S.dve(ts(gq[:], gq[:], 0.125, ALU.mult), reads=gqk, writes=["gq"])
            gbias = self.sb("gbias", [128, 16], F32)
            S.dma(gbias[:], self.w["ab_gate_bias"][j:j + 1, :].to_broadcast([128, 16]), writes=["gbias"], chan="gbias")
            xt, sq, rs, hT = self.norm_bufs()
            X3 = self.x3(X)
            cs = self.ring("cs", 2, [128, 2, 512], F32)
            sqa = self.ring("sqa", 2, [128, 2, 512], BF16)
            rq = self.ring("rq", 2, [128, 512], F32)
            an = self.ring("an", 2, [128, 2, 512], F32)
            t4 = self.ring("t4", 1, [128, 4, 512], F32)
            o12 = self.ring("o12", 3, [128, 2, 512], BF16)
            mqs = self.ring("mqs", 1, [128, 4, 512], BF16)
            mks = self.ring("mks", 1, [128, 4, 512], BF16)
            vts = self.ring("vt", 2, [128, 4, 2, 65], BF16)
            mvs = self.ring("mvt", 2, [128, 4, 4, 129], BF16)
            sgs = self.ring("sg", 1, [128, 4, 512], F32)
            gts = self.ring("gt", 2, [128, 4, 16], F32)
            for (t, k) in vts.items:
                S.pool(mset(t[:], 1.0), writes=[k])
            for (t, k) in mvs.items:
                S.pool(mset(t[:], 1.0), writes=[k])
            pf = Ring([(self.ps[i], self.pk[i]) for i in (1, 2, 3)])
            ptm = Ring([(self.ps[4], "ps4"), (self.ps[5], "ps5")])
            def cs_load(jt_):
                c_, kc = cs.next()
                S.dma(c_[:], self.ropeA[:, :, jt_ * 512:(jt_ + 1) * 512].rearrange("a p t -> p a t"), writes=[kc], chan=kc)
                return c_, kc
            nxt = xt.next()
            self.norm_load(X3, 0, 512, nxt)
            cnxt = cs_load(0)
            for jt in range(NT):
                t0 = jt * 512
                bufs = {"xt": nxt, "sq": sq.next(), "rs": rs.next(), "hT": hT.next(), "ss": (self.ps[0], "ps0")}
                hk = self.norm_tile(X3, t0, 512, bufs, load=False)
                h_ = bufs["hT"][0]
                c_, kc = cnxt
                if jt + 1 < NT:
                    nxt = xt.next()
                    self.norm_load(X3, t0 + 512, 512, nxt)
                    cnxt = cs_load(jt + 1)

                def fm(col0, M):
                    p, kp = pf.next()
                    S.pe(mmg([(p[0:M, :], w[:, c, col0:col0 + M], h_[:, c, :]) for c in range(8)]), reads=hk + ["wab"], writes=[kp])
                    return p, kp

                for grp in range(3):
                    M = 128 if grp < 2 else 64
                    colA = grp * 256 if grp < 2 else 512
                    colB = colA + M
                    gg, ggk = (gq, ["gq"]) if grp < 2 else (gk, gkk)
                    pa, kpa = fm(colA, M)
                    pb, kpb = fm(colB, M)
                    s_, ks = sqa.next()
                    S.act(actf(s_[0:M, 0, :], pa[0:M, :], AF.Square), reads=[kpa], writes=[ks + "a"])
                    S.act(actf(s_[0:M, 1, :], pb[0:M, :], AF.Square), reads=[kpb], writes=[ks + "b"])
                    S.pe(mmg([(self.ps[0][0:M, :], self.bd32_b[0:M, 0:M], s_[0:M, 0, :]),
                              (self.ps[0][0:M, :], self.bd32_b[0:M, 0:M], s_[0:M, 1, :])]), reads=[ks + "a", ks + "b", "cmb"], writes=["ps0"])
                    r_, kr = rq.next()
                    S.act(actf(r_[0:M, :], self.ps[0][0:M, :], AF.Sqrt, bias=EPS, scale=1.0 / 64), reads=["ps0"], writes=[kr])
                    S.dve(recip(r_[0:M, :], r_[0:M, :]), reads=[kr], writes=[kr])
                    a_, kan = an.next()
                    S.dve(stt(a_[0:M, 0, :], pa[0:M, :], gg[0:M, 0:1], r_[0:M, :], ALU.mult, ALU.mult), reads=[kpa, kr] + ggk, writes=[kan + "a"])
                    S.dve(stt(a_[0:M, 1, :], pb[0:M, :], gg[0:M, 1:2], r_[0:M, :], ALU.mult, ALU.mult), reads=[kpb, kr] + ggk, writes=[kan + "b"])
                    t_, kt = t4.next()
                    S.pool(tt(t_[0:M, 0, :], a_[0:M, 0, :], c_[0:M, 0, :], ALU.mult), reads=[kan + "a", kc], writes=[kt + "0"])
                    S.pool(tt(t_[0:M, 1, :], a_[0:M, 1, :], c_[0:M, 1, :], ALU.mult), reads=[kan + "b", kc], writes=[kt + "1"])
                    S.dve(tt(t_[0:M, 2, :], a_[0:M, 1, :], c_[0:M, 0, :], ALU.mult), reads=[kan + "b", kc], writes=[kt + "2"])
                    S.pool(tt(t_[0:M, 3, :], a_[0:M, 0, :], c_[0:M, 1, :], ALU.mult), reads=[kan + "a", kc], writes=[kt + "3"])
                    o_, ko = o12.next()
                    S.pool(tt(o_[0:M, 0, :], t_[0:M, 0, :], t_[0:M, 1, :], ALU.subtract), reads=[kt + "0", kt + "1"], writes=[ko + "0"])
                    S.dve(tt(o_[0:M, 1, :], t_[0:M, 2, :], t_[0:M, 3, :], ALU.add), reads=[kt + "2", kt + "3"], writes=[ko + "1"])
                    for hl in range(M // 32):
                        for half in range(2):
                            if grp < 2:
                                dst = self.QA[grp * 4 + hl, half * 32:(half + 1) * 32, t0:t0 + 512]
                            else:
                                dst = self.KA[hl * 64 + half * 32: hl * 64 + (half + 1) * 32, t0:t0 + 512]
                            S.dma(dst, o_[hl * 32:(hl + 1) * 32, half, :], reads=[ko + str(half)], writes=["QK"], chan=ko + "s%d%d" % (hl, half))
                mq_, kmq = mqs.next()
                mk_, kmk = mks.next()
                for h in range(4):
                    p, kp = fm(640 + h * 128, 128)
                    S.act(actf(mq_[:, h, :], p[:], AF.Copy), reads=[kp], writes=[kmq])
                for h in range(4):
                    p, kp = fm(1152 + h * 128, 128)
                    S.act(actf(mk_[:, h, :], p[:], AF.Copy, scale=float(128 ** -0.5)), reads=[kp], writes=[kmk])
                S.dma(self.MQ[:, :, t0:t0 + 512].rearrange("h d t -> d h t"), mq_[:], reads=[kmq], writes=["MQ"], chan=kmq + "s")
                S.dma(self.MKs[:, :, t0:t0 + 512].rearrange("h d t -> d h t"), mk_[:], reads=[kmk], writes=["MK"], chan=kmk + "s")
                vt, kv = vts.next()
                mv, kmv = mvs.next()
                sg, ksg = sgs.next()
                gt, kgt = gts.next()
                for sub in range(4):
                    hs = lambda c: h_[:, c, sub * 128:(sub + 1) * 128]
                    p1, k1 = ptm.next()
                    S.pe(mmg([(p1[:, 0:144], hs(c), w[:, c, 1664:1808]) for c in range(8)]), reads=hk + ["wab"], writes=[k1])
                    S.act(actf(vt[:, sub, :, 0:64], p1[:, 0:128].rearrange("p (h d) -> p h d", d=64), AF.Copy), reads=[k1], writes=[kv])
                    S.dve(tt(gt[:, sub, :], p1[:, 128:144], gbias[:], ALU.add), reads=[k1, "gbias"], writes=[kgt])
                    p2, k2 = ptm.next()
                    S.pe(mmg([(p2[:], hs(c), w[:, c, 1808:2320]) for c in range(8)]), reads=hk + ["wab"], writes=[k2])
                    S.dve(cp(mv[:, sub, :, 0:128], p2[:].rearrange("p (h d) -> p h d", d=128)), reads=[k2], writes=[kmv])
                    p3, k3 = ptm.next()
                    S.pe(mmg([(p3[:], hs(c), w[:, c, 2320:2832]) for c in range(8)]), reads=hk + ["wab"], writes=[k3])
                    S.act(actf(sg[:, sub, :], p3[:], AF.Sigmoid), reads=[k3], writes=[ksg])
                S.dma(self.VA[t0:t0 + 512, :].rearrange("(s p) c -> p s c", p=128), vt[:].rearrange("p s h d -> p s (h d)"), reads=[kv], writes=["VA"], chan=kv + "s")
                S.dma(self.MV[t0:t0 + 512, :].rearrange("(s p) c -> p s c", p=128), mv[:].rearrange("p s h d -> p s (h d)"), reads=[kmv], writes=["MV"], chan=kmv + "s")
                S.dma(self.SG[t0:t0 + 512, :].rearrange("(s p) c -> p s c", p=128), sg[:], reads=[ksg], writes=["SG"], chan=ksg + "s")
                S.dma(self.GT[t0:t0 + 512, :].rearrange("(s p) c -> p s c", p=128), gt[:], reads=[kgt], writes=["GT"], chan=kgt + "s")

    def attention(self):
        S, NT, NCH, T, nc = self.S, self.NT, self.NCH, self.T, self.nc
        with self.phase(psum=False):
            ph = self._ph
            ppr = Ring([(ph.enter_context(nc.psum_tensor(self.un("pp"), [128, 1024], F32)), "pp%d" % i) for i in range(2)])
            poa = (ph.enter_context(nc.psum_tensor(self.un("poa"), [128, 512], F32)), "poa")
            pob = (ph.enter_context(nc.psum_tensor(self.un("pob"), [128, 512], F32)), "pob")
            ptb = ph.enter_context(nc.psum_tensor(self.un("ptb"), [128, 1024], BF16))
            K = self.sb("Kall", [128, T], BF16)
            V = self.sb("Vall", [128, NCH, 130], BF16)
            S.dma(K[:], self.KA, writes=["K"], chan="K")
            S.dma(V[:], self.VA.rearrange("(b p) c -> p b c", p=128), writes=["V"], chan="V")
            qr = self.ring("q", 2, [128, 4, 512], BF16)
            pr = self.ring("p", 3, [128, 1024], BF16)
            rcr = self.ring("rc", 2, [128, 4], F32)
            otr = self.ring("ot", 2, [128, 4, 8, 64], BF16)
            ob = self.ring("ob", 2, [128, 4, 512], BF16)
            items = [(jq, hp, kb) for jq in range(NT) for hp in range(4) for kb in range(NCH)]
            tctx, ictx = {}, {}

            def stage1(i):
                jq, hp, kb = items[i]
                t0 = jq * 512
                if hp == 0 and kb == 0:
                    q_, kq = qr.next()
                    for kvh in range(2):
                        S.dma(q_[kvh * 64:(kvh + 1) * 64, :, :], self.QA[kvh * 4:(kvh + 1) * 4, :, t0:t0 + 512].rearrange("h d t -> d h t"),
                              writes=[kq + str(kvh)], chan=kq + str(kvh))
                    tctx[jq] = (q_, kq) + otr.next()
                q_, kq, ot, kot = tctx[jq]
                pb, kpb = ppr.next()

                def st2(e, pb=pb, q_=q_, hp=hp, kb=kb):
                    e.matmul(pb[:, 0:512], lhsT=K[0:64, kb * 128:(kb + 1) * 128], rhs=q_[0:64, hp, :], start=True, stop=True)
                    return e.matmul(pb[:, 512:1024], lhsT=K[64:128, kb * 128:(kb + 1) * 128], rhs=q_[64:128, hp, :], start=True, stop=True)
                S.pe(st2, reads=["K", kq + "0", kq + "1"], writes=[kpb])
                p_, kp_ = pr.next()
                S.act(actf(p_[:], pb[:], AF.Exp, bias=self.mb[:, jq, (kb // 4):(kb // 4) + 1]), reads=[kpb, "mb"], writes=[kp_])
                ictx[i] = (p_, kp_)

            def stage2(i):
                jq, hp, kb = items[i]
                t0 = jq * 512
                q_, kq, ot, kot = tctx[jq]
                p_, kp_ = ictx.pop(i)

                def pv8(e, p_=p_, kb=kb):
                    ins = None
                    for hh, (po, _) in enumerate((poa, pob)):
                        for sub in range(4):
                            ins = e.matmul(po[:, sub * 65:(sub + 1) * 65], lhsT=p_[:, hh * 512 + sub * 128: hh * 512 + (sub + 1) * 128],
                                           rhs=V[:, kb, hh * 65:(hh + 1) * 65], start=(kb == 0 and sub == 0), stop=(kb == NCH - 1 and sub == 3))
                    return ins
                S.pe(pv8, reads=["V", kp_], writes=["poa", "pob"])
                if kb != NCH - 1:
                    return
                for hh, (po, kpo) in enumerate((poa, pob)):
                    h = hh * 4 + hp
                    pv = po[:, 0:260].rearrange("p (s c) -> p s c", c=65)
                    r_, kr = rcr.next()
                    S.dve(recip(r_[:], pv[:, :, 64]), reads=[kpo], writes=[kr])
                    S.dve(tt(ot[:, :, h, :], pv[:, :, 0:64], r_[:].unsqueeze(2).to_broadcast([128, 4, 64]), ALU.mult), reads=[kpo, kr], writes=[kot])
                if hp != 3:
                    return
                o_, ko = ob.next()
                for sp in range(2):
                    def tr8(e, ot=ot, sp=sp):
                        ins = None
                        for hp2 in range(4):
                            for s_ in range(2):
                                slot = hp2 * 2 + s_
                                ins = e.transpose(ptb[:, slot * 128:(slot + 1) * 128],
                                                  ot[:, sp * 2 + s_, 2 * hp2:2 * hp2 + 2, :].rearrange("p h d -> p (h d)"), self.ident_b)
                        return ins
                    S.pe(tr8, reads=[kot, "cmb"], writes=["ptb"])
                    S.act(actf(o_[:, :, sp * 256:(sp + 1) * 256].rearrange("p a (s q) -> p a s q", q=128),
                               ptb[:].rearrange("p (a s q) -> p a s q", a=4, s=2), AF.Copy), reads=["ptb"], writes=[ko])
                S.dma(self.MIX[0:512, t0:t0 + 512].rearrange("(a p) t -> p a t", p=128), o_[:], reads=[ko], writes=["MIXa"], chan=ko + "s")

            self.pipeline(len(items), stage1, stage2, 1)

    def mlstm(self, j):
        S, NT, NCH, T, nc = self.S, self.NT, self.NCH, self.T, self.nc
        NG = NCH * 4
        with self.phase():
            G = self.sb("G", [128, NCH, 16], F32)
            S.dma(G[:], self.GT.rearrange("(c p) g -> p c g", p=128), writes=["G"], chan="G")
            G5 = G[:].rearrange("p c (d k h) -> p d c k h", d=2, k=2, h=4)
            gi, gf = G5[:, :, :, 0, :], G5[:, :, :, 1, :]
            shp = [128, 2, NCH, 4]
            mk = lambda n: self.sb(n, shp, F32)
            FL, Bc, A_, AMr, BLr, Mt, mt, WK, THR, KEEP = [mk(n) for n in ("FL", "Bc", "Aa", "AMr", "BLr", "Mt", "mt", "WK", "THR", "KEEP")]
            tmp = mk("tmp")
            fl2 = lambda t, d: t[:, d, :, :].rearrange("p c h -> p (c h)")
            S.act(actf(tmp[:], gf, AF.Abs), reads=["G"], writes=["tmp"])
            S.act(actf(tmp[:], tmp[:], AF.Exp, scale=-1.0), reads=["tmp"], writes=["tmp"])
            S.act(actf(tmp[:], tmp[:], AF.Ln, bias=1.0), reads=["tmp"], writes=["tmp"])
            S.dve(ts(FL[:], gf, 0.0, ALU.min), reads=["G"], writes=["FL"])
            S.dve(tt(FL[:], FL[:], tmp[:], ALU.subtract), reads=["FL", "tmp"], writes=["FL"])
            bw = min(128, NG)
            nblk = NG // bw
            am = self.sb("am", [128, 2, nblk], F32)
            dg = self.sb("dg", [128, 128], F32)
            for d in range(2):
                tri = self.cm[:, C_TRIU, :] if d == 0 else self.cm[:, C_TRIL, :]
                S.pe(mm1(self.ps[0][:, 0:NG], tri, fl2(FL, d)), reads=["FL", "cm"], writes=["ps0"])
                S.act(actf(fl2(Bc, d), self.ps[0][:, 0:NG], AF.Copy), reads=["ps0"], writes=["Bc"])
                S.pe(mm1(self.ps[1][:, 0:NG], self.ones_f, fl2(FL, d)), reads=["FL", "cm"], writes=["ps1"])
                S.act(actf(fl2(BLr, d), self.ps[1][:, 0:NG], AF.Copy), reads=["ps1"], writes=["BLr"])
                S.dve(tt(A_[:, d], gi[:, d], Bc[:, d], ALU.subtract), reads=["G", "Bc"], writes=["Aa"])
                for b in range(nblk):
                    S.pe(tr(self.ps[2][0:bw, 0:128], fl2(A_, d)[:, b * bw:(b + 1) * bw], self.ident_f), reads=["Aa", "cm"], writes=["ps2"])
                    S.dve(lambda e, d=d, b=b, ps2=self.ps[2]: e.tensor_reduce(out=am[0:bw, d, b:b + 1], in_=ps2[0:bw, 0:128], axis=AX.X, op=ALU.max),
                          reads=["ps2"], writes=["am"])
                    S.dve(ts(dg[0:bw, 0:bw], self.ident_f[0:bw, 0:bw], am[0:bw, d, b:b + 1], ALU.mult), reads=["am", "cm"], writes=["dg"])
                    S.pe(mm1(self.ps[3][:, 0:bw], self.ones_f[0:bw, :], dg[0:bw, 0:bw]), reads=["dg", "cm"], writes=["ps3"])
                    S.act(actf(fl2(AMr, d)[:, b * bw:(b + 1) * bw], self.ps[3][:, 0:bw], AF.Copy), reads=["ps3"], writes=["AMr"])
            mrun = self.sb("mrun", [128, 2, 4], F32)
            S.dve(mset(mrun[:], 0.0), writes=["mrun0", "mrun1"])
            for k in range(NCH):
                for d in range(2):
                    eng = "dve"
                    kk = "rec%d" % d
                    c = k if d == 0 else NCH - 1 - k
                    S.add(eng, ts(mt[:, d, c, :], mrun[:, d, :], self.carry[:, d, c:c + 1], ALU.mult), reads=["mrun%d" % d, "carry"], writes=[kk + "mt"])
                    S.add(eng, tt(Mt[:, d, c, :], mt[:, d, c, :], AMr[:, d, c, :], ALU.max), reads=[kk + "mt", "AMr"], writes=[kk + "Mt"])
                    S.add(eng, tt(mrun[:, d, :], Mt[:, d, c, :], BLr[:, d, c, :], ALU.add), reads=[kk + "Mt", "BLr"], writes=["mrun%d" % d])
            rk = ["rec0mt", "rec1mt", "rec0Mt", "rec1Mt"]
            S.dve(tt(KEEP[:], mt[:], Mt[:], ALU.subtract), reads=rk, writes=["KEEP"])
            S.act(actf(KEEP[:], KEEP[:], AF.Exp), reads=["KEEP"], writes=["KEEP"])
            S.dve(tt(KEEP[:].rearrange("p d c h -> p (d c) h"), KEEP[:].rearrange("p d c h -> p (d c) h"),
                     self.carry[:].rearrange("p d c -> p (d c)").unsqueeze(2).to_broadcast([128, 2 * NCH, 4]), ALU.mult), reads=["KEEP", "carry"], writes=["KEEP"])
            S.dve(tt(WK[:], A_[:], Mt[:], ALU.subtract), reads=["Aa"] + rk, writes=["WK"])
            S.act(actf(WK[:], WK[:], AF.Exp), reads=["WK"], writes=["WK"])
            S.dve(tt(THR[:], Bc[:], Mt[:], ALU.add), reads=["Bc"] + rk, writes=["THR"])
            S.act(actf(THR[:], THR[:], AF.Exp, scale=-1.0), reads=["THR"], writes=["THR"])
            maskf = self.cm[:, C_TRIU, :]
            maskb = self.cm[:, C_TRIL, :]
            qr = [self.ring("mq", 2, [128, 4, 512], BF16) for d in range(2)]
            kr = [self.ring("mk", 2, [128, 4, 512], BF16) for d in range(2)]
            vr = [self.ring("mv", 2, [128, 4, 516], BF16) for d in range(2)]
            Cst = [[self.sb("Cst", [128, 129], F32) for h in range(4)] for d in range(2)]
            Cbf = [[self.sb("Cbf", [128, 129], BF16) for h in range(4)] for d in range(2)]
            for d in range(2):
                for h in range(4):
                    S.pool(mset(Cst[d][h][:], 0.0), writes=["C%d%d" % (d, h)])
            atr = self.ring("at", 6, [128, 128], BF16)
            kwr = self.ring("kw", 6, [128, 128], BF16)
            rr_ = self.ring("r", 6, [128, 2], F32)
            hst = [self.ring("hst", 2, [128, 512], F32) for d in range(2)]
            pS = Ring([(self.ps[0], "ps0"), (self.ps[1], "ps1")])
            pH = Ring([(self.ps[2], "ps2"), (self.ps[3], "ps3")])
            pU = Ring([(self.ps[4], "ps4"), (self.ps[5], "ps5")])
            pT = Ring([(self.psb[0], "psb0"), (self.psb[1], "psb1")])
            MQ3 = self.MQ.rearrange("h d t -> d h t")
            MK3 = self.MKs.rearrange("h d t -> d h t")
            HO = [self.HF, self.HB]
            items = [(k, h, d) for k in range(NCH) for h in range(4) for d in range(2)]
            tctx, cctx, ictx = {}, {}, {}

            def geom(it):
                k, h, d = it
                c = k if d == 0 else NCH - 1 - k
                return d, c // 4, c % 4, h, c, (c // 4) * 512

            def ensure(d, jt):
                if (d, jt) in tctx or jt < 0 or jt >= NT:
                    return
                t0 = jt * 512
                q_, kq = qr[d].next()
                k_, kk = kr[d].next()
                v_, kv = vr[d].next()
                S.dma(q_[:], MQ3[:, :, t0:t0 + 512], writes=[kq], chan=kq)
                S.dma(k_[:], MK3[:, :, t0:t0 + 512], writes=[kk], chan=kk)
                S.dma(v_[:], self.MV[t0:t0 + 512, :].rearrange("(s p) c -> p s c", p=128), writes=[kv], chan=kv)
                tctx[(d, jt)] = (q_, kq, k_, kk, v_, kv)

            def stage1(i):
                d, jt, sub, h, c, t0 = geom(items[i])
                mask = maskf if d == 0 else maskb
                if h == 0:
                    ensure(d, jt)
                    if items[i][0] % 4 == 1:
                        ensure(d, jt + (1 if d == 0 else -1))
                    cctx[(d, c)] = hst[d].next()
                q_, kq, k_, kk, v_, kv = tctx[(d, jt)]
                ck = "C%d%d" % (d, h)
                qs = q_[:, h, sub * 128:(sub + 1) * 128]
                ks_ = k_[:, h, sub * 128:(sub + 1) * 128]
                wkc = WK[:, d, c, h:h + 1]
                st_, kst = pS.next()
                S.pe(mm1(st_[:, 0:128], ks_, qs), reads=[kk, kq], writes=[kst])
                at, kat = atr.next()
                S.dve(stt(at[:], st_[:, 0:128], wkc, mask, ALU.mult, ALU.mult), reads=[kst, "WK", "cm"], writes=[kat])
                ptr, kptr = pT.next()
                S.pe(tr(ptr[:, 0:128], ks_, self.ident_b), reads=[kk, "cmb"], writes=[kptr])
                kw, kkw = kwr.next()
                S.act(actf(kw[:], ptr[:, 0:128], AF.Copy, scale=wkc), reads=[kptr, "WK"], writes=[kkw])
                S.act(actf(Cbf[d][h][:], Cst[d][h][:], AF.Copy, scale=KEEP[:, d, c, h:h + 1]), reads=[ck, "KEEP"], writes=[ck + "b"])
                ictx[i] = (at, kat, kw, kkw)

            def stage2(i):
                d, jt, sub, h, c, t0 = geom(items[i])
                q_, kq, k_, kk, v_, kv = tctx[(d, jt)]
                hs_, khs = cctx[(d, c)]
                at, kat, kw, kkw = ictx.pop(i)
                ck = "C%d%d" % (d, h)
                tc0 = t0 + sub * 128
                qs = q_[:, h, sub * 128:(sub + 1) * 128]
                vs = v_[:, sub, h * 129:(h + 1) * 129]
                ph, kph = pH.next()
                S.pe(mmg([(ph[:, 0:129], at[:], vs), (ph[:, 0:129], qs, Cbf[d][h][:])]), reads=[kat, kv, kq, ck + "b"], writes=[kph])
                pu, kpu = pU.next()
                S.pe(mm1(pu[:, 0:129], kw[:], vs), reads=[kkw, kv], writes=[kpu])
                S.dve(stt(Cst[d][h][:], Cst[d][h][:], KEEP[:, d, c, h:h + 1], pu[:, 0:129], ALU.mult, ALU.add), reads=[ck, kpu, "KEEP"], writes=[ck])
                r_, kr_ = rr_.next()
                S.dve(ts(r_[:, 0:1], ph[:, 128:129], -1.0, ALU.mult, THR[:, d, c, h:h + 1], ALU.max), reads=[kph, "THR"], writes=[kr_])
                S.dve(tt(r_[:, 0:1], r_[:, 0:1], ph[:, 128:129], ALU.max), reads=[kr_, kph], writes=[kr_])
                S.dve(recip(r_[:, 1:2], r_[:, 0:1]), reads=[kr_], writes=[kr_])
                S.act(actf(hs_[:, h * 128:(h + 1) * 128], ph[:, 0:128], AF.Copy, scale=r_[:, 1:2]), reads=[kph, kr_], writes=[khs])
                if h == 3:
                    S.dma(HO[d][tc0:tc0 + 128, :], hs_[:], reads=[khs], writes=["HO"], chan=khs + "s")

            self.pipeline(len(items), stage1, stage2, 2)

        with self.phase():
            gain = self.sb("ogain", [128, 512], F32)
            S.dma(gain[:], self.w["mlstm_out_norm"][j:j + 1, :].to_broadcast([128, 512]), writes=["ogain"], chan="ogain")
            hfr = self.ring("hf", 3, [128, 512], F32)
            hbr = self.ring("hb", 3, [128, 512], F32)
            sgr = self.ring("sgl", 3, [128, 512], F32)
            ssq = self.ring("ssq", 3, [128, 8], F32)
            junk = self.ring("junk", 2, [128, 128], F32)
            ymr = self.ring("ym", 3, [128, 512], BF16)
            mxs = self.ring("mxs", 2, [128, 4, 512], BF16)
            pT = Ring([(self.psb[0], "psb0"), (self.psb[1], "psb1")])

            def ld(c):
                hf, khf = hfr.next()
                hb, khb = hbr.next()
                sg, ksg = sgr.next()
                S.dma(hf[:], self.HF[c * 128:(c + 1) * 128, :], writes=[khf], chan=khf)
                S.dma(hb[:], self.HB[c * 128:(c + 1) * 128, :], writes=[khb], chan=khb)
                S.dma(sg[:], self.SG[c * 128:(c + 1) * 128, :], writes=[ksg], chan=ksg)
                return hf, khf, hb, khb, sg, ksg
            pend = [ld(0), ld(1)] if NCH > 1 else [ld(0)]
            for c in range(NCH):
                hf, khf, hb, khb, sg, ksg = pend.pop(0)
                if c + 2 < NCH:
                    pend.append(ld(c + 2))
                sub = c % 4
                if sub == 0:
                    mx, kmx = mxs.next()
                S.dve(tt(hf[:], hf[:], hb[:], ALU.add), reads=[khf, khb], writes=[khf])
                sq_, ksq = ssq.next()
                jk, kjk = junk.next()
                S.dve(mset(sq_[:], 0.0), writes=[ksq + "a", ksq + "b"])
                for hh in range(4):
                    S.act(actf(jk[:], hf[:, hh * 128:(hh + 1) * 128], AF.Square, accum=sq_[:, hh:hh + 1]), reads=[khf], writes=[kjk, ksq + "a"])
                S.act(actf(sq_[:, 4:8], sq_[:, 0:4], AF.Sqrt, bias=EPS, scale=1.0 / 128), reads=[ksq + "a"], writes=[ksq + "b"])
                S.dve(recip(sq_[:, 4:8], sq_[:, 4:8]), reads=[ksq + "b"], writes=[ksq + "b"])
                S.dve(tt(hf[:].rearrange("p (h e) -> p h e", e=128), hf[:].rearrange("p (h e) -> p h e", e=128),
                         sq_[:, 4:8].unsqueeze(2).to_broadcast([128, 4, 128]), ALU.mult), reads=[khf, ksq + "b"], writes=[khf])
                S.pool(tt(sg[:], sg[:], gain[:], ALU.mult), reads=[ksg, "ogain"], writes=[ksg])
                ym, kym = ymr.next()
                S.dve(tt(ym[:], hf[:], sg[:], ALU.mult), reads=[khf, ksg], writes=[kym])
                ptr, kptr = pT.next()

                def tr4(e, ptr=ptr, ym=ym):
                    ins = None
                    for hh in range(4):
                        ins = e.transpose(ptr[:, hh * 128:(hh + 1) * 128], ym[:, hh * 128:(hh + 1) * 128], self.ident_b)
                    return ins
                S.pe(tr4, reads=[kym, "cmb"], writes=[kptr])
                S.act(actf(mx[:, :, sub * 128:(sub + 1) * 128], ptr[:, 0:512].rearrange("p (h e) -> p h e", e=128), AF.Copy), reads=[kptr], writes=[kmx])
                if sub == 3:
                    t0 = (c // 4) * 512
                    S.dma(self.MIX[512:1024, t0:t0 + 512].rearrange("(h p) t -> p h t", p=128), mx[:], reads=[kmx], writes=["MIXb"], chan=kmx + "s")

    def ret_inproj(self, X, j, L):
        S, NT, nc = self.S, self.NT, self.nc
        with self.phase():
            w = self.sb("wret", [128, 8, 6144], BF16)
            g = self.load_gain(self.w["norm_mix"][L], "gmix")
            self.prep_weight(w, "wret", self.w["ret_w_in"][j], 8, [(i * 2048, 2048, i * 2048, 2048, None) for i in range(3)], gain=g, gkey="gmix")
            xt, sq, rs, hT = self.norm_bufs()
            X3 = self.x3(X)
            cs = self.ring("cs", 1, [128, 4, 512], F32)
            t4 = self.ring("t4", 1, [128, 4, 512], F32)
            o12 = self.ring("o12", 2, [128, 2, 512], BF16)
            vts = self.ring("rv", 1, [128, 4, 2048], BF16)
            gts = self.ring("rg", 1, [128, 2048], F32)
            pf = Ring([(self.ps[i], self.pk[i]) for i in (1, 2, 3)])
            ptm = Ring([(self.ps[4], "ps4"), (self.ps[5], "ps5")])
            def cs_load(jt_):
                c_, kc = cs.next()
                S.dma(c_[:, 0:2, :], self.ropeR[:, :, jt_ * 512:(jt_ + 1) * 512].rearrange("a p t -> p a t"), writes=[kc, kc + "k"], chan=kc)
                return c_, kc
            nxt = xt.next()
            self.norm_load(X3, 0, 512, nxt)
            cnxt = cs_load(0)
            for jt in range(NT):
                t0 = jt * 512
                bufs = {"xt": nxt, "sq": sq.next(), "rs": rs.next(), "hT": hT.next(), "ss": (self.ps[0], "ps0")}
                hk = self.norm_tile(X3, t0, 512, bufs, load=False)
                if jt + 1 < NT:
                    nxt = xt.next()
                    self.norm_load(X3, t0 + 512, 512, nxt)
                h_ = bufs["hT"][0]
                c_, kc = cnxt
                S.act(actf(c_[:, 2:4, :], c_[:, 0:2, :], AF.Copy, scale=1.0 / 16), reads=[kc], writes=[kc + "k"])
                for qk in range(2):
                    co = 0 if qk == 0 else 2
                    ck_ = [kc] if qk == 0 else [kc + "k"]
                    dstT = self.RQ if qk == 0 else self.RK
                    for h in range(4):
                        col = qk * 1024 + h * 256
                        pa, kpa = pf.next()
                        S.pe(mmg([(pa[:], w[:, c, col:col + 128], h_[:, c, :]) for c in range(8)]), reads=hk + self.wkeys("wret", col, 128), writes=[kpa])
                        pb, kpb = pf.next()
                        S.pe(mmg([(pb[:], w[:, c, col + 128:col + 256], h_[:, c, :]) for c in range(8)]), reads=hk + self.wkeys("wret", col + 128, 128), writes=[kpb])
                        t_, kt = t4.next()
                        S.dve(tt(t_[:, 0, :], pa[:], c_[:, co, :], ALU.mult), reads=[kpa] + ck_, writes=[kt + "0"])
                        S.dve(tt(t_[:, 1, :], pb[:], c_[:, co + 1, :], ALU.mult), reads=[kpb] + ck_, writes=[kt + "1"])
                        S.dve(tt(t_[:, 2, :], pb[:], c_[:, co, :], ALU.mult), reads=[kpb] + ck_, writes=[kt + "2"])
                        S.dve(tt(t_[:, 3, :], pa[:], c_[:, co + 1, :], ALU.mult), reads=[kpa] + ck_, writes=[kt + "3"])
                        o_, ko = o12.next()
                        S.pool(tt(o_[:, 0, :], t_[:, 0, :], t_[:, 1, :], ALU.subtract), reads=[kt + "0", kt + "1"], writes=[ko])
                        S.pool(tt(o_[:, 1, :], t_[:, 2, :], t_[:, 3, :], ALU.add), reads=[kt + "2", kt + "3"], writes=[ko])
                        S.dma(dstT[2 * h:2 * h + 2, :, t0:t0 + 512].rearrange("a p t -> p a t"), o_[:], reads=[ko], writes=["RQK"], chan=ko + "s")
                if jt + 1 < NT:
                    cnxt = cs_load(jt + 1)
                vt, kv = vts.next()
                for sub in range(4):
                    hs = lambda c: h_[:, c, sub * 128:(sub + 1) * 128]
                    for blk in range(4):
                        p1, k1 = ptm.next()
                        S.pe(mmg([(p1[:], hs(c), w[:, c, 2048 + blk * 512:2048 + (blk + 1) * 512]) for c in range(8)]), reads=hk + self.wkeys("wret", 2048 + blk * 512, 512), writes=[k1])
                        S.act(actf(vt[:, sub, blk * 512:(blk + 1) * 512], p1[:], AF.Copy), reads=[k1], writes=[kv])
                    gt, kg = gts.next()
                    for blk in range(4):
                        p1, k1 = ptm.next()
                        S.pe(mmg([(p1[:], hs(c), w[:, c, 4096 + blk * 512:4096 + (blk + 1) * 512]) for c in range(8)]), reads=hk + self.wkeys("wret", 4096 + blk * 512, 512), writes=[k1])
                        S.act(actf(gt[:, blk * 512:(blk + 1) * 512], p1[:], AF.Silu), reads=[k1], writes=[kg])
                    S.dma(self.RG[t0 + sub * 128:t0 + (sub + 1) * 128, :], gt[:], reads=[kg], writes=["RG"], chan=kg + "s")
                S.dma(self.RV[t0:t0 + 512, :].rearrange("(s p) c -> p s c", p=128), vt[:], reads=[kv], writes=["RV"], chan=kv + "s")

    def retention(self, j):
        S, NT, NCH, T, nc = self.S, self.NT, self.NCH, self.T, self.nc
        with self.phase():
            lg = self.sb("lg", [128, 8], F32)
            tmp = self.sb("lgt", [128, 8], F32)
            S.dma(lg[:], self.w["ret_decay_logit"][j:j + 1].rearrange("a d h -> a (d h)").to_broadcast([128, 8]), writes=["lg"], chan="lg")
            S.act(actf(tmp[:], lg[:], AF.Abs), reads=["lg"], writes=["lgt"])
            S.act(actf(tmp[:], tmp[:], AF.Exp, scale=-1.0), reads=["lgt"], writes=["lgt"])
            S.act(actf(tmp[:], tmp[:], AF.Ln, bias=1.0), reads=["lgt"], writes=["lgt"])
            S.dve(ts(lg[:], lg[:], 0.0, ALU.min), reads=["lg"], writes=["lg"])
            S.dve(tt(lg[:], lg[:], tmp[:], ALU.subtract), reads=["lg", "lgt"], writes=["lg"])
            DT = self.sb("DT", [128, 8, 128], F32)
            QD = self.sb("QD", [128, 8, 128], F32)
            QDb = self.sb("QDb", [128, 8, 128], BF16)
            KD = self.sb("KD", [128, 8], F32)
            CDC = self.sb("CDC", [128, 8, NCH], F32)
            cdv = self.sb("cdv", [128, 8], F32)
            for hd in range(8):
                d = hd // 4
                S.act(actf(DT[:, hd, :], self.cm[:, C_DFW if d == 0 else C_DBW, :], AF.Exp, scale=lg[:, hd:hd + 1]), reads=["lg", "cm"], writes=["DT"])
                S.dve(tt(DT[:, hd, :], DT[:, hd, :], self.cm[:, C_TRIU if d == 0 else C_SL, :], ALU.mult), reads=["DT", "cm"], writes=["DT"])
                S.act(actf(QD[:, hd, :], self.cm[:, C_QDF if d == 0 else C_QDB, :], AF.Exp, scale=lg[:, hd:hd + 1]), reads=["lg", "cm"], writes=["QD"])
                S.dve(cp(QDb[:, hd, :], QD[:, hd, :]), reads=["QD"], writes=["QDb"])
                S.act(actf(KD[:, hd:hd + 1], self.cm[:, C_KD, d:d + 1], AF.Exp, scale=lg[:, hd:hd + 1]), reads=["lg", "cm"], writes=["KD"])
                S.act(actf(cdv[:, hd:hd + 1], self.cm[:, C_KD, 2:3], AF.Exp, scale=lg[:, hd:hd + 1]), reads=["lg", "cm"], writes=["cdv"])
                S.dve(ts(CDC[:, hd, :], self.carry[:, d, :], cdv[:, hd:hd + 1], ALU.mult), reads=["cdv", "carry"], writes=["CDC"])
            qr = [self.ring("rq", 2, [128, 8, 512], BF16) for d in range(2)]
            kr = [self.ring("rk", 2, [128, 8, 512], BF16) for d in range(2)]
            vr = [self.ring("rv", 3, [128, 2048], BF16) for d in range(2)]
            St = [[[self.sb("St", [128, 512], F32) for c in range(2)] for h in range(4)] for d in range(2)]
            Sb = [[[self.sb("Sb", [128, 512], BF16) for c in range(2)] for h in range(4)] for d in range(2)]
            for d in range(2):
                for h in range(4):
                    for c in range(2):
                        S.pool(mset(St[d][h][c][:], 0.0), writes=["S%d%d%d" % (d, h, c)])
            atr = self.ring("at", 6, [128, 128], BF16)
            kwr = self.ring("kw", 6, [128, 2, 128], BF16)
            qdr = self.ring("qd", 6, [128, 2, 128], BF16)
            yst = [self.ring("yst", 2, [128, 2048], F32) for d in range(2)]
            pS = Ring([(self.ps[0][:, 0:128], "ps0"), (self.ps[5][:, 0:128], "ps5")])
            pO = Ring([(self.ps[1], "ps1"), (self.ps[2], "ps2")])
            pU = Ring([(self.ps[3], "ps3"), (self.ps[4], "ps4")])
            pT = Ring([(self.psb[0], "psb0"), (self.psb[1], "psb1")])
            RQ3 = self.RQ.rearrange("a p t -> p a t")
            RK3 = self.RK.rearrange("a p t -> p a t")
            YO = [self.YF, self.YB]
            items = [(k, h, d) for k in range(NCH) for h in range(4) for d in range(2)]
            tctx, cctx, ictx = {}, {}, {}

            def geom(it):
                k, h, d = it
                c = k if d == 0 else NCH - 1 - k
                return d, c // 4, c % 4, h, c, (c // 4) * 512

            def ensure(d, jt):
                if (d, jt) in tctx or jt < 0 or jt >= NT:
                    return
                t0 = jt * 512
                q_, kq = qr[d].next()
                k_, kk = kr[d].next()
                S.dma(q_[:], RQ3[:, :, t0:t0 + 512], writes=[kq], chan=kq)
                S.dma(k_[:], RK3[:, :, t0:t0 + 512], writes=[kk], chan=kk)
                tctx[(d, jt)] = (q_, kq, k_, kk)

            def ensure_v(d, c):
                if (d, c) in cctx or c < 0 or c >= NCH:
                    return
                v_, kv = vr[d].next()
                S.dma(v_[:], self.RV[c * 128:(c + 1) * 128, :], writes=[kv], chan=kv)
                cctx[(d, c)] = (v_, kv) + yst[d].next()

            def stage1(i):
                d, jt, sub, h, c, t0 = geom(items[i])
                if h == 0:
                    ensure(d, jt)
                    if items[i][0] % 4 == 1:
                        ensure(d, jt + (1 if d == 0 else -1))
                    ensure_v(d, c)
                    ensure_v(d, c + (1 if d == 0 else -1))
                q_, kq, k_, kk = tctx[(d, jt)]
                sl = slice(sub * 128, (sub + 1) * 128)
                hd = d * 4 + h
                st_, kst = pS.next()
                S.pe(mmg([(st_, k_[:, 2 * h + cc, sl], q_[:, 2 * h + cc, sl]) for cc in range(2)]), reads=[kk, kq], writes=[kst])
                at, kat = atr.next()
                S.dve(tt(at[:], st_, DT[:, hd, :], ALU.mult), reads=[kst, "DT"], writes=[kat])
                kw, kkw = kwr.next()
                qd, kqd = qdr.next()
                ptr, kptr = pT.next()

                def tr2(e, ptr=ptr, k_=k_, h=h, sl=sl):
                    ins = None
                    for cc in range(2):
                        ins = e.transpose(ptr[:, cc * 128:(cc + 1) * 128], k_[:, 2 * h + cc, sl], self.ident_b)
                    return ins
                S.pe(tr2, reads=[kk, "cmb"], writes=[kptr])
                S.act(actf(kw[:], ptr[:, 0:256].rearrange("p (c e) -> p c e", e=128), AF.Copy, scale=KD[:, hd:hd + 1]), reads=[kptr, "KD"], writes=[kkw])
                S.pool(tt(qd[:], q_[:, 2 * h:2 * h + 2, sl], QDb[:, hd, :].unsqueeze(1).to_broadcast([128, 2, 128]), ALU.mult), reads=[kq, "QDb"], writes=[kqd])
                for cc in range(2):
                    sk = "S%d%d%d" % (d, h, cc)
                    S.act(actf(Sb[d][h][cc][:], St[d][h][cc][:], AF.Copy, scale=self.carry[:, d, c:c + 1]), reads=[sk, "carry"], writes=[sk + "b"])
                ictx[i] = (at, kat, kw, kkw, qd, kqd)

            def stage2(i):
                d, jt, sub, h, c, t0 = geom(items[i])
                v_, kv, ys, kys = cctx[(d, c)]
                at, kat, kw, kkw, qd, kqd = ictx.pop(i)
                hd = d * 4 + h
                vs = v_[:, h * 512:(h + 1) * 512]
                po, kpo = pO.next()
                S.pe(mmg([(po[:], at[:], vs), (po[:], qd[:, 0, :], Sb[d][h][0][:]), (po[:], qd[:, 1, :], Sb[d][h][1][:])]),
                     reads=[kat, kv, kqd, "S%d%d0b" % (d, h), "S%d%d1b" % (d, h)], writes=[kpo])
                for cc in range(2):
                    pu, kpu = pU.next()
                    sk = "S%d%d%d" % (d, h, cc)
                    S.pe(mm1(pu[:], kw[:, cc, :], vs), reads=[kkw, kv], writes=[kpu])
                    S.dve(stt(St[d][h][cc][:], St[d][h][cc][:], CDC[:, hd, c:c + 1], pu[:], ALU.mult, ALU.add), reads=[sk, kpu, "CDC"], writes=[sk])
                S.dve(cp(ys[:, h * 512:(h + 1) * 512], po[:]), reads=[kpo], writes=[kys])
                if h == 3:
                    S.dma(YO[d][c * 128:(c + 1) * 128, :], ys[:], reads=[kys], writes=["YO"], chan=kys + "s")

            self.pipeline(len(items), stage1, stage2, 2)

        with self.phase():
            gain = self.sb("rgain", [128, 2048], F32)
            S.dma(gain[:], self.w["ret_out_norm"][j:j + 1, :].to_broadcast([128, 2048]), writes=["rgain"], chan="rgain")
            yfr = self.ring("yf", 2, [128, 2048], F32)
            ybr = self.ring("yb", 2, [128, 2048], F32)
            rgr = self.ring("rgl", 2, [128, 2048], F32)
            ssq = self.ring("ssq", 3, [128, 8], F32)
            junk = self.ring("junk", 2, [128, 512], F32)
            ymr = self.ring("ym", 2, [128, 2048], BF16)
            mxs = self.ring("mxs", 2, [128, 16, 512], BF16)
            pT = Ring([(self.psb[0], "psb0"), (self.psb[1], "psb1")])

            def ld(c):
                yf, kyf = yfr.next()
                yb, kyb = ybr.next()
                rg, krg = rgr.next()
                S.dma(yf[:], self.YF[c * 128:(c + 1) * 128, :], writes=[kyf], chan=kyf)
                S.dma(yb[:], self.YB[c * 128:(c + 1) * 128, :], writes=[kyb], chan=kyb)
                S.dma(rg[:], self.RG[c * 128:(c + 1) * 128, :], writes=[krg], chan=krg)
                return yf, kyf, yb, kyb, rg, krg
            nxt = ld(0)
            for c in range(NCH):
                yf, kyf, yb, kyb, rg, krg = nxt
                if c + 1 < NCH:
                    nxt = ld(c + 1)
                sub = c % 4
                sl = slice(sub * 128, (sub + 1) * 128)
                if sub == 0:
                    mx, kmx = mxs.next()
                S.pool(tt(yf[:], yf[:], yb[:], ALU.add), reads=[kyf, kyb], writes=[kyf])
                sq_, ksq = ssq.next()
                jk, kjk = junk.next()
                S.dve(mset(sq_[:], 0.0), writes=[ksq + "a", ksq + "b"])
                for hh in range(4):
                    S.act(actf(jk[:], yf[:, hh * 512:(hh + 1) * 512], AF.Square, accum=sq_[:, hh:hh + 1]), reads=[kyf], writes=[kjk, ksq + "a"])
                S.act(actf(sq_[:, 4:8], sq_[:, 0:4], AF.Sqrt, bias=EPS, scale=1.0 / 512), reads=[ksq + "a"], writes=[ksq + "b"])
                S.dve(recip(sq_[:, 4:8], sq_[:, 4:8]), reads=[ksq + "b"], writes=[ksq + "b"])
                S.dve(tt(yf[:].rearrange("p (h e) -> p h e", e=512), yf[:].rearrange("p (h e) -> p h e", e=512),
                         sq_[:, 4:8].unsqueeze(2).to_broadcast([128, 4, 512]), ALU.mult), reads=[kyf, ksq + "b"], writes=[kyf])
                S.pool(tt(rg[:], rg[:], gain[:], ALU.mult), reads=[krg, "rgain"], writes=[krg])
                ym, kym = ymr.next()
                S.dve(tt(ym[:], yf[:], rg[:], ALU.mult), reads=[kyf, krg], writes=[kym])
                for bg in range(2):
                    ptr, kptr = pT.next()

                    def tr8(e, ptr=ptr, ym=ym, bg=bg):
                        ins = None
                        for b_ in range(8):
                            ins = e.transpose(ptr[:, b_ * 128:(b_ + 1) * 128], ym[:, (bg * 8 + b_) * 128:(bg * 8 + b_ + 1) * 128], self.ident_b)
                        return ins
                    S.pe(tr8, reads=[kym, "cmb"], writes=[kptr])
                    S.act(actf(mx[:, bg * 8:(bg + 1) * 8, sl], ptr[:].rearrange("p (b e) -> p b e", e=128), AF.Copy), reads=[kptr], writes=[kmx])
                if sub == 3:
                    t0 = (c // 4) * 512
                    S.dma(self.MIX[0:2048, t0:t0 + 512].rearrange("(b p) t -> p b t", p=128), mx[:], reads=[kmx], writes=["MIXr"], chan=kmx + "s")


def rope_tables(seglen, head_dim, nseg, reps):
    rows = seglen // 64
    row_idx = np.repeat(np.arange(rows, dtype=np.float32), 64)
    col_idx = np.tile(np.arange(64, dtype=np.float32), rows)
    axis_dim = head_dim // 2
    inv_freq = (np.float32(10000.0) ** (-np.arange(0, axis_dim, 2, dtype=np.float32) / np.float32(axis_dim))).astype(np.float32)
    ang = np.concatenate([row_idx[:, None] * inv_freq, col_idx[:, None] * inv_freq], axis=-1).astype(np.float32)
    cs = np.stack([np.cos(ang), np.sin(ang)], 0).astype(np.float32)
    cs = np.tile(cs, (1, nseg, 1))
    cs = cs.transpose(0, 2, 1)
    return np.ascontiguousarray(np.tile(cs, (1, reps, 1)))


def core_tables(T, nseg):
    NT, NCH = T // 512, T // 128
    seglen = T // nseg
    tps = NT // nseg
    cps = NCH // nseg
    seg_t = np.arange(NT) // tps
    mb = np.where(seg_t[:, None] == seg_t[None, :], 0.0, -30000.0).astype(np.float32)
    carry = np.ones((2, NCH), np.float32)
    carry[0, np.arange(NCH) % cps == 0] = 0.0
    carry[1, np.arange(NCH) % cps == cps - 1] = 0.0
    flb = np.ones((NT, 2), np.float32)
    flb[np.arange(NT) % tps == 0, 0] = 0.0
    flb[np.arange(NT) % tps == tps - 1, 1] = 0.0
    rep = lambda a: np.ascontiguousarray(np.broadcast_to(a.reshape(1, -1), (128, a.size))).astype(np.float32)
    return {
        "maskb": rep(mb), "carry": rep(carry), "flb": rep(flb),
        "ropeA": rope_tables(seglen, 64, nseg, 4), "ropeR": rope_tables(seglen, 256, nseg, 1),
    }


DBG = {}
FULL_STEPS = [("ab", 0, 0), ("ffn", 0), ("ret", 0, 1), ("ffn", 1), ("ab", 1, 2), ("ffn", 2), ("ret", 1, 3), ("ffn", 3), ("final",)]
_CACHE = {}


def run_cores(T, steps, core_x, core_nseg, weights):
    key = (T, tuple(steps))
    if key not in _CACHE:
        _CACHE[key] = MK(T, steps).build()
    nc = _CACHE[key]
    cm = host_cmat()
    tabs = {}
    in_maps = []
    for x, ns in zip(core_x, core_nseg):
        if ns not in tabs:
            tabs[ns] = core_tables(T, ns)
        m = {"xT": np.ascontiguousarray(np.asarray(x, np.float32).T), "cmat": cm}
        m.update(tabs[ns])
        for n, _ in MK.W_SPECS:
            m[n] = weights[n]
        in_maps.append(m)
    res = run_bass_kernel_spmd(nc, in_maps, core_ids=list(range(len(in_maps))))
    return [np.ascontiguousarray(r["yT"].T) for r in res.results]


def kernel(x_prompt, x_sample, **weights):
    weights = {k: np.ascontiguousarray(np.asarray(v, np.float32)) for k, v in weights.items()}
    xp = np.asarray(x_prompt, np.float32)
    xs = np.asarray(x_sample, np.float32)
    T = 8192
    core_x = [xp[0], xp[1]] + [xs[4 * i:4 * i + 4].reshape(T, 1024) for i in range(4)]
    nseg = [1, 1, 4, 4, 4, 4]
    core_x += [core_x[5], core_x[5]]
    nseg += [4, 4]
    outs = run_cores(T, FULL_STEPS, core_x, nseg, weights)
    y_prompt = np.stack([outs[0], outs[1]], 0)
    y_sample = np.concatenate([outs[2 + i].reshape(4, 2048, 1024) for i in range(4)], 0)
    return (y_prompt, y_sample)
```

```python
import contextlib
import numpy as np
import concourse.bass as bass
import concourse.mybir as mybir
from concourse.bass_utils import run_bass_kernel_spmd

F32 = mybir.dt.float32
BF16 = mybir.dt.bfloat16
AF = mybir.ActivationFunctionType
ALU = mybir.AluOpType
AX = mybir.AxisListType
EPS = 1e-6


class _Op:
    __slots__ = ("eng", "fn", "waits", "signal", "idx", "chan", "cidx", "sigval")

    def __init__(self, eng, fn, chan):
        self.eng = eng
        self.fn = fn
        self.chan = chan
        self.waits = []
        self.signal = False
        self.idx = -1
        self.cidx = -1
        self.sigval = 0


class Sched:
    ENG = ("pe", "act", "dve", "pool", "sp")

    def __init__(self, nc):
        self.nc = nc
        self.ops = {e: [] for e in self.ENG}
        self.res = {}
        self.waited = {e: {} for e in self.ENG}
        self.chan_last = {}
        self.chan_n = {}
        self.chan_phase = {}
        self.phase_id = 0

    def _wait(self, op, d):
        eng = op.eng
        if d is op:
            return
        if d.chan is not None:
            ek, val = ("c", d.chan), d.cidx
        else:
            if d.eng == eng and eng == "pe":
                return
            ek, val = d.eng, d.idx
        w = self.waited[eng]
        if w.get(ek, -1) >= val:
            return
        w[ek] = val
        op.waits.append(d)
        d.signal = True

    def add(self, eng, fn, reads=(), writes=(), chan=None):
        if chan is not None:
            chan = (self.phase_id, chan)
        op = _Op(eng, fn, chan)
        deps = []
        res = self.res
        for k in reads:
            st = res.get(k)
            if st is not None and st[0] is not None:
                deps.append(st[0])
        for k in writes:
            st = res.get(k)
            if st is not None:
                if st[0] is not None:
                    deps.append(st[0])
                deps.extend(st[1])
        if chan is not None:
            prev = self.chan_last.get(chan)
            if prev is not None:
                deps.append(prev)
            op.cidx = self.chan_n.get(chan, 0)
            if op.cidx == 0:
                self.chan_phase[chan] = self.phase_id
            assert self.chan_phase[chan] == self.phase_id, chan
            self.chan_n[chan] = op.cidx + 1
            self.chan_last[chan] = op
            op.signal = True
        op.idx = len(self.ops[eng])
        for d in deps:
            self._wait(op, d)
        self.ops[eng].append(op)
        for k in writes:
            res[k] = [op, []]
        for k in reads:
            st = res.get(k)
            if st is None:
                res[k] = [None, [op]]
            elif st[0] is not op:
                st[1].append(op)
        return op

    def barrier(self):
        lasts = []
        for e in self.ENG:
            for o in reversed(self.ops[e]):
                if o.fn is not None and o.chan is None:
                    lasts.append(o)
                    break
        lasts.extend(self.chan_last.values())
        for e in self.ENG:
            op = _Op(e, None, None)
            op.idx = len(self.ops[e])
            for d in lasts:
                self._wait(op, d)
            self.ops[e].append(op)
        self.res = {}
        self.phase_id += 1

    def pe(self, fn, reads=(), writes=()):
        return self.add("pe", fn, reads, writes)

    def act(self, fn, reads=(), writes=()):
        return self.add("act", fn, reads, writes)

    def dve(self, fn, reads=(), writes=()):
        return self.add("dve", fn, reads, writes)

    def pool(self, fn, reads=(), writes=()):
        return self.add("pool", fn, reads, writes)

    def dma(self, out, in_, reads=(), writes=(), chan=None, eng="sp", **kw):
        assert chan is not None
        return self.add(eng, lambda e: e.dma_start(out=out, in_=in_, **kw), reads, writes, chan=chan)

    def emit(self, stack):
        nc = self.nc
        esem = {}
        for e in self.ENG:
            if e != "sp":
                esem[e] = stack.enter_context(nc.semaphore("s_" + e))
        csem = {}
        cbase = {}
        pool = []
        by_phase = {}
        for c, ph in self.chan_phase.items():
            by_phase.setdefault(ph, []).append(c)
        nsem = 0
        for ph in sorted(by_phase):
            used = []
            for c in by_phase[ph]:
                if pool:
                    sv = pool.pop()
                else:
                    sv = [stack.enter_context(nc.semaphore("c%d" % nsem)), 0]
                    nsem += 1
                csem[c] = sv[0]
                cbase[c] = sv[1]
                sv[1] += 16 * self.chan_n[c]
                used.append(sv)
            pool.extend(used)
        self.nsem = nsem
        for e in self.ENG:
            cnt = 0
            for op in self.ops[e]:
                if op.chan is not None:
                    op.sigval = cbase[op.chan] + 16 * (op.cidx + 1)
                elif op.signal:
                    cnt += 1
                    op.sigval = cnt

        def run(e, eng):
            for op in self.ops[e]:
                for d in op.waits:
                    sem = csem[d.chan] if d.chan is not None else esem[d.eng]
                    eng.wait_ge(sem, d.sigval)
                if op.fn is None:
                    continue
                ins = op.fn(eng)
                if op.chan is not None:
                    ins.then_inc(csem[op.chan], 16)
                elif op.signal:
                    ins.then_inc(esem[e], 1)

        with nc.Block() as block:
            @block.sync
            def _(eng):
                run("sp", eng)

            @block.tensor
            def _(eng):
                run("pe", eng)

            @block.scalar
            def _(eng):
                run("act", eng)

            @block.vector
            def _(eng):
                run("dve", eng)

            @block.gpsimd
            def _(eng):
                run("pool", eng)


def mmg(items):
    n = len(items)

    def f(e):
        ins = None
        for i, (o, l, r) in enumerate(items):
            ins = e.matmul(o, lhsT=l, rhs=r, start=(i == 0), stop=(i == n - 1))
        return ins
    return f


def mm1(o, l, r):
    return lambda e: e.matmul(o, lhsT=l, rhs=r, start=True, stop=True)


def tr(o, i, ident):
    return lambda e: e.transpose(o, i, ident)


def actf(o, i, func, bias=None, scale=None, accum=None):
    kw = {}
    if bias is not None:
        kw["bias"] = bias
    if scale is not None:
        kw["scale"] = scale
    if accum is not None:
        kw["accum_out"] = accum
    return lambda e: e.activation(out=o, in_=i, func=func, **kw)


def tt(o, a, b, op):
    return lambda e: e.tensor_tensor(out=o, in0=a, in1=b, op=op)


def ts(o, a, s1, op0, s2=None, op1=None):
    if op1 is None:
        return lambda e: e.tensor_scalar(out=o, in0=a, scalar1=s1, scalar2=None, op0=op0)
    return lambda e: e.tensor_scalar(out=o, in0=a, scalar1=s1, scalar2=s2, op0=op0, op1=op1)


def stt(o, a, s, b, op0, op1):
    return lambda e: e.scalar_tensor_tensor(out=o, in0=a, scalar=s, in1=b, op0=op0, op1=op1)


def cp(o, i):
    return lambda e: e.tensor_copy(out=o, in_=i)


def recip(o, i):
    return lambda e: e.reciprocal(out=o, in_=i)


def mset(o, v):
    return lambda e: e.memset(o, v)


C_ID, C_ONES, C_BD32, C_TRIU, C_TRIL, C_SL, C_DFW, C_DBW, C_QDF, C_QDB, C_SEL, C_KD = range(12)
NCM = 12


def host_cmat():
    s = np.arange(128)[:, None].astype(np.float32)
    l = np.arange(128)[None, :].astype(np.float32)
    m = np.zeros((NCM, 128, 128), np.float32)
    m[C_ID] = np.eye(128)
    m[C_ONES] = 1.0
    m[C_BD32] = (np.arange(128)[:, None] // 32 == np.arange(128)[None, :] // 32)
    m[C_TRIU] = (s <= l)
    m[C_TRIL] = (s >= l)
    m[C_SL] = (s > l)
    m[C_DFW] = np.maximum(l - s, 0)
    m[C_DBW] = np.maximum(s - l, 0)
    m[C_QDF] = np.broadcast_to(l + 1.0, (128, 128))
    m[C_QDB] = np.broadcast_to(128.0 - l, (128, 128))
    m[C_SEL][64, :] = 1.0
    m[C_KD][:, 0] = 127.0 - np.arange(128)
    m[C_KD][:, 1] = np.arange(128)
    m[C_KD][:, 2] = 128.0
    return np.ascontiguousarray(m.transpose(1, 0, 2))


class Ring:
    def __init__(self, items):
        self.items = items
        self.i = 0

    def next(self):
        it = self.items[self.i % len(self.items)]
        self.i += 1
        return it


class MK:
    W_SPECS = [
        ("norm_mix", (4, 1024)), ("norm_ffn", (4, 1024)), ("norm_final", (1024,)),
        ("ab_w_in", (2, 1024, 2832)), ("ab_gate_bias", (2, 16)), ("attn_q_norm", (2, 64)),
        ("attn_k_norm", (2, 64)), ("mlstm_out_norm", (2, 512)), ("ab_w_out", (2, 1024, 1024)),
        ("ret_w_in", (2, 1024, 6144)), ("ret_decay_logit", (2, 2, 4)), ("ret_out_norm", (2, 2048)),
        ("ret_w_out", (2, 2048, 1024)), ("ffn_w_up", (4, 1024, 5632)), ("ffn_conv_w", (4, 3, 2816)),
        ("ffn_conv_b", (4, 2816)), ("ffn_w_down", (4, 2816, 1024)),
    ]

    def __init__(self, T, steps):
        self.T = T
        self.NT = T // 512
        self.NCH = T // 128
        self.steps = steps
        self.nc = nc = bass.Bass("TRN2", target_bir_lowering=False)
        self.S = Sched(nc)
        self._uid = 0
        NT, NCH = self.NT, self.NCH
        di = lambda n, s, dt=F32: nc.dram_tensor(n, list(s), dt, kind="ExternalInput").ap()
        ds = lambda n, s, dt: nc.dram_tensor(n, list(s), dt, kind="Internal").ap()
        self.xT = di("xT", (1024, T))
        self.w = {n: di(n, s) for n, s in self.W_SPECS}
        self.cmat_d = di("cmat", (128, NCM, 128))
        self.ropeA = di("ropeA", (2, 128, T))
        self.ropeR = di("ropeR", (2, 128, T))
        self.mb_d = di("maskb", (128, NT * NT))
        self.carry_d = di("carry", (128, 2 * NCH))
        self.flb_d = di("flb", (128, NT * 2))
        self.yT = nc.dram_tensor("yT", [1024, T], F32, kind="ExternalOutput").ap()
        self.XM = ds("XM", (1024, T), F32)
        self.XR = ds("XR", (1024, T), F32)
        self.ACTS = ds("ACTS", (2816, T), BF16)
        self.MIX = ds("MIX", (2048, T), BF16)
        self.QA = ds("QA", (8, 64, T), BF16)
        self.KA = ds("KA", (128, T), BF16)
        self.VA = ds("VA", (T, 130), BF16)
        self.MQ = ds("MQ", (4, 128, T), BF16)
        self.MKs = ds("MKs", (4, 128, T), BF16)
        self.MV = ds("MV", (T, 516), BF16)
        self.SG = ds("SG", (T, 512), F32)
        self.GT = ds("GT", (T, 16), F32)
        self.HF = ds("HF", (T, 512), F32)
        self.HB = ds("HB", (T, 512), F32)
        self.YB = ds("YB", (T, 2048), F32)
        self.RQ = ds("RQ", (8, 128, T), BF16)
        self.RK = ds("RK", (8, 128, T), BF16)
        self.RV = ds("RV", (T, 2048), BF16)
        self.RG = ds("RG", (T, 2048), F32)
        self.YF = ds("YF", (T, 2048), F32)

    def un(self, n):
        self._uid += 1
        return "%s_%d" % (n, self._uid)

    def sb(self, name, shape, dt):
        return self._ph.enter_context(self.nc.sbuf_tensor(self.un(name), list(shape), dt))

    @contextlib.contextmanager
    def phase(self, psum=True):
        self.S.barrier()
        with contextlib.ExitStack() as ph:
            self._ph = ph
            if psum:
                nc = self.nc
                self.ps = [ph.enter_context(nc.psum_tensor(self.un("ps%d" % i), [128, 512], F32)) for i in range(6)]
                self.psb = [ph.enter_context(nc.psum_tensor(self.un("psb%d" % i), [128, 1024], BF16)) for i in range(2)]
            yield
        self._ph = self._gl

    def ring(self, name, n, shape, dt):
        items = []
        for i in range(n):
            t = self.sb(name, shape, dt)
            items.append((t, self.un(name)))
        return Ring(items)

    def pipeline(self, n, stage1, stage2, la):
        for i in range(n + la):
            if i < n:
                stage1(i)
            if i >= la:
                stage2(i - la)

    def rr(self, engs):
        self._rr = getattr(self, "_rr", 0) + 1
        return engs[self._rr % len(engs)]

    def build(self):
        nc, S = self.nc, self.S
        with contextlib.ExitStack() as gl:
            self._gl = gl
            self._ph = gl
            self.pk = ["ps%d" % i for i in range(6)]
            self.cm = self.sb("cm", [128, NCM, 128], F32)
            self.cmb = self.sb("cmb", [128, 3, 128], BF16)
            S.dma(self.cm[:], self.cmat_d, writes=["cm"], chan="cm")
            S.dve(cp(self.cmb[:], self.cm[:, 0:3, :]), reads=["cm"], writes=["cmb"])
            self.ident_f = self.cm[:, C_ID, :]
            self.ones_f = self.cm[:, C_ONES, :]
            self.ident_b = self.cmb[:, C_ID, :]
            self.ones_b = self.cmb[:, C_ONES, :]
            self.bd32_b = self.cmb[:, C_BD32, :]
            self.carry = self.sb("carry", [128, 2, self.NCH], F32)
            S.dma(self.carry[:], self.carry_d.rearrange("p (d c) -> p d c", d=2), writes=["carry"], chan="carry")
            self.mb = self.sb("mb", [128, self.NT, self.NT], F32)
            S.dma(self.mb[:], self.mb_d.rearrange("p (a b) -> p a b", a=self.NT), writes=["mb"], chan="mb")
            self.flb = self.sb("flb", [128, self.NT * 2], F32)
            S.dma(self.flb[:], self.flb_d, writes=["flb"], chan="flb")
            cur = self.xT
            for st in self.steps:
                kind = st[0]
                if kind == "ab":
                    j, L = st[1], st[2]
                    self.ab_inproj(cur, j, L)
                    self.attention()
                    self.mlstm(j)
                    self.proj_resid(self.MIX, 8, self.w["ab_w_out"][j], cur, self.XM)
                    cur = self.XM
                elif kind == "ret":
                    j, L = st[1], st[2]
                    self.ret_inproj(cur, j, L)
                    if not DBG.get("noscan"):
                        self.retention(j)
                    self.proj_resid(self.MIX, 16, self.w["ret_w_out"][j], cur, self.XM)
                    cur = self.XM
                elif kind == "ffn":
                    L = st[1]
                    self.ffn_up(cur, L)
                    self.proj_resid(self.ACTS, 22, self.w["ffn_w_down"][L], cur, self.XR)
                    cur = self.XR
                elif kind == "final":
                    self.final_norm(cur)
                    cur = None
                elif kind == "copy":
                    self.copy_out(cur)
                    cur = None
            S.barrier()
            S.emit(gl)
        return nc

    def x3(self, X):
        return X.rearrange("(c p) t -> p c t", p=128)

    def load_gain(self, vec_ap, name):
        g = self.sb(name, [128, 8], F32)
        self.S.dma(g[:], vec_ap.rearrange("(c p) -> p c", p=128), writes=[name], chan=name, allow_slow_non_contiguous=True)
        return g

    def wkeys(self, dkey, col0, n, bw=1024):
        return ["%s#%d" % (dkey, b_) for b_ in range(col0 // bw, (col0 + n - 1) // bw + 1)]

    def prep_weight(self, dst, dkey, src, KC, blocks, gain=None, gkey=None, bw=1024, order=None):
        S = self.S
        stg = self.ring("wstg", 2, [128, 1024], F32)
        pw = min(bw, 1024)
        pieces = []
        for (d0, nd, s0, ns, vf) in blocks:
            o = 0
            while o < ns:
                n_ = min(pw - (d0 + o) % pw, ns - o)
                pieces.append((d0 + o, n_, s0 + o))
                o += n_
        if order is not None:
            pieces.sort(key=lambda p: (order.index(p[0] // bw) if (p[0] // bw) in order else 999, p[0]))
        for (d0, n_, s0) in pieces:
            bk = "%s#%d" % (dkey, d0 // bw)
            for c in range(KC):
                t, k = stg.next()
                S.dma(t[:, 0:n_], src[c * 128:(c + 1) * 128, s0:s0 + n_], writes=[k], chan=k)
                iv = t[:, 0:n_]
                ov = dst[:, c, d0:d0 + n_]
                eng = self.rr(["dve", "act", "dve", "act", "pool"])
                rd = [k] + ([gkey] if gain is not None else [])
                if gain is None:
                    if eng == "act":
                        S.act(actf(ov, iv, AF.Copy), reads=rd, writes=[bk])
                    else:
                        S.add(eng, cp(ov, iv), reads=rd, writes=[bk])
                else:
                    if eng == "act":
                        S.act(actf(ov, iv, AF.Copy, scale=gain[:, c:c + 1]), reads=rd, writes=[bk])
                    else:
                        S.add(eng, ts(ov, iv, gain[:, c:c + 1], ALU.mult), reads=rd, writes=[bk])

    def norm_load(self, src3, t0, n, ent):
        xt, kx = ent
        self.S.dma(xt[:, :, 0:n], src3[:, :, t0:t0 + n], writes=[kx], chan=kx)

    def norm_tile(self, src3, t0, n, bufs, D_feat=1024, load=True):
        S = self.S
        xt, kx = bufs["xt"]
        sq, ksq = bufs["sq"]
        rs, krs = bufs["rs"]
        hT, kh = bufs["hT"]
        ssp, kss = bufs["ss"]
        if load:
            S.dma(xt[:, :, 0:n], src3[:, :, t0:t0 + n], writes=[kx], chan=kx)
        S.act(actf(sq[:, :, 0:n], xt[:, :, 0:n], AF.Square), reads=[kx], writes=[ksq])
        S.pe(mmg([(ssp[:, 0:n], self.ones_b, sq[:, c, 0:n]) for c in range(8)]), reads=[ksq, "cmb"], writes=[kss])
        S.act(actf(rs[:, 0:n], ssp[:, 0:n], AF.Sqrt, bias=EPS, scale=1.0 / D_feat), reads=[kss], writes=[krs])
        S.dve(recip(rs[:, 0:n], rs[:, 0:n]), reads=[krs], writes=[krs])
        S.dve(tt(hT[:, 0:4, 0:n], xt[:, 0:4, 0:n], rs[:, 0:n].unsqueeze(1).to_broadcast([128, 4, n]), ALU.mult),
              reads=[kx, krs], writes=[kh + "a"])
        S.pool(tt(hT[:, 4:8, 0:n], xt[:, 4:8, 0:n], rs[:, 0:n].unsqueeze(1).to_broadcast([128, 4, n]), ALU.mult),
               reads=[kx, krs], writes=[kh + "b"])
        return [kh + "a", kh + "b"]

    def norm_pipeline(self, X3, xt, sq, rs, hT):
        NT = self.NT
        st = {"nxt": xt.next()}
        self.norm_load(X3, 0, 512, st["nxt"])

        def prep(j):
            bufs = {"xt": st["nxt"], "sq": sq.next(), "rs": rs.next(), "hT": hT.next(), "ss": (self.ps[0], "ps0")}
            hk = self.norm_tile(X3, j * 512, 512, bufs, load=False)
            if j + 1 < NT:
                st["nxt"] = xt.next()
                self.norm_load(X3, (j + 1) * 512, 512, st["nxt"])
            return bufs, hk
        return prep

    def norm_bufs(self, nbuf=1, n=512):
        xt = self.ring("xt", nbuf, [128, 8, n], F32)
        sq = self.ring("sq", 1, [128, 8, n], BF16)
        rs = self.ring("rs", 2, [128, n], F32)
        hT = self.ring("hT", 2, [128, 8, n], BF16)
        return xt, sq, rs, hT

    def proj_resid(self, A, KC, w_d, Xin, Xout):
        S, NT = self.S, self.NT
        with self.phase():
            w = self.sb("wpr", [128, KC, 1024], BF16)
            self.prep_weight(w, "wpr", w_d, KC, [(0, 1024, 0, 1024, None)], bw=256)
            ar = self.ring("a", 2, [128, KC, 512], BF16)
            xr = self.ring("x", 2, [128, 8, 512], F32)
            A3 = A[0:KC * 128, :].rearrange("(c p) t -> p c t", p=128)
            Xi3, Xo3 = self.x3(Xin), self.x3(Xout)
            psr = Ring([(self.ps[i], self.pk[i]) for i in range(4)])
            def pr_load(j):
                at, ka = ar.next()
                xt, kx = xr.next()
                S.dma(at[:], A3[:, :, j * 512:(j + 1) * 512], writes=[ka], chan=ka)
                S.dma(xt[:], Xi3[:, :, j * 512:(j + 1) * 512], writes=[kx], chan=kx)
                return at, ka, xt, kx
            nxt = pr_load(0)
            for j in range(NT):
                t0 = j * 512
                at, ka, xt, kx = nxt
                if j + 1 < NT:
                    nxt = pr_load(j + 1)
                for d in range(8):
                    p, kp = psr.next()
                    S.pe(mmg([(p[:], w[:, c, d * 128:(d + 1) * 128], at[:, c, :]) for c in range(KC)]),
                         reads=[ka] + self.wkeys("wpr", d * 128, 128, 256), writes=[kp])
                    S.dve(tt(xt[:, d, :], p[:], xt[:, d, :], ALU.add), reads=[kp, kx], writes=[kx])
                S.dma(Xo3[:, :, t0:t0 + 512], xt[:], reads=[kx], writes=["Xo"], chan=kx + "s")

    def final_norm(self, X):
        S, NT = self.S, self.NT
        with self.phase():
            g = self.load_gain(self.w["norm_final"], "gfin")
            xt, sq, rs, hT = self.norm_bufs(nbuf=2)
            yr = self.ring("y", 2, [128, 8, 512], F32)
            X3, Y3 = self.x3(X), self.x3(self.yT)
            nxt = xt.next()
            self.norm_load(X3, 0, 512, nxt)
            for j in range(NT):
                t0 = j * 512
                x_, kx = nxt
                if j + 1 < NT:
                    nxt = xt.next()
                    self.norm_load(X3, t0 + 512, 512, nxt)
                q_, kq = sq.next()
                r_, kr = rs.next()
                y_, ky = yr.next()
                S.act(actf(q_[:], x_[:], AF.Square), reads=[kx], writes=[kq])
                S.pe(mmg([(self.ps[0][:], self.ones_b, q_[:, c, :]) for c in range(8)]), reads=[kq, "cmb"], writes=["ps0"])
                S.act(actf(r_[:], self.ps[0][:], AF.Sqrt, bias=EPS, scale=1.0 / 1024), reads=["ps0"], writes=[kr])
                S.dve(recip(r_[:], r_[:]), reads=[kr], writes=[kr])
                for c in range(8):
                    S.dve(stt(y_[:, c, :], x_[:, c, :], g[:, c:c + 1], r_[:], ALU.mult, ALU.mult),
                          reads=[kx, kr, "gfin"], writes=[ky])
                S.dma(Y3[:, :, t0:t0 + 512], y_[:], reads=[ky], writes=["Y"], chan=ky + "s")

    def copy_out(self, X):
        S, NT = self.S, self.NT
        with self.phase():
            xr = self.ring("x", 2, [128, 8, 512], F32)
            X3, Y3 = self.x3(X), self.x3(self.yT)
            for j in range(NT):
                x_, kx = xr.next()
                S.dma(x_[:], X3[:, :, j * 512:(j + 1) * 512], writes=[kx], chan=kx)
                S.dma(Y3[:, :, j * 512:(j + 1) * 512], x_[:], reads=[kx], writes=["Y"], chan=kx + "s")

    def ffn_up(self, X, L):
        S, NT = self.S, self.NT
        nc = self.nc
        NB = 2 * (NT - 1)
        with self.phase():
            w = self.sb("wup", [128, 8, 5632], BF16)
            g = self.load_gain(self.w["norm_ffn"][L], "gffn")
            cw = self.sb("cw", [128, 22, 3], F32)
            cb = self.sb("cb", [128, 22], F32)
            for k3 in range(3):
                S.dma(cw[:, :, k3], self.w["ffn_conv_w"][L, k3].rearrange("(f p) -> p f", p=128), writes=["cw"], chan="cw", allow_slow_non_contiguous=True)
            S.dma(cb[:], self.w["ffn_conv_b"][L].rearrange("(f p) -> p f", p=128), writes=["cb"], chan="cb", allow_slow_non_contiguous=True)
            self.prep_weight(w, "wup", self.w["ffn_w_up"][L], 8,
                             [(i * 2048, min(2048, 5632 - i * 2048), i * 2048, min(2048, 5632 - i * 2048), None) for i in range(3)],
                             gain=g, gkey="gffn", order=[0, 2, 3, 1, 4, 5])
            xt, sq, rs, hT = self.norm_bufs()
            X3 = self.x3(X)
            gh = self.sb("gh", [128, 22, NT, 2], F32)
            S.pool(mset(gh[:], 0.0), writes=["gh"])
            if NB > 0:
                xb = self.sb("xb", [128, 8, NT - 1, 2], F32)
                sqb = self.sb("sqb", [128, 8, NB], BF16)
                rsb = self.sb("rsb", [128, NB], F32)
                hb = self.sb("hb", [128, 8, NB], BF16)
                for c in range(8):
                    src = X3[:, c, 511:511 + 512 * (NT - 1)].rearrange("p (b r) -> p b r", r=512)[:, :, 0:2]
                    S.dma(xb[:, c, :, :], src, writes=["xb%d" % c], chan="xb", allow_slow_non_contiguous=True)
                xbf = xb[:].rearrange("p c b e -> p c (b e)")
                xk = ["xb%d" % c for c in range(8)]
                S.act(actf(sqb[:], xbf, AF.Square), reads=xk, writes=["sqb"])
                S.pe(mmg([(self.ps[0][:, 0:NB], self.ones_b, sqb[:, c, :]) for c in range(8)]), reads=["sqb", "cmb"], writes=["ps0"])
                S.act(actf(rsb[:], self.ps[0][:, 0:NB], AF.Sqrt, bias=EPS, scale=1.0 / 1024), reads=["ps0"], writes=["rsb"])
                S.dve(recip(rsb[:], rsb[:]), reads=["rsb"], writes=["rsb"])
                S.dve(tt(hb[:], xbf, rsb[:].unsqueeze(1).to_broadcast([128, 8, NB]), ALU.mult), reads=xk + ["rsb"], writes=["hb"])
                pr = Ring([(self.ps[1], "ps1"), (self.ps[2], "ps2")])
                for f in range(22):
                    p, kp = pr.next()
                    S.pe(mmg([(p[:, 0:NB], w[:, c, 2816 + f * 128:2816 + (f + 1) * 128], hb[:, c, :]) for c in range(8)]),
                         reads=["hb"] + self.wkeys("wup", 2816 + f * 128, 128), writes=[kp])
                    pv = p[:, 0:NB].rearrange("p (b e) -> p b e", e=2)
                    S.dve(cp(gh[:, f, 1:NT, 0], pv[:, :, 0]), reads=[kp], writes=["gh"])
                    S.dve(cp(gh[:, f, 0:NT - 1, 1], pv[:, :, 1]), reads=[kp], writes=["gh"])
            ghf = gh[:].rearrange("p f j e -> p f (j e)")
            S.dve(tt(ghf, ghf, self.flb[:].unsqueeze(1).to_broadcast([128, 22, NT * 2]), ALU.mult), reads=["gh", "flb"], writes=["gh"])
            actr = self.ring("act", 2, [128, 22, 512], BF16)
            gbr = self.ring("gb", 2, [128, 514], F32)
            tr_ = self.ring("tc", 2, [128, 512], F32)
            pu = Ring([(self.ps[1], "ps1"), (self.ps[2], "ps2")])
            pg = Ring([(self.ps[3], "ps3"), (self.ps[4], "ps4"), (self.ps[5], "ps5")])
            A3 = self.ACTS.rearrange("(f p) t -> p f t", p=128)
            prep = self.norm_pipeline(X3, xt, sq, rs, hT)
            cur = prep(0)
            for j in range(NT):
                t0 = j * 512
                bufs, hk = cur
                h_ = bufs["hT"][0]
                a_, ka = actr.next()
                for f in range(22):
                    if f == 3 and j + 1 < NT:
                        cur = prep(j + 1)
                    u, ku = pu.next()
                    gp, kg = pg.next()
                    S.pe(mmg([(u[:], w[:, c, f * 128:(f + 1) * 128], h_[:, c, :]) for c in range(8)]), reads=hk + self.wkeys("wup", f * 128, 128), writes=[ku])
                    S.pe(mmg([(gp[:], w[:, c, 2816 + f * 128:2816 + (f + 1) * 128], h_[:, c, :]) for c in range(8)]), reads=hk + self.wkeys("wup", 2816 + f * 128, 128), writes=[kg])
                    gb, kb = gbr.next()
                    tc, kt = tr_.next()
                    S.act(actf(gb[:, 1:513], gp[:], AF.Copy), reads=[kg], writes=[kb + "m"])
                    S.dve(cp(gb[:, 0:514:513], gh[:, f, j, :]), reads=["gh"], writes=[kb + "h"])
                    S.act(actf(tc[:], gp[:], AF.Identity, bias=cb[:, f:f + 1], scale=cw[:, f, 1:2]), reads=[kg, "cw", "cb"], writes=[kt])
                    S.dve(stt(tc[:], gb[:, 0:512], cw[:, f, 0:1], tc[:], ALU.mult, ALU.add), reads=[kb + "m", kb + "h", kt, "cw"], writes=[kt])
                    S.dve(stt(tc[:], gb[:, 2:514], cw[:, f, 2:3], tc[:], ALU.mult, ALU.add), reads=[kb + "m", kb + "h", kt, "cw"], writes=[kt])
                    S.act(actf(tc[:], tc[:], AF.Gelu), reads=[kt], writes=[kt])
                    S.dve(tt(a_[:, f, :], tc[:], u[:], ALU.mult), reads=[kt, ku], writes=[ka])
                S.dma(A3[:, :, t0:t0 + 512], a_[:], reads=[ka], writes=["ACTS"], chan=ka + "s")

    def ab_inproj(self, X, j, L):
        S, NT, nc = self.S, self.NT, self.nc
        with self.phase():
            w = self.sb("wab", [128, 8, 2832], BF16)
            g = self.load_gain(self.w["norm_mix"][L], "gmix")
            stg = self.ring("wstg", 2, [128, 2832], F32)
            src = self.w["ab_w_in"][j]

            def hsplit(ap, nh, half):
                return ap.rearrange("p (h d) -> p h d", d=64)[:, :, half * 32:(half + 1) * 32]

            def h32(ap, nh):
                return ap.rearrange("p (h d) -> p h d", d=32)

            for c in range(8):
                t, k = stg.next()
                S.dma(t[:], src[c * 128:(c + 1) * 128, :], writes=[k], chan=k)
                gc = g[:, c:c + 1]
                engs = ["dve", "pool"]
                for gi in range(2):
                    for half in range(2):
                        S.add(self.rr(engs), ts(h32(w[:, c, gi * 256 + half * 128: gi * 256 + half * 128 + 128], 4),
                                                hsplit(t[:, gi * 256:(gi + 1) * 256], 4, half), gc, ALU.mult),
                              reads=[k, "gmix"], writes=["wab"])
                for half in range(2):
                    S.add(self.rr(engs), ts(h32(w[:, c, 512 + half * 64: 512 + half * 64 + 64], 2),
                                            hsplit(t[:, 512:640], 2, half), gc, ALU.mult), reads=[k, "gmix"], writes=["wab"])
                for (d0, s0, n) in [(1664, 640, 128), (640, 768, 1024), (1808, 1792, 1024), (1792, 2816, 16)]:
                    S.add(self.rr(engs), ts(w[:, c, d0:d0 + n], t[:, s0:s0 + n], gc, ALU.mult), reads=[k, "gmix"], writes=["wab"])
            gq = self.sb("gq", [128, 2], F32)
            gk = self.sb("gk", [128, 2], F32)
            for r in range(4):
                for half in range(2):
                    S.dma(gq[r * 32:(r + 1) * 32, half:half + 1], self.w["attn_q_norm"][j, half * 32:(half + 1) * 32].unsqueeze(1),
                          writes=["gq%d%d" % (r, half)], chan="gqld", allow_slow_non_contiguous=True)
                    S.dma(gk[r * 32:(r + 1) * 32, half:half + 1], self.w["attn_k_norm"][j, half * 32:(half + 1) * 32].unsqueeze(1),
                          writes=["gk%d%d" % (r, half)], chan="gkld", allow_slow_non_contiguous=True)
            gqk = ["gq%d%d" % (r, h) for r in range(4) for h in range(2)]
            gkk = ["gk%d%d" % (r, h) for r in range(4) for h in range(2)]
            S.dve(ts(gq[:], gq[:], 0.125, ALU.mult), reads=gqk, writes=["gq"])
            gbias = self.sb("gbias", [128, 16], F32)
            S.dma(gbias[:], self.w["ab_gate_bias"][j:j + 1, :].to_broadcast([128, 16]), writes=["gbias"], chan="gbias")
            xt, sq, rs, hT = self.norm_bufs()
            X3 = self.x3(X)
            cs = self.ring("cs", 2, [128, 2, 512], F32)
            sqa = self.ring("sqa", 2, [128, 2, 512], BF16)
            rq = self.ring("rq", 2, [128, 512], F32)
            an = self.ring("an", 2, [128, 2, 512], F32)
            t4 = self.ring("t4", 1, [128, 4, 512], F32)
            o12 = self.ring("o12", 3, [128, 2, 512], BF16)
            mqs = self.ring("mqs", 1, [128, 4, 512], BF16)
            mks = self.ring("mks", 1, [128, 4, 512], BF16)
            vts = self.ring("vt", 2, [128, 4, 2, 65], BF16)
            mvs = self.ring("mvt", 2, [128, 4, 4, 129], BF16)
            sgs = self.ring("sg", 1, [128, 4, 512], F32)
            gts = self.ring("gt", 2, [128, 4, 16], F32)
            for (t, k) in vts.items:
                S.pool(mset(t[:], 1.0), writes=[k])
            for (t, k) in mvs.items:
                S.pool(mset(t[:], 1.0), writes=[k])
            pf = Ring([(self.ps[i], self.pk[i]) for i in (1, 2, 3)])
            ptm = Ring([(self.ps[4], "ps4"), (self.ps[5], "ps5")])
            def cs_load(jt_):
                c_, kc = cs.next()
                S.dma(c_[:], self.ropeA[:, :, jt_ * 512:(jt_ + 1) * 512].rearrange("a p t -> p a t"), writes=[kc], chan=kc)
                return c_, kc
            prep = self.norm_pipeline(X3, xt, sq, rs, hT)
            cur = prep(0)
            cnxt = cs_load(0)
            for jt in range(NT):
                t0 = jt * 512
                bufs, hk = cur
                h_ = bufs["hT"][0]
                c_, kc = cnxt
                if jt + 1 < NT:
                    cnxt = cs_load(jt + 1)

                def fm(col0, M):
                    p, kp = pf.next()
                    S.pe(mmg([(p[0:M, :], w[:, c, col0:col0 + M], h_[:, c, :]) for c in range(8)]), reads=hk + ["wab"], writes=[kp])
                    return p, kp

                for grp in range(3):
                    M = 128 if grp < 2 else 64
                    colA = grp * 256 if grp < 2 else 512
                    colB = colA + M
                    gg, ggk = (gq, ["gq"]) if grp < 2 else (gk, gkk)
                    pa, kpa = fm(colA, M)
                    pb, kpb = fm(colB, M)
                    s_, ks = sqa.next()
                    S.act(actf(s_[0:M, 0, :], pa[0:M, :], AF.Square), reads=[kpa], writes=[ks + "a"])
                    S.act(actf(s_[0:M, 1, :], pb[0:M, :], AF.Square), reads=[kpb], writes=[ks + "b"])
                    S.pe(mmg([(self.ps[0][0:M, :], self.bd32_b[0:M, 0:M], s_[0:M, 0, :]),
                              (self.ps[0][0:M, :], self.bd32_b[0:M, 0:M], s_[0:M, 1, :])]), reads=[ks + "a", ks + "b", "cmb"], writes=["ps0"])
                    r_, kr = rq.next()
                    S.act(actf(r_[0:M, :], self.ps[0][0:M, :], AF.Sqrt, bias=EPS, scale=1.0 / 64), reads=["ps0"], writes=[kr])
                    S.dve(recip(r_[0:M, :], r_[0:M, :]), reads=[kr], writes=[kr])
                    a_, kan = an.next()
                    S.dve(stt(a_[0:M, 0, :], pa[0:M, :], gg[0:M, 0:1], r_[0:M, :], ALU.mult, ALU.mult), reads=[kpa, kr] + ggk, writes=[kan + "a"])
                    S.dve(stt(a_[0:M, 1, :], pb[0:M, :], gg[0:M, 1:2], r_[0:M, :], ALU.mult, ALU.mult), reads=[kpb, kr] + ggk, writes=[kan + "b"])
                    t_, kt = t4.next()
                    S.pool(tt(t_[0:M, 0, :], a_[0:M, 0, :], c_[0:M, 0, :], ALU.mult), reads=[kan + "a", kc], writes=[kt + "0"])
                    S.pool(tt(t_[0:M, 1, :], a_[0:M, 1, :], c_[0:M, 1, :], ALU.mult), reads=[kan + "b", kc], writes=[kt + "1"])
                    S.dve(tt(t_[0:M, 2, :], a_[0:M, 1, :], c_[0:M, 0, :], ALU.mult), reads=[kan + "b", kc], writes=[kt + "2"])
                    S.pool(tt(t_[0:M, 3, :], a_[0:M, 0, :], c_[0:M, 1, :], ALU.mult), reads=[kan + "a", kc], writes=[kt + "3"])
                    o_, ko = o12.next()
                    S.pool(tt(o_[0:M, 0, :], t_[0:M, 0, :], t_[0:M, 1, :], ALU.subtract), reads=[kt + "0", kt + "1"], writes=[ko + "0"])
                    S.dve(tt(o_[0:M, 1, :], t_[0:M, 2, :], t_[0:M, 3, :], ALU.add), reads=[kt + "2", kt + "3"], writes=[ko + "1"])
                    for hl in range(M // 32):
                        for half in range(2):
                            if grp < 2:
                                dst = self.QA[grp * 4 + hl, half * 32:(half + 1) * 32, t0:t0 + 512]
                            else:
                                dst = self.KA[hl * 64 + half * 32: hl * 64 + (half + 1) * 32, t0:t0 + 512]
                            S.dma(dst, o_[hl * 32:(hl + 1) * 32, half, :], reads=[ko + str(half)], writes=["QK"], chan=ko + "s%d%d" % (hl, half))
                if jt + 1 < NT:
                    cur = prep(jt + 1)
                mq_, kmq = mqs.next()
                mk_, kmk = mks.next()
                for h in range(4):
                    p, kp = fm(640 + h * 128, 128)
                    S.act(actf(mq_[:, h, :], p[:], AF.Copy), reads=[kp], writes=[kmq])
                for h in range(4):
                    p, kp = fm(1152 + h * 128, 128)
                    S.act(actf(mk_[:, h, :], p[:], AF.Copy, scale=float(128 ** -0.5)), reads=[kp], writes=[kmk])
                S.dma(self.MQ[:, :, t0:t0 + 512].rearrange("h d t -> d h t"), mq_[:], reads=[kmq], writes=["MQ"], chan=kmq + "s")
                S.dma(self.MKs[:, :, t0:t0 + 512].rearrange("h d t -> d h t"), mk_[:], reads=[kmk], writes=["MK"], chan=kmk + "s")
                vt, kv = vts.next()
                mv, kmv = mvs.next()
                sg, ksg = sgs.next()
                gt, kgt = gts.next()
                for sub in range(4):
                    hs = lambda c: h_[:, c, sub * 128:(sub + 1) * 128]
                    p1, k1 = ptm.next()
                    S.pe(mmg([(p1[:, 0:144], hs(c), w[:, c, 1664:1808]) for c in range(8)]), reads=hk + ["wab"], writes=[k1])
                    S.act(actf(vt[:, sub, :, 0:64], p1[:, 0:128].rearrange("p (h d) -> p h d", d=64), AF.Copy), reads=[k1], writes=[kv])
                    S.dve(tt(gt[:, sub, :], p1[:, 128:144], gbias[:], ALU.add), reads=[k1, "gbias"], writes=[kgt])
                    p2, k2 = ptm.next()
                    S.pe(mmg([(p2[:], hs(c), w[:, c, 1808:2320]) for c in range(8)]), reads=hk + ["wab"], writes=[k2])
                    S.dve(cp(mv[:, sub, :, 0:128], p2[:].rearrange("p (h d) -> p h d", d=128)), reads=[k2], writes=[kmv])
                    p3, k3 = ptm.next()
                    S.pe(mmg([(p3[:], hs(c), w[:, c, 2320:2832]) for c in range(8)]), reads=hk + ["wab"], writes=[k3])
                    S.act(actf(sg[:, sub, :], p3[:], AF.Sigmoid), reads=[k3], writes=[ksg])
                S.dma(self.VA[t0:t0 + 512, :].rearrange("(s p) c -> p s c", p=128), vt[:].rearrange("p s h d -> p s (h d)"), reads=[kv], writes=["VA"], chan=kv + "s")
                S.dma(self.MV[t0:t0 + 512, :].rearrange("(s p) c -> p s c", p=128), mv[:].rearrange("p s h d -> p s (h d)"), reads=[kmv], writes=["MV"], chan=kmv + "s")
                S.dma(self.SG[t0:t0 + 512, :].rearrange("(s p) c -> p s c", p=128), sg[:], reads=[ksg], writes=["SG"], chan=ksg + "s")
                S.dma(self.GT[t0:t0 + 512, :].rearrange("(s p) c -> p s c", p=128), gt[:], reads=[kgt], writes=["GT"], chan=kgt + "s")

    def attention(self):
        S, NT, NCH, T, nc = self.S, self.NT, self.NCH, self.T, self.nc
        with self.phase(psum=False):
            ph = self._ph
            ppr = Ring([(ph.enter_context(nc.psum_tensor(self.un("pp"), [128, 1024], F32)), "pp%d" % i) for i in range(2)])
            poa = (ph.enter_context(nc.psum_tensor(self.un("poa"), [128, 512], F32)), "poa")
            pob = (ph.enter_context(nc.psum_tensor(self.un("pob"), [128, 512], F32)), "pob")
            ptb = ph.enter_context(nc.psum_tensor(self.un("ptb"), [128, 1024], BF16))
            K = self.sb("Kall", [128, T], BF16)
            V = self.sb("Vall", [128, NCH, 130], BF16)
            S.dma(K[:], self.KA, writes=["K"], chan="K")
            S.dma(V[:], self.VA.rearrange("(b p) c -> p b c", p=128), writes=["V"], chan="V")
            qr = self.ring("q", 2, [128, 4, 512], BF16)
            pr = self.ring("p", 3, [128, 1024], BF16)
            rcr = self.ring("rc", 2, [128, 4], F32)
            otr = self.ring("ot", 2, [128, 4, 8, 64], BF16)
            ob = self.ring("ob", 2, [128, 4, 512], BF16)
            items = [(jq, hp, kb) for jq in range(NT) for hp in range(4) for kb in range(NCH)]
            tctx, ictx = {}, {}

            def stage1(i):
                jq, hp, kb = items[i]
                t0 = jq * 512
                if hp == 0 and kb == 0:
                    q_, kq = qr.next()
                    for kvh in range(2):
                        S.dma(q_[kvh * 64:(kvh + 1) * 64, :, :], self.QA[kvh * 4:(kvh + 1) * 4, :, t0:t0 + 512].rearrange("h d t -> d h t"),
                              writes=[kq + str(kvh)], chan=kq + str(kvh))
                    tctx[jq] = (q_, kq) + otr.next()
                q_, kq, ot, kot = tctx[jq]
                pb, kpb = ppr.next()

                def st2(e, pb=pb, q_=q_, hp=hp, kb=kb):
                    e.matmul(pb[:, 0:512], lhsT=K[0:64, kb * 128:(kb + 1) * 128], rhs=q_[0:64, hp, :], start=True, stop=True)
                    return e.matmul(pb[:, 512:1024], lhsT=K[64:128, kb * 128:(kb + 1) * 128], rhs=q_[64:128, hp, :], start=True, stop=True)
                S.pe(st2, reads=["K", kq + "0", kq + "1"], writes=[kpb])
                p_, kp_ = pr.next()
                S.act(actf(p_[:], pb[:], AF.Exp, bias=self.mb[:, jq, (kb // 4):(kb // 4) + 1]), reads=[kpb, "mb"], writes=[kp_])
                ictx[i] = (p_, kp_)

            def stage2(i):
                jq, hp, kb = items[i]
                t0 = jq * 512
                q_, kq, ot, kot = tctx[jq]
                p_, kp_ = ictx.pop(i)

                def pv8(e, p_=p_, kb=kb):
                    ins = None
                    for hh, (po, _) in enumerate((poa, pob)):
                        for sub in range(4):
                            ins = e.matmul(po[:, sub * 65:(sub + 1) * 65], lhsT=p_[:, hh * 512 + sub * 128: hh * 512 + (sub + 1) * 128],
                                           rhs=V[:, kb, hh * 65:(hh + 1) * 65], start=(kb == 0 and sub == 0), stop=(kb == NCH - 1 and sub == 3))
                    return ins
                S.pe(pv8, reads=["V", kp_], writes=["poa", "pob"])
                if kb != NCH - 1:
                    return
                for hh, (po, kpo) in enumerate((poa, pob)):
                    h = hh * 4 + hp
                    pv = po[:, 0:260].rearrange("p (s c) -> p s c", c=65)
                    r_, kr = rcr.next()
                    S.dve(recip(r_[:], pv[:, :, 64]), reads=[kpo], writes=[kr])
                    S.dve(tt(ot[:, :, h, :], pv[:, :, 0:64], r_[:].unsqueeze(2).to_broadcast([128, 4, 64]), ALU.mult), reads=[kpo, kr], writes=[kot])
                if hp != 3:
                    return
                o_, ko = ob.next()
                for sp in range(2):
                    def tr8(e, ot=ot, sp=sp):
                        ins = None
                        for hp2 in range(4):
                            for s_ in range(2):
                                slot = hp2 * 2 + s_
                                ins = e.transpose(ptb[:, slot * 128:(slot + 1) * 128],
                                                  ot[:, sp * 2 + s_, 2 * hp2:2 * hp2 + 2, :].rearrange("p h d -> p (h d)"), self.ident_b)
                        return ins
                    S.pe(tr8, reads=[kot, "cmb"], writes=["ptb"])
                    S.act(actf(o_[:, :, sp * 256:(sp + 1) * 256].rearrange("p a (s q) -> p a s q", q=128),
                               ptb[:].rearrange("p (a s q) -> p a s q", a=4, s=2), AF.Copy), reads=["ptb"], writes=[ko])
                S.dma(self.MIX[0:512, t0:t0 + 512].rearrange("(a p) t -> p a t", p=128), o_[:], reads=[ko], writes=["MIXa"], chan=ko + "s")

            self.pipeline(len(items), stage1, stage2, 1)

    def mlstm(self, j):
        S, NT, NCH, T, nc = self.S, self.NT, self.NCH, self.T, self.nc
        NG = NCH * 4
        with self.phase():
            G = self.sb("G", [128, NCH, 16], F32)
            S.dma(G[:], self.GT.rearrange("(c p) g -> p c g", p=128), writes=["G"], chan="G")
            G5 = G[:].rearrange("p c (d k h) -> p d c k h", d=2, k=2, h=4)
            gi, gf = G5[:, :, :, 0, :], G5[:, :, :, 1, :]
            shp = [128, 2, NCH, 4]
            mk = lambda n: self.sb(n, shp, F32)
            FL, Bc, A_, AMr, BLr, Mt, mt, WK, THR, KEEP = [mk(n) for n in ("FL", "Bc", "Aa", "AMr", "BLr", "Mt", "mt", "WK", "THR", "KEEP")]
            tmp = mk("tmp")
            fl2 = lambda t, d: t[:, d, :, :].rearrange("p c h -> p (c h)")
            S.act(actf(tmp[:], gf, AF.Abs), reads=["G"], writes=["tmp"])
            S.act(actf(tmp[:], tmp[:], AF.Exp, scale=-1.0), reads=["tmp"], writes=["tmp"])
            S.act(actf(tmp[:], tmp[:], AF.Ln, bias=1.0), reads=["tmp"], writes=["tmp"])
            S.dve(ts(FL[:], gf, 0.0, ALU.min), reads=["G"], writes=["FL"])
            S.dve(tt(FL[:], FL[:], tmp[:], ALU.subtract), reads=["FL", "tmp"], writes=["FL"])
            bw = min(128, NG)
            nblk = NG // bw
            am = self.sb("am", [128, 2, nblk], F32)
            dg = self.sb("dg", [128, 128], F32)
            for d in range(2):
                tri = self.cm[:, C_TRIU, :] if d == 0 else self.cm[:, C_TRIL, :]
                S.pe(mm1(self.ps[0][:, 0:NG], tri, fl2(FL, d)), reads=["FL", "cm"], writes=["ps0"])
                S.act(actf(fl2(Bc, d), self.ps[0][:, 0:NG], AF.Copy), reads=["ps0"], writes=["Bc"])
                S.pe(mm1(self.ps[1][:, 0:NG], self.ones_f, fl2(FL, d)), reads=["FL", "cm"], writes=["ps1"])
                S.act(actf(fl2(BLr, d), self.ps[1][:, 0:NG], AF.Copy), reads=["ps1"], writes=["BLr"])
                S.dve(tt(A_[:, d], gi[:, d], Bc[:, d], ALU.subtract), reads=["G", "Bc"], writes=["Aa"])
                for b in range(nblk):
                    S.pe(tr(self.ps[2][0:bw, 0:128], fl2(A_, d)[:, b * bw:(b + 1) * bw], self.ident_f), reads=["Aa", "cm"], writes=["ps2"])
                    S.dve(lambda e, d=d, b=b, ps2=self.ps[2]: e.tensor_reduce(out=am[0:bw, d, b:b + 1], in_=ps2[0:bw, 0:128], axis=AX.X, op=ALU.max),
                          reads=["ps2"], writes=["am"])
                    S.dve(ts(dg[0:bw, 0:bw], self.ident_f[0:bw, 0:bw], am[0:bw, d, b:b + 1], ALU.mult), reads=["am", "cm"], writes=["dg"])
                    S.pe(mm1(self.ps[3][:, 0:bw], self.ones_f[0:bw, :], dg[0:bw, 0:bw]), reads=["dg", "cm"], writes=["ps3"])
                    S.act(actf(fl2(AMr, d)[:, b * bw:(b + 1) * bw], self.ps[3][:, 0:bw], AF.Copy), reads=["ps3"], writes=["AMr"])
            mrun = self.sb("mrun", [128, 2, 4], F32)
            S.dve(mset(mrun[:], 0.0), writes=["mrun0", "mrun1"])
            for k in range(NCH):
                for d in range(2):
                    eng = "dve"
                    kk = "rec%d" % d
                    c = k if d == 0 else NCH - 1 - k
                    S.add(eng, ts(mt[:, d, c, :], mrun[:, d, :], self.carry[:, d, c:c + 1], ALU.mult), reads=["mrun%d" % d, "carry"], writes=[kk + "mt"])
                    S.add(eng, tt(Mt[:, d, c, :], mt[:, d, c, :], AMr[:, d, c, :], ALU.max), reads=[kk + "mt", "AMr"], writes=[kk + "Mt"])
                    S.add(eng, tt(mrun[:, d, :], Mt[:, d, c, :], BLr[:, d, c, :], ALU.add), reads=[kk + "Mt", "BLr"], writes=["mrun%d" % d])
            rk = ["rec0mt", "rec1mt", "rec0Mt", "rec1Mt"]
            S.dve(tt(KEEP[:], mt[:], Mt[:], ALU.subtract), reads=rk, writes=["KEEP"])
            S.act(actf(KEEP[:], KEEP[:], AF.Exp), reads=["KEEP"], writes=["KEEP"])
            S.dve(tt(KEEP[:].rearrange("p d c h -> p (d c) h"), KEEP[:].rearrange("p d c h -> p (d c) h"),
                     self.carry[:].rearrange("p d c -> p (d c)").unsqueeze(2).to_broadcast([128, 2 * NCH, 4]), ALU.mult), reads=["KEEP", "carry"], writes=["KEEP"])
            S.dve(tt(WK[:], A_[:], Mt[:], ALU.subtract), reads=["Aa"] + rk, writes=["WK"])
            S.act(actf(WK[:], WK[:], AF.Exp), reads=["WK"], writes=["WK"])
            S.dve(tt(THR[:], Bc[:], Mt[:], ALU.add), reads=["Bc"] + rk, writes=["THR"])
            S.act(actf(THR[:], THR[:], AF.Exp, scale=-1.0), reads=["THR"], writes=["THR"])
            maskf = self.cm[:, C_TRIU, :]
            maskb = self.cm[:, C_TRIL, :]
            qr = [self.ring("mq", 2, [128, 4, 512], BF16) for d in range(2)]
            kr = [self.ring("mk", 2, [128, 4, 512], BF16) for d in range(2)]
            vr = [self.ring("mv", 2, [128, 4, 516], BF16) for d in range(2)]
            Cst = [[self.sb("Cst", [128, 129], F32) for h in range(4)] for d in range(2)]
            Cbf = [[self.sb("Cbf", [128, 129], BF16) for h in range(4)] for d in range(2)]
            for d in range(2):
                for h in range(4):
                    S.pool(mset(Cst[d][h][:], 0.0), writes=["C%d%d" % (d, h)])
            atr = self.ring("at", 6, [128, 128], BF16)
            kwr = self.ring("kw", 6, [128, 128], BF16)
            rr_ = self.ring("r", 6, [128, 2], F32)
            hst = [self.ring("hst", 2, [128, 512], F32) for d in range(2)]
            pS = Ring([(self.ps[0], "ps0"), (self.ps[1], "ps1")])
            pH = Ring([(self.ps[2], "ps2"), (self.ps[3], "ps3")])
            pU = Ring([(self.ps[4], "ps4"), (self.ps[5], "ps5")])
            pT = Ring([(self.psb[0], "psb0"), (self.psb[1], "psb1")])
            MQ3 = self.MQ.rearrange("h d t -> d h t")
            MK3 = self.MKs.rearrange("h d t -> d h t")
            HO = [self.HF, self.HB]
            items = [(k, h, d) for k in range(NCH) for h in range(4) for d in range(2)]
            tctx, cctx, ictx = {}, {}, {}

            def geom(it):
                k, h, d = it
                c = k if d == 0 else NCH - 1 - k
                return d, c // 4, c % 4, h, c, (c // 4) * 512

            def ensure(d, jt):
                if (d, jt) in tctx or jt < 0 or jt >= NT:
                    return
                t0 = jt * 512
                q_, kq = qr[d].next()
                k_, kk = kr[d].next()
                v_, kv = vr[d].next()
                S.dma(q_[:], MQ3[:, :, t0:t0 + 512], writes=[kq], chan=kq)
                S.dma(k_[:], MK3[:, :, t0:t0 + 512], writes=[kk], chan=kk)
                S.dma(v_[:], self.MV[t0:t0 + 512, :].rearrange("(s p) c -> p s c", p=128), writes=[kv], chan=kv)
                tctx[(d, jt)] = (q_, kq, k_, kk, v_, kv)

            def stage1(i):
                d, jt, sub, h, c, t0 = geom(items[i])
                mask = maskf if d == 0 else maskb
                if h == 0:
                    ensure(d, jt)
                    if items[i][0] % 4 == 1:
                        ensure(d, jt + (1 if d == 0 else -1))
                    cctx[(d, c)] = hst[d].next()
                q_, kq, k_, kk, v_, kv = tctx[(d, jt)]
                ck = "C%d%d" % (d, h)
                qs = q_[:, h, sub * 128:(sub + 1) * 128]
                ks_ = k_[:, h, sub * 128:(sub + 1) * 128]
                wkc = WK[:, d, c, h:h + 1]
                st_, kst = pS.next()
                S.pe(mm1(st_[:, 0:128], ks_, qs), reads=[kk, kq], writes=[kst])
                at, kat = atr.next()
                S.dve(stt(at[:], st_[:, 0:128], wkc, mask, ALU.mult, ALU.mult), reads=[kst, "WK", "cm"], writes=[kat])
                ptr, kptr = pT.next()
                S.pe(tr(ptr[:, 0:128], ks_, self.ident_b), reads=[kk, "cmb"], writes=[kptr])
                kw, kkw = kwr.next()
                S.act(actf(kw[:], ptr[:, 0:128], AF.Copy, scale=wkc), reads=[kptr, "WK"], writes=[kkw])
                S.act(actf(Cbf[d][h][:], Cst[d][h][:], AF.Copy, scale=KEEP[:, d, c, h:h + 1]), reads=[ck, "KEEP"], writes=[ck + "b"])
                ictx[i] = (at, kat, kw, kkw)

            def stage2(i):
                d, jt, sub, h, c, t0 = geom(items[i])
                q_, kq, k_, kk, v_, kv = tctx[(d, jt)]
                hs_, khs = cctx[(d, c)]
                at, kat, kw, kkw = ictx.pop(i)
                ck = "C%d%d" % (d, h)
                tc0 = t0 + sub * 128
                qs = q_[:, h, sub * 128:(sub + 1) * 128]
                vs = v_[:, sub, h * 129:(h + 1) * 129]
                ph, kph = pH.next()
                S.pe(mmg([(ph[:, 0:129], at[:], vs), (ph[:, 0:129], qs, Cbf[d][h][:])]), reads=[kat, kv, kq, ck + "b"], writes=[kph])
                pu, kpu = pU.next()
                S.pe(mm1(pu[:, 0:129], kw[:], vs), reads=[kkw, kv], writes=[kpu])
                S.dve(stt(Cst[d][h][:], Cst[d][h][:], KEEP[:, d, c, h:h + 1], pu[:, 0:129], ALU.mult, ALU.add), reads=[ck, kpu, "KEEP"], writes=[ck])
                r_, kr_ = rr_.next()
                S.dve(ts(r_[:, 0:1], ph[:, 128:129], -1.0, ALU.mult, THR[:, d, c, h:h + 1], ALU.max), reads=[kph, "THR"], writes=[kr_])
                S.dve(tt(r_[:, 0:1], r_[:, 0:1], ph[:, 128:129], ALU.max), reads=[kr_, kph], writes=[kr_])
                S.dve(recip(r_[:, 1:2], r_[:, 0:1]), reads=[kr_], writes=[kr_])
                S.act(actf(hs_[:, h * 128:(h + 1) * 128], ph[:, 0:128], AF.Copy, scale=r_[:, 1:2]), reads=[kph, kr_], writes=[khs])
                if h == 3:
                    S.dma(HO[d][tc0:tc0 + 128, :], hs_[:], reads=[khs], writes=["HO"], chan=khs + "s")

            self.pipeline(len(items), stage1, stage2, 2)

        with self.phase():
            gain = self.sb("ogain", [128, 512], F32)
            S.dma(gain[:], self.w["mlstm_out_norm"][j:j + 1, :].to_broadcast([128, 512]), writes=["ogain"], chan="ogain")
            hfr = self.ring("hf", 3, [128, 512], F32)
            hbr = self.ring("hb", 3, [128, 512], F32)
            sgr = self.ring("sgl", 3, [128, 512], F32)
            ssq = self.ring("ssq", 3, [128, 8], F32)
            junk = self.ring("junk", 2, [128, 128], F32)
            ymr = self.ring("ym", 3, [128, 512], BF16)
            mxs = self.ring("mxs", 2, [128, 4, 512], BF16)
            pT = Ring([(self.psb[0], "psb0"), (self.psb[1], "psb1")])

            def ld(c):
                hf, khf = hfr.next()
                hb, khb = hbr.next()
                sg, ksg = sgr.next()
                S.dma(hf[:], self.HF[c * 128:(c + 1) * 128, :], writes=[khf], chan=khf)
                S.dma(hb[:], self.HB[c * 128:(c + 1) * 128, :], writes=[khb], chan=khb)
                S.dma(sg[:], self.SG[c * 128:(c + 1) * 128, :], writes=[ksg], chan=ksg)
                return hf, khf, hb, khb, sg, ksg
            pend = [ld(0), ld(1)] if NCH > 1 else [ld(0)]
            for c in range(NCH):
                hf, khf, hb, khb, sg, ksg = pend.pop(0)
                if c + 2 < NCH:
                    pend.append(ld(c + 2))
                sub = c % 4
                if sub == 0:
                    mx, kmx = mxs.next()
                S.dve(tt(hf[:], hf[:], hb[:], ALU.add), reads=[khf, khb], writes=[khf])
                sq_, ksq = ssq.next()
                jk, kjk = junk.next()
                S.dve(mset(sq_[:], 0.0), writes=[ksq + "a", ksq + "b"])
                for hh in range(4):
                    S.act(actf(jk[:], hf[:, hh * 128:(hh + 1) * 128], AF.Square, accum=sq_[:, hh:hh + 1]), reads=[khf], writes=[kjk, ksq + "a"])
                S.act(actf(sq_[:, 4:8], sq_[:, 0:4], AF.Sqrt, bias=EPS, scale=1.0 / 128), reads=[ksq + "a"], writes=[ksq + "b"])
                S.dve(recip(sq_[:, 4:8], sq_[:, 4:8]), reads=[ksq + "b"], writes=[ksq + "b"])
                S.dve(tt(hf[:].rearrange("p (h e) -> p h e", e=128), hf[:].rearrange("p (h e) -> p h e", e=128),
                         sq_[:, 4:8].unsqueeze(2).to_broadcast([128, 4, 128]), ALU.mult), reads=[khf, ksq + "b"], writes=[khf])
                S.pool(tt(sg[:], sg[:], gain[:], ALU.mult), reads=[ksg, "ogain"], writes=[ksg])
                ym, kym = ymr.next()
                S.dve(tt(ym[:], hf[:], sg[:], ALU.mult), reads=[khf, ksg], writes=[kym])
                ptr, kptr = pT.next()

                def tr4(e, ptr=ptr, ym=ym):
                    ins = None
                    for hh in range(4):
                        ins = e.transpose(ptr[:, hh * 128:(hh + 1) * 128], ym[:, hh * 128:(hh + 1) * 128], self.ident_b)
                    return ins
                S.pe(tr4, reads=[kym, "cmb"], writes=[kptr])
                S.act(actf(mx[:, :, sub * 128:(sub + 1) * 128], ptr[:, 0:512].rearrange("p (h e) -> p h e", e=128), AF.Copy), reads=[kptr], writes=[kmx])
                if sub == 3:
                    t0 = (c // 4) * 512
                    S.dma(self.MIX[512:1024, t0:t0 + 512].rearrange("(h p) t -> p h t", p=128), mx[:], reads=[kmx], writes=["MIXb"], chan=kmx + "s")

    def ret_inproj(self, X, j, L):
        S, NT, nc = self.S, self.NT, self.nc
        with self.phase():
            w = self.sb("wret", [128, 8, 6144], BF16)
            g = self.load_gain(self.w["norm_mix"][L], "gmix")
            self.prep_weight(w, "wret", self.w["ret_w_in"][j], 8, [(i * 2048, 2048, i * 2048, 2048, None) for i in range(3)], gain=g, gkey="gmix")
            xt, sq, rs, hT = self.norm_bufs()
            X3 = self.x3(X)
            cs = self.ring("cs", 1, [128, 4, 512], F32)
            t4 = self.ring("t4", 1, [128, 4, 512], F32)
            o12 = self.ring("o12", 2, [128, 2, 512], BF16)
            vts = self.ring("rv", 1, [128, 4, 2048], BF16)
            gts = self.ring("rg", 1, [128, 2048], F32)
            pf = Ring([(self.ps[i], self.pk[i]) for i in (1, 2, 3)])
            ptm = Ring([(self.ps[4], "ps4"), (self.ps[5], "ps5")])
            def cs_load(jt_):
                c_, kc = cs.next()
                S.dma(c_[:, 0:2, :], self.ropeR[:, :, jt_ * 512:(jt_ + 1) * 512].rearrange("a p t -> p a t"), writes=[kc, kc + "k"], chan=kc)
                return c_, kc
            prep = self.norm_pipeline(X3, xt, sq, rs, hT)
            cur = prep(0)
            cnxt = cs_load(0)
            for jt in range(NT):
                t0 = jt * 512
                bufs, hk = cur
                h_ = bufs["hT"][0]
                c_, kc = cnxt
                S.act(actf(c_[:, 2:4, :], c_[:, 0:2, :], AF.Copy, scale=1.0 / 16), reads=[kc], writes=[kc + "k"])
                for qk in range(2):
                    if qk == 1 and jt + 1 < NT:
                        cur = prep(jt + 1)
                    co = 0 if qk == 0 else 2
                    ck_ = [kc] if qk == 0 else [kc + "k"]
                    dstT = self.RQ if qk == 0 else self.RK
                    for h in range(4):
                        col = qk * 1024 + h * 256
                        pa, kpa = pf.next()
                        S.pe(mmg([(pa[:], w[:, c, col:col + 128], h_[:, c, :]) for c in range(8)]), reads=hk + self.wkeys("wret", col, 128), writes=[kpa])
                        pb, kpb = pf.next()
                        S.pe(mmg([(pb[:], w[:, c, col + 128:col + 256], h_[:, c, :]) for c in range(8)]), reads=hk + self.wkeys("wret", col + 128, 128), writes=[kpb])
                        t_, kt = t4.next()
                        S.dve(tt(t_[:, 0, :], pa[:], c_[:, co, :], ALU.mult), reads=[kpa] + ck_, writes=[kt + "0"])
                        S.dve(tt(t_[:, 1, :], pb[:], c_[:, co + 1, :], ALU.mult), reads=[kpb] + ck_, writes=[kt + "1"])
                        S.dve(tt(t_[:, 2, :], pb[:], c_[:, co, :], ALU.mult), reads=[kpb] + ck_, writes=[kt + "2"])
                        S.dve(tt(t_[:, 3, :], pa[:], c_[:, co + 1, :], ALU.mult), reads=[kpa] + ck_, writes=[kt + "3"])
                        o_, ko = o12.next()
                        S.pool(tt(o_[:, 0, :], t_[:, 0, :], t_[:, 1, :], ALU.subtract), reads=[kt + "0", kt + "1"], writes=[ko])
                        S.pool(tt(o_[:, 1, :], t_[:, 2, :], t_[:, 3, :], ALU.add), reads=[kt + "2", kt + "3"], writes=[ko])
                        S.dma(dstT[2 * h:2 * h + 2, :, t0:t0 + 512].rearrange("a p t -> p a t"), o_[:], reads=[ko], writes=["RQK"], chan=ko + "s")
                if jt + 1 < NT:
                    cnxt = cs_load(jt + 1)
                vt, kv = vts.next()
                for sub in range(4):
                    hs = lambda c: h_[:, c, sub * 128:(sub + 1) * 128]
                    for blk in range(4):
                        p1, k1 = ptm.next()
                        S.pe(mmg([(p1[:], hs(c), w[:, c, 2048 + blk * 512:2048 + (blk + 1) * 512]) for c in range(8)]), reads=hk + self.wkeys("wret", 2048 + blk * 512, 512), writes=[k1])
                        S.act(actf(vt[:, sub, blk * 512:(blk + 1) * 512], p1[:], AF.Copy), reads=[k1], writes=[kv])
                    gt, kg = gts.next()
                    for blk in range(4):
                        p1, k1 = ptm.next()
                        S.pe(mmg([(p1[:], hs(c), w[:, c, 4096 + blk * 512:4096 + (blk + 1) * 512]) for c in range(8)]), reads=hk + self.wkeys("wret", 4096 + blk * 512, 512), writes=[k1])
                        S.act(actf(gt[:, blk * 512:(blk + 1) * 512], p1[:], AF.Silu), reads=[k1], writes=[kg])
                    S.dma(self.RG[t0 + sub * 128:t0 + (sub + 1) * 128, :], gt[:], reads=[kg], writes=["RG"], chan=kg + "s")
                S.dma(self.RV[t0:t0 + 512, :].rearrange("(s p) c -> p s c", p=128), vt[:], reads=[kv], writes=["RV"], chan=kv + "s")

    def retention(self, j):
        S, NT, NCH, T, nc = self.S, self.NT, self.NCH, self.T, self.nc
        with self.phase():
            lg = self.sb("lg", [128, 8], F32)
            tmp = self.sb("lgt", [128, 8], F32)
            S.dma(lg[:], self.w["ret_decay_logit"][j:j + 1].rearrange("a d h -> a (d h)").to_broadcast([128, 8]), writes=["lg"], chan="lg")
            S.act(actf(tmp[:], lg[:], AF.Abs), reads=["lg"], writes=["lgt"])
            S.act(actf(tmp[:], tmp[:], AF.Exp, scale=-1.0), reads=["lgt"], writes=["lgt"])
            S.act(actf(tmp[:], tmp[:], AF.Ln, bias=1.0), reads=["lgt"], writes=["lgt"])
            S.dve(ts(lg[:], lg[:], 0.0, ALU.min), reads=["lg"], writes=["lg"])
            S.dve(tt(lg[:], lg[:], tmp[:], ALU.subtract), reads=["lg", "lgt"], writes=["lg"])
            DT = self.sb("DT", [128, 8, 128], F32)
            QD = self.sb("QD", [128, 8, 128], F32)
            QDb = self.sb("QDb", [128, 8, 128], BF16)
            KD = self.sb("KD", [128, 8], F32)
            CDC = self.sb("CDC", [128, 8, NCH], F32)
            cdv = self.sb("cdv", [128, 8], F32)
            for hd in range(8):
                d = hd // 4
                S.act(actf(DT[:, hd, :], self.cm[:, C_DFW if d == 0 else C_DBW, :], AF.Exp, scale=lg[:, hd:hd + 1]), reads=["lg", "cm"], writes=["DT"])
                S.dve(tt(DT[:, hd, :], DT[:, hd, :], self.cm[:, C_TRIU if d == 0 else C_SL, :], ALU.mult), reads=["DT", "cm"], writes=["DT"])
                S.act(actf(QD[:, hd, :], self.cm[:, C_QDF if d == 0 else C_QDB, :], AF.Exp, scale=lg[:, hd:hd + 1]), reads=["lg", "cm"], writes=["QD"])
                S.dve(cp(QDb[:, hd, :], QD[:, hd, :]), reads=["QD"], writes=["QDb"])
                S.act(actf(KD[:, hd:hd + 1], self.cm[:, C_KD, d:d + 1], AF.Exp, scale=lg[:, hd:hd + 1]), reads=["lg", "cm"], writes=["KD"])
                S.act(actf(cdv[:, hd:hd + 1], self.cm[:, C_KD, 2:3], AF.Exp, scale=lg[:, hd:hd + 1]), reads=["lg", "cm"], writes=["cdv"])
                S.dve(ts(CDC[:, hd, :], self.carry[:, d, :], cdv[:, hd:hd + 1], ALU.mult), reads=["cdv", "carry"], writes=["CDC"])
            qr = [self.ring("rq", 2, [128, 8, 512], BF16) for d in range(2)]
            kr = [self.ring("rk", 2, [128, 8, 512], BF16) for d in range(2)]
            vr = [self.ring("rv", 3, [128, 2048], BF16) for d in range(2)]
            St = [[[self.sb("St", [128, 512], F32) for c in range(2)] for h in range(4)] for d in range(2)]
            Sb = [[[self.sb("Sb", [128, 512], BF16) for c in range(2)] for h in range(4)] for d in range(2)]
            for d in range(2):
                for h in range(4):
                    for c in range(2):
                        S.pool(mset(St[d][h][c][:], 0.0), writes=["S%d%d%d" % (d, h, c)])
            atr = self.ring("at", 6, [128, 128], BF16)
            kwr = self.ring("kw", 6, [128, 2, 128], BF16)
            qdr = self.ring("qd", 6, [128, 2, 128], BF16)
            yst = [self.ring("yst", 2, [128, 2048], F32) for d in range(2)]
            pS = Ring([(self.ps[0][:, 0:128], "ps0"), (self.ps[5][:, 0:128], "ps5")])
            pO = Ring([(self.ps[1], "ps1"), (self.ps[2], "ps2")])
            pU = Ring([(self.ps[3], "ps3"), (self.ps[4], "ps4")])
            pT = Ring([(self.psb[0], "psb0"), (self.psb[1], "psb1")])
            RQ3 = self.RQ.rearrange("a p t -> p a t")
            RK3 = self.RK.rearrange("a p t -> p a t")
            YO = [self.YF, self.YB]
            items = [(k, h, d) for k in range(NCH) for h in range(4) for d in range(2)]
            tctx, cctx, ictx = {}, {}, {}

            def geom(it):
                k, h, d = it
                c = k if d == 0 else NCH - 1 - k
                return d, c // 4, c % 4, h, c, (c // 4) * 512

            def ensure(d, jt):
                if (d, jt) in tctx or jt < 0 or jt >= NT:
                    return
                t0 = jt * 512
                q_, kq = qr[d].next()
                k_, kk = kr[d].next()
                S.dma(q_[:], RQ3[:, :, t0:t0 + 512], writes=[kq], chan=kq)
                S.dma(k_[:], RK3[:, :, t0:t0 + 512], writes=[kk], chan=kk)
                tctx[(d, jt)] = (q_, kq, k_, kk)

            def ensure_v(d, c):
                if (d, c) in cctx or c < 0 or c >= NCH:
                    return
                v_, kv = vr[d].next()
                S.dma(v_[:], self.RV[c * 128:(c + 1) * 128, :], writes=[kv], chan=kv)
                cctx[(d, c)] = (v_, kv) + yst[d].next()

            def stage1(i):
                d, jt, sub, h, c, t0 = geom(items[i])
                if h == 0:
                    ensure(d, jt)
                    if items[i][0] % 4 == 1:
                        ensure(d, jt + (1 if d == 0 else -1))
                    ensure_v(d, c)
                    ensure_v(d, c + (1 if d == 0 else -1))
                q_, kq, k_, kk = tctx[(d, jt)]
                sl = slice(sub * 128, (sub + 1) * 128)
                hd = d * 4 + h
                st_, kst = pS.next()
                S.pe(mmg([(st_, k_[:, 2 * h + cc, sl], q_[:, 2 * h + cc, sl]) for cc in range(2)]), reads=[kk, kq], writes=[kst])
                at, kat = atr.next()
                S.dve(tt(at[:], st_, DT[:, hd, :], ALU.mult), reads=[kst, "DT"], writes=[kat])
                kw, kkw = kwr.next()
                qd, kqd = qdr.next()
                ptr, kptr = pT.next()

                def tr2(e, ptr=ptr, k_=k_, h=h, sl=sl):
                    ins = None
                    for cc in range(2):
                        ins = e.transpose(ptr[:, cc * 128:(cc + 1) * 128], k_[:, 2 * h + cc, sl], self.ident_b)
                    return ins
                S.pe(tr2, reads=[kk, "cmb"], writes=[kptr])
                S.act(actf(kw[:], ptr[:, 0:256].rearrange("p (c e) -> p c e", e=128), AF.Copy, scale=KD[:, hd:hd + 1]), reads=[kptr, "KD"], writes=[kkw])
                S.pool(tt(qd[:], q_[:, 2 * h:2 * h + 2, sl], QDb[:, hd, :].unsqueeze(1).to_broadcast([128, 2, 128]), ALU.mult), reads=[kq, "QDb"], writes=[kqd])
                for cc in range(2):
                    sk = "S%d%d%d" % (d, h, cc)
                    S.act(actf(Sb[d][h][cc][:], St[d][h][cc][:], AF.Copy, scale=self.carry[:, d, c:c + 1]), reads=[sk, "carry"], writes=[sk + "b"])
                ictx[i] = (at, kat, kw, kkw, qd, kqd)

            def stage2(i):
                d, jt, sub, h, c, t0 = geom(items[i])
                v_, kv, ys, kys = cctx[(d, c)]
                at, kat, kw, kkw, qd, kqd = ictx.pop(i)
                hd = d * 4 + h
                vs = v_[:, h * 512:(h + 1) * 512]
                po, kpo = pO.next()
                S.pe(mmg([(po[:], at[:], vs), (po[:], qd[:, 0, :], Sb[d][h][0][:]), (po[:], qd[:, 1, :], Sb[d][h][1][:])]),
                     reads=[kat, kv, kqd, "S%d%d0b" % (d, h), "S%d%d1b" % (d, h)], writes=[kpo])
                for cc in range(2):
                    pu, kpu = pU.next()
                    sk = "S%d%d%d" % (d, h, cc)
                    S.pe(mm1(pu[:], kw[:, cc, :], vs), reads=[kkw, kv], writes=[kpu])
                    S.dve(stt(St[d][h][cc][:], St[d][h][cc][:], CDC[:, hd, c:c + 1], pu[:], ALU.mult, ALU.add), reads=[sk, kpu, "CDC"], writes=[sk])
                S.dve(cp(ys[:, h * 512:(h + 1) * 512], po[:]), reads=[kpo], writes=[kys])
                if h == 3:
                    S.dma(YO[d][c * 128:(c + 1) * 128, :], ys[:], reads=[kys], writes=["YO"], chan=kys + "s")

            self.pipeline(len(items), stage1, stage2, 2)

        with self.phase():
            gain = self.sb("rgain", [128, 2048], F32)
            S.dma(gain[:], self.w["ret_out_norm"][j:j + 1, :].to_broadcast([128, 2048]), writes=["rgain"], chan="rgain")
            yfr = self.ring("yf", 2, [128, 2048], F32)
            ybr = self.ring("yb", 2, [128, 2048], F32)
            rgr = self.ring("rgl", 2, [128, 2048], F32)
            ssq = self.ring("ssq", 3, [128, 8], F32)
            junk = self.ring("junk", 2, [128, 512], F32)
            ymr = self.ring("ym", 2, [128, 2048], BF16)
            mxs = self.ring("mxs", 2, [128, 16, 512], BF16)
            pT = Ring([(self.psb[0], "psb0"), (self.psb[1], "psb1")])

            def ld(c):
                yf, kyf = yfr.next()
                yb, kyb = ybr.next()
                rg, krg = rgr.next()
                S.dma(yf[:], self.YF[c * 128:(c + 1) * 128, :], writes=[kyf], chan=kyf)
                S.dma(yb[:], self.YB[c * 128:(c + 1) * 128, :], writes=[kyb], chan=kyb)
                S.dma(rg[:], self.RG[c * 128:(c + 1) * 128, :], writes=[krg], chan=krg)
                return yf, kyf, yb, kyb, rg, krg
            nxt = ld(0)
            for c in range(NCH):
                yf, kyf, yb, kyb, rg, krg = nxt
                if c + 1 < NCH:
                    nxt = ld(c + 1)
                sub = c % 4
                sl = slice(sub * 128, (sub + 1) * 128)
                if sub == 0:
                    mx, kmx = mxs.next()
                S.pool(tt(yf[:], yf[:], yb[:], ALU.add), reads=[kyf, kyb], writes=[kyf])
                sq_, ksq = ssq.next()
                jk, kjk = junk.next()
                S.dve(mset(sq_[:], 0.0), writes=[ksq + "a", ksq + "b"])
                for hh in range(4):
                    S.act(actf(jk[:], yf[:, hh * 512:(hh + 1) * 512], AF.Square, accum=sq_[:, hh:hh + 1]), reads=[kyf], writes=[kjk, ksq + "a"])
                S.act(actf(sq_[:, 4:8], sq_[:, 0:4], AF.Sqrt, bias=EPS, scale=1.0 / 512), reads=[ksq + "a"], writes=[ksq + "b"])
                S.dve(recip(sq_[:, 4:8], sq_[:, 4:8]), reads=[ksq + "b"], writes=[ksq + "b"])
                S.dve(tt(yf[:].rearrange("p (h e) -> p h e", e=512), yf[:].rearrange("p (h e) -> p h e", e=512),
                         sq_[:, 4:8].unsqueeze(2).to_broadcast([128, 4, 512]), ALU.mult), reads=[kyf, ksq + "b"], writes=[kyf])
                S.pool(tt(rg[:], rg[:], gain[:], ALU.mult), reads=[krg, "rgain"], writes=[krg])
                ym, kym = ymr.next()
                S.dve(tt(ym[:], yf[:], rg[:], ALU.mult), reads=[kyf, krg], writes=[kym])
                for bg in range(2):
                    ptr, kptr = pT.next()

                    def tr8(e, ptr=ptr, ym=ym, bg=bg):
                        ins = None
                        for b_ in range(8):
                            ins = e.transpose(ptr[:, b_ * 128:(b_ + 1) * 128], ym[:, (bg * 8 + b_) * 128:(bg * 8 + b_ + 1) * 128], self.ident_b)
                        return ins
                    S.pe(tr8, reads=[kym, "cmb"], writes=[kptr])
                    S.act(actf(mx[:, bg * 8:(bg + 1) * 8, sl], ptr[:].rearrange("p (b e) -> p b e", e=128), AF.Copy), reads=[kptr], writes=[kmx])
                if sub == 3:
                    t0 = (c // 4) * 512
                    S.dma(self.MIX[0:2048, t0:t0 + 512].rearrange("(b p) t -> p b t", p=128), mx[:], reads=[kmx], writes=["MIXr"], chan=kmx + "s")


def rope_tables(seglen, head_dim, nseg, reps):
    rows = seglen // 64
    row_idx = np.repeat(np.arange(rows, dtype=np.float32), 64)
    col_idx = np.tile(np.arange(64, dtype=np.float32), rows)
    axis_dim = head_dim // 2
    inv_freq = (np.float32(10000.0) ** (-np.arange(0, axis_dim, 2, dtype=np.float32) / np.float32(axis_dim))).astype(np.float32)
    ang = np.concatenate([row_idx[:, None] * inv_freq, col_idx[:, None] * inv_freq], axis=-1).astype(np.float32)
    cs = np.stack([np.cos(ang), np.sin(ang)], 0).astype(np.float32)
    cs = np.tile(cs, (1, nseg, 1))
    cs = cs.transpose(0, 2, 1)
    return np.ascontiguousarray(np.tile(cs, (1, reps, 1)))


def core_tables(T, nseg):
    NT, NCH = T // 512, T // 128
    seglen = T // nseg
    tps = NT // nseg
    cps = NCH // nseg
    seg_t = np.arange(NT) // tps
    mb = np.where(seg_t[:, None] == seg_t[None, :], 0.0, -30000.0).astype(np.float32)
    carry = np.ones((2, NCH), np.float32)
    carry[0, np.arange(NCH) % cps == 0] = 0.0
    carry[1, np.arange(NCH) % cps == cps - 1] = 0.0
    flb = np.ones((NT, 2), np.float32)
    flb[np.arange(NT) % tps == 0, 0] = 0.0
    flb[np.arange(NT) % tps == tps - 1, 1] = 0.0
    rep = lambda a: np.ascontiguousarray(np.broadcast_to(a.reshape(1, -1), (128, a.size))).astype(np.float32)
    return {
        "maskb": rep(mb), "carry": rep(carry), "flb": rep(flb),
        "ropeA": rope_tables(seglen, 64, nseg, 4), "ropeR": rope_tables(seglen, 256, nseg, 1),
    }


DBG = {}
FULL_STEPS = [("ab", 0, 0), ("ffn", 0), ("ret", 0, 1), ("ffn", 1), ("ab", 1, 2), ("ffn", 2), ("ret", 1, 3), ("ffn", 3), ("final",)]
_CACHE = {}


def run_cores(T, steps, core_x, core_nseg, weights):
    key = (T, tuple(steps))
    if key not in _CACHE:
        _CACHE[key] = MK(T, steps).build()
    nc = _CACHE[key]
    cm = host_cmat()
    tabs = {}
    in_maps = []
    for x, ns in zip(core_x, core_nseg):
        if ns not in tabs:
            tabs[ns] = core_tables(T, ns)
        m = {"xT": np.ascontiguousarray(np.asarray(x, np.float32).T), "cmat": cm}
        m.update(tabs[ns])
        for n, _ in MK.W_SPECS:
            m[n] = weights[n]
        in_maps.append(m)
    res = run_bass_kernel_spmd(nc, in_maps, core_ids=list(range(len(in_maps))))
    return [np.ascontiguousarray(r["yT"].T) for r in res.results]


def kernel(x_prompt, x_sample, **weights):
    weights = {k: np.ascontiguousarray(np.asarray(v, np.float32)) for k, v in weights.items()}
    xp = np.asarray(x_prompt, np.float32)
    xs = np.asarray(x_sample, np.float32)
    T = 8192
    core_x = [xp[0], xp[1]] + [xs[4 * i:4 * i + 4].reshape(T, 1024) for i in range(4)]
    nseg = [1, 1, 4, 4, 4, 4]
    core_x += [core_x[5], core_x[5]]
    nseg += [4, 4]
    outs = run_cores(T, FULL_STEPS, core_x, nseg, weights)
    y_prompt = np.stack([outs[0], outs[1]], 0)
    y_sample = np.concatenate([outs[2 + i].reshape(4, 2048, 1024) for i in range(4)], 0)
    return (y_prompt, y_sample)
```

```python
import contextlib
import numpy as np
import concourse.bass as bass
import concourse.mybir as mybir
from concourse.bass_utils import run_bass_kernel_spmd

F32 = mybir.dt.float32
BF16 = mybir.dt.bfloat16
AF = mybir.ActivationFunctionType
ALU = mybir.AluOpType
AX = mybir.AxisListType
EPS = 1e-6


class _Op:
    __slots__ = ("eng", "fn", "waits", "signal", "idx", "chan", "cidx", "sigval")

    def __init__(self, eng, fn, chan):
        self.eng = eng
        self.fn = fn
        self.chan = chan
        self.waits = []
        self.signal = False
        self.idx = -1
        self.cidx = -1
        self.sigval = 0


class Sched:
    ENG = ("pe", "act", "dve", "pool", "sp")

    def __init__(self, nc):
        self.nc = nc
        self.ops = {e: [] for e in self.ENG}
        self.res = {}
        self.waited = {e: {} for e in self.ENG}
        self.chan_last = {}
        self.chan_n = {}
        self.chan_phase = {}
        self.phase_id = 0

    def _wait(self, op, d):
        eng = op.eng
        if d is op:
            return
        if d.chan is not None:
            ek, val = ("c", d.chan), d.cidx
        else:
            if d.eng == eng and eng == "pe":
                return
            ek, val = d.eng, d.idx
        w = self.waited[eng]
        if w.get(ek, -1) >= val:
            return
        w[ek] = val
        op.waits.append(d)
        d.signal = True

    def add(self, eng, fn, reads=(), writes=(), chan=None):
        if chan is not None:
            chan = (self.phase_id, chan)
        op = _Op(eng, fn, chan)
        deps = []
        res = self.res
        for k in reads:
            st = res.get(k)
            if st is not None and st[0] is not None:
                deps.append(st[0])
        for k in writes:
            st = res.get(k)
            if st is not None:
                if st[0] is not None:
                    deps.append(st[0])
                deps.extend(st[1])
        if chan is not None:
            prev = self.chan_last.get(chan)
            if prev is not None:
                deps.append(prev)
            op.cidx = self.chan_n.get(chan, 0)
            if op.cidx == 0:
                self.chan_phase[chan] = self.phase_id
            assert self.chan_phase[chan] == self.phase_id, chan
            self.chan_n[chan] = op.cidx + 1
            self.chan_last[chan] = op
            op.signal = True
        op.idx = len(self.ops[eng])
        for d in deps:
            self._wait(op, d)
        self.ops[eng].append(op)
        for k in writes:
            res[k] = [op, []]
        for k in reads:
            st = res.get(k)
            if st is None:
                res[k] = [None, [op]]
            elif st[0] is not op:
                st[1].append(op)
        return op

    def barrier(self):
        lasts = []
        for e in self.ENG:
            for o in reversed(self.ops[e]):
                if o.fn is not None and o.chan is None:
                    lasts.append(o)
                    break
        lasts.extend(self.chan_last.values())
        for e in self.ENG:
            op = _Op(e, None, None)
            op.idx = len(self.ops[e])
            for d in lasts:
                self._wait(op, d)
            self.ops[e].append(op)
        self.res = {}
        self.phase_id += 1

    def pe(self, fn, reads=(), writes=()):
        return self.add("pe", fn, reads, writes)

    def act(self, fn, reads=(), writes=()):
        return self.add("act", fn, reads, writes)

    def dve(self, fn, reads=(), writes=()):
        return self.add("dve", fn, reads, writes)

    def pool(self, fn, reads=(), writes=()):
        return self.add("pool", fn, reads, writes)

    def dma(self, out, in_, reads=(), writes=(), chan=None, eng="sp", **kw):
        assert chan is not None
        return self.add(eng, lambda e: e.dma_start(out=out, in_=in_, **kw), reads, writes, chan=chan)

    def emit(self, stack):
        nc = self.nc
        esem = {}
        for e in self.ENG:
            if e != "sp":
                esem[e] = stack.enter_context(nc.semaphore("s_" + e))
        csem = {}
        cbase = {}
        pool = []
        by_phase = {}
        for c, ph in self.chan_phase.items():
            by_phase.setdefault(ph, []).append(c)
        nsem = 0
        for ph in sorted(by_phase):
            used = []
            for c in by_phase[ph]:
                if pool:
                    sv = pool.pop()
                else:
                    sv = [stack.enter_context(nc.semaphore("c%d" % nsem)), 0]
                    nsem += 1
                csem[c] = sv[0]
                cbase[c] = sv[1]
                sv[1] += 16 * self.chan_n[c]
                used.append(sv)
            pool.extend(used)
        self.nsem = nsem
        for e in self.ENG:
            cnt = 0
            for op in self.ops[e]:
                if op.chan is not None:
                    op.sigval = cbase[op.chan] + 16 * (op.cidx + 1)
                elif op.signal:
                    cnt += 1
                    op.sigval = cnt

        def run(e, eng):
            for op in self.ops[e]:
                for d in op.waits:
                    sem = csem[d.chan] if d.chan is not None else esem[d.eng]
                    eng.wait_ge(sem, d.sigval)
                if op.fn is None:
                    continue
                ins = op.fn(eng)
                if op.chan is not None:
                    ins.then_inc(csem[op.chan], 16)
                elif op.signal:
                    ins.then_inc(esem[e], 1)

        with nc.Block() as block:
            @block.sync
            def _(eng):
                run("sp", eng)

            @block.tensor
            def _(eng):
                run("pe", eng)

            @block.scalar
            def _(eng):
                run("act", eng)

            @block.vector
            def _(eng):
                run("dve", eng)

            @block.gpsimd
            def _(eng):
                run("pool", eng)


def mmg(items):
    n = len(items)

    def f(e):
        ins = None
        for i, (o, l, r) in enumerate(items):
            ins = e.matmul(o, lhsT=l, rhs=r, start=(i == 0), stop=(i == n - 1))
        return ins
    return f


def mm1(o, l, r):
    return lambda e: e.matmul(o, lhsT=l, rhs=r, start=True, stop=True)


def tr(o, i, ident):
    return lambda e: e.transpose(o, i, ident)


def actf(o, i, func, bias=None, scale=None, accum=None):
    kw = {}
    if bias is not None:
        kw["bias"] = bias
    if scale is not None:
        kw["scale"] = scale
    if accum is not None:
        kw["accum_out"] = accum
    return lambda e: e.activation(out=o, in_=i, func=func, **kw)


def tt(o, a, b, op):
    return lambda e: e.tensor_tensor(out=o, in0=a, in1=b, op=op)


def ts(o, a, s1, op0, s2=None, op1=None):
    if op1 is None:
        return lambda e: e.tensor_scalar(out=o, in0=a, scalar1=s1, scalar2=None, op0=op0)
    return lambda e: e.tensor_scalar(out=o, in0=a, scalar1=s1, scalar2=s2, op0=op0, op1=op1)


def stt(o, a, s, b, op0, op1):
    return lambda e: e.scalar_tensor_tensor(out=o, in0=a, scalar=s, in1=b, op0=op0, op1=op1)


def cp(o, i):
    return lambda e: e.tensor_copy(out=o, in_=i)


def recip(o, i):
    return lambda e: e.reciprocal(out=o, in_=i)


def mset(o, v):
    return lambda e: e.memset(o, v)


C_ID, C_ONES, C_BD32, C_TRIU, C_TRIL, C_SL, C_DFW, C_DBW, C_QDF, C_QDB, C_SEL, C_KD = range(12)
NCM = 12


def host_cmat():
    s = np.arange(128)[:, None].astype(np.float32)
    l = np.arange(128)[None, :].astype(np.float32)
    m = np.zeros((NCM, 128, 128), np.float32)
    m[C_ID] = np.eye(128)
    m[C_ONES] = 1.0
    m[C_BD32] = (np.arange(128)[:, None] // 32 == np.arange(128)[None, :] // 32)
    m[C_TRIU] = (s <= l)
    m[C_TRIL] = (s >= l)
    m[C_SL] = (s > l)
    m[C_DFW] = np.maximum(l - s, 0)
    m[C_DBW] = np.maximum(s - l, 0)
    m[C_QDF] = np.broadcast_to(l + 1.0, (128, 128))
    m[C_QDB] = np.broadcast_to(128.0 - l, (128, 128))
    m[C_SEL][64, :] = 1.0
    m[C_KD][:, 0] = 127.0 - np.arange(128)
    m[C_KD][:, 1] = np.arange(128)
    m[C_KD][:, 2] = 128.0
    return np.ascontiguousarray(m.transpose(1, 0, 2))


class Ring:
    def __init__(self, items):
        self.items = items
        self.i = 0

    def next(self):
        it = self.items[self.i % len(self.items)]
        self.i += 1
        return it


class MK:
    W_SPECS = [
        ("norm_mix", (4, 1024)), ("norm_ffn", (4, 1024)), ("norm_final", (1024,)),
        ("ab_w_in", (2, 1024, 2832)), ("ab_gate_bias", (2, 16)), ("attn_q_norm", (2, 64)),
        ("attn_k_norm", (2, 64)), ("mlstm_out_norm", (2, 512)), ("ab_w_out", (2, 1024, 1024)),
        ("ret_w_in", (2, 1024, 6144)), ("ret_decay_logit", (2, 2, 4)), ("ret_out_norm", (2, 2048)),
        ("ret_w_out", (2, 2048, 1024)), ("ffn_w_up", (4, 1024, 5632)), ("ffn_conv_w", (4, 3, 2816)),
        ("ffn_conv_b", (4, 2816)), ("ffn_w_down", (4, 2816, 1024)),
    ]

    def __init__(self, T, steps):
        self.T = T
        self.NT = T // 512
        self.NCH = T // 128
        self.steps = steps
        self.nc = nc = bass.Bass("TRN2", target_bir_lowering=False)
        self.S = Sched(nc)
        self._uid = 0
        NT, NCH = self.NT, self.NCH
        di = lambda n, s, dt=F32: nc.dram_tensor(n, list(s), dt, kind="ExternalInput").ap()
        ds = lambda n, s, dt: nc.dram_tensor(n, list(s), dt, kind="Internal").ap()
        self.xT = di("xT", (1024, T))
        self.w = {n: di(n, s) for n, s in self.W_SPECS}
        self.cmat_d = di("cmat", (128, NCM, 128))
        self.ropeA = di("ropeA", (2, 128, T))
        self.ropeR = di("ropeR", (2, 128, T))
        self.mb_d = di("maskb", (128, NT * NT))
        self.carry_d = di("carry", (128, 2 * NCH))
        self.flb_d = di("flb", (128, NT * 2))
        self.yT = nc.dram_tensor("yT", [1024, T], F32, kind="ExternalOutput").ap()
        self.XM = ds("XM", (1024, T), F32)
        self.XR = ds("XR", (1024, T), F32)
        self.ACTS = ds("ACTS", (2816, T), BF16)
        self.MIX = ds("MIX", (2048, T), BF16)
        self.QA = ds("QA", (8, 64, T), BF16)
        self.KA = ds("KA", (128, T), BF16)
        self.VA = ds("VA", (T, 130), BF16)
        self.MQ = ds("MQ", (4, 128, T), BF16)
        self.MKs = ds("MKs", (4, 128, T), BF16)
        self.MV = ds("MV", (T, 516), BF16)
        self.SG = ds("SG", (T, 512), F32)
        self.GT = ds("GT", (T, 16), F32)
        self.HF = ds("HF", (T, 512), F32)
        self.HB = ds("HB", (T, 512), F32)
        self.YB = ds("YB", (T, 2048), F32)
        self.RQ = ds("RQ", (8, 128, T), BF16)
        self.RK = ds("RK", (8, 128, T), BF16)
        self.RV = ds("RV", (T, 2048), BF16)
        self.RG = ds("RG", (T, 2048), F32)
        self.YF = ds("YF", (T, 2048), F32)

    def un(self, n):
        self._uid += 1
        return "%s_%d" % (n, self._uid)

    def sb(self, name, shape, dt):
        return self._ph.enter_context(self.nc.sbuf_tensor(self.un(name), list(shape), dt))

    @contextlib.contextmanager
    def phase(self, psum=True):
        self.S.barrier()
        with contextlib.ExitStack() as ph:
            self._ph = ph
            if psum:
                nc = self.nc
                self.ps = [ph.enter_context(nc.psum_tensor(self.un("ps%d" % i), [128, 512], F32)) for i in range(6)]
                self.psb = [ph.enter_context(nc.psum_tensor(self.un("psb%d" % i), [128, 1024], BF16)) for i in range(2)]
            yield
        self._ph = self._gl

    def ring(self, name, n, shape, dt):
        items = []
        for i in range(n):
            t = self.sb(name, shape, dt)
            items.append((t, self.un(name)))
        return Ring(items)

    def pipeline(self, n, stage1, stage2, la):
        for i in range(n + la):
            if i < n:
                stage1(i)
            if i >= la:
                stage2(i - la)

    def rr(self, engs):
        self._rr = getattr(self, "_rr", 0) + 1
        return engs[self._rr % len(engs)]

    def build(self):
        nc, S = self.nc, self.S
        with contextlib.ExitStack() as gl:
            self._gl = gl
            self._ph = gl
            self.pk = ["ps%d" % i for i in range(6)]
            self.cm = self.sb("cm", [128, NCM, 128], F32)
            self.cmb = self.sb("cmb", [128, 3, 128], BF16)
            S.dma(self.cm[:], self.cmat_d, writes=["cm"], chan="cm")
            S.dve(cp(self.cmb[:], self.cm[:, 0:3, :]), reads=["cm"], writes=["cmb"])
            self.ident_f = self.cm[:, C_ID, :]
            self.ones_f = self.cm[:, C_ONES, :]
            self.ident_b = self.cmb[:, C_ID, :]
            self.ones_b = self.cmb[:, C_ONES, :]
            self.bd32_b = self.cmb[:, C_BD32, :]
            self.carry = self.sb("carry", [128, 2, self.NCH], F32)
            S.dma(self.carry[:], self.carry_d.rearrange("p (d c) -> p d c", d=2), writes=["carry"], chan="carry")
            self.mb = self.sb("mb", [128, self.NT, self.NT], F32)
            S.dma(self.mb[:], self.mb_d.rearrange("p (a b) -> p a b", a=self.NT), writes=["mb"], chan="mb")
            self.flb = self.sb("flb", [128, self.NT * 2], F32)
            S.dma(self.flb[:], self.flb_d, writes=["flb"], chan="flb")
            cur = self.xT
            for st in self.steps:
                kind = st[0]
                if kind == "ab":
                    j, L = st[1], st[2]
                    self.ab_inproj(cur, j, L)
                    self.attention()
                    self.mlstm(j)
                    self.proj_resid(self.MIX, 8, self.w["ab_w_out"][j], cur, self.XM)
                    cur = self.XM
                elif kind == "ret":
                    j, L = st[1], st[2]
                    self.ret_inproj(cur, j, L)
                    if not DBG.get("noscan"):
                        self.retention(j)
                    self.proj_resid(self.MIX, 16, self.w["ret_w_out"][j], cur, self.XM)
                    cur = self.XM
                elif kind == "ffn":
                    L = st[1]
                    self.ffn_up(cur, L)
                    self.proj_resid(self.ACTS, 22, self.w["ffn_w_down"][L], cur, self.XR)
                    cur = self.XR
                elif kind == "final":
                    self.final_norm(cur)
                    cur = None
                elif kind == "copy":
                    self.copy_out(cur)
                    cur = None
            S.barrier()
            S.emit(gl)
        return nc

    def x3(self, X):
        return X.rearrange("(c p) t -> p c t", p=128)

    def load_gain(self, vec_ap, name):
        g = self.sb(name, [128, 8], F32)
        self.S.dma(g[:], vec_ap.rearrange("(c p) -> p c", p=128), writes=[name], chan=name, allow_slow_non_contiguous=True)
        return g

    def wkeys(self, dkey, col0, n, bw=1024):
        return ["%s#%d" % (dkey, b_) for b_ in range(col0 // bw, (col0 + n - 1) // bw + 1)]

    def prep_weight(self, dst, dkey, src, KC, blocks, gain=None, gkey=None, bw=1024, order=None):
        S = self.S
        stg = self.ring("wstg", 2, [128, 1024], F32)
        pw = min(bw, 1024)
        pieces = []
        for (d0, nd, s0, ns, vf) in blocks:
            o = 0
            while o < ns:
                n_ = min(pw - (d0 + o) % pw, ns - o)
                pieces.append((d0 + o, n_, s0 + o))
                o += n_
        if order is not None:
            pieces.sort(key=lambda p: (order.index(p[0] // bw) if (p[0] // bw) in order else 999, p[0]))
        for (d0, n_, s0) in pieces:
            bk = "%s#%d" % (dkey, d0 // bw)
            for c in range(KC):
                t, k = stg.next()
                S.dma(t[:, 0:n_], src[c * 128:(c + 1) * 128, s0:s0 + n_], writes=[k], chan=k)
                iv = t[:, 0:n_]
                ov = dst[:, c, d0:d0 + n_]
                eng = self.rr(["dve", "act", "dve", "act", "pool"])
                rd = [k] + ([gkey] if gain is not None else [])
                if gain is None:
                    if eng == "act":
                        S.act(actf(ov, iv, AF.Copy), reads=rd, writes=[bk])
                    else:
                        S.add(eng, cp(ov, iv), reads=rd, writes=[bk])
                else:
                    if eng == "act":
                        S.act(actf(ov, iv, AF.Copy, scale=gain[:, c:c + 1]), reads=rd, writes=[bk])
                    else:
                        S.add(eng, ts(ov, iv, gain[:, c:c + 1], ALU.mult), reads=rd, writes=[bk])

    def norm_load(self, src3, t0, n, ent):
        xt, kx = ent
        self.S.dma(xt[:, :, 0:n], src3[:, :, t0:t0 + n], writes=[kx], chan=kx)

    def norm_tile(self, src3, t0, n, bufs, D_feat=1024, load=True):
        S = self.S
        xt, kx = bufs["xt"]
        sq, ksq = bufs["sq"]
        rs, krs = bufs["rs"]
        hT, kh = bufs["hT"]
        ssp, kss = bufs["ss"]
        if load:
            S.dma(xt[:, :, 0:n], src3[:, :, t0:t0 + n], writes=[kx], chan=kx)
        S.act(actf(sq[:, :, 0:n], xt[:, :, 0:n], AF.Square), reads=[kx], writes=[ksq])
        S.pe(mmg([(ssp[:, 0:n], self.ones_b, sq[:, c, 0:n]) for c in range(8)]), reads=[ksq, "cmb"], writes=[kss])
        S.act(actf(rs[:, 0:n], ssp[:, 0:n], AF.Sqrt, bias=EPS, scale=1.0 / D_feat), reads=[kss], writes=[krs])
        S.dve(recip(rs[:, 0:n], rs[:, 0:n]), reads=[krs], writes=[krs])
        S.dve(tt(hT[:, 0:4, 0:n], xt[:, 0:4, 0:n], rs[:, 0:n].unsqueeze(1).to_broadcast([128, 4, n]), ALU.mult),
              reads=[kx, krs], writes=[kh + "a"])
        S.pool(tt(hT[:, 4:8, 0:n], xt[:, 4:8, 0:n], rs[:, 0:n].unsqueeze(1).to_broadcast([128, 4, n]), ALU.mult),
               reads=[kx, krs], writes=[kh + "b"])
        return [kh + "a", kh + "b"]

    def norm_pipeline(self, X3, xt, sq, rs, hT):
        NT = self.NT
        st = {"nxt": xt.next()}
        self.norm_load(X3, 0, 512, st["nxt"])

        def prep(j):
            bufs = {"xt": st["nxt"], "sq": sq.next(), "rs": rs.next(), "hT": hT.next(), "ss": (self.ps[0], "ps0")}
            hk = self.norm_tile(X3, j * 512, 512, bufs, load=False)
            if j + 1 < NT:
                st["nxt"] = xt.next()
                self.norm_load(X3, (j + 1) * 512, 512, st["nxt"])
            return bufs, hk
        return prep

    def norm_bufs(self, nbuf=1, n=512):
        xt = self.ring("xt", nbuf, [128, 8, n], F32)
        sq = self.ring("sq", 1, [128, 8, n], BF16)
        rs = self.ring("rs", 2, [128, n], F32)
        hT = self.ring("hT", 2, [128, 8, n], BF16)
        return xt, sq, rs, hT

    def proj_resid(self, A, KC, w_d, Xin, Xout):
        S, NT = self.S, self.NT
        with self.phase():
            w = self.sb("wpr", [128, KC, 1024], BF16)
            self.prep_weight(w, "wpr", w_d, KC, [(0, 1024, 0, 1024, None)], bw=256)
            ar = self.ring("a", 2, [128, KC, 512], BF16)
            xr = self.ring("x", 2, [128, 8, 512], F32)
            A3 = A[0:KC * 128, :].rearrange("(c p) t -> p c t", p=128)
            Xi3, Xo3 = self.x3(Xin), self.x3(Xout)
            psr = Ring([(self.ps[i], self.pk[i]) for i in range(4)])
            def pr_load(j):
                at, ka = ar.next()
                xt, kx = xr.next()
                S.dma(at[:], A3[:, :, j * 512:(j + 1) * 512], writes=[ka], chan=ka)
                S.dma(xt[:], Xi3[:, :, j * 512:(j + 1) * 512], writes=[kx], chan=kx)
                return at, ka, xt, kx
            nxt = pr_load(0)
            for j in range(NT):
                t0 = j * 512
                at, ka, xt, kx = nxt
                if j + 1 < NT:
                    nxt = pr_load(j + 1)
                for d in range(8):
                    p, kp = psr.next()
                    S.pe(mmg([(p[:], w[:, c, d * 128:(d + 1) * 128], at[:, c, :]) for c in range(KC)]),
                         reads=[ka] + self.wkeys("wpr", d * 128, 128, 256), writes=[kp])
                    S.dve(tt(xt[:, d, :], p[:], xt[:, d, :], ALU.add), reads=[kp, kx], writes=[kx])
                S.dma(Xo3[:, :, t0:t0 + 512], xt[:], reads=[kx], writes=["Xo"], chan=kx + "s")

    def final_norm(self, X):
        S, NT = self.S, self.NT
        with self.phase():
            g = self.load_gain(self.w["norm_final"], "gfin")
            xt, sq, rs, hT = self.norm_bufs(nbuf=2)
            yr = self.ring("y", 2, [128, 8, 512], F32)
            X3, Y3 = self.x3(X), self.x3(self.yT)
            nxt = xt.next()
            self.norm_load(X3, 0, 512, nxt)
            for j in range(NT):
                t0 = j * 512
                x_, kx = nxt
                if j + 1 < NT:
                    nxt = xt.next()
                    self.norm_load(X3, t0 + 512, 512, nxt)
                q_, kq = sq.next()
                r_, kr = rs.next()
                y_, ky = yr.next()
                S.act(actf(q_[:], x_[:], AF.Square), reads=[kx], writes=[kq])
                S.pe(mmg([(self.ps[0][:], self.ones_b, q_[:, c, :]) for c in range(8)]), reads=[kq, "cmb"], writes=["ps0"])
                S.act(actf(r_[:], self.ps[0][:], AF.Sqrt, bias=EPS, scale=1.0 / 1024), reads=["ps0"], writes=[kr])
                S.dve(recip(r_[:], r_[:]), reads=[kr], writes=[kr])
                for c in range(8):
                    S.dve(stt(y_[:, c, :], x_[:, c, :], g[:, c:c + 1], r_[:], ALU.mult, ALU.mult),
                          reads=[kx, kr, "gfin"], writes=[ky])
                S.dma(Y3[:, :, t0:t0 + 512], y_[:], reads=[ky], writes=["Y"], chan=ky + "s")

    def copy_out(self, X):
        S, NT = self.S, self.NT
        with self.phase():
            xr = self.ring("x", 2, [128, 8, 512], F32)
            X3, Y3 = self.x3(X), self.x3(self.yT)
            for j in range(NT):
                x_, kx = xr.next()
                S.dma(x_[:], X3[:, :, j * 512:(j + 1) * 512], writes=[kx], chan=kx)
                S.dma(Y3[:, :, j * 512:(j + 1) * 512], x_[:], reads=[kx], writes=["Y"], chan=kx + "s")

    def ffn_up(self, X, L):
        S, NT = self.S, self.NT
        nc = self.nc
        NB = 2 * (NT - 1)
        with self.phase():
            w = self.sb("wup", [128, 8, 5632], BF16)
            g = self.load_gain(self.w["norm_ffn"][L], "gffn")
            cw = self.sb("cw", [128, 22, 3], F32)
            cb = self.sb("cb", [128, 22], F32)
            for k3 in range(3):
                S.dma(cw[:, :, k3], self.w["ffn_conv_w"][L, k3].rearrange("(f p) -> p f", p=128), writes=["cw"], chan="cw", allow_slow_non_contiguous=True)
            S.dma(cb[:], self.w["ffn_conv_b"][L].rearrange("(f p) -> p f", p=128), writes=["cb"], chan="cb", allow_slow_non_contiguous=True)
            self.prep_weight(w, "wup", self.w["ffn_w_up"][L], 8,
                             [(i * 2048, min(2048, 5632 - i * 2048), i * 2048, min(2048, 5632 - i * 2048), None) for i in range(3)],
                             gain=g, gkey="gffn", order=[0, 2, 3, 1, 4, 5])
            xt, sq, rs, hT = self.norm_bufs()
            X3 = self.x3(X)
            gh = self.sb("gh", [128, 22, NT, 2], F32)
            S.pool(mset(gh[:], 0.0), writes=["gh"])
            if NB > 0:
                xb = self.sb("xb", [128, 8, NT - 1, 2], F32)
                sqb = self.sb("sqb", [128, 8, NB], BF16)
                rsb = self.sb("rsb", [128, NB], F32)
                hb = self.sb("hb", [128, 8, NB], BF16)
                for c in range(8):
                    src = X3[:, c, 511:511 + 512 * (NT - 1)].rearrange("p (b r) -> p b r", r=512)[:, :, 0:2]
                    S.dma(xb[:, c, :, :], src, writes=["xb%d" % c], chan="xb", allow_slow_non_contiguous=True)
                xbf = xb[:].rearrange("p c b e -> p c (b e)")
                xk = ["xb%d" % c for c in range(8)]
                S.act(actf(sqb[:], xbf, AF.Square), reads=xk, writes=["sqb"])
                S.pe(mmg([(self.ps[0][:, 0:NB], self.ones_b, sqb[:, c, :]) for c in range(8)]), reads=["sqb", "cmb"], writes=["ps0"])
                S.act(actf(rsb[:], self.ps[0][:, 0:NB], AF.Sqrt, bias=EPS, scale=1.0 / 1024), reads=["ps0"], writes=["rsb"])
                S.dve(recip(rsb[:], rsb[:]), reads=["rsb"], writes=["rsb"])
                S.dve(tt(hb[:], xbf, rsb[:].unsqueeze(1).to_broadcast([128, 8, NB]), ALU.mult), reads=xk + ["rsb"], writes=["hb"])
                pr = Ring([(self.ps[1], "ps1"), (self.ps[2], "ps2")])
                for f in range(22):
                    p, kp = pr.next()
                    S.pe(mmg([(p[:, 0:NB], w[:, c, 2816 + f * 128:2816 + (f + 1) * 128], hb[:, c, :]) for c in range(8)]),
                         reads=["hb"] + self.wkeys("wup", 2816 + f * 128, 128), writes=[kp])
                    pv = p[:, 0:NB].rearrange("p (b e) -> p b e", e=2)
                    S.dve(cp(gh[:, f, 1:NT, 0], pv[:, :, 0]), reads=[kp], writes=["gh"])
                    S.dve(cp(gh[:, f, 0:NT - 1, 1], pv[:, :, 1]), reads=[kp], writes=["gh"])
            ghf = gh[:].rearrange("p f j e -> p f (j e)")
            S.dve(tt(ghf, ghf, self.flb[:].unsqueeze(1).to_broadcast([128, 22, NT * 2]), ALU.mult), reads=["gh", "flb"], writes=["gh"])
            actr = self.ring("act", 2, [128, 22, 512], BF16)
            gbr = self.ring("gb", 2, [128, 514], F32)
            tr_ = self.ring("tc", 2, [128, 512], F32)
            pu = Ring([(self.ps[1], "ps1"), (self.ps[2], "ps2")])
            pg = Ring([(self.ps[3], "ps3"), (self.ps[4], "ps4"), (self.ps[5], "ps5")])
            A3 = self.ACTS.rearrange("(f p) t -> p f t", p=128)
            prep = self.norm_pipeline(X3, xt, sq, rs, hT)
            cur = prep(0)
            for j in range(NT):
                t0 = j * 512
                bufs, hk = cur
                h_ = bufs["hT"][0]
                a_, ka = actr.next()
                for f in range(22):
                    if f == 3 and j + 1 < NT:
                        cur = prep(j + 1)
                    u, ku = pu.next()
                    gp, kg = pg.next()
                    S.pe(mmg([(u[:], w[:, c, f * 128:(f + 1) * 128], h_[:, c, :]) for c in range(8)]), reads=hk + self.wkeys("wup", f * 128, 128), writes=[ku])
                    S.pe(mmg([(gp[:], w[:, c, 2816 + f * 128:2816 + (f + 1) * 128], h_[:, c, :]) for c in range(8)]), reads=hk + self.wkeys("wup", 2816 + f * 128, 128), writes=[kg])
                    gb, kb = gbr.next()
                    tc, kt = tr_.next()
                    S.act(actf(gb[:, 1:513], gp[:], AF.Copy), reads=[kg], writes=[kb + "m"])
                    S.dve(cp(gb[:, 0:514:513], gh[:, f, j, :]), reads=["gh"], writes=[kb + "h"])
                    S.act(actf(tc[:], gp[:], AF.Identity, bias=cb[:, f:f + 1], scale=cw[:, f, 1:2]), reads=[kg, "cw", "cb"], writes=[kt])
                    S.dve(stt(tc[:], gb[:, 0:512], cw[:, f, 0:1], tc[:], ALU.mult, ALU.add), reads=[kb + "m", kb + "h", kt, "cw"], writes=[kt])
                    S.dve(stt(tc[:], gb[:, 2:514], cw[:, f, 2:3], tc[:], ALU.mult, ALU.add), reads=[kb + "m", kb + "h", kt, "cw"], writes=[kt])
                    S.act(actf(tc[:], tc[:], AF.Gelu), reads=[kt], writes=[kt])
                    S.dve(tt(a_[:, f, :], tc[:], u[:], ALU.mult), reads=[kt, ku], writes=[ka])
                S.dma(A3[:, :, t0:t0 + 512], a_[:], reads=[ka], writes=["ACTS"], chan=ka + "s")

    def ab_inproj(self, X, j, L):
        S, NT, nc = self.S, self.NT, self.nc
        with self.phase():
            w = self.sb("wab", [128, 8, 2832], BF16)
            g = self.load_gain(self.w["norm_mix"][L], "gmix")
            stg = self.ring("wstg", 2, [128, 2832], F32)
            src = self.w["ab_w_in"][j]

            def hsplit(ap, nh, half):
                return ap.rearrange("p (h d) -> p h d", d=64)[:, :, half * 32:(half + 1) * 32]

            def h32(ap, nh):
                return ap.rearrange("p (h d) -> p h d", d=32)

            for c in range(8):
                t, k = stg.next()
                S.dma(t[:], src[c * 128:(c + 1) * 128, :], writes=[k], chan=k)
                gc = g[:, c:c + 1]
                engs = ["dve", "pool"]
                for gi in range(2):
                    for half in range(2):
                        S.add(self.rr(engs), ts(h32(w[:, c, gi * 256 + half * 128: gi * 256 + half * 128 + 128], 4),
                                                hsplit(t[:, gi * 256:(gi + 1) * 256], 4, half), gc, ALU.mult),
                              reads=[k, "gmix"], writes=["wab"])
                for half in range(2):
                    S.add(self.rr(engs), ts(h32(w[:, c, 512 + half * 64: 512 + half * 64 + 64], 2),
                                            hsplit(t[:, 512:640], 2, half), gc, ALU.mult), reads=[k, "gmix"], writes=["wab"])
                for (d0, s0, n) in [(1664, 640, 128), (640, 768, 1024), (1808, 1792, 1024), (1792, 2816, 16)]:
                    S.add(self.rr(engs), ts(w[:, c, d0:d0 + n], t[:, s0:s0 + n], gc, ALU.mult), reads=[k, "gmix"], writes=["wab"])
            gq = self.sb("gq", [128, 2], F32)
            gk = self.sb("gk", [128, 2], F32)
            for r in range(4):
                for half in range(2):
                    S.dma(gq[r * 32:(r + 1) * 32, half:half + 1], self.w["attn_q_norm"][j, half * 32:(half + 1) * 32].unsqueeze(1),
                          writes=["gq%d%d" % (r, half)], chan="gqld", allow_slow_non_contiguous=True)
                    S.dma(gk[r * 32:(r + 1) * 32, half:half + 1], self.w["attn_k_norm"][j, half * 32:(half + 1) * 32].unsqueeze(1),
                          writes=["gk%d%d" % (r, half)], chan="gkld", allow_slow_non_contiguous=True)
            gqk = ["gq%d%d" % (r, h) for r in range(4) for h in range(2)]
            gkk = ["gk%d%d" % (r, h) for r in range(4) for h in range(2)]
            S.dve(ts(gq[:], gq[:], 0.125, ALU.mult), reads=gqk, writes=["gq"])
            gbias = self.sb("gbias", [128, 16], F32)
            S.dma(gbias[:], self.w["ab_gate_bias"][j:j + 1, :].to_broadcast([128, 16]), writes=["gbias"], chan="gbias")
            xt, sq, rs, hT = self.norm_bufs()
            X3 = self.x3(X)
            cs = self.ring("cs", 2, [128, 2, 512], F32)
            sqa = self.ring("sqa", 2, [128, 2, 512], BF16)
            rq = self.ring("rq", 2, [128, 512], F32)
            an = self.ring("an", 2, [128, 2, 512], F32)
            t4 = self.ring("t4", 1, [128, 4, 512], F32)
            o12 = self.ring("o12", 3, [128, 2, 512], BF16)
            mqs = self.ring("mqs", 1, [128, 4, 512], BF16)
            mks = self.ring("mks", 1, [128, 4, 512], BF16)
            vts = self.ring("vt", 2, [128, 4, 2, 65], BF16)
            mvs = self.ring("mvt", 2, [128, 4, 4, 129], BF16)
            sgs = self.ring("sg", 1, [128, 4, 512], F32)
            gts = self.ring("gt", 2, [128, 4, 16], F32)
            for (t, k) in vts.items:
                S.pool(mset(t[:], 1.0), writes=[k])
            for (t, k) in mvs.items:
                S.pool(mset(t[:], 1.0), writes=[k])
            pf = Ring([(self.ps[i], self.pk[i]) for i in (1, 2, 3)])
            ptm = Ring([(self.ps[4], "ps4"), (self.ps[5], "ps5")])
            def cs_load(jt_):
                c_, kc = cs.next()
                S.dma(c_[:], self.ropeA[:, :, jt_ * 512:(jt_ + 1) * 512].rearrange("a p t -> p a t"), writes=[kc], chan=kc)
                return c_, kc
            prep = self.norm_pipeline(X3, xt, sq, rs, hT)
            cur = prep(0)
            cnxt = cs_load(0)
            for jt in range(NT):
                t0 = jt * 512
                bufs, hk = cur
                h_ = bufs["hT"][0]
                c_, kc = cnxt
                if jt + 1 < NT:
                    cnxt = cs_load(jt + 1)

                def fm(col0, M):
                    p, kp = pf.next()
                    S.pe(mmg([(p[0:M, :], w[:, c, col0:col0 + M], h_[:, c, :]) for c in range(8)]), reads=hk + ["wab"], writes=[kp])
                    return p, kp

                for grp in range(3):
                    M = 128 if grp < 2 else 64
                    colA = grp * 256 if grp < 2 else 512
                    colB = colA + M
                    gg, ggk = (gq, ["gq"]) if grp < 2 else (gk, gkk)
                    pa, kpa = fm(colA, M)
                    pb, kpb = fm(colB, M)
                    s_, ks = sqa.next()
                    S.act(actf(s_[0:M, 0, :], pa[0:M, :], AF.Square), reads=[kpa], writes=[ks + "a"])
                    S.act(actf(s_[0:M, 1, :], pb[0:M, :], AF.Square), reads=[kpb], writes=[ks + "b"])
                    S.pe(mmg([(self.ps[0][0:M, :], self.bd32_b[0:M, 0:M], s_[0:M, 0, :]),
                              (self.ps[0][0:M, :], self.bd32_b[0:M, 0:M], s_[0:M, 1, :])]), reads=[ks + "a", ks + "b", "cmb"], writes=["ps0"])
                    r_, kr = rq.next()
                    S.act(actf(r_[0:M, :], self.ps[0][0:M, :], AF.Sqrt, bias=EPS, scale=1.0 / 64), reads=["ps0"], writes=[kr])
                    S.dve(recip(r_[0:M, :], r_[0:M, :]), reads=[kr], writes=[kr])
                    a_, kan = an.next()
                    S.dve(stt(a_[0:M, 0, :], pa[0:M, :], gg[0:M, 0:1], r_[0:M, :], ALU.mult, ALU.mult), reads=[kpa, kr] + ggk, writes=[kan + "a"])
                    S.dve(stt(a_[0:M, 1, :], pb[0:M, :], gg[0:M, 1:2], r_[0:M, :], ALU.mult, ALU.mult), reads=[kpb, kr] + ggk, writes=[kan + "b"])
                    t_, kt = t4.next()
                    S.pool(tt(t_[0:M, 0, :], a_[0:M, 0, :], c_[0:M, 0, :], ALU.mult), reads=[kan + "a", kc], writes=[kt + "0"])
                    S.pool(tt(t_[0:M, 1, :], a_[0:M, 1, :], c_[0:M, 1, :], ALU.mult), reads=[kan + "b", kc], writes=[kt + "1"])
                    S.dve(tt(t_[0:M, 2, :], a_[0:M, 1, :], c_[0:M, 0, :], ALU.mult), reads=[kan + "b", kc], writes=[kt + "2"])
                    S.pool(tt(t_[0:M, 3, :], a_[0:M, 0, :], c_[0:M, 1, :], ALU.mult), reads=[kan + "a", kc], writes=[kt + "3"])
                    o_, ko = o12.next()
                    S.pool(tt(o_[0:M, 0, :], t_[0:M, 0, :], t_[0:M, 1, :], ALU.subtract), reads=[kt + "0", kt + "1"], writes=[ko + "0"])
                    S.dve(tt(o_[0:M, 1, :], t_[0:M, 2, :], t_[0:M, 3, :], ALU.add), reads=[kt + "2", kt + "3"], writes=[ko + "1"])
                    for hl in range(M // 32):
                        for half in range(2):
                            if grp < 2:
                                dst = self.QA[grp * 4 + hl, half * 32:(half + 1) * 32, t0:t0 + 512]
                            else:
                                dst = self.KA[hl * 64 + half * 32: hl * 64 + (half + 1) * 32, t0:t0 + 512]
                            S.dma(dst, o_[hl * 32:(hl + 1) * 32, half, :], reads=[ko + str(half)], writes=["QK"], chan=ko + "s%d%d" % (hl, half))
                if jt + 1 < NT:
                    cur = prep(jt + 1)
                mq_, kmq = mqs.next()
                mk_, kmk = mks.next()
                for h in range(4):
                    p, kp = fm(640 + h * 128, 128)
                    S.act(actf(mq_[:, h, :], p[:], AF.Copy), reads=[kp], writes=[kmq])
                for h in range(4):
                    p, kp = fm(1152 + h * 128, 128)
                    S.act(actf(mk_[:, h, :], p[:], AF.Copy, scale=float(128 ** -0.5)), reads=[kp], writes=[kmk])
                S.dma(self.MQ[:, :, t0:t0 + 512].rearrange("h d t -> d h t"), mq_[:], reads=[kmq], writes=["MQ"], chan=kmq + "s")
                S.dma(self.MKs[:, :, t0:t0 + 512].rearrange("h d t -> d h t"), mk_[:], reads=[kmk], writes=["MK"], chan=kmk + "s")
                vt, kv = vts.next()
                mv, kmv = mvs.next()
                sg, ksg = sgs.next()
                gt, kgt = gts.next()
                for sub in range(4):
                    hs = lambda c: h_[:, c, sub * 128:(sub + 1) * 128]
                    p1, k1 = ptm.next()
                    S.pe(mmg([(p1[:, 0:144], hs(c), w[:, c, 1664:1808]) for c in range(8)]), reads=hk + ["wab"], writes=[k1])
                    S.act(actf(vt[:, sub, :, 0:64], p1[:, 0:128].rearrange("p (h d) -> p h d", d=64), AF.Copy), reads=[k1], writes=[kv])
                    S.dve(tt(gt[:, sub, :], p1[:, 128:144], gbias[:], ALU.add), reads=[k1, "gbias"], writes=[kgt])
                    p2, k2 = ptm.next()
                    S.pe(mmg([(p2[:], hs(c), w[:, c, 1808:2320]) for c in range(8)]), reads=hk + ["wab"], writes=[k2])
                    S.dve(cp(mv[:, sub, :, 0:128], p2[:].rearrange("p (h d) -> p h d", d=128)), reads=[k2], writes=[kmv])
                    p3, k3 = ptm.next()
                    S.pe(mmg([(p3[:], hs(c), w[:, c, 2320:2832]) for c in range(8)]), reads=hk + ["wab"], writes=[k3])
                    S.act(actf(sg[:, sub, :], p3[:], AF.Sigmoid), reads=[k3], writes=[ksg])
                S.dma(self.VA[t0:t0 + 512, :].rearrange("(s p) c -> p s c", p=128), vt[:].rearrange("p s h d -> p s (h d)"), reads=[kv], writes=["VA"], chan=kv + "s")
                S.dma(self.MV[t0:t0 + 512, :].rearrange("(s p) c -> p s c", p=128), mv[:].rearrange("p s h d -> p s (h d)"), reads=[kmv], writes=["MV"], chan=kmv + "s")
                S.dma(self.SG[t0:t0 + 512, :].rearrange("(s p) c -> p s c", p=128), sg[:], reads=[ksg], writes=["SG"], chan=ksg + "s")
                S.dma(self.GT[t0:t0 + 512, :].rearrange("(s p) c -> p s c", p=128), gt[:], reads=[kgt], writes=["GT"], chan=kgt + "s")

    def attention(self):
        S, NT, NCH, T, nc = self.S, self.NT, self.NCH, self.T, self.nc
        with self.phase(psum=False):
            ph = self._ph
            ppr = Ring([(ph.enter_context(nc.psum_tensor(self.un("pp"), [128, 1024], F32)), "pp%d" % i) for i in range(2)])
            poa = (ph.enter_context(nc.psum_tensor(self.un("poa"), [128, 512], F32)), "poa")
            pob = (ph.enter_context(nc.psum_tensor(self.un("pob"), [128, 512], F32)), "pob")
            ptb = ph.enter_context(nc.psum_tensor(self.un("ptb"), [128, 1024], BF16))
            K = self.sb("Kall", [128, T], BF16)
            V = self.sb("Vall", [128, NCH, 130], BF16)
            S.dma(K[:], self.KA, writes=["K"], chan="K")
            S.dma(V[:], self.VA.rearrange("(b p) c -> p b c", p=128), writes=["V"], chan="V")
            qr = self.ring("q", 2, [128, 8, 512], BF16)
            for (t_, k_) in qr.items:
                S.dve(mset(t_[:], 0.0), writes=[k_ + "0", k_ + "1"])
            pr = self.ring("p", 3, [128, 1024], BF16)
            rcr = self.ring("rc", 2, [128, 4], F32)
            otr = self.ring("ot", 2, [128, 4, 8, 64], BF16)
            ob = self.ring("ob", 2, [128, 4, 512], BF16)
            items = [(jq, hp, kb) for jq in range(NT) for hp in range(4) for kb in range(NCH)]
            tctx, ictx = {}, {}

            def stage1(i):
                jq, hp, kb = items[i]
                t0 = jq * 512
                if hp == 0 and kb == 0:
                    q_, kq = qr.next()
                    for kvh in range(2):
                        S.dma(q_[kvh * 64:(kvh + 1) * 64, kvh * 4:(kvh + 1) * 4, :], self.QA[kvh * 4:(kvh + 1) * 4, :, t0:t0 + 512].rearrange("h d t -> d h t"),
                              writes=[kq + str(kvh)], chan=kq + str(kvh))
                    tctx[jq] = (q_, kq) + otr.next()
                q_, kq, ot, kot = tctx[jq]
                pb, kpb = ppr.next()

                def st2(e, pb=pb, q_=q_, hp=hp, kb=kb):
                    e.matmul(pb[:, 0:512], lhsT=K[:, kb * 128:(kb + 1) * 128], rhs=q_[:, hp, :], start=True, stop=True)
                    return e.matmul(pb[:, 512:1024], lhsT=K[:, kb * 128:(kb + 1) * 128], rhs=q_[:, 4 + hp, :], start=True, stop=True)
                S.pe(st2, reads=["K", kq + "0", kq + "1"], writes=[kpb])
                p_, kp_ = pr.next()
                S.act(actf(p_[:], pb[:], AF.Exp, bias=self.mb[:, jq, (kb // 4):(kb // 4) + 1]), reads=[kpb, "mb"], writes=[kp_])
                ictx[i] = (p_, kp_)

            def stage2(i):
                jq, hp, kb = items[i]
                t0 = jq * 512
                q_, kq, ot, kot = tctx[jq]
                p_, kp_ = ictx.pop(i)

                def pv8(e, p_=p_, kb=kb):
                    ins = None
                    for hh, (po, _) in enumerate((poa, pob)):
                        for sub in range(4):
                            ins = e.matmul(po[:, sub * 65:(sub + 1) * 65], lhsT=p_[:, hh * 512 + sub * 128: hh * 512 + (sub + 1) * 128],
                                           rhs=V[:, kb, hh * 65:(hh + 1) * 65], start=(kb == 0 and sub == 0), stop=(kb == NCH - 1 and sub == 3))
                    return ins
                S.pe(pv8, reads=["V", kp_], writes=["poa", "pob"])
                if kb != NCH - 1:
                    return
                for hh, (po, kpo) in enumerate((poa, pob)):
                    h = hh * 4 + hp
                    pv = po[:, 0:260].rearrange("p (s c) -> p s c", c=65)
                    r_, kr = rcr.next()
                    S.dve(recip(r_[:], pv[:, :, 64]), reads=[kpo], writes=[kr])
                    S.dve(tt(ot[:, :, h, :], pv[:, :, 0:64], r_[:].unsqueeze(2).to_broadcast([128, 4, 64]), ALU.mult), reads=[kpo, kr], writes=[kot])
                if hp != 3:
                    return
                o_, ko = ob.next()
                for sp in range(2):
                    def tr8(e, ot=ot, sp=sp):
                        ins = None
                        for hp2 in range(4):
                            for s_ in range(2):
                                slot = hp2 * 2 + s_
                                ins = e.transpose(ptb[:, slot * 128:(slot + 1) * 128],
                                                  ot[:, sp * 2 + s_, 2 * hp2:2 * hp2 + 2, :].rearrange("p h d -> p (h d)"), self.ident_b)
                        return ins
                    S.pe(tr8, reads=[kot, "cmb"], writes=["ptb"])
                    S.act(actf(o_[:, :, sp * 256:(sp + 1) * 256].rearrange("p a (s q) -> p a s q", q=128),
                               ptb[:].rearrange("p (a s q) -> p a s q", a=4, s=2), AF.Copy), reads=["ptb"], writes=[ko])
                S.dma(self.MIX[0:512, t0:t0 + 512].rearrange("(a p) t -> p a t", p=128), o_[:], reads=[ko], writes=["MIXa"], chan=ko + "s")

            self.pipeline(len(items), stage1, stage2, 1)

    def mlstm(self, j):
        S, NT, NCH, T, nc = self.S, self.NT, self.NCH, self.T, self.nc
        NG = NCH * 4
        with self.phase():
            G = self.sb("G", [128, NCH, 16], F32)
            S.dma(G[:], self.GT.rearrange("(c p) g -> p c g", p=128), writes=["G"], chan="G")
            G5 = G[:].rearrange("p c (d k h) -> p d c k h", d=2, k=2, h=4)
            gi, gf = G5[:, :, :, 0, :], G5[:, :, :, 1, :]
            shp = [128, 2, NCH, 4]
            mk = lambda n: self.sb(n, shp, F32)
            FL, Bc, A_, AMr, BLr, Mt, mt, WK, THR, KEEP = [mk(n) for n in ("FL", "Bc", "Aa", "AMr", "BLr", "Mt", "mt", "WK", "THR", "KEEP")]
            tmp = mk("tmp")
            fl2 = lambda t, d: t[:, d, :, :].rearrange("p c h -> p (c h)")
            S.act(actf(tmp[:], gf, AF.Abs), reads=["G"], writes=["tmp"])
            S.act(actf(tmp[:], tmp[:], AF.Exp, scale=-1.0), reads=["tmp"], writes=["tmp"])
            S.act(actf(tmp[:], tmp[:], AF.Ln, bias=1.0), reads=["tmp"], writes=["tmp"])
            S.dve(ts(FL[:], gf, 0.0, ALU.min), reads=["G"], writes=["FL"])
            S.dve(tt(FL[:], FL[:], tmp[:], ALU.subtract), reads=["FL", "tmp"], writes=["FL"])
            bw = min(128, NG)
            nblk = NG // bw
            am = self.sb("am", [128, 2, nblk], F32)
            dg = self.sb("dg", [128, 128], F32)
            for d in range(2):
                tri = self.cm[:, C_TRIU, :] if d == 0 else self.cm[:, C_TRIL, :]
                S.pe(mm1(self.ps[0][:, 0:NG], tri, fl2(FL, d)), reads=["FL", "cm"], writes=["ps0"])
                S.act(actf(fl2(Bc, d), self.ps[0][:, 0:NG], AF.Copy), reads=["ps0"], writes=["Bc"])
                S.pe(mm1(self.ps[1][:, 0:NG], self.ones_f, fl2(FL, d)), reads=["FL", "cm"], writes=["ps1"])
                S.act(actf(fl2(BLr, d), self.ps[1][:, 0:NG], AF.Copy), reads=["ps1"], writes=["BLr"])
                S.dve(tt(A_[:, d], gi[:, d], Bc[:, d], ALU.subtract), reads=["G", "Bc"], writes=["Aa"])
                for b in range(nblk):
                    S.pe(tr(self.ps[2][0:bw, 0:128], fl2(A_, d)[:, b * bw:(b + 1) * bw], self.ident_f), reads=["Aa", "cm"], writes=["ps2"])
                    S.dve(lambda e, d=d, b=b, ps2=self.ps[2]: e.tensor_reduce(out=am[0:bw, d, b:b + 1], in_=ps2[0:bw, 0:128], axis=AX.X, op=ALU.max),
                          reads=["ps2"], writes=["am"])
                    S.dve(ts(dg[0:bw, 0:bw], self.ident_f[0:bw, 0:bw], am[0:bw, d, b:b + 1], ALU.mult), reads=["am", "cm"], writes=["dg"])
                    S.pe(mm1(self.ps[3][:, 0:bw], self.ones_f[0:bw, :], dg[0:bw, 0:bw]), reads=["dg", "cm"], writes=["ps3"])
                    S.act(actf(fl2(AMr, d)[:, b * bw:(b + 1) * bw], self.ps[3][:, 0:bw], AF.Copy), reads=["ps3"], writes=["AMr"])
            mrun = self.sb("mrun", [128, 2, 4], F32)
            S.dve(mset(mrun[:], 0.0), writes=["mrun0", "mrun1"])
            for k in range(NCH):
                for d in range(2):
                    eng = "dve"
                    kk = "rec%d" % d
                    c = k if d == 0 else NCH - 1 - k
                    S.add(eng, ts(mt[:, d, c, :], mrun[:, d, :], self.carry[:, d, c:c + 1], ALU.mult), reads=["mrun%d" % d, "carry"], writes=[kk + "mt"])
                    S.add(eng, tt(Mt[:, d, c, :], mt[:, d, c, :], AMr[:, d, c, :], ALU.max), reads=[kk + "mt", "AMr"], writes=[kk + "Mt"])
                    S.add(eng, tt(mrun[:, d, :], Mt[:, d, c, :], BLr[:, d, c, :], ALU.add), reads=[kk + "Mt", "BLr"], writes=["mrun%d" % d])
            rk = ["rec0mt", "rec1mt", "rec0Mt", "rec1Mt"]
            S.dve(tt(KEEP[:], mt[:], Mt[:], ALU.subtract), reads=rk, writes=["KEEP"])
            S.act(actf(KEEP[:], KEEP[:], AF.Exp), reads=["KEEP"], writes=["KEEP"])
            S.dve(tt(KEEP[:].rearrange("p d c h -> p (d c) h"), KEEP[:].rearrange("p d c h -> p (d c) h"),
                     self.carry[:].rearrange("p d c -> p (d c)").unsqueeze(2).to_broadcast([128, 2 * NCH, 4]), ALU.mult), reads=["KEEP", "carry"], writes=["KEEP"])
            S.dve(tt(WK[:], A_[:], Mt[:], ALU.subtract), reads=["Aa"] + rk, writes=["WK"])
            S.act(actf(WK[:], WK[:], AF.Exp), reads=["WK"], writes=["WK"])
            S.dve(tt(THR[:], Bc[:], Mt[:], ALU.add), reads=["Bc"] + rk, writes=["THR"])
            S.act(actf(THR[:], THR[:], AF.Exp, scale=-1.0), reads=["THR"], writes=["THR"])
            maskf = self.cm[:, C_TRIU, :]
            maskb = self.cm[:, C_TRIL, :]
            qr = [self.ring("mq", 2, [128, 4, 512], BF16) for d in range(2)]
            kr = [self.ring("mk", 2, [128, 4, 512], BF16) for d in range(2)]
            vr = [self.ring("mv", 2, [128, 4, 516], BF16) for d in range(2)]
            Cst = [[self.sb("Cst", [128, 129], F32) for h in range(4)] for d in range(2)]
            Cbf = [[self.sb("Cbf", [128, 129], BF16) for h in range(4)] for d in range(2)]
            for d in range(2):
                for h in range(4):
                    S.pool(mset(Cst[d][h][:], 0.0), writes=["C%d%d" % (d, h)])
            atr = self.ring("at", 6, [128, 128], BF16)
            kwr = self.ring("kw", 6, [128, 128], BF16)
            rr_ = self.ring("r", 6, [128, 2], F32)
            hst = [self.ring("hst", 2, [128, 512], F32) for d in range(2)]
            pS = Ring([(self.ps[0], "ps0"), (self.ps[1], "ps1")])
            pH = Ring([(self.ps[2], "ps2"), (self.ps[3], "ps3")])
            pU = Ring([(self.ps[4], "ps4"), (self.ps[5], "ps5")])
            pT = Ring([(self.psb[0], "psb0"), (self.psb[1], "psb1")])
            MQ3 = self.MQ.rearrange("h d t -> d h t")
            MK3 = self.MKs.rearrange("h d t -> d h t")
            HO = [self.HF, self.HB]
            items = [(k, h, d) for k in range(NCH) for h in range(4) for d in range(2)]
            tctx, cctx, ictx = {}, {}, {}

            def geom(it):
                k, h, d = it
                c = k if d == 0 else NCH - 1 - k
                return d, c // 4, c % 4, h, c, (c // 4) * 512

            def ensure(d, jt):
                if (d, jt) in tctx or jt < 0 or jt >= NT:
                    return
                t0 = jt * 512
                q_, kq = qr[d].next()
                k_, kk = kr[d].next()
                v_, kv = vr[d].next()
                S.dma(q_[:], MQ3[:, :, t0:t0 + 512], writes=[kq], chan=kq)
                S.dma(k_[:], MK3[:, :, t0:t0 + 512], writes=[kk], chan=kk)
                S.dma(v_[:], self.MV[t0:t0 + 512, :].rearrange("(s p) c -> p s c", p=128), writes=[kv], chan=kv)
                tctx[(d, jt)] = (q_, kq, k_, kk, v_, kv)

            def stage1(i):
                d, jt, sub, h, c, t0 = geom(items[i])
                mask = maskf if d == 0 else maskb
                if h == 0:
                    ensure(d, jt)
                    if items[i][0] % 4 == 1:
                        ensure(d, jt + (1 if d == 0 else -1))
                    cctx[(d, c)] = hst[d].next()
                q_, kq, k_, kk, v_, kv = tctx[(d, jt)]
                ck = "C%d%d" % (d, h)
                qs = q_[:, h, sub * 128:(sub + 1) * 128]
                ks_ = k_[:, h, sub * 128:(sub + 1) * 128]
                wkc = WK[:, d, c, h:h + 1]
                st_, kst = pS.next()
                S.pe(mm1(st_[:, 0:128], ks_, qs), reads=[kk, kq], writes=[kst])
                at, kat = atr.next()
                S.dve(stt(at[:], st_[:, 0:128], wkc, mask, ALU.mult, ALU.mult), reads=[kst, "WK", "cm"], writes=[kat])
                ptr, kptr = pT.next()
                S.pe(tr(ptr[:, 0:128], ks_, self.ident_b), reads=[kk, "cmb"], writes=[kptr])
                kw, kkw = kwr.next()
                S.act(actf(kw[:], ptr[:, 0:128], AF.Copy, scale=wkc), reads=[kptr, "WK"], writes=[kkw])
                S.act(actf(Cbf[d][h][:], Cst[d][h][:], AF.Copy, scale=KEEP[:, d, c, h:h + 1]), reads=[ck, "KEEP"], writes=[ck + "b"])
                ictx[i] = (at, kat, kw, kkw)

            def stage2(i):
                d, jt, sub, h, c, t0 = geom(items[i])
                q_, kq, k_, kk, v_, kv = tctx[(d, jt)]
                hs_, khs = cctx[(d, c)]
                at, kat, kw, kkw = ictx.pop(i)
                ck = "C%d%d" % (d, h)
                tc0 = t0 + sub * 128
                qs = q_[:, h, sub * 128:(sub + 1) * 128]
                vs = v_[:, sub, h * 129:(h + 1) * 129]
                ph, kph = pH.next()
                S.pe(mmg([(ph[:, 0:129], at[:], vs), (ph[:, 0:129], qs, Cbf[d][h][:])]), reads=[kat, kv, kq, ck + "b"], writes=[kph])
                pu, kpu = pU.next()
                S.pe(mm1(pu[:, 0:129], kw[:], vs), reads=[kkw, kv], writes=[kpu])
                S.dve(stt(Cst[d][h][:], Cst[d][h][:], KEEP[:, d, c, h:h + 1], pu[:, 0:129], ALU.mult, ALU.add), reads=[ck, kpu, "KEEP"], writes=[ck])
                r_, kr_ = rr_.next()
                S.dve(ts(r_[:, 0:1], ph[:, 128:129], -1.0, ALU.mult, THR[:, d, c, h:h + 1], ALU.max), reads=[kph, "THR"], writes=[kr_])
                S.dve(tt(r_[:, 0:1], r_[:, 0:1], ph[:, 128:129], ALU.max), reads=[kr_, kph], writes=[kr_])
                S.dve(recip(r_[:, 1:2], r_[:, 0:1]), reads=[kr_], writes=[kr_])
                S.act(actf(hs_[:, h * 128:(h + 1) * 128], ph[:, 0:128], AF.Copy, scale=r_[:, 1:2]), reads=[kph, kr_], writes=[khs])
                if h == 3:
                    S.dma(HO[d][tc0:tc0 + 128, :], hs_[:], reads=[khs], writes=["HO"], chan=khs + "s")

            self.pipeline(len(items), stage1, stage2, 2)

        with self.phase():
            gain = self.sb("ogain", [128, 512], F32)
            S.dma(gain[:], self.w["mlstm_out_norm"][j:j + 1, :].to_broadcast([128, 512]), writes=["ogain"], chan="ogain")
            hfr = self.ring("hf", 3, [128, 512], F32)
            hbr = self.ring("hb", 3, [128, 512], F32)
            sgr = self.ring("sgl", 3, [128, 512], F32)
            ssq = self.ring("ssq", 3, [128, 8], F32)
            junk = self.ring("junk", 2, [128, 128], F32)
            ymr = self.ring("ym", 3, [128, 512], BF16)
            mxs = self.ring("mxs", 2, [128, 4, 512], BF16)
            pT = Ring([(self.psb[0], "psb0"), (self.psb[1], "psb1")])

            def ld(c):
                hf, khf = hfr.next()
                hb, khb = hbr.next()
                sg, ksg = sgr.next()
                S.dma(hf[:], self.HF[c * 128:(c + 1) * 128, :], writes=[khf], chan=khf)
                S.dma(hb[:], self.HB[c * 128:(c + 1) * 128, :], writes=[khb], chan=khb)
                S.dma(sg[:], self.SG[c * 128:(c + 1) * 128, :], writes=[ksg], chan=ksg)
                return hf, khf, hb, khb, sg, ksg
            pend = [ld(0), ld(1)] if NCH > 1 else [ld(0)]
            for c in range(NCH):
                hf, khf, hb, khb, sg, ksg = pend.pop(0)
                if c + 2 < NCH:
                    pend.append(ld(c + 2))
                sub = c % 4
                if sub == 0:
                    mx, kmx = mxs.next()
                S.dve(tt(hf[:], hf[:], hb[:], ALU.add), reads=[khf, khb], writes=[khf])
                sq_, ksq = ssq.next()
                jk, kjk = junk.next()
                S.dve(mset(sq_[:], 0.0), writes=[ksq + "a", ksq + "b"])
                for hh in range(4):
                    S.act(actf(jk[:], hf[:, hh * 128:(hh + 1) * 128], AF.Square, accum=sq_[:, hh:hh + 1]), reads=[khf], writes=[kjk, ksq + "a"])
                S.act(actf(sq_[:, 4:8], sq_[:, 0:4], AF.Sqrt, bias=EPS, scale=1.0 / 128), reads=[ksq + "a"], writes=[ksq + "b"])
                S.dve(recip(sq_[:, 4:8], sq_[:, 4:8]), reads=[ksq + "b"], writes=[ksq + "b"])
                S.dve(tt(hf[:].rearrange("p (h e) -> p h e", e=128), hf[:].rearrange("p (h e) -> p h e", e=128),
                         sq_[:, 4:8].unsqueeze(2).to_broadcast([128, 4, 128]), ALU.mult), reads=[khf, ksq + "b"], writes=[khf])
                S.pool(tt(sg[:], sg[:], gain[:], ALU.mult), reads=[ksg, "ogain"], writes=[ksg])
                ym, kym = ymr.next()
                S.dve(tt(ym[:], hf[:], sg[:], ALU.mult), reads=[khf, ksg], writes=[kym])
                ptr, kptr = pT.next()

                def tr4(e, ptr=ptr, ym=ym):
                    ins = None
                    for hh in range(4):
                        ins = e.transpose(ptr[:, hh * 128:(hh + 1) * 128], ym[:, hh * 128:(hh + 1) * 128], self.ident_b)
                    return ins
                S.pe(tr4, reads=[kym, "cmb"], writes=[kptr])
                S.act(actf(mx[:, :, sub * 128:(sub + 1) * 128], ptr[:, 0:512].rearrange("p (h e) -> p h e", e=128), AF.Copy), reads=[kptr], writes=[kmx])
                if sub == 3:
                    t0 = (c // 4) * 512
                    S.dma(self.MIX[512:1024, t0:t0 + 512].rearrange("(h p) t -> p h t", p=128), mx[:], reads=[kmx], writes=["MIXb"], chan=kmx + "s")

    def ret_inproj(self, X, j, L):
        S, NT, nc = self.S, self.NT, self.nc
        with self.phase():
            w = self.sb("wret", [128, 8, 6144], BF16)
            g = self.load_gain(self.w["norm_mix"][L], "gmix")
            self.prep_weight(w, "wret", self.w["ret_w_in"][j], 8, [(i * 2048, 2048, i * 2048, 2048, None) for i in range(3)], gain=g, gkey="gmix")
            xt, sq, rs, hT = self.norm_bufs()
            X3 = self.x3(X)
            cs = self.ring("cs", 1, [128, 4, 512], F32)
            t4 = self.ring("t4", 1, [128, 4, 512], F32)
            o12 = self.ring("o12", 2, [128, 2, 512], BF16)
            vts = self.ring("rv", 1, [128, 4, 2048], BF16)
            gts = self.ring("rg", 1, [128, 2048], F32)
            pf = Ring([(self.ps[i], self.pk[i]) for i in (1, 2, 3)])
            ptm = Ring([(self.ps[4], "ps4"), (self.ps[5], "ps5")])
            def cs_load(jt_):
                c_, kc = cs.next()
                S.dma(c_[:, 0:2, :], self.ropeR[:, :, jt_ * 512:(jt_ + 1) * 512].rearrange("a p t -> p a t"), writes=[kc, kc + "k"], chan=kc)
                return c_, kc
            prep = self.norm_pipeline(X3, xt, sq, rs, hT)
            cur = prep(0)
            cnxt = cs_load(0)
            for jt in range(NT):
                t0 = jt * 512
                bufs, hk = cur
                h_ = bufs["hT"][0]
                c_, kc = cnxt
                S.act(actf(c_[:, 2:4, :], c_[:, 0:2, :], AF.Copy, scale=1.0 / 16), reads=[kc], writes=[kc + "k"])
                for qk in range(2):
                    if qk == 1 and jt + 1 < NT:
                        cur = prep(jt + 1)
                    co = 0 if qk == 0 else 2
                    ck_ = [kc] if qk == 0 else [kc + "k"]
                    dstT = self.RQ if qk == 0 else self.RK
                    for h in range(4):
                        col = qk * 1024 + h * 256
                        pa, kpa = pf.next()
                        S.pe(mmg([(pa[:], w[:, c, col:col + 128], h_[:, c, :]) for c in range(8)]), reads=hk + self.wkeys("wret", col, 128), writes=[kpa])
                        pb, kpb = pf.next()
                        S.pe(mmg([(pb[:], w[:, c, col + 128:col + 256], h_[:, c, :]) for c in range(8)]), reads=hk + self.wkeys("wret", col + 128, 128), writes=[kpb])
                        t_, kt = t4.next()
                        S.dve(tt(t_[:, 0, :], pa[:], c_[:, co, :], ALU.mult), reads=[kpa] + ck_, writes=[kt + "0"])
                        S.dve(tt(t_[:, 1, :], pb[:], c_[:, co + 1, :], ALU.mult), reads=[kpb] + ck_, writes=[kt + "1"])
                        S.dve(tt(t_[:, 2, :], pb[:], c_[:, co, :], ALU.mult), reads=[kpb] + ck_, writes=[kt + "2"])
                        S.dve(tt(t_[:, 3, :], pa[:], c_[:, co + 1, :], ALU.mult), reads=[kpa] + ck_, writes=[kt + "3"])
                        o_, ko = o12.next()
                        S.pool(tt(o_[:, 0, :], t_[:, 0, :], t_[:, 1, :], ALU.subtract), reads=[kt + "0", kt + "1"], writes=[ko])
                        S.pool(tt(o_[:, 1, :], t_[:, 2, :], t_[:, 3, :], ALU.add), reads=[kt + "2", kt + "3"], writes=[ko])
                        S.dma(dstT[2 * h:2 * h + 2, :, t0:t0 + 512].rearrange("a p t -> p a t"), o_[:], reads=[ko], writes=["RQK"], chan=ko + "s")
                if jt + 1 < NT:
                    cnxt = cs_load(jt + 1)
                vt, kv = vts.next()
                for sub in range(4):
                    hs = lambda c: h_[:, c, sub * 128:(sub + 1) * 128]
                    for blk in range(4):
                        p1, k1 = ptm.next()
                        S.pe(mmg([(p1[:], hs(c), w[:, c, 2048 + blk * 512:2048 + (blk + 1) * 512]) for c in range(8)]), reads=hk + self.wkeys("wret", 2048 + blk * 512, 512), writes=[k1])
                        S.act(actf(vt[:, sub, blk * 512:(blk + 1) * 512], p1[:], AF.Copy), reads=[k1], writes=[kv])
                    gt, kg = gts.next()
                    for blk in range(4):
                        p1, k1 = ptm.next()
                        S.pe(mmg([(p1[:], hs(c), w[:, c, 4096 + blk * 512:4096 + (blk + 1) * 512]) for c in range(8)]), reads=hk + self.wkeys("wret", 4096 + blk * 512, 512), writes=[k1])
                        S.act(actf(gt[:, blk * 512:(blk + 1) * 512], p1[:], AF.Silu), reads=[k1], writes=[kg])
                    S.dma(self.RG[t0 + sub * 128:t0 + (sub + 1) * 128, :], gt[:], reads=[kg], writes=["RG"], chan=kg + "s")
                S.dma(self.RV[t0:t0 + 512, :].rearrange("(s p) c -> p s c", p=128), vt[:], reads=[kv], writes=["RV"], chan=kv + "s")

    def retention(self, j):
        S, NT, NCH, T, nc = self.S, self.NT, self.NCH, self.T, self.nc
        with self.phase():
            lg = self.sb("lg", [128, 8], F32)
            tmp = self.sb("lgt", [128, 8], F32)
            S.dma(lg[:], self.w["ret_decay_logit"][j:j + 1].rearrange("a d h -> a (d h)").to_broadcast([128, 8]), writes=["lg"], chan="lg")
            S.act(actf(tmp[:], lg[:], AF.Abs), reads=["lg"], writes=["lgt"])
            S.act(actf(tmp[:], tmp[:], AF.Exp, scale=-1.0), reads=["lgt"], writes=["lgt"])
            S.act(actf(tmp[:], tmp[:], AF.Ln, bias=1.0), reads=["lgt"], writes=["lgt"])
            S.dve(ts(lg[:], lg[:], 0.0, ALU.min), reads=["lg"], writes=["lg"])
            S.dve(tt(lg[:], lg[:], tmp[:], ALU.subtract), reads=["lg", "lgt"], writes=["lg"])
            DT = self.sb("DT", [128, 8, 128], F32)
            QD = self.sb("QD", [128, 8, 128], F32)
            QDb = self.sb("QDb", [128, 8, 128], BF16)
            KD = self.sb("KD", [128, 8], F32)
            CDC = self.sb("CDC", [128, 8, NCH], F32)
            cdv = self.sb("cdv", [128, 8], F32)
            for hd in range(8):
                d = hd // 4
                S.act(actf(DT[:, hd, :], self.cm[:, C_DFW if d == 0 else C_DBW, :], AF.Exp, scale=lg[:, hd:hd + 1]), reads=["lg", "cm"], writes=["DT"])
                S.dve(tt(DT[:, hd, :], DT[:, hd, :], self.cm[:, C_TRIU if d == 0 else C_SL, :], ALU.mult), reads=["DT", "cm"], writes=["DT"])
                S.act(actf(QD[:, hd, :], self.cm[:, C_QDF if d == 0 else C_QDB, :], AF.Exp, scale=lg[:, hd:hd + 1]), reads=["lg", "cm"], writes=["QD"])
                S.dve(cp(QDb[:, hd, :], QD[:, hd, :]), reads=["QD"], writes=["QDb"])
                S.act(actf(KD[:, hd:hd + 1], self.cm[:, C_KD, d:d + 1], AF.Exp, scale=lg[:, hd:hd + 1]), reads=["lg", "cm"], writes=["KD"])
                S.act(actf(cdv[:, hd:hd + 1], self.cm[:, C_KD, 2:3], AF.Exp, scale=lg[:, hd:hd + 1]), reads=["lg", "cm"], writes=["cdv"])
                S.dve(ts(CDC[:, hd, :], self.carry[:, d, :], cdv[:, hd:hd + 1], ALU.mult), reads=["cdv", "carry"], writes=["CDC"])
            qr = [self.ring("rq", 2, [128, 8, 512], BF16) for d in range(2)]
            kr = [self.ring("rk", 2, [128, 8, 512], BF16) for d in range(2)]
            vr = [self.ring("rv", 3, [128, 2048], BF16) for d in range(2)]
            St = [[[self.sb("St", [128, 512], F32) for c in range(2)] for h in range(4)] for d in range(2)]
            Sb = [[[self.sb("Sb", [128, 512], BF16) for c in range(2)] for h in range(4)] for d in range(2)]
            for d in range(2):
                for h in range(4):
                    for c in range(2):
                        S.pool(mset(St[d][h][c][:], 0.0), writes=["S%d%d%d" % (d, h, c)])
            atr = self.ring("at", 6, [128, 128], BF16)
            kwr = self.ring("kw", 6, [128, 2, 128], BF16)
            qdr = self.ring("qd", 6, [128, 2, 128], BF16)
            yst = [self.ring("yst", 2, [128, 2048], F32) for d in range(2)]
            pS = Ring([(self.ps[0][:, 0:128], "ps0"), (self.ps[5][:, 0:128], "ps5")])
            pO = Ring([(self.ps[1], "ps1"), (self.ps[2], "ps2")])
            pU = Ring([(self.ps[3], "ps3"), (self.ps[4], "ps4")])
            pT = Ring([(self.psb[0], "psb0"), (self.psb[1], "psb1")])
            RQ3 = self.RQ.rearrange("a p t -> p a t")
            RK3 = self.RK.rearrange("a p t -> p a t")
            YO = [self.YF, self.YB]
            items = [(k, h, d) for k in range(NCH) for h in range(4) for d in range(2)]
            tctx, cctx, ictx = {}, {}, {}

            def geom(it):
                k, h, d = it
                c = k if d == 0 else NCH - 1 - k
                return d, c // 4, c % 4, h, c, (c // 4) * 512

            def ensure(d, jt):
                if (d, jt) in tctx or jt < 0 or jt >= NT:
                    return
                t0 = jt * 512
                q_, kq = qr[d].next()
                k_, kk = kr[d].next()
                S.dma(q_[:], RQ3[:, :, t0:t0 + 512], writes=[kq], chan=kq)
                S.dma(k_[:], RK3[:, :, t0:t0 + 512], writes=[kk], chan=kk)
                tctx[(d, jt)] = (q_, kq, k_, kk)

            def ensure_v(d, c):
                if (d, c) in cctx or c < 0 or c >= NCH:
                    return
                v_, kv = vr[d].next()
                S.dma(v_[:], self.RV[c * 128:(c + 1) * 128, :], writes=[kv], chan=kv)
                cctx[(d, c)] = (v_, kv) + yst[d].next()

            def stage1(i):
                d, jt, sub, h, c, t0 = geom(items[i])
                if h == 0:
                    ensure(d, jt)
                    if items[i][0] % 4 == 1:
                        ensure(d, jt + (1 if d == 0 else -1))
                    ensure_v(d, c)
                    ensure_v(d, c + (1 if d == 0 else -1))
                q_, kq, k_, kk = tctx[(d, jt)]
                sl = slice(sub * 128, (sub + 1) * 128)
                hd = d * 4 + h
                st_, kst = pS.next()
                S.pe(mmg([(st_, k_[:, 2 * h + cc, sl], q_[:, 2 * h + cc, sl]) for cc in range(2)]), reads=[kk, kq], writes=[kst])
                at, kat = atr.next()
                S.dve(tt(at[:], st_, DT[:, hd, :], ALU.mult), reads=[kst, "DT"], writes=[kat])
                kw, kkw = kwr.next()
                qd, kqd = qdr.next()
                ptr, kptr = pT.next()

                def tr2(e, ptr=ptr, k_=k_, h=h, sl=sl):
                    ins = None
                    for cc in range(2):
                        ins = e.transpose(ptr[:, cc * 128:(cc + 1) * 128], k_[:, 2 * h + cc, sl], self.ident_b)
                    return ins
                S.pe(tr2, reads=[kk, "cmb"], writes=[kptr])
                S.act(actf(kw[:], ptr[:, 0:256].rearrange("p (c e) -> p c e", e=128), AF.Copy, scale=KD[:, hd:hd + 1]), reads=[kptr, "KD"], writes=[kkw])
                S.pool(tt(qd[:], q_[:, 2 * h:2 * h + 2, sl], QDb[:, hd, :].unsqueeze(1).to_broadcast([128, 2, 128]), ALU.mult), reads=[kq, "QDb"], writes=[kqd])
                for cc in range(2):
                    sk = "S%d%d%d" % (d, h, cc)
                    S.act(actf(Sb[d][h][cc][:], St[d][h][cc][:], AF.Copy, scale=self.carry[:, d, c:c + 1]), reads=[sk, "carry"], writes=[sk + "b"])
                ictx[i] = (at, kat, kw, kkw, qd, kqd)

            def stage2(i):
                d, jt, sub, h, c, t0 = geom(items[i])
                v_, kv, ys, kys = cctx[(d, c)]
                at, kat, kw, kkw, qd, kqd = ictx.pop(i)
                hd = d * 4 + h
                vs = v_[:, h * 512:(h + 1) * 512]
                po, kpo = pO.next()
                S.pe(mmg([(po[:], at[:], vs), (po[:], qd[:, 0, :], Sb[d][h][0][:]), (po[:], qd[:, 1, :], Sb[d][h][1][:])]),
                     reads=[kat, kv, kqd, "S%d%d0b" % (d, h), "S%d%d1b" % (d, h)], writes=[kpo])
                for cc in range(2):
                    pu, kpu = pU.next()
                    sk = "S%d%d%d" % (d, h, cc)
                    S.pe(mm1(pu[:], kw[:, cc, :], vs), reads=[kkw, kv], writes=[kpu])
                    S.dve(stt(St[d][h][cc][:], St[d][h][cc][:], CDC[:, hd, c:c + 1], pu[:], ALU.mult, ALU.add), reads=[sk, kpu, "CDC"], writes=[sk])
                S.dve(cp(ys[:, h * 512:(h + 1) * 512], po[:]), reads=[kpo], writes=[kys])
                if h == 3:
                    S.dma(YO[d][c * 128:(c + 1) * 128, :], ys[:], reads=[kys], writes=["YO"], chan=kys + "s")

            self.pipeline(len(items), stage1, stage2, 2)

        with self.phase():
            gain = self.sb("rgain", [128, 2048], F32)
            S.dma(gain[:], self.w["ret_out_norm"][j:j + 1, :].to_broadcast([128, 2048]), writes=["rgain"], chan="rgain")
            yfr = self.ring("yf", 2, [128, 2048], F32)
            ybr = self.ring("yb", 2, [128, 2048], F32)
            rgr = self.ring("rgl", 2, [128, 2048], F32)
            ssq = self.ring("ssq", 3, [128, 8], F32)
            junk = self.ring("junk", 2, [128, 512], F32)
            ymr = self.ring("ym", 2, [128, 2048], BF16)
            mxs = self.ring("mxs", 2, [128, 16, 512], BF16)
            pT = Ring([(self.psb[0], "psb0"), (self.psb[1], "psb1")])

            def ld(c):
                yf, kyf = yfr.next()
                yb, kyb = ybr.next()
                rg, krg = rgr.next()
                S.dma(yf[:], self.YF[c * 128:(c + 1) * 128, :], writes=[kyf], chan=kyf)
                S.dma(yb[:], self.YB[c * 128:(c + 1) * 128, :], writes=[kyb], chan=kyb)
                S.dma(rg[:], self.RG[c * 128:(c + 1) * 128, :], writes=[krg], chan=krg)
                return yf, kyf, yb, kyb, rg, krg
            nxt = ld(0)
            for c in range(NCH):
                yf, kyf, yb, kyb, rg, krg = nxt
                if c + 1 < NCH:
                    nxt = ld(c + 1)
                sub = c % 4
                sl = slice(sub * 128, (sub + 1) * 128)
                if sub == 0:
                    mx, kmx = mxs.next()
                S.pool(tt(yf[:], yf[:], yb[:], ALU.add), reads=[kyf, kyb], writes=[kyf])
                sq_, ksq = ssq.next()
                jk, kjk = junk.next()
                S.dve(mset(sq_[:], 0.0), writes=[ksq + "a", ksq + "b"])
                for hh in range(4):
                    S.act(actf(jk[:], yf[:, hh * 512:(hh + 1) * 512], AF.Square, accum=sq_[:, hh:hh + 1]), reads=[kyf], writes=[kjk, ksq + "a"])
                S.act(actf(sq_[:, 4:8], sq_[:, 0:4], AF.Sqrt, bias=EPS, scale=1.0 / 512), reads=[ksq + "a"], writes=[ksq + "b"])
                S.dve(recip(sq_[:, 4:8], sq_[:, 4:8]), reads=[ksq + "b"], writes=[ksq + "b"])
                S.dve(tt(yf[:].rearrange("p (h e) -> p h e", e=512), yf[:].rearrange("p (h e) -> p h e", e=512),
                         sq_[:, 4:8].unsqueeze(2).to_broadcast([128, 4, 512]), ALU.mult), reads=[kyf, ksq + "b"], writes=[kyf])
                S.pool(tt(rg[:], rg[:], gain[:], ALU.mult), reads=[krg, "rgain"], writes=[krg])
                ym, kym = ymr.next()
                S.dve(tt(ym[:], yf[:], rg[:], ALU.mult), reads=[kyf, krg], writes=[kym])
                for bg in range(2):
                    ptr, kptr = pT.next()

                    def tr8(e, ptr=ptr, ym=ym, bg=bg):
                        ins = None
                        for b_ in range(8):
                            ins = e.transpose(ptr[:, b_ * 128:(b_ + 1) * 128], ym[:, (bg * 8 + b_) * 128:(bg * 8 + b_ + 1) * 128], self.ident_b)
                        return ins
                    S.pe(tr8, reads=[kym, "cmb"], writes=[kptr])
                    S.act(actf(mx[:, bg * 8:(bg + 1) * 8, sl], ptr[:].rearrange("p (b e) -> p b e", e=128), AF.Copy), reads=[kptr], writes=[kmx])
                if sub == 3:
                    t0 = (c // 4) * 512
                    S.dma(self.MIX[0:2048, t0:t0 + 512].rearrange("(b p) t -> p b t", p=128), mx[:], reads=[kmx], writes=["MIXr"], chan=kmx + "s")


def rope_tables(seglen, head_dim, nseg, reps):
    rows = seglen // 64
    row_idx = np.repeat(np.arange(rows, dtype=np.float32), 64)
    col_idx = np.tile(np.arange(64, dtype=np.float32), rows)
    axis_dim = head_dim // 2
    inv_freq = (np.float32(10000.0) ** (-np.arange(0, axis_dim, 2, dtype=np.float32) / np.float32(axis_dim))).astype(np.float32)
    ang = np.concatenate([row_idx[:, None] * inv_freq, col_idx[:, None] * inv_freq], axis=-1).astype(np.float32)
    cs = np.stack([np.cos(ang), np.sin(ang)], 0).astype(np.float32)
    cs = np.tile(cs, (1, nseg, 1))
    cs = cs.transpose(0, 2, 1)
    return np.ascontiguousarray(np.tile(cs, (1, reps, 1)))


def core_tables(T, nseg):
    NT, NCH = T // 512, T // 128
    seglen = T // nseg
    tps = NT // nseg
    cps = NCH // nseg
    seg_t = np.arange(NT) // tps
    mb = np.where(seg_t[:, None] == seg_t[None, :], 0.0, -30000.0).astype(np.float32)
    carry = np.ones((2, NCH), np.float32)
    carry[0, np.arange(NCH) % cps == 0] = 0.0
    carry[1, np.arange(NCH) % cps == cps - 1] = 0.0
    flb = np.ones((NT, 2), np.float32)
    flb[np.arange(NT) % tps == 0, 0] = 0.0
    flb[np.arange(NT) % tps == tps - 1, 1] = 0.0
    rep = lambda a: np.ascontiguousarray(np.broadcast_to(a.reshape(1, -1), (128, a.size))).astype(np.float32)
    return {
        "maskb": rep(mb), "carry": rep(carry), "flb": rep(flb),
        "ropeA": rope_tables(seglen, 64, nseg, 4), "ropeR": rope_tables(seglen, 256, nseg, 1),
    }


DBG = {}
FULL_STEPS = [("ab", 0, 0), ("ffn", 0), ("ret", 0, 1), ("ffn", 1), ("ab", 1, 2), ("ffn", 2), ("ret", 1, 3), ("ffn", 3), ("final",)]
_CACHE = {}


def run_cores(T, steps, core_x, core_nseg, weights):
    key = (T, tuple(steps))
    if key not in _CACHE:
        _CACHE[key] = MK(T, steps).build()
    nc = _CACHE[key]
    cm = host_cmat()
    tabs = {}
    in_maps = []
    for x, ns in zip(core_x, core_nseg):
        if ns not in tabs:
            tabs[ns] = core_tables(T, ns)
        m = {"xT": np.ascontiguousarray(np.asarray(x, np.float32).T), "cmat": cm}
        m.update(tabs[ns])
        for n, _ in MK.W_SPECS:
            m[n] = weights[n]
        in_maps.append(m)
    res = run_bass_kernel_spmd(nc, in_maps, core_ids=list(range(len(in_maps))))
    return [np.ascontiguousarray(r["yT"].T) for r in res.results]


def kernel(x_prompt, x_sample, **weights):
    weights = {k: np.ascontiguousarray(np.asarray(v, np.float32)) for k, v in weights.items()}
    xp = np.asarray(x_prompt, np.float32)
    xs = np.asarray(x_sample, np.float32)
    T = 8192
    core_x = [xp[0], xp[1]] + [xs[4 * i:4 * i + 4].reshape(T, 1024) for i in range(4)]
    nseg = [1, 1, 4, 4, 4, 4]
    core_x += [core_x[5], core_x[5]]
    nseg += [4, 4]
    outs = run_cores(T, FULL_STEPS, core_x, nseg, weights)
    y_prompt = np.stack([outs[0], outs[1]], 0)
    y_sample = np.concatenate([outs[2 + i].reshape(4, 2048, 1024) for i in range(4)], 0)
    return (y_prompt, y_sample)
```

```python
import contextlib
import numpy as np
import concourse.bass as bass
import concourse.mybir as mybir
from concourse.bass_utils import run_bass_kernel_spmd

F32 = mybir.dt.float32
BF16 = mybir.dt.bfloat16
AF = mybir.ActivationFunctionType
ALU = mybir.AluOpType
AX = mybir.AxisListType
EPS = 1e-6


class _Op:
    __slots__ = ("eng", "fn", "waits", "signal", "idx", "chan", "cidx", "sigval")

    def __init__(self, eng, fn, chan):
        self.eng = eng
        self.fn = fn
        self.chan = chan
        self.waits = []
        self.signal = False
        self.idx = -1
        self.cidx = -1
        self.sigval = 0


class Sched:
    ENG = ("pe", "act", "dve", "pool", "sp")

    def __init__(self, nc):
        self.nc = nc
        self.ops = {e: [] for e in self.ENG}
        self.res = {}
        self.waited = {e: {} for e in self.ENG}
        self.chan_last = {}
        self.chan_n = {}
        self.chan_phase = {}
        self.phase_id = 0

    def _wait(self, op, d):
        eng = op.eng
        if d is op:
            return
        if d.chan is not None:
            ek, val = ("c", d.chan), d.cidx
        else:
            if d.eng == eng and eng == "pe":
                return
            ek, val = d.eng, d.idx
        w = self.waited[eng]
        if w.get(ek, -1) >= val:
            return
        w[ek] = val
        op.waits.append(d)
        d.signal = True

    def add(self, eng, fn, reads=(), writes=(), chan=None):
        if chan is not None:
            chan = (self.phase_id, chan)
        op = _Op(eng, fn, chan)
        deps = []
        res = self.res
        for k in reads:
            st = res.get(k)
            if st is not None and st[0] is not None:
                deps.append(st[0])
        for k in writes:
            st = res.get(k)
            if st is not None:
                if st[0] is not None:
                    deps.append(st[0])
                deps.extend(st[1])
        if chan is not None:
            prev = self.chan_last.get(chan)
            if prev is not None:
                deps.append(prev)
            op.cidx = self.chan_n.get(chan, 0)
            if op.cidx == 0:
                self.chan_phase[chan] = self.phase_id
            assert self.chan_phase[chan] == self.phase_id, chan
            self.chan_n[chan] = op.cidx + 1
            self.chan_last[chan] = op
            op.signal = True
        op.idx = len(self.ops[eng])
        for d in deps:
            self._wait(op, d)
        self.ops[eng].append(op)
        for k in writes:
            res[k] = [op, []]
        for k in reads:
            st = res.get(k)
            if st is None:
                res[k] = [None, [op]]
            elif st[0] is not op:
                st[1].append(op)
        return op

    def barrier(self):
        lasts = []
        for e in self.ENG:
            for o in reversed(self.ops[e]):
                if o.fn is not None and o.chan is None:
                    lasts.append(o)
                    break
        lasts.extend(self.chan_last.values())
        for e in self.ENG:
            op = _Op(e, None, None)
            op.idx = len(self.ops[e])
            for d in lasts:
                self._wait(op, d)
            self.ops[e].append(op)
        self.res = {}
        self.phase_id += 1

    def pe(self, fn, reads=(), writes=()):
        return self.add("pe", fn, reads, writes)

    def act(self, fn, reads=(), writes=()):
        return self.add("act", fn, reads, writes)

    def dve(self, fn, reads=(), writes=()):
        return self.add("dve", fn, reads, writes)

    def pool(self, fn, reads=(), writes=()):
        return self.add("pool", fn, reads, writes)

    def dma(self, out, in_, reads=(), writes=(), chan=None, eng="sp", **kw):
        assert chan is not None
        return self.add(eng, lambda e: e.dma_start(out=out, in_=in_, **kw), reads, writes, chan=chan)

    def emit(self, stack):
        nc = self.nc
        esem = {}
        for e in self.ENG:
            if e != "sp":
                esem[e] = stack.enter_context(nc.semaphore("s_" + e))
        csem = {}
        cbase = {}
        pool = []
        by_phase = {}
        for c, ph in self.chan_phase.items():
            by_phase.setdefault(ph, []).append(c)
        nsem = 0
        for ph in sorted(by_phase):
            used = []
            for c in by_phase[ph]:
                if pool:
                    sv = pool.pop()
                else:
                    sv = [stack.enter_context(nc.semaphore("c%d" % nsem)), 0]
                    nsem += 1
                csem[c] = sv[0]
                cbase[c] = sv[1]
                sv[1] += 16 * self.chan_n[c]
                used.append(sv)
            pool.extend(used)
        self.nsem = nsem
        for e in self.ENG:
            cnt = 0
            for op in self.ops[e]:
                if op.chan is not None:
                    op.sigval = cbase[op.chan] + 16 * (op.cidx + 1)
                elif op.signal:
                    cnt += 1
                    op.sigval = cnt

        def run(e, eng):
            for op in self.ops[e]:
                for d in op.waits:
                    sem = csem[d.chan] if d.chan is not None else esem[d.eng]
                    eng.wait_ge(sem, d.sigval)
                if op.fn is None:
                    continue
                ins = op.fn(eng)
                if op.chan is not None:
                    ins.then_inc(csem[op.chan], 16)
                elif op.signal:
                    ins.then_inc(esem[e], 1)

        with nc.Block() as block:
            @block.sync
            def _(eng):
                run("sp", eng)

            @block.tensor
            def _(eng):
                run("pe", eng)

            @block.scalar
            def _(eng):
                run("act", eng)

            @block.vector
            def _(eng):
                run("dve", eng)

            @block.gpsimd
            def _(eng):
                run("pool", eng)


def mmg(items):
    n = len(items)

    def f(e):
        ins = None
        for i, (o, l, r) in enumerate(items):
            ins = e.matmul(o, lhsT=l, rhs=r, start=(i == 0), stop=(i == n - 1))
        return ins
    return f


def mm1(o, l, r):
    return lambda e: e.matmul(o, lhsT=l, rhs=r, start=True, stop=True)


def tr(o, i, ident):
    return lambda e: e.transpose(o, i, ident)


def actf(o, i, func, bias=None, scale=None, accum=None):
    kw = {}
    if bias is not None:
        kw["bias"] = bias
    if scale is not None:
        kw["scale"] = scale
    if accum is not None:
        kw["accum_out"] = accum
    return lambda e: e.activation(out=o, in_=i, func=func, **kw)


def tt(o, a, b, op):
    return lambda e: e.tensor_tensor(out=o, in0=a, in1=b, op=op)


def ts(o, a, s1, op0, s2=None, op1=None):
    if op1 is None:
        return lambda e: e.tensor_scalar(out=o, in0=a, scalar1=s1, scalar2=None, op0=op0)
    return lambda e: e.tensor_scalar(out=o, in0=a, scalar1=s1, scalar2=s2, op0=op0, op1=op1)


def stt(o, a, s, b, op0, op1):
    return lambda e: e.scalar_tensor_tensor(out=o, in0=a, scalar=s, in1=b, op0=op0, op1=op1)


def cp(o, i):
    return lambda e: e.tensor_copy(out=o, in_=i)


def recip(o, i):
    return lambda e: e.reciprocal(out=o, in_=i)


def mset(o, v):
    return lambda e: e.memset(o, v)


C_ID, C_ONES, C_BD32, C_TRIU, C_TRIL, C_SL, C_DFW, C_DBW, C_QDF, C_QDB, C_SEL, C_KD = range(12)
NCM = 12


def host_cmat():
    s = np.arange(128)[:, None].astype(np.float32)
    l = np.arange(128)[None, :].astype(np.float32)
    m = np.zeros((NCM, 128, 128), np.float32)
    m[C_ID] = np.eye(128)
    m[C_ONES] = 1.0
    m[C_BD32] = (np.arange(128)[:, None] // 32 == np.arange(128)[None, :] // 32)
    m[C_TRIU] = (s <= l)
    m[C_TRIL] = (s >= l)
    m[C_SL] = (s > l)
    m[C_DFW] = np.maximum(l - s, 0)
    m[C_DBW] = np.maximum(s - l, 0)
    m[C_QDF] = np.broadcast_to(l + 1.0, (128, 128))
    m[C_QDB] = np.broadcast_to(128.0 - l, (128, 128))
    m[C_SEL][64, :] = 1.0
    m[C_KD][:, 0] = 127.0 - np.arange(128)
    m[C_KD][:, 1] = np.arange(128)
    m[C_KD][:, 2] = 128.0
    return np.ascontiguousarray(m.transpose(1, 0, 2))


class Ring:
    def __init__(self, items):
        self.items = items
        self.i = 0

    def next(self):
        it = self.items[self.i % len(self.items)]
        self.i += 1
        return it


class MK:
    W_SPECS = [
        ("norm_mix", (4, 1024)), ("norm_ffn", (4, 1024)), ("norm_final", (1024,)),
        ("ab_w_in", (2, 1024, 2832)), ("ab_gate_bias", (2, 16)), ("attn_q_norm", (2, 64)),
        ("attn_k_norm", (2, 64)), ("mlstm_out_norm", (2, 512)), ("ab_w_out", (2, 1024, 1024)),
        ("ret_w_in", (2, 1024, 6144)), ("ret_decay_logit", (2, 2, 4)), ("ret_out_norm", (2, 2048)),
        ("ret_w_out", (2, 2048, 1024)), ("ffn_w_up", (4, 1024, 5632)), ("ffn_conv_w", (4, 3, 2816)),
        ("ffn_conv_b", (4, 2816)), ("ffn_w_down", (4, 2816, 1024)),
    ]

    def __init__(self, T, steps):
        self.T = T
        self.NT = T // 512
        self.NCH = T // 128
        self.steps = steps
        self.nc = nc = bass.Bass("TRN2", target_bir_lowering=False)
        self.S = Sched(nc)
        self._uid = 0
        NT, NCH = self.NT, self.NCH
        di = lambda n, s, dt=F32: nc.dram_tensor(n, list(s), dt, kind="ExternalInput").ap()
        ds = lambda n, s, dt: nc.dram_tensor(n, list(s), dt, kind="Internal").ap()
        self.xT = di("xT", (1024, T))
        self.w = {n: di(n, s) for n, s in self.W_SPECS}
        self.cmat_d = di("cmat", (128, NCM, 128))
        self.ropeA = di("ropeA", (2, 128, T))
        self.ropeR = di("ropeR", (2, 128, T))
        self.mb_d = di("maskb", (128, NT * NT))
        self.carry_d = di("carry", (128, 2 * NCH))
        self.flb_d = di("flb", (128, NT * 2))
        self.yT = nc.dram_tensor("yT", [1024, T], F32, kind="ExternalOutput").ap()
        self.XM = ds("XM", (1024, T), F32)
        self.XR = ds("XR", (1024, T), F32)
        self.ACTS = ds("ACTS", (2816, T), BF16)
        self.MIX = ds("MIX", (2048, T), BF16)
        self.QA = ds("QA", (8, 64, T), BF16)
        self.KA = ds("KA", (128, T), BF16)
        self.VA = ds("VA", (T, 130), BF16)
        self.MQ = ds("MQ", (4, 128, T), BF16)
        self.MKs = ds("MKs", (4, 128, T), BF16)
        self.MV = ds("MV", (T, 516), BF16)
        self.SG = ds("SG", (T, 512), F32)
        self.GT = ds("GT", (T, 16), F32)
        self.HF = ds("HF", (T, 512), F32)
        self.HB = ds("HB", (T, 512), F32)
        self.YB = ds("YB", (T, 2048), F32)
        self.RQ = ds("RQ", (8, 128, T), BF16)
        self.RK = ds("RK", (8, 128, T), BF16)
        self.RV = ds("RV", (T, 2048), BF16)
        self.RG = ds("RG", (T, 2048), F32)
        self.YF = ds("YF", (T, 2048), F32)

    def un(self, n):
        self._uid += 1
        return "%s_%d" % (n, self._uid)

    def sb(self, name, shape, dt):
        return self._ph.enter_context(self.nc.sbuf_tensor(self.un(name), list(shape), dt))

    @contextlib.contextmanager
    def phase(self, psum=True):
        self.S.barrier()
        with contextlib.ExitStack() as ph:
            self._ph = ph
            if psum:
                nc = self.nc
                self.ps = [ph.enter_context(nc.psum_tensor(self.un("ps%d" % i), [128, 512], F32)) for i in range(6)]
                self.psb = [ph.enter_context(nc.psum_tensor(self.un("psb%d" % i), [128, 1024], BF16)) for i in range(2)]
            yield
        self._ph = self._gl

    def ring(self, name, n, shape, dt):
        items = []
        for i in range(n):
            t = self.sb(name, shape, dt)
            items.append((t, self.un(name)))
        return Ring(items)

    def pipeline(self, n, stage1, stage2, la):
        for i in range(n + la):
            if i < n:
                stage1(i)
            if i >= la:
                stage2(i - la)

    def rr(self, engs):
        self._rr = getattr(self, "_rr", 0) + 1
        return engs[self._rr % len(engs)]

    def build(self):
        nc, S = self.nc, self.S
        with contextlib.ExitStack() as gl:
            self._gl = gl
            self._ph = gl
            self.pk = ["ps%d" % i for i in range(6)]
            self.cm = self.sb("cm", [128, NCM, 128], F32)
            self.cmb = self.sb("cmb", [128, 3, 128], BF16)
            S.dma(self.cm[:], self.cmat_d, writes=["cm"], chan="cm")
            S.dve(cp(self.cmb[:], self.cm[:, 0:3, :]), reads=["cm"], writes=["cmb"])
            self.ident_f = self.cm[:, C_ID, :]
            self.ones_f = self.cm[:, C_ONES, :]
            self.ident_b = self.cmb[:, C_ID, :]
            self.ones_b = self.cmb[:, C_ONES, :]
            self.bd32_b = self.cmb[:, C_BD32, :]
            self.carry = self.sb("carry", [128, 2, self.NCH], F32)
            S.dma(self.carry[:], self.carry_d.rearrange("p (d c) -> p d c", d=2), writes=["carry"], chan="carry")
            self.mb = self.sb("mb", [128, self.NT, self.NT], F32)
            S.dma(self.mb[:], self.mb_d.rearrange("p (a b) -> p a b", a=self.NT), writes=["mb"], chan="mb")
            self.flb = self.sb("flb", [128, self.NT * 2], F32)
            S.dma(self.flb[:], self.flb_d, writes=["flb"], chan="flb")
            cur = self.xT
            for st in self.steps:
                kind = st[0]
                if kind == "ab":
                    j, L = st[1], st[2]
                    self.ab_inproj(cur, j, L)
                    self.attention()
                    self.mlstm(j)
                    self.proj_resid(self.MIX, 8, self.w["ab_w_out"][j], cur, self.XM)
                    cur = self.XM
                elif kind == "ret":
                    j, L = st[1], st[2]
                    self.ret_inproj(cur, j, L)
                    if not DBG.get("noscan"):
                        self.retention(j)
                    self.proj_resid(self.MIX, 16, self.w["ret_w_out"][j], cur, self.XM)
                    cur = self.XM
                elif kind == "ffn":
                    L = st[1]
                    self.ffn_up(cur, L)
                    self.proj_resid(self.ACTS, 22, self.w["ffn_w_down"][L], cur, self.XR)
                    cur = self.XR
                elif kind == "final":
                    self.final_norm(cur)
                    cur = None
                elif kind == "copy":
                    self.copy_out(cur)
                    cur = None
            S.barrier()
            S.emit(gl)
        return nc

    def x3(self, X):
        return X.rearrange("(c p) t -> p c t", p=128)

    def load_gain(self, vec_ap, name):
        g = self.sb(name, [128, 8], F32)
        self.S.dma(g[:], vec_ap.rearrange("(c p) -> p c", p=128), writes=[name], chan=name, allow_slow_non_contiguous=True)
        return g

    def wkeys(self, dkey, col0, n, bw=1024):
        return ["%s#%d" % (dkey, b_) for b_ in range(col0 // bw, (col0 + n - 1) // bw + 1)]

    def prep_weight(self, dst, dkey, src, KC, blocks, gain=None, gkey=None, bw=1024, order=None):
        S = self.S
        stg = self.ring("wstg", 2, [128, 1024], F32)
        pw = min(bw, 1024)
        pieces = []
        for (d0, nd, s0, ns, vf) in blocks:
            o = 0
            while o < ns:
                n_ = min(pw - (d0 + o) % pw, ns - o)
                pieces.append((d0 + o, n_, s0 + o))
                o += n_
        if order is not None:
            pieces.sort(key=lambda p: (order.index(p[0] // bw) if (p[0] // bw) in order else 999, p[0]))
        for (d0, n_, s0) in pieces:
            bk = "%s#%d" % (dkey, d0 // bw)
            for c in range(KC):
                t, k = stg.next()
                S.dma(t[:, 0:n_], src[c * 128:(c + 1) * 128, s0:s0 + n_], writes=[k], chan=k)
                iv = t[:, 0:n_]
                ov = dst[:, c, d0:d0 + n_]
                eng = self.rr(["dve", "act", "dve", "act", "pool"])
                rd = [k] + ([gkey] if gain is not None else [])
                if gain is None:
                    if eng == "act":
                        S.act(actf(ov, iv, AF.Copy), reads=rd, writes=[bk])
                    else:
                        S.add(eng, cp(ov, iv), reads=rd, writes=[bk])
                else:
                    if eng == "act":
                        S.act(actf(ov, iv, AF.Copy, scale=gain[:, c:c + 1]), reads=rd, writes=[bk])
                    else:
                        S.add(eng, ts(ov, iv, gain[:, c:c + 1], ALU.mult), reads=rd, writes=[bk])

    def norm_load(self, src3, t0, n, ent):
        xt, kx = ent
        self.S.dma(xt[:, :, 0:n], src3[:, :, t0:t0 + n], writes=[kx], chan=kx)

    def norm_tile(self, src3, t0, n, bufs, D_feat=1024, load=True, part=0):
        S = self.S
        xt, kx = bufs["xt"]
        sq, ksq = bufs["sq"]
        rs, krs = bufs["rs"]
        hT, kh = bufs["hT"]
        ssp, kss = bufs["ss"]
        if load:
            S.dma(xt[:, :, 0:n], src3[:, :, t0:t0 + n], writes=[kx], chan=kx)
        if part in (0, 1):
            S.act(actf(sq[:, :, 0:n], xt[:, :, 0:n], AF.Square), reads=[kx], writes=[ksq])
        if part == 1:
            return None
        S.pe(mmg([(ssp[:, 0:n], self.ones_b, sq[:, c, 0:n]) for c in range(8)]), reads=[ksq, "cmb"], writes=[kss])
        S.act(actf(rs[:, 0:n], ssp[:, 0:n], AF.Sqrt, bias=EPS, scale=1.0 / D_feat), reads=[kss], writes=[krs])
        S.dve(recip(rs[:, 0:n], rs[:, 0:n]), reads=[krs], writes=[krs])
        S.dve(tt(hT[:, 0:4, 0:n], xt[:, 0:4, 0:n], rs[:, 0:n].unsqueeze(1).to_broadcast([128, 4, n]), ALU.mult),
              reads=[kx, krs], writes=[kh + "a"])
        S.pool(tt(hT[:, 4:8, 0:n], xt[:, 4:8, 0:n], rs[:, 0:n].unsqueeze(1).to_broadcast([128, 4, n]), ALU.mult),
               reads=[kx, krs], writes=[kh + "b"])
        return [kh + "a", kh + "b"]

    def norm_pipeline(self, X3, xt, sq, rs, hT):
        NT = self.NT
        st = {"nxt": xt.next()}
        self.norm_load(X3, 0, 512, st["nxt"])

        def prep(j, part=0):
            if part in (0, 1):
                st["bufs"] = {"xt": st["nxt"], "sq": sq.next(), "rs": rs.next(), "hT": hT.next(), "ss": (self.ps[0], "ps0")}
            bufs = st["bufs"]
            hk = self.norm_tile(X3, j * 512, 512, bufs, load=False, part=part)
            if part == 1:
                return None
            if j + 1 < NT:
                st["nxt"] = xt.next()
                self.norm_load(X3, (j + 1) * 512, 512, st["nxt"])
            return bufs, hk
        return prep

    def norm_bufs(self, nbuf=1, n=512):
        xt = self.ring("xt", nbuf, [128, 8, n], F32)
        sq = self.ring("sq", 1, [128, 8, n], BF16)
        rs = self.ring("rs", 2, [128, n], F32)
        hT = self.ring("hT", 2, [128, 8, n], BF16)
        return xt, sq, rs, hT

    def proj_resid(self, A, KC, w_d, Xin, Xout):
        S, NT = self.S, self.NT
        with self.phase():
            w = self.sb("wpr", [128, KC, 1024], BF16)
            self.prep_weight(w, "wpr", w_d, KC, [(0, 1024, 0, 1024, None)], bw=256)
            ar = self.ring("a", 2, [128, KC, 512], BF16)
            xr = self.ring("x", 2, [128, 8, 512], F32)
            A3 = A[0:KC * 128, :].rearrange("(c p) t -> p c t", p=128)
            Xi3, Xo3 = self.x3(Xin), self.x3(Xout)
            psr = Ring([(self.ps[i], self.pk[i]) for i in range(4)])
            def pr_load(j):
                at, ka = ar.next()
                xt, kx = xr.next()
                S.dma(at[:], A3[:, :, j * 512:(j + 1) * 512], writes=[ka], chan=ka)
                S.dma(xt[:], Xi3[:, :, j * 512:(j + 1) * 512], writes=[kx], chan=kx)
                return at, ka, xt, kx
            nxt = pr_load(0)
            for j in range(NT):
                t0 = j * 512
                at, ka, xt, kx = nxt
                if j + 1 < NT:
                    nxt = pr_load(j + 1)
                for d in range(8):
                    p, kp = psr.next()
                    S.pe(mmg([(p[:], w[:, c, d * 128:(d + 1) * 128], at[:, c, :]) for c in range(KC)]),
                         reads=[ka] + self.wkeys("wpr", d * 128, 128, 256), writes=[kp])
                    S.dve(tt(xt[:, d, :], p[:], xt[:, d, :], ALU.add), reads=[kp, kx], writes=[kx])
                S.dma(Xo3[:, :, t0:t0 + 512], xt[:], reads=[kx], writes=["Xo"], chan=kx + "s")

    def final_norm(self, X):
        S, NT = self.S, self.NT
        with self.phase():
            g = self.load_gain(self.w["norm_final"], "gfin")
            xt, sq, rs, hT = self.norm_bufs(nbuf=2)
            yr = self.ring("y", 2, [128, 8, 512], F32)
            X3, Y3 = self.x3(X), self.x3(self.yT)
            nxt = xt.next()
            self.norm_load(X3, 0, 512, nxt)
            for j in range(NT):
                t0 = j * 512
                x_, kx = nxt
                if j + 1 < NT:
                    nxt = xt.next()
                    self.norm_load(X3, t0 + 512, 512, nxt)
                q_, kq = sq.next()
                r_, kr = rs.next()
                y_, ky = yr.next()
                S.act(actf(q_[:], x_[:], AF.Square), reads=[kx], writes=[kq])
                S.pe(mmg([(self.ps[0][:], self.ones_b, q_[:, c, :]) for c in range(8)]), reads=[kq, "cmb"], writes=["ps0"])
                S.act(actf(r_[:], self.ps[0][:], AF.Sqrt, bias=EPS, scale=1.0 / 1024), reads=["ps0"], writes=[kr])
                S.dve(recip(r_[:], r_[:]), reads=[kr], writes=[kr])
                for c in range(8):
                    S.dve(stt(y_[:, c, :], x_[:, c, :], g[:, c:c + 1], r_[:], ALU.mult, ALU.mult),
                          reads=[kx, kr, "gfin"], writes=[ky])
                S.dma(Y3[:, :, t0:t0 + 512], y_[:], reads=[ky], writes=["Y"], chan=ky + "s")

    def copy_out(self, X):
        S, NT = self.S, self.NT
        with self.phase():
            xr = self.ring("x", 2, [128, 8, 512], F32)
            X3, Y3 = self.x3(X), self.x3(self.yT)
            for j in range(NT):
                x_, kx = xr.next()
                S.dma(x_[:], X3[:, :, j * 512:(j + 1) * 512], writes=[kx], chan=kx)
                S.dma(Y3[:, :, j * 512:(j + 1) * 512], x_[:], reads=[kx], writes=["Y"], chan=kx + "s")

    def ffn_up(self, X, L):
        S, NT = self.S, self.NT
        nc = self.nc
        NB = 2 * (NT - 1)
        with self.phase():
            w = self.sb("wup", [128, 8, 5632], BF16)
            g = self.load_gain(self.w["norm_ffn"][L], "gffn")
            cw = self.sb("cw", [128, 22, 3], F32)
            cb = self.sb("cb", [128, 22], F32)
            for k3 in range(3):
                S.dma(cw[:, :, k3], self.w["ffn_conv_w"][L, k3].rearrange("(f p) -> p f", p=128), writes=["cw"], chan="cw", allow_slow_non_contiguous=True)
            S.dma(cb[:], self.w["ffn_conv_b"][L].rearrange("(f p) -> p f", p=128), writes=["cb"], chan="cb", allow_slow_non_contiguous=True)
            self.prep_weight(w, "wup", self.w["ffn_w_up"][L], 8,
                             [(i * 2048, min(2048, 5632 - i * 2048), i * 2048, min(2048, 5632 - i * 2048), None) for i in range(3)],
                             gain=g, gkey="gffn", order=[0, 2, 3, 1, 4, 5])
            xt, sq, rs, hT = self.norm_bufs()
            X3 = self.x3(X)
            gh = self.sb("gh", [128, 22, NT, 2], F32)
            S.pool(mset(gh[:], 0.0), writes=["gh"])
            if NB > 0:
                xb = self.sb("xb", [128, 8, NT - 1, 2], F32)
                sqb = self.sb("sqb", [128, 8, NB], BF16)
                rsb = self.sb("rsb", [128, NB], F32)
                hb = self.sb("hb", [128, 8, NB], BF16)
                for c in range(8):
                    src = X3[:, c, 511:511 + 512 * (NT - 1)].rearrange("p (b r) -> p b r", r=512)[:, :, 0:2]
                    S.dma(xb[:, c, :, :], src, writes=["xb%d" % c], chan="xb", allow_slow_non_contiguous=True)
                xbf = xb[:].rearrange("p c b e -> p c (b e)")
                xk = ["xb%d" % c for c in range(8)]
                S.act(actf(sqb[:], xbf, AF.Square), reads=xk, writes=["sqb"])
                S.pe(mmg([(self.ps[0][:, 0:NB], self.ones_b, sqb[:, c, :]) for c in range(8)]), reads=["sqb", "cmb"], writes=["ps0"])
                S.act(actf(rsb[:], self.ps[0][:, 0:NB], AF.Sqrt, bias=EPS, scale=1.0 / 1024), reads=["ps0"], writes=["rsb"])
                S.dve(recip(rsb[:], rsb[:]), reads=["rsb"], writes=["rsb"])
                S.dve(tt(hb[:], xbf, rsb[:].unsqueeze(1).to_broadcast([128, 8, NB]), ALU.mult), reads=xk + ["rsb"], writes=["hb"])
                pr = Ring([(self.ps[1], "ps1"), (self.ps[2], "ps2")])
                for f in range(22):
                    p, kp = pr.next()
                    S.pe(mmg([(p[:, 0:NB], w[:, c, 2816 + f * 128:2816 + (f + 1) * 128], hb[:, c, :]) for c in range(8)]),
                         reads=["hb"] + self.wkeys("wup", 2816 + f * 128, 128), writes=[kp])
                    pv = p[:, 0:NB].rearrange("p (b e) -> p b e", e=2)
                    S.dve(cp(gh[:, f, 1:NT, 0], pv[:, :, 0]), reads=[kp], writes=["gh"])
                    S.dve(cp(gh[:, f, 0:NT - 1, 1], pv[:, :, 1]), reads=[kp], writes=["gh"])
            ghf = gh[:].rearrange("p f j e -> p f (j e)")
            S.dve(tt(ghf, ghf, self.flb[:].unsqueeze(1).to_broadcast([128, 22, NT * 2]), ALU.mult), reads=["gh", "flb"], writes=["gh"])
            actr = self.ring("act", 2, [128, 22, 512], BF16)
            gbr = self.ring("gb", 2, [128, 514], F32)
            tr_ = self.ring("tc", 2, [128, 512], F32)
            pu = Ring([(self.ps[1], "ps1"), (self.ps[2], "ps2"), (self.ps[5], "ps5")])
            pg = Ring([(self.ps[3], "ps3"), (self.ps[4], "ps4")])
            A3 = self.ACTS.rearrange("(f p) t -> p f t", p=128)
            prep = self.norm_pipeline(X3, xt, sq, rs, hT)
            cur = prep(0)
            for j in range(NT):
                t0 = j * 512
                bufs, hk = cur
                h_ = bufs["hT"][0]
                a_, ka = actr.next()
                for f in range(22):
                    if f == 1 and j + 1 < NT:
                        prep(j + 1, 1)
                    if f == 6 and j + 1 < NT:
                        cur = prep(j + 1, 2)
                    u, ku = pu.next()
                    gp, kg = pg.next()
                    S.pe(mmg([(u[:], w[:, c, f * 128:(f + 1) * 128], h_[:, c, :]) for c in range(8)]), reads=hk + self.wkeys("wup", f * 128, 128), writes=[ku])
                    S.pe(mmg([(gp[:], w[:, c, 2816 + f * 128:2816 + (f + 1) * 128], h_[:, c, :]) for c in range(8)]), reads=hk + self.wkeys("wup", 2816 + f * 128, 128), writes=[kg])
                    gb, kb = gbr.next()
                    tc, kt = tr_.next()
                    S.act(actf(gb[:, 1:513], gp[:], AF.Copy), reads=[kg], writes=[kb + "m"])
                    S.dve(cp(gb[:, 0:514:513], gh[:, f, j, :]), reads=["gh"], writes=[kb + "h"])
                    S.act(actf(tc[:], gp[:], AF.Identity, bias=cb[:, f:f + 1], scale=cw[:, f, 1:2]), reads=[kg, "cw", "cb"], writes=[kt])
                    S.dve(stt(tc[:], gb[:, 0:512], cw[:, f, 0:1], tc[:], ALU.mult, ALU.add), reads=[kb + "m", kb + "h", kt, "cw"], writes=[kt])
                    S.dve(stt(tc[:], gb[:, 2:514], cw[:, f, 2:3], tc[:], ALU.mult, ALU.add), reads=[kb + "m", kb + "h", kt, "cw"], writes=[kt])
                    S.act(actf(tc[:], tc[:], AF.Gelu), reads=[kt], writes=[kt])
                    S.dve(tt(a_[:, f, :], tc[:], u[:], ALU.mult), reads=[kt, ku], writes=[ka])
                S.dma(A3[:, :, t0:t0 + 512], a_[:], reads=[ka], writes=["ACTS"], chan=ka + "s")

    def ab_inproj(self, X, j, L):
        S, NT, nc = self.S, self.NT, self.nc
        with self.phase():
            w = self.sb("wab", [128, 8, 2832], BF16)
            g = self.load_gain(self.w["norm_mix"][L], "gmix")
            stg = self.ring("wstg", 2, [128, 2832], F32)
            src = self.w["ab_w_in"][j]

            def hsplit(ap, nh, half):
                return ap.rearrange("p (h d) -> p h d", d=64)[:, :, half * 32:(half + 1) * 32]

            def h32(ap, nh):
                return ap.rearrange("p (h d) -> p h d", d=32)

            for c in range(8):
                t, k = stg.next()
                S.dma(t[:], src[c * 128:(c + 1) * 128, :], writes=[k], chan=k)
                gc = g[:, c:c + 1]
                engs = ["dve", "pool"]
                for gi in range(2):
                    for half in range(2):
                        S.add(self.rr(engs), ts(h32(w[:, c, gi * 256 + half * 128: gi * 256 + half * 128 + 128], 4),
                                                hsplit(t[:, gi * 256:(gi + 1) * 256], 4, half), gc, ALU.mult),
                              reads=[k, "gmix"], writes=["wab"])
                for half in range(2):
                    S.add(self.rr(engs), ts(h32(w[:, c, 512 + half * 64: 512 + half * 64 + 64], 2),
                                            hsplit(t[:, 512:640], 2, half), gc, ALU.mult), reads=[k, "gmix"], writes=["wab"])
                for (d0, s0, n) in [(1664, 640, 128), (640, 768, 1024), (1808, 1792, 1024), (1792, 2816, 16)]:
                    S.add(self.rr(engs), ts(w[:, c, d0:d0 + n], t[:, s0:s0 + n], gc, ALU.mult), reads=[k, "gmix"], writes=["wab"])
            gq = self.sb("gq", [128, 2], F32)
            gk = self.sb("gk", [128, 2], F32)
            for r in range(4):
                for half in range(2):
                    S.dma(gq[r * 32:(r + 1) * 32, half:half + 1], self.w["attn_q_norm"][j, half * 32:(half + 1) * 32].unsqueeze(1),
                          writes=["gq%d%d" % (r, half)], chan="gqld", allow_slow_non_contiguous=True)
                    S.dma(gk[r * 32:(r + 1) * 32, half:half + 1], self.w["attn_k_norm"][j, half * 32:(half + 1) * 32].unsqueeze(1),
                          writes=["gk%d%d" % (r, half)], chan="gkld", allow_slow_non_contiguous=True)
            gqk = ["gq%d%d" % (r, h) for r in range(4) for h in range(2)]
            gkk = ["gk%d%d" % (r, h) for r in range(4) for h in range(2)]
            S.dve(ts(gq[:], gq[:], 0.125, ALU.mult), reads=gqk, writes=["gq"])
            gbias = self.sb("gbias", [128, 16], F32)
            S.dma(gbias[:], self.w["ab_gate_bias"][j:j + 1, :].to_broadcast([128, 16]), writes=["gbias"], chan="gbias")
            xt, sq, rs, hT = self.norm_bufs()
            X3 = self.x3(X)
            cs = self.ring("cs", 2, [128, 2, 512], F32)
            sqa = self.ring("sqa", 2, [128, 2, 512], BF16)
            rq = self.ring("rq", 2, [128, 512], F32)
            an = self.ring("an", 2, [128, 2, 512], F32)
            t4 = self.ring("t4", 1, [128, 4, 512], F32)
            o12 = self.ring("o12", 3, [128, 2, 512], BF16)
            mqs = self.ring("mqs", 1, [128, 4, 512], BF16)
            mks = self.ring("mks", 1, [128, 4, 512], BF16)
            vts = self.ring("vt", 2, [128, 4, 2, 65], BF16)
            mvs = self.ring("mvt", 2, [128, 4, 4, 129], BF16)
            sgs = self.ring("sg", 1, [128, 4, 512], F32)
            gts = self.ring("gt", 2, [128, 4, 16], F32)
            for (t, k) in vts.items:
                S.pool(mset(t[:], 1.0), writes=[k])
            for (t, k) in mvs.items:
                S.pool(mset(t[:], 1.0), writes=[k])
            pf = Ring([(self.ps[i], self.pk[i]) for i in (1, 2, 3)])
            ptm = Ring([(self.ps[4], "ps4"), (self.ps[5], "ps5")])
            def cs_load(jt_):
                c_, kc = cs.next()
                S.dma(c_[:], self.ropeA[:, :, jt_ * 512:(jt_ + 1) * 512].rearrange("a p t -> p a t"), writes=[kc], chan=kc)
                return c_, kc
            prep = self.norm_pipeline(X3, xt, sq, rs, hT)
            cur = prep(0)
            cnxt = cs_load(0)
            for jt in range(NT):
                t0 = jt * 512
                bufs, hk = cur
                h_ = bufs["hT"][0]
                c_, kc = cnxt
                if jt + 1 < NT:
                    cnxt = cs_load(jt + 1)

                def fm(col0, M):
                    p, kp = pf.next()
                    S.pe(mmg([(p[0:M, :], w[:, c, col0:col0 + M], h_[:, c, :]) for c in range(8)]), reads=hk + ["wab"], writes=[kp])
                    return p, kp

                for grp in range(3):
                    if grp == 1 and jt + 1 < NT:
                        prep(jt + 1, 1)
                    M = 128 if grp < 2 else 64
                    colA = grp * 256 if grp < 2 else 512
                    colB = colA + M
                    gg, ggk = (gq, ["gq"]) if grp < 2 else (gk, gkk)
                    pa, kpa = fm(colA, M)
                    pb, kpb = fm(colB, M)
                    s_, ks = sqa.next()
                    S.act(actf(s_[0:M, 0, :], pa[0:M, :], AF.Square), reads=[kpa], writes=[ks + "a"])
                    S.act(actf(s_[0:M, 1, :], pb[0:M, :], AF.Square), reads=[kpb], writes=[ks + "b"])
                    S.pe(mmg([(self.ps[0][0:M, :], self.bd32_b[0:M, 0:M], s_[0:M, 0, :]),
                              (self.ps[0][0:M, :], self.bd32_b[0:M, 0:M], s_[0:M, 1, :])]), reads=[ks + "a", ks + "b", "cmb"], writes=["ps0"])
                    r_, kr = rq.next()
                    S.act(actf(r_[0:M, :], self.ps[0][0:M, :], AF.Sqrt, bias=EPS, scale=1.0 / 64), reads=["ps0"], writes=[kr])
                    S.dve(recip(r_[0:M, :], r_[0:M, :]), reads=[kr], writes=[kr])
                    a_, kan = an.next()
                    S.dve(stt(a_[0:M, 0, :], pa[0:M, :], gg[0:M, 0:1], r_[0:M, :], ALU.mult, ALU.mult), reads=[kpa, kr] + ggk, writes=[kan + "a"])
                    S.dve(stt(a_[0:M, 1, :], pb[0:M, :], gg[0:M, 1:2], r_[0:M, :], ALU.mult, ALU.mult), reads=[kpb, kr] + ggk, writes=[kan + "b"])
                    t_, kt = t4.next()
                    S.pool(tt(t_[0:M, 0, :], a_[0:M, 0, :], c_[0:M, 0, :], ALU.mult), reads=[kan + "a", kc], writes=[kt + "0"])
                    S.pool(tt(t_[0:M, 1, :], a_[0:M, 1, :], c_[0:M, 1, :], ALU.mult), reads=[kan + "b", kc], writes=[kt + "1"])
                    S.dve(tt(t_[0:M, 2, :], a_[0:M, 1, :], c_[0:M, 0, :], ALU.mult), reads=[kan + "b", kc], writes=[kt + "2"])
                    S.pool(tt(t_[0:M, 3, :], a_[0:M, 0, :], c_[0:M, 1, :], ALU.mult), reads=[kan + "a", kc], writes=[kt + "3"])
                    o_, ko = o12.next()
                    S.pool(tt(o_[0:M, 0, :], t_[0:M, 0, :], t_[0:M, 1, :], ALU.subtract), reads=[kt + "0", kt + "1"], writes=[ko + "0"])
                    S.dve(tt(o_[0:M, 1, :], t_[0:M, 2, :], t_[0:M, 3, :], ALU.add), reads=[kt + "2", kt + "3"], writes=[ko + "1"])
                    for hl in range(M // 32):
                        for half in range(2):
                            if grp < 2:
                                dst = self.QA[grp * 4 + hl, half * 32:(half + 1) * 32, t0:t0 + 512]
                            else:
                                dst = self.KA[hl * 64 + half * 32: hl * 64 + (half + 1) * 32, t0:t0 + 512]
                            S.dma(dst, o_[hl * 32:(hl + 1) * 32, half, :], reads=[ko + str(half)], writes=["QK"], chan=ko + "s%d%d" % (hl, half))
                if jt + 1 < NT:
                    cur = prep(jt + 1, 2)
                mq_, kmq = mqs.next()
                mk_, kmk = mks.next()
                for h in range(4):
                    p, kp = fm(640 + h * 128, 128)
                    S.act(actf(mq_[:, h, :], p[:], AF.Copy), reads=[kp], writes=[kmq])
                for h in range(4):
                    p, kp = fm(1152 + h * 128, 128)
                    S.act(actf(mk_[:, h, :], p[:], AF.Copy, scale=float(128 ** -0.5)), reads=[kp], writes=[kmk])
                S.dma(self.MQ[:, :, t0:t0 + 512].rearrange("h d t -> d h t"), mq_[:], reads=[kmq], writes=["MQ"], chan=kmq + "s")
                S.dma(self.MKs[:, :, t0:t0 + 512].rearrange("h d t -> d h t"), mk_[:], reads=[kmk], writes=["MK"], chan=kmk + "s")
                vt, kv = vts.next()
                mv, kmv = mvs.next()
                sg, ksg = sgs.next()
                gt, kgt = gts.next()
                for sub in range(4):
                    hs = lambda c: h_[:, c, sub * 128:(sub + 1) * 128]
                    p1, k1 = ptm.next()
                    S.pe(mmg([(p1[:, 0:144], hs(c), w[:, c, 1664:1808]) for c in range(8)]), reads=hk + ["wab"], writes=[k1])
                    S.act(actf(vt[:, sub, :, 0:64], p1[:, 0:128].rearrange("p (h d) -> p h d", d=64), AF.Copy), reads=[k1], writes=[kv])
                    S.dve(tt(gt[:, sub, :], p1[:, 128:144], gbias[:], ALU.add), reads=[k1, "gbias"], writes=[kgt])
                    p2, k2 = ptm.next()
                    S.pe(mmg([(p2[:], hs(c), w[:, c, 1808:2320]) for c in range(8)]), reads=hk + ["wab"], writes=[k2])
                    S.dve(cp(mv[:, sub, :, 0:128], p2[:].rearrange("p (h d) -> p h d", d=128)), reads=[k2], writes=[kmv])
                    p3, k3 = ptm.next()
                    S.pe(mmg([(p3[:], hs(c), w[:, c, 2320:2832]) for c in range(8)]), reads=hk + ["wab"], writes=[k3])
                    S.act(actf(sg[:, sub, :], p3[:], AF.Sigmoid), reads=[k3], writes=[ksg])
                S.dma(self.VA[t0:t0 + 512, :].rearrange("(s p) c -> p s c", p=128), vt[:].rearrange("p s h d -> p s (h d)"), reads=[kv], writes=["VA"], chan=kv + "s")
                S.dma(self.MV[t0:t0 + 512, :].rearrange("(s p) c -> p s c", p=128), mv[:].rearrange("p s h d -> p s (h d)"), reads=[kmv], writes=["MV"], chan=kmv + "s")
                S.dma(self.SG[t0:t0 + 512, :].rearrange("(s p) c -> p s c", p=128), sg[:], reads=[ksg], writes=["SG"], chan=ksg + "s")
                S.dma(self.GT[t0:t0 + 512, :].rearrange("(s p) c -> p s c", p=128), gt[:], reads=[kgt], writes=["GT"], chan=kgt + "s")

    def attention(self):
        S, NT, NCH, T, nc = self.S, self.NT, self.NCH, self.T, self.nc
        with self.phase(psum=False):
            ph = self._ph
            ppr = Ring([(ph.enter_context(nc.psum_tensor(self.un("pp"), [128, 1024], F32)), "pp%d" % i) for i in range(2)])
            poa = (ph.enter_context(nc.psum_tensor(self.un("poa"), [128, 512], F32)), "poa")
            pob = (ph.enter_context(nc.psum_tensor(self.un("pob"), [128, 512], F32)), "pob")
            ptb = ph.enter_context(nc.psum_tensor(self.un("ptb"), [128, 1024], BF16))
            K = self.sb("Kall", [128, T], BF16)
            V = self.sb("Vall", [128, NCH, 130], BF16)
            S.dma(K[:], self.KA, writes=["K"], chan="K")
            S.dma(V[:], self.VA.rearrange("(b p) c -> p b c", p=128), writes=["V"], chan="V")
            qr = self.ring("q", 2, [128, 8, 512], BF16)
            for (t_, k_) in qr.items:
                S.dve(mset(t_[:], 0.0), writes=[k_ + "0", k_ + "1"])
            pr = self.ring("p", 3, [128, 1024], BF16)
            rcr = self.ring("rc", 2, [128, 4], F32)
            otr = self.ring("ot", 2, [128, 4, 8, 64], BF16)
            ob = self.ring("ob", 2, [128, 4, 512], BF16)
            items = [(jq, hp, kb) for jq in range(NT) for hp in range(4) for kb in range(NCH)]
            tctx, ictx = {}, {}

            def stage1(i):
                jq, hp, kb = items[i]
                t0 = jq * 512
                if hp == 0 and kb == 0:
                    q_, kq = qr.next()
                    for kvh in range(2):
                        S.dma(q_[kvh * 64:(kvh + 1) * 64, kvh * 4:(kvh + 1) * 4, :], self.QA[kvh * 4:(kvh + 1) * 4, :, t0:t0 + 512].rearrange("h d t -> d h t"),
                              writes=[kq + str(kvh)], chan=kq + str(kvh))
                    tctx[jq] = (q_, kq) + otr.next()
                q_, kq, ot, kot = tctx[jq]
                pb, kpb = ppr.next()

                def st2(e, pb=pb, q_=q_, hp=hp, kb=kb):
                    e.matmul(pb[:, 0:512], lhsT=K[:, kb * 128:(kb + 1) * 128], rhs=q_[:, hp, :], start=True, stop=True)
                    return e.matmul(pb[:, 512:1024], lhsT=K[:, kb * 128:(kb + 1) * 128], rhs=q_[:, 4 + hp, :], start=True, stop=True)
                S.pe(st2, reads=["K", kq + "0", kq + "1"], writes=[kpb])
                p_, kp_ = pr.next()
                S.act(actf(p_[:], pb[:], AF.Exp, bias=self.mb[:, jq, (kb // 4):(kb // 4) + 1]), reads=[kpb, "mb"], writes=[kp_])
                ictx[i] = (p_, kp_)

            def stage2(i):
                jq, hp, kb = items[i]
                t0 = jq * 512
                q_, kq, ot, kot = tctx[jq]
                p_, kp_ = ictx.pop(i)

                def pv8(e, p_=p_, kb=kb):
                    ins = None
                    for hh, (po, _) in enumerate((poa, pob)):
                        for sub in range(4):
                            ins = e.matmul(po[:, sub * 65:(sub + 1) * 65], lhsT=p_[:, hh * 512 + sub * 128: hh * 512 + (sub + 1) * 128],
                                           rhs=V[:, kb, hh * 65:(hh + 1) * 65], start=(kb == 0 and sub == 0), stop=(kb == NCH - 1 and sub == 3))
                    return ins
                S.pe(pv8, reads=["V", kp_], writes=["poa", "pob"])
                if kb != NCH - 1:
                    return
                for hh, (po, kpo) in enumerate((poa, pob)):
                    h = hh * 4 + hp
                    pv = po[:, 0:260].rearrange("p (s c) -> p s c", c=65)
                    r_, kr = rcr.next()
                    S.dve(recip(r_[:], pv[:, :, 64]), reads=[kpo], writes=[kr])
                    S.dve(tt(ot[:, :, h, :], pv[:, :, 0:64], r_[:].unsqueeze(2).to_broadcast([128, 4, 64]), ALU.mult), reads=[kpo, kr], writes=[kot])
                if hp != 3:
                    return
                o_, ko = ob.next()
                for sp in range(2):
                    def tr8(e, ot=ot, sp=sp):
                        ins = None
                        for hp2 in range(4):
                            for s_ in range(2):
                                slot = hp2 * 2 + s_
                                ins = e.transpose(ptb[:, slot * 128:(slot + 1) * 128],
                                                  ot[:, sp * 2 + s_, 2 * hp2:2 * hp2 + 2, :].rearrange("p h d -> p (h d)"), self.ident_b)
                        return ins
                    S.pe(tr8, reads=[kot, "cmb"], writes=["ptb"])
                    S.act(actf(o_[:, :, sp * 256:(sp + 1) * 256].rearrange("p a (s q) -> p a s q", q=128),
                               ptb[:].rearrange("p (a s q) -> p a s q", a=4, s=2), AF.Copy), reads=["ptb"], writes=[ko])
                S.dma(self.MIX[0:512, t0:t0 + 512].rearrange("(a p) t -> p a t", p=128), o_[:], reads=[ko], writes=["MIXa"], chan=ko + "s")

            self.pipeline(len(items), stage1, stage2, 1)

    def mlstm(self, j):
        S, NT, NCH, T, nc = self.S, self.NT, self.NCH, self.T, self.nc
        NG = NCH * 4
        with self.phase():
            G = self.sb("G", [128, NCH, 16], F32)
            S.dma(G[:], self.GT.rearrange("(c p) g -> p c g", p=128), writes=["G"], chan="G")
            G5 = G[:].rearrange("p c (d k h) -> p d c k h", d=2, k=2, h=4)
            gi, gf = G5[:, :, :, 0, :], G5[:, :, :, 1, :]
            shp = [128, 2, NCH, 4]
            mk = lambda n: self.sb(n, shp, F32)
            FL, Bc, A_, AMr, BLr, Mt, mt, WK, THR, KEEP = [mk(n) for n in ("FL", "Bc", "Aa", "AMr", "BLr", "Mt", "mt", "WK", "THR", "KEEP")]
            tmp = mk("tmp")
            fl2 = lambda t, d: t[:, d, :, :].rearrange("p c h -> p (c h)")
            S.act(actf(tmp[:], gf, AF.Abs), reads=["G"], writes=["tmp"])
            S.act(actf(tmp[:], tmp[:], AF.Exp, scale=-1.0), reads=["tmp"], writes=["tmp"])
            S.act(actf(tmp[:], tmp[:], AF.Ln, bias=1.0), reads=["tmp"], writes=["tmp"])
            S.dve(ts(FL[:], gf, 0.0, ALU.min), reads=["G"], writes=["FL"])
            S.dve(tt(FL[:], FL[:], tmp[:], ALU.subtract), reads=["FL", "tmp"], writes=["FL"])
            bw = min(128, NG)
            nblk = NG // bw
            am = self.sb("am", [128, 2, nblk], F32)
            dg = self.sb("dg", [128, 128], F32)
            for d in range(2):
                tri = self.cm[:, C_TRIU, :] if d == 0 else self.cm[:, C_TRIL, :]
                S.pe(mm1(self.ps[0][:, 0:NG], tri, fl2(FL, d)), reads=["FL", "cm"], writes=["ps0"])
                S.act(actf(fl2(Bc, d), self.ps[0][:, 0:NG], AF.Copy), reads=["ps0"], writes=["Bc"])
                S.pe(mm1(self.ps[1][:, 0:NG], self.ones_f, fl2(FL, d)), reads=["FL", "cm"], writes=["ps1"])
                S.act(actf(fl2(BLr, d), self.ps[1][:, 0:NG], AF.Copy), reads=["ps1"], writes=["BLr"])
                S.dve(tt(A_[:, d], gi[:, d], Bc[:, d], ALU.subtract), reads=["G", "Bc"], writes=["Aa"])
                for b in range(nblk):
                    S.pe(tr(self.ps[2][0:bw, 0:128], fl2(A_, d)[:, b * bw:(b + 1) * bw], self.ident_f), reads=["Aa", "cm"], writes=["ps2"])
                    S.dve(lambda e, d=d, b=b, ps2=self.ps[2]: e.tensor_reduce(out=am[0:bw, d, b:b + 1], in_=ps2[0:bw, 0:128], axis=AX.X, op=ALU.max),
                          reads=["ps2"], writes=["am"])
                    S.dve(ts(dg[0:bw, 0:bw], self.ident_f[0:bw, 0:bw], am[0:bw, d, b:b + 1], ALU.mult), reads=["am", "cm"], writes=["dg"])
                    S.pe(mm1(self.ps[3][:, 0:bw], self.ones_f[0:bw, :], dg[0:bw, 0:bw]), reads=["dg", "cm"], writes=["ps3"])
                    S.act(actf(fl2(AMr, d)[:, b * bw:(b + 1) * bw], self.ps[3][:, 0:bw], AF.Copy), reads=["ps3"], writes=["AMr"])
            mrun = self.sb("mrun", [128, 2, 4], F32)
            S.dve(mset(mrun[:], 0.0), writes=["mrun0", "mrun1"])
            for k in range(NCH):
                for d in range(2):
                    eng = "dve"
                    kk = "rec%d" % d
                    c = k if d == 0 else NCH - 1 - k
                    S.add(eng, ts(mt[:, d, c, :], mrun[:, d, :], self.carry[:, d, c:c + 1], ALU.mult), reads=["mrun%d" % d, "carry"], writes=[kk + "mt"])
                    S.add(eng, tt(Mt[:, d, c, :], mt[:, d, c, :], AMr[:, d, c, :], ALU.max), reads=[kk + "mt", "AMr"], writes=[kk + "Mt"])
                    S.add(eng, tt(mrun[:, d, :], Mt[:, d, c, :], BLr[:, d, c, :], ALU.add), reads=[kk + "Mt", "BLr"], writes=["mrun%d" % d])
            rk = ["rec0mt", "rec1mt", "rec0Mt", "rec1Mt"]
            S.dve(tt(KEEP[:], mt[:], Mt[:], ALU.subtract), reads=rk, writes=["KEEP"])
            S.act(actf(KEEP[:], KEEP[:], AF.Exp), reads=["KEEP"], writes=["KEEP"])
            S.dve(tt(KEEP[:].rearrange("p d c h -> p (d c) h"), KEEP[:].rearrange("p d c h -> p (d c) h"),
                     self.carry[:].rearrange("p d c -> p (d c)").unsqueeze(2).to_broadcast([128, 2 * NCH, 4]), ALU.mult), reads=["KEEP", "carry"], writes=["KEEP"])
            S.dve(tt(WK[:], A_[:], Mt[:], ALU.subtract), reads=["Aa"] + rk, writes=["WK"])
            S.act(actf(WK[:], WK[:], AF.Exp), reads=["WK"], writes=["WK"])
            S.dve(tt(THR[:], Bc[:], Mt[:], ALU.add), reads=["Bc"] + rk, writes=["THR"])
            S.act(actf(THR[:], THR[:], AF.Exp, scale=-1.0), reads=["THR"], writes=["THR"])
            maskf = self.cm[:, C_TRIU, :]
            maskb = self.cm[:, C_TRIL, :]
            qr = [self.ring("mq", 2, [128, 4, 512], BF16) for d in range(2)]
            kr = [self.ring("mk", 2, [128, 4, 512], BF16) for d in range(2)]
            vr = [self.ring("mv", 2, [128, 4, 516], BF16) for d in range(2)]
            Cst = [[self.sb("Cst", [128, 129], F32) for h in range(4)] for d in range(2)]
            Cbf = [[self.sb("Cbf", [128, 129], BF16) for h in range(4)] for d in range(2)]
            for d in range(2):
                for h in range(4):
                    S.pool(mset(Cst[d][h][:], 0.0), writes=["C%d%d" % (d, h)])
            atr = self.ring("at", 6, [128, 128], BF16)
            kwr = self.ring("kw", 6, [128, 128], BF16)
            rr_ = self.ring("r", 6, [128, 2], F32)
            hst = [self.ring("hst", 2, [128, 512], F32) for d in range(2)]
            pS = Ring([(self.ps[0], "ps0"), (self.ps[1], "ps1")])
            pH = Ring([(self.ps[2], "ps2"), (self.ps[3], "ps3")])
            pU = Ring([(self.ps[4], "ps4"), (self.ps[5], "ps5")])
            pT = Ring([(self.psb[0], "psb0"), (self.psb[1], "psb1")])
            MQ3 = self.MQ.rearrange("h d t -> d h t")
            MK3 = self.MKs.rearrange("h d t -> d h t")
            HO = [self.HF, self.HB]
            items = [(k, h, d) for k in range(NCH) for h in range(4) for d in range(2)]
            tctx, cctx, ictx = {}, {}, {}

            def geom(it):
                k, h, d = it
                c = k if d == 0 else NCH - 1 - k
                return d, c // 4, c % 4, h, c, (c // 4) * 512

            def ensure(d, jt):
                if (d, jt) in tctx or jt < 0 or jt >= NT:
                    return
                t0 = jt * 512
                q_, kq = qr[d].next()
                k_, kk = kr[d].next()
                v_, kv = vr[d].next()
                S.dma(q_[:], MQ3[:, :, t0:t0 + 512], writes=[kq], chan=kq)
                S.dma(k_[:], MK3[:, :, t0:t0 + 512], writes=[kk], chan=kk)
                S.dma(v_[:], self.MV[t0:t0 + 512, :].rearrange("(s p) c -> p s c", p=128), writes=[kv], chan=kv)
                tctx[(d, jt)] = (q_, kq, k_, kk, v_, kv)

            def stage1(i):
                d, jt, sub, h, c, t0 = geom(items[i])
                mask = maskf if d == 0 else maskb
                if h == 0:
                    ensure(d, jt)
                    if items[i][0] % 4 == 1:
                        ensure(d, jt + (1 if d == 0 else -1))
                    cctx[(d, c)] = hst[d].next()
                q_, kq, k_, kk, v_, kv = tctx[(d, jt)]
                ck = "C%d%d" % (d, h)
                qs = q_[:, h, sub * 128:(sub + 1) * 128]
                ks_ = k_[:, h, sub * 128:(sub + 1) * 128]
                wkc = WK[:, d, c, h:h + 1]
                st_, kst = pS.next()
                S.pe(mm1(st_[:, 0:128], ks_, qs), reads=[kk, kq], writes=[kst])
                at, kat = atr.next()
                S.dve(stt(at[:], st_[:, 0:128], wkc, mask, ALU.mult, ALU.mult), reads=[kst, "WK", "cm"], writes=[kat])
                ptr, kptr = pT.next()
                S.pe(tr(ptr[:, 0:128], ks_, self.ident_b), reads=[kk, "cmb"], writes=[kptr])
                kw, kkw = kwr.next()
                S.act(actf(kw[:], ptr[:, 0:128], AF.Copy, scale=wkc), reads=[kptr, "WK"], writes=[kkw])
                S.act(actf(Cbf[d][h][:], Cst[d][h][:], AF.Copy, scale=KEEP[:, d, c, h:h + 1]), reads=[ck, "KEEP"], writes=[ck + "b"])
                ictx[i] = (at, kat, kw, kkw)

            def stage2(i):
                d, jt, sub, h, c, t0 = geom(items[i])
                q_, kq, k_, kk, v_, kv = tctx[(d, jt)]
                hs_, khs = cctx[(d, c)]
                at, kat, kw, kkw = ictx.pop(i)
                ck = "C%d%d" % (d, h)
                tc0 = t0 + sub * 128
                qs = q_[:, h, sub * 128:(sub + 1) * 128]
                vs = v_[:, sub, h * 129:(h + 1) * 129]
                ph, kph = pH.next()
                S.pe(mmg([(ph[:, 0:129], at[:], vs), (ph[:, 0:129], qs, Cbf[d][h][:])]), reads=[kat, kv, kq, ck + "b"], writes=[kph])
                pu, kpu = pU.next()
                S.pe(mm1(pu[:, 0:129], kw[:], vs), reads=[kkw, kv], writes=[kpu])
                S.dve(stt(Cst[d][h][:], Cst[d][h][:], KEEP[:, d, c, h:h + 1], pu[:, 0:129], ALU.mult, ALU.add), reads=[ck, kpu, "KEEP"], writes=[ck])
                r_, kr_ = rr_.next()
                S.dve(ts(r_[:, 0:1], ph[:, 128:129], -1.0, ALU.mult, THR[:, d, c, h:h + 1], ALU.max), reads=[kph, "THR"], writes=[kr_])
                S.dve(tt(r_[:, 0:1], r_[:, 0:1], ph[:, 128:129], ALU.max), reads=[kr_, kph], writes=[kr_])
                S.dve(recip(r_[:, 1:2], r_[:, 0:1]), reads=[kr_], writes=[kr_])
                S.act(actf(hs_[:, h * 128:(h + 1) * 128], ph[:, 0:128], AF.Copy, scale=r_[:, 1:2]), reads=[kph, kr_], writes=[khs])
                if h == 3:
                    S.dma(HO[d][tc0:tc0 + 128, :], hs_[:], reads=[khs], writes=["HO"], chan=khs + "s")

            self.pipeline(len(items), stage1, stage2, 2)

        with self.phase():
            gain = self.sb("ogain", [128, 512], F32)
            S.dma(gain[:], self.w["mlstm_out_norm"][j:j + 1, :].to_broadcast([128, 512]), writes=["ogain"], chan="ogain")
            hfr = self.ring("hf", 3, [128, 512], F32)
            hbr = self.ring("hb", 3, [128, 512], F32)
            sgr = self.ring("sgl", 3, [128, 512], F32)
            ssq = self.ring("ssq", 3, [128, 8], F32)
            junk = self.ring("junk", 2, [128, 128], F32)
            ymr = self.ring("ym", 3, [128, 512], BF16)
            mxs = self.ring("mxs", 2, [128, 4, 512], BF16)
            pT = Ring([(self.psb[0], "psb0"), (self.psb[1], "psb1")])

            def ld(c):
                hf, khf = hfr.next()
                hb, khb = hbr.next()
                sg, ksg = sgr.next()
                S.dma(hf[:], self.HF[c * 128:(c + 1) * 128, :], writes=[khf], chan=khf)
                S.dma(hb[:], self.HB[c * 128:(c + 1) * 128, :], writes=[khb], chan=khb)
                S.dma(sg[:], self.SG[c * 128:(c + 1) * 128, :], writes=[ksg], chan=ksg)
                return hf, khf, hb, khb, sg, ksg
            pend = [ld(0), ld(1)] if NCH > 1 else [ld(0)]
            for c in range(NCH):
                hf, khf, hb, khb, sg, ksg = pend.pop(0)
                if c + 2 < NCH:
                    pend.append(ld(c + 2))
                sub = c % 4
                if sub == 0:
                    mx, kmx = mxs.next()
                S.dve(tt(hf[:], hf[:], hb[:], ALU.add), reads=[khf, khb], writes=[khf])
                sq_, ksq = ssq.next()
                jk, kjk = junk.next()
                S.dve(mset(sq_[:], 0.0), writes=[ksq + "a", ksq + "b"])
                for hh in range(4):
                    S.act(actf(jk[:], hf[:, hh * 128:(hh + 1) * 128], AF.Square, accum=sq_[:, hh:hh + 1]), reads=[khf], writes=[kjk, ksq + "a"])
                S.act(actf(sq_[:, 4:8], sq_[:, 0:4], AF.Sqrt, bias=EPS, scale=1.0 / 128), reads=[ksq + "a"], writes=[ksq + "b"])
                S.dve(recip(sq_[:, 4:8], sq_[:, 4:8]), reads=[ksq + "b"], writes=[ksq + "b"])
                S.dve(tt(hf[:].rearrange("p (h e) -> p h e", e=128), hf[:].rearrange("p (h e) -> p h e", e=128),
                         sq_[:, 4:8].unsqueeze(2).to_broadcast([128, 4, 128]), ALU.mult), reads=[khf, ksq + "b"], writes=[khf])
                S.pool(tt(sg[:], sg[:], gain[:], ALU.mult), reads=[ksg, "ogain"], writes=[ksg])
                ym, kym = ymr.next()
                S.dve(tt(ym[:], hf[:], sg[:], ALU.mult), reads=[khf, ksg], writes=[kym])
                ptr, kptr = pT.next()

                def tr4(e, ptr=ptr, ym=ym):
                    ins = None
                    for hh in range(4):
                        ins = e.transpose(ptr[:, hh * 128:(hh + 1) * 128], ym[:, hh * 128:(hh + 1) * 128], self.ident_b)
                    return ins
                S.pe(tr4, reads=[kym, "cmb"], writes=[kptr])
                S.act(actf(mx[:, :, sub * 128:(sub + 1) * 128], ptr[:, 0:512].rearrange("p (h e) -> p h e", e=128), AF.Copy), reads=[kptr], writes=[kmx])
                if sub == 3:
                    t0 = (c // 4) * 512
                    S.dma(self.MIX[512:1024, t0:t0 + 512].rearrange("(h p) t -> p h t", p=128), mx[:], reads=[kmx], writes=["MIXb"], chan=kmx + "s")

    def ret_inproj(self, X, j, L):
        S, NT, nc = self.S, self.NT, self.nc
        with self.phase():
            w = self.sb("wret", [128, 8, 6144], BF16)
            g = self.load_gain(self.w["norm_mix"][L], "gmix")
            self.prep_weight(w, "wret", self.w["ret_w_in"][j], 8, [(i * 2048, 2048, i * 2048, 2048, None) for i in range(3)], gain=g, gkey="gmix")
            xt, sq, rs, hT = self.norm_bufs()
            X3 = self.x3(X)
            cs = self.ring("cs", 1, [128, 4, 512], F32)
            t4 = self.ring("t4", 1, [128, 4, 512], F32)
            o12 = self.ring("o12", 2, [128, 2, 512], BF16)
            vts = self.ring("rv", 1, [128, 4, 2048], BF16)
            gts = self.ring("rg", 1, [128, 2048], F32)
            pf = Ring([(self.ps[i], self.pk[i]) for i in (1, 2, 3)])
            ptm = Ring([(self.ps[4], "ps4"), (self.ps[5], "ps5")])
            def cs_load(jt_):
                c_, kc = cs.next()
                S.dma(c_[:, 0:2, :], self.ropeR[:, :, jt_ * 512:(jt_ + 1) * 512].rearrange("a p t -> p a t"), writes=[kc, kc + "k"], chan=kc)
                return c_, kc
            prep = self.norm_pipeline(X3, xt, sq, rs, hT)
            cur = prep(0)
            cnxt = cs_load(0)
            for jt in range(NT):
                t0 = jt * 512
                bufs, hk = cur
                h_ = bufs["hT"][0]
                c_, kc = cnxt
                S.act(actf(c_[:, 2:4, :], c_[:, 0:2, :], AF.Copy, scale=1.0 / 16), reads=[kc], writes=[kc + "k"])
                for qk in range(2):
                    if qk == 1 and jt + 1 < NT:
                        cur = prep(jt + 1, 2)
                    co = 0 if qk == 0 else 2
                    ck_ = [kc] if qk == 0 else [kc + "k"]
                    dstT = self.RQ if qk == 0 else self.RK
                    for h in range(4):
                        if qk == 0 and h == 1 and jt + 1 < NT:
                            prep(jt + 1, 1)
                        col = qk * 1024 + h * 256
                        pa, kpa = pf.next()
                        S.pe(mmg([(pa[:], w[:, c, col:col + 128], h_[:, c, :]) for c in range(8)]), reads=hk + self.wkeys("wret", col, 128), writes=[kpa])
                        pb, kpb = pf.next()
                        S.pe(mmg([(pb[:], w[:, c, col + 128:col + 256], h_[:, c, :]) for c in range(8)]), reads=hk + self.wkeys("wret", col + 128, 128), writes=[kpb])
                        t_, kt = t4.next()
                        S.dve(tt(t_[:, 0, :], pa[:], c_[:, co, :], ALU.mult), reads=[kpa] + ck_, writes=[kt + "0"])
                        S.dve(tt(t_[:, 1, :], pb[:], c_[:, co + 1, :], ALU.mult), reads=[kpb] + ck_, writes=[kt + "1"])
                        S.dve(tt(t_[:, 2, :], pb[:], c_[:, co, :], ALU.mult), reads=[kpb] + ck_, writes=[kt + "2"])
                        S.dve(tt(t_[:, 3, :], pa[:], c_[:, co + 1, :], ALU.mult), reads=[kpa] + ck_, writes=[kt + "3"])
                        o_, ko = o12.next()
                        S.pool(tt(o_[:, 0, :], t_[:, 0, :], t_[:, 1, :], ALU.subtract), reads=[kt + "0", kt + "1"], writes=[ko])
                        S.pool(tt(o_[:, 1, :], t_[:, 2, :], t_[:, 3, :], ALU.add), reads=[kt + "2", kt + "3"], writes=[ko])
                        S.dma(dstT[2 * h:2 * h + 2, :, t0:t0 + 512].rearrange("a p t -> p a t"), o_[:], reads=[ko], writes=["RQK"], chan=ko + "s")
                if jt + 1 < NT:
                    cnxt = cs_load(jt + 1)
                vt, kv = vts.next()
                for sub in range(4):
                    hs = lambda c: h_[:, c, sub * 128:(sub + 1) * 128]
                    for blk in range(4):
                        p1, k1 = ptm.next()
                        S.pe(mmg([(p1[:], hs(c), w[:, c, 2048 + blk * 512:2048 + (blk + 1) * 512]) for c in range(8)]), reads=hk + self.wkeys("wret", 2048 + blk * 512, 512), writes=[k1])
                        S.act(actf(vt[:, sub, blk * 512:(blk + 1) * 512], p1[:], AF.Copy), reads=[k1], writes=[kv])
                    gt, kg = gts.next()
                    for blk in range(4):
                        p1, k1 = ptm.next()
                        S.pe(mmg([(p1[:], hs(c), w[:, c, 4096 + blk * 512:4096 + (blk + 1) * 512]) for c in range(8)]), reads=hk + self.wkeys("wret", 4096 + blk * 512, 512), writes=[k1])
                        S.act(actf(gt[:, blk * 512:(blk + 1) * 512], p1[:], AF.Silu), reads=[k1], writes=[kg])
                    S.dma(self.RG[t0 + sub * 128:t0 + (sub + 1) * 128, :], gt[:], reads=[kg], writes=["RG"], chan=kg + "s")
                S.dma(self.RV[t0:t0 + 512, :].rearrange("(s p) c -> p s c", p=128), vt[:], reads=[kv], writes=["RV"], chan=kv + "s")

    def retention(self, j):
        S, NT, NCH, T, nc = self.S, self.NT, self.NCH, self.T, self.nc
        with self.phase():
            lg = self.sb("lg", [128, 8], F32)
            tmp = self.sb("lgt", [128, 8], F32)
            S.dma(lg[:], self.w["ret_decay_logit"][j:j + 1].rearrange("a d h -> a (d h)").to_broadcast([128, 8]), writes=["lg"], chan="lg")
            S.act(actf(tmp[:], lg[:], AF.Abs), reads=["lg"], writes=["lgt"])
            S.act(actf(tmp[:], tmp[:], AF.Exp, scale=-1.0), reads=["lgt"], writes=["lgt"])
            S.act(actf(tmp[:], tmp[:], AF.Ln, bias=1.0), reads=["lgt"], writes=["lgt"])
            S.dve(ts(lg[:], lg[:], 0.0, ALU.min), reads=["lg"], writes=["lg"])
            S.dve(tt(lg[:], lg[:], tmp[:], ALU.subtract), reads=["lg", "lgt"], writes=["lg"])
            DT = self.sb("DT", [128, 8, 128], F32)
            QD = self.sb("QD", [128, 8, 128], F32)
            QDb = self.sb("QDb", [128, 8, 128], BF16)
            KD = self.sb("KD", [128, 8], F32)
            CDC = self.sb("CDC", [128, 8, NCH], F32)
            cdv = self.sb("cdv", [128, 8], F32)
            for hd in range(8):
                d = hd // 4
                S.act(actf(DT[:, hd, :], self.cm[:, C_DFW if d == 0 else C_DBW, :], AF.Exp, scale=lg[:, hd:hd + 1]), reads=["lg", "cm"], writes=["DT"])
                S.dve(tt(DT[:, hd, :], DT[:, hd, :], self.cm[:, C_TRIU if d == 0 else C_SL, :], ALU.mult), reads=["DT", "cm"], writes=["DT"])
                S.act(actf(QD[:, hd, :], self.cm[:, C_QDF if d == 0 else C_QDB, :], AF.Exp, scale=lg[:, hd:hd + 1]), reads=["lg", "cm"], writes=["QD"])
                S.dve(cp(QDb[:, hd, :], QD[:, hd, :]), reads=["QD"], writes=["QDb"])
                S.act(actf(KD[:, hd:hd + 1], self.cm[:, C_KD, d:d + 1], AF.Exp, scale=lg[:, hd:hd + 1]), reads=["lg", "cm"], writes=["KD"])
                S.act(actf(cdv[:, hd:hd + 1], self.cm[:, C_KD, 2:3], AF.Exp, scale=lg[:, hd:hd + 1]), reads=["lg", "cm"], writes=["cdv"])
                S.dve(ts(CDC[:, hd, :], self.carry[:, d, :], cdv[:, hd:hd + 1], ALU.mult), reads=["cdv", "carry"], writes=["CDC"])
            qr = [self.ring("rq", 2, [128, 8, 512], BF16) for d in range(2)]
            kr = [self.ring("rk", 2, [128, 8, 512], BF16) for d in range(2)]
            vr = [self.ring("rv", 3, [128, 2048], BF16) for d in range(2)]
            St = [[[self.sb("St", [128, 512], F32) for c in range(2)] for h in range(4)] for d in range(2)]
            Sb = [[[self.sb("Sb", [128, 512], BF16) for c in range(2)] for h in range(4)] for d in range(2)]
            for d in range(2):
                for h in range(4):
                    for c in range(2):
                        S.pool(mset(St[d][h][c][:], 0.0), writes=["S%d%d%d" % (d, h, c)])
            atr = self.ring("at", 6, [128, 128], BF16)
            kwr = self.ring("kw", 6, [128, 2, 128], BF16)
            qdr = self.ring("qd", 6, [128, 2, 128], BF16)
            yst = [self.ring("yst", 2, [128, 2048], F32) for d in range(2)]
            pS = Ring([(self.ps[0][:, 0:128], "ps0"), (self.ps[5][:, 0:128], "ps5")])
            pO = Ring([(self.ps[1], "ps1"), (self.ps[2], "ps2")])
            pU = Ring([(self.ps[3], "ps3"), (self.ps[4], "ps4")])
            pT = Ring([(self.psb[0], "psb0"), (self.psb[1], "psb1")])
            RQ3 = self.RQ.rearrange("a p t -> p a t")
            RK3 = self.RK.rearrange("a p t -> p a t")
            YO = [self.YF, self.YB]
            items = [(k, h, d) for k in range(NCH) for h in range(4) for d in range(2)]
            tctx, cctx, ictx = {}, {}, {}

            def geom(it):
                k, h, d = it
                c = k if d == 0 else NCH - 1 - k
                return d, c // 4, c % 4, h, c, (c // 4) * 512

            def ensure(d, jt):
                if (d, jt) in tctx or jt < 0 or jt >= NT:
                    return
                t0 = jt * 512
                q_, kq = qr[d].next()
                k_, kk = kr[d].next()
                S.dma(q_[:], RQ3[:, :, t0:t0 + 512], writes=[kq], chan=kq)
                S.dma(k_[:], RK3[:, :, t0:t0 + 512], writes=[kk], chan=kk)
                tctx[(d, jt)] = (q_, kq, k_, kk)

            def ensure_v(d, c):
                if (d, c) in cctx or c < 0 or c >= NCH:
                    return
                v_, kv = vr[d].next()
                S.dma(v_[:], self.RV[c * 128:(c + 1) * 128, :], writes=[kv], chan=kv)
                cctx[(d, c)] = (v_, kv) + yst[d].next()

            def stage1(i):
                d, jt, sub, h, c, t0 = geom(items[i])
                if h == 0:
                    ensure(d, jt)
                    if items[i][0] % 4 == 1:
                        ensure(d, jt + (1 if d == 0 else -1))
                    ensure_v(d, c)
                    ensure_v(d, c + (1 if d == 0 else -1))
                q_, kq, k_, kk = tctx[(d, jt)]
                sl = slice(sub * 128, (sub + 1) * 128)
                hd = d * 4 + h
                st_, kst = pS.next()
                S.pe(mmg([(st_, k_[:, 2 * h + cc, sl], q_[:, 2 * h + cc, sl]) for cc in range(2)]), reads=[kk, kq], writes=[kst])
                at, kat = atr.next()
                S.dve(tt(at[:], st_, DT[:, hd, :], ALU.mult), reads=[kst, "DT"], writes=[kat])
                kw, kkw = kwr.next()
                qd, kqd = qdr.next()
                ptr, kptr = pT.next()

                def tr2(e, ptr=ptr, k_=k_, h=h, sl=sl):
                    ins = None
                    for cc in range(2):
                        ins = e.transpose(ptr[:, cc * 128:(cc + 1) * 128], k_[:, 2 * h + cc, sl], self.ident_b)
                    return ins
                S.pe(tr2, reads=[kk, "cmb"], writes=[kptr])
                S.act(actf(kw[:], ptr[:, 0:256].rearrange("p (c e) -> p c e", e=128), AF.Copy, scale=KD[:, hd:hd + 1]), reads=[kptr, "KD"], writes=[kkw])
                S.pool(tt(qd[:], q_[:, 2 * h:2 * h + 2, sl], QDb[:, hd, :].unsqueeze(1).to_broadcast([128, 2, 128]), ALU.mult), reads=[kq, "QDb"], writes=[kqd])
                for cc in range(2):
                    sk = "S%d%d%d" % (d, h, cc)
                    S.act(actf(Sb[d][h][cc][:], St[d][h][cc][:], AF.Copy, scale=self.carry[:, d, c:c + 1]), reads=[sk, "carry"], writes=[sk + "b"])
                ictx[i] = (at, kat, kw, kkw, qd, kqd)

            def stage2(i):
                d, jt, sub, h, c, t0 = geom(items[i])
                v_, kv, ys, kys = cctx[(d, c)]
                at, kat, kw, kkw, qd, kqd = ictx.pop(i)
                hd = d * 4 + h
                vs = v_[:, h * 512:(h + 1) * 512]
                po, kpo = pO.next()
                S.pe(mmg([(po[:], at[:], vs), (po[:], qd[:, 0, :], Sb[d][h][0][:]), (po[:], qd[:, 1, :], Sb[d][h][1][:])]),
                     reads=[kat, kv, kqd, "S%d%d0b" % (d, h), "S%d%d1b" % (d, h)], writes=[kpo])
                for cc in range(2):
                    pu, kpu = pU.next()
                    sk = "S%d%d%d" % (d, h, cc)
                    S.pe(mm1(pu[:], kw[:, cc, :], vs), reads=[kkw, kv], writes=[kpu])
                    S.dve(stt(St[d][h][cc][:], St[d][h][cc][:], CDC[:, hd, c:c + 1], pu[:], ALU.mult, ALU.add), reads=[sk, kpu, "CDC"], writes=[sk])
                S.dve(cp(ys[:, h * 512:(h + 1) * 512], po[:]), reads=[kpo], writes=[kys])
                if h == 3:
                    S.dma(YO[d][c * 128:(c + 1) * 128, :], ys[:], reads=[kys], writes=["YO"], chan=kys + "s")

            self.pipeline(len(items), stage1, stage2, 2)

        with self.phase():
            gain = self.sb("rgain", [128, 2048], F32)
            S.dma(gain[:], self.w["ret_out_norm"][j:j + 1, :].to_broadcast([128, 2048]), writes=["rgain"], chan="rgain")
            yfr = self.ring("yf", 2, [128, 2048], F32)
            ybr = self.ring("yb", 2, [128, 2048], F32)
            rgr = self.ring("rgl", 2, [128, 2048], F32)
            ssq = self.ring("ssq", 3, [128, 8], F32)
            junk = self.ring("junk", 2, [128, 512], F32)
            ymr = self.ring("ym", 2, [128, 2048], BF16)
            mxs = self.ring("mxs", 2, [128, 16, 512], BF16)
            pT = Ring([(self.psb[0], "psb0"), (self.psb[1], "psb1")])

            def ld(c):
                yf, kyf = yfr.next()
                yb, kyb = ybr.next()
                rg, krg = rgr.next()
                S.dma(yf[:], self.YF[c * 128:(c + 1) * 128, :], writes=[kyf], chan=kyf)
                S.dma(yb[:], self.YB[c * 128:(c + 1) * 128, :], writes=[kyb], chan=kyb)
                S.dma(rg[:], self.RG[c * 128:(c + 1) * 128, :], writes=[krg], chan=krg)
                return yf, kyf, yb, kyb, rg, krg
            nxt = ld(0)
            for c in range(NCH):
                yf, kyf, yb, kyb, rg, krg = nxt
                if c + 1 < NCH:
                    nxt = ld(c + 1)
                sub = c % 4
                sl = slice(sub * 128, (sub + 1) * 128)
                if sub == 0:
                    mx, kmx = mxs.next()
                S.pool(tt(yf[:], yf[:], yb[:], ALU.add), reads=[kyf, kyb], writes=[kyf])
                sq_, ksq = ssq.next()
                jk, kjk = junk.next()
                S.dve(mset(sq_[:], 0.0), writes=[ksq + "a", ksq + "b"])
                for hh in range(4):
                    S.act(actf(jk[:], yf[:, hh * 512:(hh + 1) * 512], AF.Square, accum=sq_[:, hh:hh + 1]), reads=[kyf], writes=[kjk, ksq + "a"])
                S.act(actf(sq_[:, 4:8], sq_[:, 0:4], AF.Sqrt, bias=EPS, scale=1.0 / 512), reads=[ksq + "a"], writes=[ksq + "b"])
                S.dve(recip(sq_[:, 4:8], sq_[:, 4:8]), reads=[ksq + "b"], writes=[ksq + "b"])
                S.dve(tt(yf[:].rearrange("p (h e) -> p h e", e=512), yf[:].rearrange("p (h e) -> p h e", e=512),
                         sq_[:, 4:8].unsqueeze(2).to_broadcast([128, 4, 512]), ALU.mult), reads=[kyf, ksq + "b"], writes=[kyf])
                S.pool(tt(rg[:], rg[:], gain[:], ALU.mult), reads=[krg, "rgain"], writes=[krg])
                ym, kym = ymr.next()
                S.dve(tt(ym[:], yf[:], rg[:], ALU.mult), reads=[kyf, krg], writes=[kym])
                for bg in range(2):
                    ptr, kptr = pT.next()

                    def tr8(e, ptr=ptr, ym=ym, bg=bg):
                        ins = None
                        for b_ in range(8):
                            ins = e.transpose(ptr[:, b_ * 128:(b_ + 1) * 128], ym[:, (bg * 8 + b_) * 128:(bg * 8 + b_ + 1) * 128], self.ident_b)
                        return ins
                    S.pe(tr8, reads=[kym, "cmb"], writes=[kptr])
                    S.act(actf(mx[:, bg * 8:(bg + 1) * 8, sl], ptr[:].rearrange("p (b e) -> p b e", e=128), AF.Copy), reads=[kptr], writes=[kmx])
                if sub == 3:
                    t0 = (c // 4) * 512
                    S.dma(self.MIX[0:2048, t0:t0 + 512].rearrange("(b p) t -> p b t", p=128), mx[:], reads=[kmx], writes=["MIXr"], chan=kmx + "s")


def rope_tables(seglen, head_dim, nseg, reps):
    rows = seglen // 64
    row_idx = np.repeat(np.arange(rows, dtype=np.float32), 64)
    col_idx = np.tile(np.arange(64, dtype=np.float32), rows)
    axis_dim = head_dim // 2
    inv_freq = (np.float32(10000.0) ** (-np.arange(0, axis_dim, 2, dtype=np.float32) / np.float32(axis_dim))).astype(np.float32)
    ang = np.concatenate([row_idx[:, None] * inv_freq, col_idx[:, None] * inv_freq], axis=-1).astype(np.float32)
    cs = np.stack([np.cos(ang), np.sin(ang)], 0).astype(np.float32)
    cs = np.tile(cs, (1, nseg, 1))
    cs = cs.transpose(0, 2, 1)
    return np.ascontiguousarray(np.tile(cs, (1, reps, 1)))


def core_tables(T, nseg):
    NT, NCH = T // 512, T // 128
    seglen = T // nseg
    tps = NT // nseg
    cps = NCH // nseg
    seg_t = np.arange(NT) // tps
    mb = np.where(seg_t[:, None] == seg_t[None, :], 0.0, -30000.0).astype(np.float32)
    carry = np.ones((2, NCH), np.float32)
    carry[0, np.arange(NCH) % cps == 0] = 0.0
    carry[1, np.arange(NCH) % cps == cps - 1] = 0.0
    flb = np.ones((NT, 2), np.float32)
    flb[np.arange(NT) % tps == 0, 0] = 0.0
    flb[np.arange(NT) % tps == tps - 1, 1] = 0.0
    rep = lambda a: np.ascontiguousarray(np.broadcast_to(a.reshape(1, -1), (128, a.size))).astype(np.float32)
    return {
        "maskb": rep(mb), "carry": rep(carry), "flb": rep(flb),
        "ropeA": rope_tables(seglen, 64, nseg, 4), "ropeR": rope_tables(seglen, 256, nseg, 1),
    }


DBG = {}
FULL_STEPS = [("ab", 0, 0), ("ffn", 0), ("ret", 0, 1), ("ffn", 1), ("ab", 1, 2), ("ffn", 2), ("ret", 1, 3), ("ffn", 3), ("final",)]
_CACHE = {}


def run_cores(T, steps, core_x, core_nseg, weights):
    key = (T, tuple(steps))
    if key not in _CACHE:
        _CACHE[key] = MK(T, steps).build()
    nc = _CACHE[key]
    cm = host_cmat()
    tabs = {}
    in_maps = []
    for x, ns in zip(core_x, core_nseg):
        if ns not in tabs:
            tabs[ns] = core_tables(T, ns)
        m = {"xT": np.ascontiguousarray(np.asarray(x, np.float32).T), "cmat": cm}
        m.update(tabs[ns])
        for n, _ in MK.W_SPECS:
            m[n] = weights[n]
        in_maps.append(m)
    res = run_bass_kernel_spmd(nc, in_maps, core_ids=list(range(len(in_maps))))
    return [np.ascontiguousarray(r["yT"].T) for r in res.results]


def kernel(x_prompt, x_sample, **weights):
    weights = {k: np.ascontiguousarray(np.asarray(v, np.float32)) for k, v in weights.items()}
    xp = np.asarray(x_prompt, np.float32)
    xs = np.asarray(x_sample, np.float32)
    T = 8192
    core_x = [xp[0], xp[1]] + [xs[4 * i:4 * i + 4].reshape(T, 1024) for i in range(4)]
    nseg = [1, 1, 4, 4, 4, 4]
    core_x += [core_x[5], core_x[5]]
    nseg += [4, 4]
    outs = run_cores(T, FULL_STEPS, core_x, nseg, weights)
    y_prompt = np.stack([outs[0], outs[1]], 0)
    y_sample = np.concatenate([outs[2 + i].reshape(4, 2048, 1024) for i in range(4)], 0)
    return (y_prompt, y_sample)
```

```python
import contextlib
import numpy as np
import concourse.bass as bass
import concourse.mybir as mybir
from concourse.bass_utils import run_bass_kernel_spmd

F32 = mybir.dt.float32
BF16 = mybir.dt.bfloat16
AF = mybir.ActivationFunctionType
ALU = mybir.AluOpType
AX = mybir.AxisListType
EPS = 1e-6


class _Op:
    __slots__ = ("eng", "fn", "waits", "signal", "idx", "chan", "cidx", "sigval")

    def __init__(self, eng, fn, chan):
        self.eng = eng
        self.fn = fn
        self.chan = chan
        self.waits = []
        self.signal = False
        self.idx = -1
        self.cidx = -1
        self.sigval = 0


class Sched:
    ENG = ("pe", "act", "dve", "pool", "sp")

    def __init__(self, nc):
        self.nc = nc
        self.ops = {e: [] for e in self.ENG}
        self.res = {}
        self.waited = {e: {} for e in self.ENG}
        self.chan_last = {}
        self.chan_n = {}
        self.chan_phase = {}
        self.phase_id = 0

    def _wait(self, op, d):
        eng = op.eng
        if d is op:
            return
        if d.chan is not None:
            ek, val = ("c", d.chan), d.cidx
        else:
            if d.eng == eng and eng == "pe":
                return
            ek, val = d.eng, d.idx
        w = self.waited[eng]
        if w.get(ek, -1) >= val:
            return
        w[ek] = val
        op.waits.append(d)
        d.signal = True

    def add(self, eng, fn, reads=(), writes=(), chan=None):
        if chan is not None:
            chan = (self.phase_id, chan)
        op = _Op(eng, fn, chan)
        deps = []
        res = self.res
        for k in reads:
            st = res.get(k)
            if st is not None and st[0] is not None:
                deps.append(st[0])
        for k in writes:
            st = res.get(k)
            if st is not None:
                if st[0] is not None:
                    deps.append(st[0])
                deps.extend(st[1])
        if chan is not None:
            prev = self.chan_last.get(chan)
            if prev is not None:
                deps.append(prev)
            op.cidx = self.chan_n.get(chan, 0)
            if op.cidx == 0:
                self.chan_phase[chan] = self.phase_id
            assert self.chan_phase[chan] == self.phase_id, chan
            self.chan_n[chan] = op.cidx + 1
            self.chan_last[chan] = op
            op.signal = True
        op.idx = len(self.ops[eng])
        for d in deps:
            self._wait(op, d)
        self.ops[eng].append(op)
        for k in writes:
            res[k] = [op, []]
        for k in reads:
            st = res.get(k)
            if st is None:
                res[k] = [None, [op]]
            elif st[0] is not op:
                st[1].append(op)
        return op

    def barrier(self):
        lasts = []
        for e in self.ENG:
            for o in reversed(self.ops[e]):
                if o.fn is not None and o.chan is None:
                    lasts.append(o)
                    break
        lasts.extend(self.chan_last.values())
        for e in self.ENG:
            op = _Op(e, None, None)
            op.idx = len(self.ops[e])
            for d in lasts:
                self._wait(op, d)
            self.ops[e].append(op)
        self.res = {}
        self.phase_id += 1

    def pe(self, fn, reads=(), writes=()):
        return self.add("pe", fn, reads, writes)

    def act(self, fn, reads=(), writes=()):
        return self.add("act", fn, reads, writes)

    def dve(self, fn, reads=(), writes=()):
        return self.add("dve", fn, reads, writes)

    def pool(self, fn, reads=(), writes=()):
        return self.add("pool", fn, reads, writes)

    def dma(self, out, in_, reads=(), writes=(), chan=None, eng="sp", **kw):
        assert chan is not None
        return self.add(eng, lambda e: e.dma_start(out=out, in_=in_, **kw), reads, writes, chan=chan)

    def emit(self, stack):
        nc = self.nc
        esem = {}
        for e in self.ENG:
            if e != "sp":
                esem[e] = stack.enter_context(nc.semaphore("s_" + e))
        csem = {}
        cbase = {}
        pool = []
        by_phase = {}
        for c, ph in self.chan_phase.items():
            by_phase.setdefault(ph, []).append(c)
        nsem = 0
        for ph in sorted(by_phase):
            used = []
            for c in by_phase[ph]:
                if pool:
                    sv = pool.pop()
                else:
                    sv = [stack.enter_context(nc.semaphore("c%d" % nsem)), 0]
                    nsem += 1
                csem[c] = sv[0]
                cbase[c] = sv[1]
                sv[1] += 16 * self.chan_n[c]
                used.append(sv)
            pool.extend(used)
        self.nsem = nsem
        for e in self.ENG:
            cnt = 0
            for op in self.ops[e]:
                if op.chan is not None:
                    op.sigval = cbase[op.chan] + 16 * (op.cidx + 1)
                elif op.signal:
                    cnt += 1
                    op.sigval = cnt

        def run(e, eng):
            for op in self.ops[e]:
                for d in op.waits:
                    sem = csem[d.chan] if d.chan is not None else esem[d.eng]
                    eng.wait_ge(sem, d.sigval)
                if op.fn is None:
                    continue
                ins = op.fn(eng)
                if op.chan is not None:
                    ins.then_inc(csem[op.chan], 16)
                elif op.signal:
                    ins.then_inc(esem[e], 1)

        with nc.Block() as block:
            @block.sync
            def _(eng):
                run("sp", eng)

            @block.tensor
            def _(eng):
                run("pe", eng)

            @block.scalar
            def _(eng):
                run("act", eng)

            @block.vector
            def _(eng):
                run("dve", eng)

            @block.gpsimd
            def _(eng):
                run("pool", eng)


def mmg(items):
    n = len(items)

    def f(e):
        ins = None
        for i, (o, l, r) in enumerate(items):
            ins = e.matmul(o, lhsT=l, rhs=r, start=(i == 0), stop=(i == n - 1))
        return ins
    return f


def mm1(o, l, r):
    return lambda e: e.matmul(o, lhsT=l, rhs=r, start=True, stop=True)


def tr(o, i, ident):
    return lambda e: e.transpose(o, i, ident)


def actf(o, i, func, bias=None, scale=None, accum=None):
    kw = {}
    if bias is not None:
        kw["bias"] = bias
    if scale is not None:
        kw["scale"] = scale
    if accum is not None:
        kw["accum_out"] = accum
    return lambda e: e.activation(out=o, in_=i, func=func, **kw)


def tt(o, a, b, op):
    return lambda e: e.tensor_tensor(out=o, in0=a, in1=b, op=op)


def ts(o, a, s1, op0, s2=None, op1=None):
    if op1 is None:
        return lambda e: e.tensor_scalar(out=o, in0=a, scalar1=s1, scalar2=None, op0=op0)
    return lambda e: e.tensor_scalar(out=o, in0=a, scalar1=s1, scalar2=s2, op0=op0, op1=op1)


def stt(o, a, s, b, op0, op1):
    return lambda e: e.scalar_tensor_tensor(out=o, in0=a, scalar=s, in1=b, op0=op0, op1=op1)


def cp(o, i):
    return lambda e: e.tensor_copy(out=o, in_=i)


def recip(o, i):
    return lambda e: e.reciprocal(out=o, in_=i)


def mset(o, v):
    return lambda e: e.memset(o, v)


C_ID, C_ONES, C_BD32, C_TRIU, C_TRIL, C_SL, C_DFW, C_DBW, C_QDF, C_QDB, C_SEL, C_KD = range(12)
NCM = 12


def host_cmat():
    s = np.arange(128)[:, None].astype(np.float32)
    l = np.arange(128)[None, :].astype(np.float32)
    m = np.zeros((NCM, 128, 128), np.float32)
    m[C_ID] = np.eye(128)
    m[C_ONES] = 1.0
    m[C_BD32] = (np.arange(128)[:, None] // 32 == np.arange(128)[None, :] // 32)
    m[C_TRIU] = (s <= l)
    m[C_TRIL] = (s >= l)
    m[C_SL] = (s > l)
    m[C_DFW] = np.maximum(l - s, 0)
    m[C_DBW] = np.maximum(s - l, 0)
    m[C_QDF] = np.broadcast_to(l + 1.0, (128, 128))
    m[C_QDB] = np.broadcast_to(128.0 - l, (128, 128))
    m[C_SEL][64, :] = 1.0
    m[C_KD][:, 0] = 127.0 - np.arange(128)
    m[C_KD][:, 1] = np.arange(128)
    m[C_KD][:, 2] = 128.0
    return np.ascontiguousarray(m.transpose(1, 0, 2))


class Ring:
    def __init__(self, items):
        self.items = items
        self.i = 0

    def next(self):
        it = self.items[self.i % len(self.items)]
        self.i += 1
        return it


class MK:
    W_SPECS = [
        ("norm_mix", (4, 1024)), ("norm_ffn", (4, 1024)), ("norm_final", (1024,)),
        ("ab_w_in", (2, 1024, 2832)), ("ab_gate_bias", (2, 16)), ("attn_q_norm", (2, 64)),
        ("attn_k_norm", (2, 64)), ("mlstm_out_norm", (2, 512)), ("ab_w_out", (2, 1024, 1024)),
        ("ret_w_in", (2, 1024, 6144)), ("ret_decay_logit", (2, 2, 4)), ("ret_out_norm", (2, 2048)),
        ("ret_w_out", (2, 2048, 1024)), ("ffn_w_up", (4, 1024, 5632)), ("ffn_conv_w", (4, 3, 2816)),
        ("ffn_conv_b", (4, 2816)), ("ffn_w_down", (4, 2816, 1024)),
    ]

    def __init__(self, T, steps):
        self.T = T
        self.NT = T // 512
        self.NCH = T // 128
        self.steps = steps
        self.nc = nc = bass.Bass("TRN2", target_bir_lowering=False)
        self.S = Sched(nc)
        self._uid = 0
        NT, NCH = self.NT, self.NCH
        di = lambda n, s, dt=F32: nc.dram_tensor(n, list(s), dt, kind="ExternalInput").ap()
        ds = lambda n, s, dt: nc.dram_tensor(n, list(s), dt, kind="Internal").ap()
        self.xT = di("xT", (1024, T))
        self.w = {n: di(n, s) for n, s in self.W_SPECS}
        self.cmat_d = di("cmat", (128, NCM, 128))
        self.ropeA = di("ropeA", (2, 128, T))
        self.ropeR = di("ropeR", (2, 128, T))
        self.mb_d = di("maskb", (128, NT * NT))
        self.carry_d = di("carry", (128, 2 * NCH))
        self.flb_d = di("flb", (128, NT * 2))
        self.yT = nc.dram_tensor("yT", [1024, T], F32, kind="ExternalOutput").ap()
        self.XM = ds("XM", (1024, T), F32)
        self.XR = ds("XR", (1024, T), F32)
        self.ACTS = ds("ACTS", (2816, T), BF16)
        self.MIX = ds("MIX", (2048, T), BF16)
        self.QA = ds("QA", (8, 64, T), BF16)
        self.KA = ds("KA", (128, T), BF16)
        self.VA = ds("VA", (T, 130), BF16)
        self.MQ = ds("MQ", (4, 128, T), BF16)
        self.MKs = ds("MKs", (4, 128, T), BF16)
        self.MV = ds("MV", (T, 516), BF16)
        self.SG = ds("SG", (T, 512), F32)
        self.GT = ds("GT", (T, 16), F32)
        self.HF = ds("HF", (T, 512), F32)
        self.HB = ds("HB", (T, 512), F32)
        self.YB = ds("YB", (T, 2048), F32)
        self.RQ = ds("RQ", (8, 128, T), BF16)
        self.RK = ds("RK", (8, 128, T), BF16)
        self.RV = ds("RV", (T, 2048), BF16)
        self.RG = ds("RG", (T, 2048), F32)
        self.YF = ds("YF", (T, 2048), F32)

    def un(self, n):
        self._uid += 1
        return "%s_%d" % (n, self._uid)

    def sb(self, name, shape, dt):
        return self._ph.enter_context(self.nc.sbuf_tensor(self.un(name), list(shape), dt))

    @contextlib.contextmanager
    def phase(self, psum=True):
        self.S.barrier()
        with contextlib.ExitStack() as ph:
            self._ph = ph
            if psum:
                nc = self.nc
                self.ps = [ph.enter_context(nc.psum_tensor(self.un("ps%d" % i), [128, 512], F32)) for i in range(6)]
                self.psb = [ph.enter_context(nc.psum_tensor(self.un("psb%d" % i), [128, 1024], BF16)) for i in range(2)]
            yield
        self._ph = self._gl

    def ring(self, name, n, shape, dt):
        items = []
        for i in range(n):
            t = self.sb(name, shape, dt)
            items.append((t, self.un(name)))
        return Ring(items)

    def pipeline(self, n, stage1, stage2, la):
        for i in range(n + la):
            if i < n:
                stage1(i)
            if i >= la:
                stage2(i - la)

    def rr(self, engs):
        self._rr = getattr(self, "_rr", 0) + 1
        return engs[self._rr % len(engs)]

    def build(self):
        nc, S = self.nc, self.S
        with contextlib.ExitStack() as gl:
            self._gl = gl
            self._ph = gl
            self.pk = ["ps%d" % i for i in range(6)]
            self.cm = self.sb("cm", [128, NCM, 128], F32)
            self.cmb = self.sb("cmb", [128, 3, 128], BF16)
            S.dma(self.cm[:], self.cmat_d, writes=["cm"], chan="cm")
            S.dve(cp(self.cmb[:], self.cm[:, 0:3, :]), reads=["cm"], writes=["cmb"])
            self.ident_f = self.cm[:, C_ID, :]
            self.ones_f = self.cm[:, C_ONES, :]
            self.ident_b = self.cmb[:, C_ID, :]
            self.ones_b = self.cmb[:, C_ONES, :]
            self.bd32_b = self.cmb[:, C_BD32, :]
            self.carry = self.sb("carry", [128, 2, self.NCH], F32)
            S.dma(self.carry[:], self.carry_d.rearrange("p (d c) -> p d c", d=2), writes=["carry"], chan="carry")
            self.mb = self.sb("mb", [128, self.NT, self.NT], F32)
            S.dma(self.mb[:], self.mb_d.rearrange("p (a b) -> p a b", a=self.NT), writes=["mb"], chan="mb")
            self.flb = self.sb("flb", [128, self.NT * 2], F32)
            S.dma(self.flb[:], self.flb_d, writes=["flb"], chan="flb")
            cur = self.xT
            for st in self.steps:
                kind = st[0]
                if kind == "ab":
                    j, L = st[1], st[2]
                    self.ab_inproj(cur, j, L)
                    self.attention()
                    self.mlstm(j)
                    self.proj_resid(self.MIX, 8, self.w["ab_w_out"][j], cur, self.XM)
                    cur = self.XM
                elif kind == "ret":
                    j, L = st[1], st[2]
                    self.ret_inproj(cur, j, L)
                    if not DBG.get("noscan"):
                        self.retention(j)
                    self.proj_resid(self.MIX, 16, self.w["ret_w_out"][j], cur, self.XM)
                    cur = self.XM
                elif kind == "ffn":
                    L = st[1]
                    self.ffn_up(cur, L)
                    self.proj_resid(self.ACTS, 22, self.w["ffn_w_down"][L], cur, self.XR)
                    cur = self.XR
                elif kind == "final":
                    self.final_norm(cur)
                    cur = None
                elif kind == "copy":
                    self.copy_out(cur)
                    cur = None
            S.barrier()
            S.emit(gl)
        return nc

    def x3(self, X):
        return X.rearrange("(c p) t -> p c t", p=128)

    def load_gain(self, vec_ap, name):
        g = self.sb(name, [128, 8], F32)
        self.S.dma(g[:], vec_ap.rearrange("(c p) -> p c", p=128), writes=[name], chan=name, allow_slow_non_contiguous=True)
        return g

    def wkeys(self, dkey, col0, n, bw=1024):
        return ["%s#%d" % (dkey, b_) for b_ in range(col0 // bw, (col0 + n - 1) // bw + 1)]

    def prep_weight(self, dst, dkey, src, KC, blocks, gain=None, gkey=None, bw=1024, order=None, reserve=0):
        S = self.S
        nstg = 2
        while nstg < 6 and self.nc.sbuf_bytes_remaining >= (nstg + 1) * 4096 + reserve:
            nstg += 1
        stg = self.ring("wstg", nstg, [128, 1024], F32)
        pw = min(bw, 1024)
        pieces = []
        for (d0, nd, s0, ns, vf) in blocks:
            o = 0
            while o < ns:
                n_ = min(pw - (d0 + o) % pw, ns - o)
                pieces.append((d0 + o, n_, s0 + o))
                o += n_
        if order is not None:
            pieces.sort(key=lambda p: (order.index(p[0] // bw) if (p[0] // bw) in order else 999, p[0]))
        for (d0, n_, s0) in pieces:
            bk = "%s#%d" % (dkey, d0 // bw)
            for c in range(KC):
                t, k = stg.next()
                S.dma(t[:, 0:n_], src[c * 128:(c + 1) * 128, s0:s0 + n_], writes=[k], chan=k)
                iv = t[:, 0:n_]
                ov = dst[:, c, d0:d0 + n_]
                eng = self.rr(["dve", "act", "dve", "act", "pool"])
                rd = [k] + ([gkey] if gain is not None else [])
                if gain is None:
                    if eng == "act":
                        S.act(actf(ov, iv, AF.Copy), reads=rd, writes=[bk])
                    else:
                        S.add(eng, cp(ov, iv), reads=rd, writes=[bk])
                else:
                    if eng == "act":
                        S.act(actf(ov, iv, AF.Copy, scale=gain[:, c:c + 1]), reads=rd, writes=[bk])
                    else:
                        S.add(eng, ts(ov, iv, gain[:, c:c + 1], ALU.mult), reads=rd, writes=[bk])

    def norm_load(self, src3, t0, n, ent):
        xt, kx = ent
        self.S.dma(xt[:, :, 0:n], src3[:, :, t0:t0 + n], writes=[kx], chan=kx)

    def norm_tile(self, src3, t0, n, bufs, D_feat=1024, load=True, part=0):
        S = self.S
        xt, kx = bufs["xt"]
        sq, ksq = bufs["sq"]
        rs, krs = bufs["rs"]
        hT, kh = bufs["hT"]
        ssp, kss = bufs["ss"]
        if load:
            S.dma(xt[:, :, 0:n], src3[:, :, t0:t0 + n], writes=[kx], chan=kx)
        if part in (0, 1):
            S.act(actf(sq[:, :, 0:n], xt[:, :, 0:n], AF.Square), reads=[kx], writes=[ksq])
        if part == 1:
            return None
        S.pe(mmg([(ssp[:, 0:n], self.ones_b, sq[:, c, 0:n]) for c in range(8)]), reads=[ksq, "cmb"], writes=[kss])
        S.act(actf(rs[:, 0:n], ssp[:, 0:n], AF.Sqrt, bias=EPS, scale=1.0 / D_feat), reads=[kss], writes=[krs])
        S.dve(recip(rs[:, 0:n], rs[:, 0:n]), reads=[krs], writes=[krs])
        S.dve(tt(hT[:, 0:4, 0:n], xt[:, 0:4, 0:n], rs[:, 0:n].unsqueeze(1).to_broadcast([128, 4, n]), ALU.mult),
              reads=[kx, krs], writes=[kh + "a"])
        S.pool(tt(hT[:, 4:8, 0:n], xt[:, 4:8, 0:n], rs[:, 0:n].unsqueeze(1).to_broadcast([128, 4, n]), ALU.mult),
               reads=[kx, krs], writes=[kh + "b"])
        return [kh + "a", kh + "b"]

    def norm_pipeline(self, X3, xt, sq, rs, hT):
        NT = self.NT
        st = {"nxt": xt.next()}
        self.norm_load(X3, 0, 512, st["nxt"])

        def prep(j, part=0):
            if part in (0, 1):
                st["bufs"] = {"xt": st["nxt"], "sq": sq.next(), "rs": rs.next(), "hT": hT.next(), "ss": (self.ps[0], "ps0")}
            bufs = st["bufs"]
            hk = self.norm_tile(X3, j * 512, 512, bufs, load=False, part=part)
            if part == 1:
                return None
            if j + 1 < NT:
                st["nxt"] = xt.next()
                self.norm_load(X3, (j + 1) * 512, 512, st["nxt"])
            return bufs, hk
        return prep

    def norm_bufs(self, nbuf=1, n=512):
        xt = self.ring("xt", nbuf, [128, 8, n], F32)
        sq = self.ring("sq", 1, [128, 8, n], BF16)
        rs = self.ring("rs", 2, [128, n], F32)
        hT = self.ring("hT", 2, [128, 8, n], BF16)
        return xt, sq, rs, hT

    def proj_resid(self, A, KC, w_d, Xin, Xout):
        S, NT = self.S, self.NT
        with self.phase():
            w = self.sb("wpr", [128, KC, 1024], BF16)
            self.prep_weight(w, "wpr", w_d, KC, [(0, 1024, 0, 1024, None)], bw=512, reserve=(KC * 2 + 32 + 4) * 1024)
            ar = self.ring("a", 2, [128, KC, 512], BF16)
            xr = self.ring("x", 2, [128, 8, 512], F32)
            A3 = A[0:KC * 128, :].rearrange("(c p) t -> p c t", p=128)
            Xi3, Xo3 = self.x3(Xin), self.x3(Xout)
            psr = Ring([(self.ps[i], self.pk[i]) for i in range(4)])
            def pr_load(j):
                at, ka = ar.next()
                xt, kx = xr.next()
                S.dma(at[:], A3[:, :, j * 512:(j + 1) * 512], writes=[ka], chan=ka)
                S.dma(xt[:], Xi3[:, :, j * 512:(j + 1) * 512], writes=[kx], chan=kx)
                return at, ka, xt, kx
            nxt = pr_load(0)
            for j in range(NT):
                t0 = j * 512
                at, ka, xt, kx = nxt
                if j + 1 < NT:
                    nxt = pr_load(j + 1)
                for d in range(8):
                    p, kp = psr.next()
                    S.pe(mmg([(p[:], w[:, c, d * 128:(d + 1) * 128], at[:, c, :]) for c in range(KC)]),
                         reads=[ka] + self.wkeys("wpr", d * 128, 128, 512), writes=[kp])
                    S.dve(tt(xt[:, d, :], p[:], xt[:, d, :], ALU.add), reads=[kp, kx], writes=[kx])
                S.dma(Xo3[:, :, t0:t0 + 512], xt[:], reads=[kx], writes=["Xo"], chan=kx + "s")

    def final_norm(self, X):
        S, NT = self.S, self.NT
        with self.phase():
            g = self.load_gain(self.w["norm_final"], "gfin")
            xt, sq, rs, hT = self.norm_bufs(nbuf=2)
            yr = self.ring("y", 2, [128, 8, 512], F32)
            X3, Y3 = self.x3(X), self.x3(self.yT)
            nxt = xt.next()
            self.norm_load(X3, 0, 512, nxt)
            for j in range(NT):
                t0 = j * 512
                x_, kx = nxt
                if j + 1 < NT:
                    nxt = xt.next()
                    self.norm_load(X3, t0 + 512, 512, nxt)
                q_, kq = sq.next()
                r_, kr = rs.next()
                y_, ky = yr.next()
                S.act(actf(q_[:], x_[:], AF.Square), reads=[kx], writes=[kq])
                S.pe(mmg([(self.ps[0][:], self.ones_b, q_[:, c, :]) for c in range(8)]), reads=[kq, "cmb"], writes=["ps0"])
                S.act(actf(r_[:], self.ps[0][:], AF.Sqrt, bias=EPS, scale=1.0 / 1024), reads=["ps0"], writes=[kr])
                S.dve(recip(r_[:], r_[:]), reads=[kr], writes=[kr])
                for c in range(8):
                    S.dve(stt(y_[:, c, :], x_[:, c, :], g[:, c:c + 1], r_[:], ALU.mult, ALU.mult),
                          reads=[kx, kr, "gfin"], writes=[ky])
                S.dma(Y3[:, :, t0:t0 + 512], y_[:], reads=[ky], writes=["Y"], chan=ky + "s")

    def copy_out(self, X):
        S, NT = self.S, self.NT
        with self.phase():
            xr = self.ring("x", 2, [128, 8, 512], F32)
            X3, Y3 = self.x3(X), self.x3(self.yT)
            for j in range(NT):
                x_, kx = xr.next()
                S.dma(x_[:], X3[:, :, j * 512:(j + 1) * 512], writes=[kx], chan=kx)
                S.dma(Y3[:, :, j * 512:(j + 1) * 512], x_[:], reads=[kx], writes=["Y"], chan=kx + "s")

    def ffn_up(self, X, L):
        S, NT = self.S, self.NT
        nc = self.nc
        NB = 2 * (NT - 1)
        with self.phase():
            w = self.sb("wup", [128, 8, 5632], BF16)
            g = self.load_gain(self.w["norm_ffn"][L], "gffn")
            cw = self.sb("cw", [128, 22, 3], F32)
            cb = self.sb("cb", [128, 22], F32)
            for k3 in range(3):
                S.dma(cw[:, :, k3], self.w["ffn_conv_w"][L, k3].rearrange("(f p) -> p f", p=128), writes=["cw"], chan="cw", allow_slow_non_contiguous=True)
            S.dma(cb[:], self.w["ffn_conv_b"][L].rearrange("(f p) -> p f", p=128), writes=["cb"], chan="cb", allow_slow_non_contiguous=True)
            self.prep_weight(w, "wup", self.w["ffn_w_up"][L], 8,
                             [(i * 2048, min(2048, 5632 - i * 2048), i * 2048, min(2048, 5632 - i * 2048), None) for i in range(3)],
                             gain=g, gkey="gffn", order=[0, 2, 3, 1, 4, 5], reserve=106 * 1024)
            xt, sq, rs, hT = self.norm_bufs()
            X3 = self.x3(X)
            gh = self.sb("gh", [128, 22, NT, 2], F32)
            S.pool(mset(gh[:], 0.0), writes=["gh"])
            if NB > 0:
                xb = self.sb("xb", [128, 8, NT - 1, 2], F32)
                sqb = self.sb("sqb", [128, 8, NB], BF16)
                rsb = self.sb("rsb", [128, NB], F32)
                hb = self.sb("hb", [128, 8, NB], BF16)
                for c in range(8):
                    src = X3[:, c, 511:511 + 512 * (NT - 1)].rearrange("p (b r) -> p b r", r=512)[:, :, 0:2]
                    S.dma(xb[:, c, :, :], src, writes=["xb%d" % c], chan="xb", allow_slow_non_contiguous=True)
                xbf = xb[:].rearrange("p c b e -> p c (b e)")
                xk = ["xb%d" % c for c in range(8)]
                S.act(actf(sqb[:], xbf, AF.Square), reads=xk, writes=["sqb"])
                S.pe(mmg([(self.ps[0][:, 0:NB], self.ones_b, sqb[:, c, :]) for c in range(8)]), reads=["sqb", "cmb"], writes=["ps0"])
                S.act(actf(rsb[:], self.ps[0][:, 0:NB], AF.Sqrt, bias=EPS, scale=1.0 / 1024), reads=["ps0"], writes=["rsb"])
                S.dve(recip(rsb[:], rsb[:]), reads=["rsb"], writes=["rsb"])
                S.dve(tt(hb[:], xbf, rsb[:].unsqueeze(1).to_broadcast([128, 8, NB]), ALU.mult), reads=xk + ["rsb"], writes=["hb"])
                pr = Ring([(self.ps[1], "ps1"), (self.ps[2], "ps2")])
                for f in range(22):
                    p, kp = pr.next()
                    S.pe(mmg([(p[:, 0:NB], w[:, c, 2816 + f * 128:2816 + (f + 1) * 128], hb[:, c, :]) for c in range(8)]),
                         reads=["hb"] + self.wkeys("wup", 2816 + f * 128, 128), writes=[kp])
                    pv = p[:, 0:NB].rearrange("p (b e) -> p b e", e=2)
                    S.dve(cp(gh[:, f, 1:NT, 0], pv[:, :, 0]), reads=[kp], writes=["gh"])
                    S.dve(cp(gh[:, f, 0:NT - 1, 1], pv[:, :, 1]), reads=[kp], writes=["gh"])
            ghf = gh[:].rearrange("p f j e -> p f (j e)")
            S.dve(tt(ghf, ghf, self.flb[:].unsqueeze(1).to_broadcast([128, 22, NT * 2]), ALU.mult), reads=["gh", "flb"], writes=["gh"])
            actr = self.ring("act", 2, [128, 22, 512], BF16)
            gbr = self.ring("gb", 2, [128, 514], F32)
            tr_ = self.ring("tc", 2, [128, 512], F32)
            pu = Ring([(self.ps[1], "ps1"), (self.ps[2], "ps2"), (self.ps[5], "ps5")])
            pg = Ring([(self.ps[3], "ps3"), (self.ps[4], "ps4")])
            A3 = self.ACTS.rearrange("(f p) t -> p f t", p=128)
            prep = self.norm_pipeline(X3, xt, sq, rs, hT)
            cur = prep(0)
            for j in range(NT):
                t0 = j * 512
                bufs, hk = cur
                h_ = bufs["hT"][0]
                a_, ka = actr.next()
                for f in range(22):
                    if f == 1 and j + 1 < NT:
                        prep(j + 1, 1)
                    if f == 6 and j + 1 < NT:
                        cur = prep(j + 1, 2)
                    u, ku = pu.next()
                    gp, kg = pg.next()
                    S.pe(mmg([(u[:], w[:, c, f * 128:(f + 1) * 128], h_[:, c, :]) for c in range(8)]), reads=hk + self.wkeys("wup", f * 128, 128), writes=[ku])
                    S.pe(mmg([(gp[:], w[:, c, 2816 + f * 128:2816 + (f + 1) * 128], h_[:, c, :]) for c in range(8)]), reads=hk + self.wkeys("wup", 2816 + f * 128, 128), writes=[kg])
                    gb, kb = gbr.next()
                    tc, kt = tr_.next()
                    S.act(actf(gb[:, 1:513], gp[:], AF.Copy), reads=[kg], writes=[kb + "m"])
                    S.dve(cp(gb[:, 0:514:513], gh[:, f, j, :]), reads=["gh"], writes=[kb + "h"])
                    S.act(actf(tc[:], gp[:], AF.Identity, bias=cb[:, f:f + 1], scale=cw[:, f, 1:2]), reads=[kg, "cw", "cb"], writes=[kt])
                    S.dve(stt(tc[:], gb[:, 0:512], cw[:, f, 0:1], tc[:], ALU.mult, ALU.add), reads=[kb + "m", kb + "h", kt, "cw"], writes=[kt])
                    S.dve(stt(tc[:], gb[:, 2:514], cw[:, f, 2:3], tc[:], ALU.mult, ALU.add), reads=[kb + "m", kb + "h", kt, "cw"], writes=[kt])
                    S.act(actf(tc[:], tc[:], AF.Gelu), reads=[kt], writes=[kt])
                    S.dve(tt(a_[:, f, :], tc[:], u[:], ALU.mult), reads=[kt, ku], writes=[ka])
                S.dma(A3[:, :, t0:t0 + 512], a_[:], reads=[ka], writes=["ACTS"], chan=ka + "s")

    def ab_inproj(self, X, j, L):
        S, NT, nc = self.S, self.NT, self.nc
        with self.phase():
            w = self.sb("wab", [128, 8, 2832], BF16)
            g = self.load_gain(self.w["norm_mix"][L], "gmix")
            stg = self.ring("wstg", 2, [128, 2832], F32)
            src = self.w["ab_w_in"][j]

            def hsplit(ap, nh, half):
                return ap.rearrange("p (h d) -> p h d", d=64)[:, :, half * 32:(half + 1) * 32]

            def h32(ap, nh):
                return ap.rearrange("p (h d) -> p h d", d=32)

            for c in range(8):
                t, k = stg.next()
                S.dma(t[:], src[c * 128:(c + 1) * 128, :], writes=[k], chan=k)
                gc = g[:, c:c + 1]
                engs = ["dve", "pool"]
                for gi in range(2):
                    for half in range(2):
                        S.add(self.rr(engs), ts(h32(w[:, c, gi * 256 + half * 128: gi * 256 + half * 128 + 128], 4),
                                                hsplit(t[:, gi * 256:(gi + 1) * 256], 4, half), gc, ALU.mult),
                              reads=[k, "gmix"], writes=["wab"])
                for half in range(2):
                    S.add(self.rr(engs), ts(h32(w[:, c, 512 + half * 64: 512 + half * 64 + 64], 2),
                                            hsplit(t[:, 512:640], 2, half), gc, ALU.mult), reads=[k, "gmix"], writes=["wab"])
                for (d0, s0, n) in [(1664, 640, 128), (640, 768, 1024), (1808, 1792, 1024), (1792, 2816, 16)]:
                    S.add(self.rr(engs), ts(w[:, c, d0:d0 + n], t[:, s0:s0 + n], gc, ALU.mult), reads=[k, "gmix"], writes=["wab"])
            gq = self.sb("gq", [128, 2], F32)
            gk = self.sb("gk", [128, 2], F32)
            for r in range(4):
                for half in range(2):
                    S.dma(gq[r * 32:(r + 1) * 32, half:half + 1], self.w["attn_q_norm"][j, half * 32:(half + 1) * 32].unsqueeze(1),
                          writes=["gq%d%d" % (r, half)], chan="gqld", allow_slow_non_contiguous=True)
                    S.dma(gk[r * 32:(r + 1) * 32, half:half + 1], self.w["attn_k_norm"][j, half * 32:(half + 1) * 32].unsqueeze(1),
                          writes=["gk%d%d" % (r, half)], chan="gkld", allow_slow_non_contiguous=True)
            gqk = ["gq%d%d" % (r, h) for r in range(4) for h in range(2)]
            gkk = ["gk%d%d" % (r, h) for r in range(4) for h in range(2)]
            S.dve(ts(gq[:], gq[:], 0.125, ALU.mult), reads=gqk, writes=["gq"])
            gbias = self.sb("gbias", [128, 16], F32)
            S.dma(gbias[:], self.w["ab_gate_bias"][j:j + 1, :].to_broadcast([128, 16]), writes=["gbias"], chan="gbias")
            xt, sq, rs, hT = self.norm_bufs()
            X3 = self.x3(X)
            cs = self.ring("cs", 2, [128, 2, 512], F32)
            sqa = self.ring("sqa", 2, [128, 2, 512], BF16)
            rq = self.ring("rq", 2, [128, 512], F32)
            an = self.ring("an", 2, [128, 2, 512], F32)
            t4 = self.ring("t4", 1, [128, 4, 512], F32)
            o12 = self.ring("o12", 3, [128, 2, 512], BF16)
            mqs = self.ring("mqs", 1, [128, 4, 512], BF16)
            mks = self.ring("mks", 1, [128, 4, 512], BF16)
            vts = self.ring("vt", 2, [128, 4, 2, 65], BF16)
            mvs = self.ring("mvt", 2, [128, 4, 4, 129], BF16)
            sgs = self.ring("sg", 1, [128, 4, 512], F32)
            gts = self.ring("gt", 2, [128, 4, 16], F32)
            for (t, k) in vts.items:
                S.pool(mset(t[:], 1.0), writes=[k])
            for (t, k) in mvs.items:
                S.pool(mset(t[:], 1.0), writes=[k])
            pf = Ring([(self.ps[i], self.pk[i]) for i in (1, 2, 3)])
            ptm = Ring([(self.ps[4], "ps4"), (self.ps[5], "ps5")])
            def cs_load(jt_):
                c_, kc = cs.next()
                S.dma(c_[:], self.ropeA[:, :, jt_ * 512:(jt_ + 1) * 512].rearrange("a p t -> p a t"), writes=[kc], chan=kc)
                return c_, kc
            prep = self.norm_pipeline(X3, xt, sq, rs, hT)
            cur = prep(0)
            cnxt = cs_load(0)
            for jt in range(NT):
                t0 = jt * 512
                bufs, hk = cur
                h_ = bufs["hT"][0]
                c_, kc = cnxt
                if jt + 1 < NT:
                    cnxt = cs_load(jt + 1)

                def fm(col0, M):
                    p, kp = pf.next()
                    S.pe(mmg([(p[0:M, :], w[:, c, col0:col0 + M], h_[:, c, :]) for c in range(8)]), reads=hk + ["wab"], writes=[kp])
                    return p, kp

                for grp in range(3):
                    if grp == 1 and jt + 1 < NT:
                        prep(jt + 1, 1)
                    M = 128 if grp < 2 else 64
                    colA = grp * 256 if grp < 2 else 512
                    colB = colA + M
                    gg, ggk = (gq, ["gq"]) if grp < 2 else (gk, gkk)
                    pa, kpa = fm(colA, M)
                    pb, kpb = fm(colB, M)
                    s_, ks = sqa.next()
                    S.act(actf(s_[0:M, 0, :], pa[0:M, :], AF.Square), reads=[kpa], writes=[ks + "a"])
                    S.act(actf(s_[0:M, 1, :], pb[0:M, :], AF.Square), reads=[kpb], writes=[ks + "b"])
                    S.pe(mmg([(self.ps[0][0:M, :], self.bd32_b[0:M, 0:M], s_[0:M, 0, :]),
                              (self.ps[0][0:M, :], self.bd32_b[0:M, 0:M], s_[0:M, 1, :])]), reads=[ks + "a", ks + "b", "cmb"], writes=["ps0"])
                    r_, kr = rq.next()
                    S.act(actf(r_[0:M, :], self.ps[0][0:M, :], AF.Sqrt, bias=EPS, scale=1.0 / 64), reads=["ps0"], writes=[kr])
                    S.dve(recip(r_[0:M, :], r_[0:M, :]), reads=[kr], writes=[kr])
                    a_, kan = an.next()
                    S.dve(stt(a_[0:M, 0, :], pa[0:M, :], gg[0:M, 0:1], r_[0:M, :], ALU.mult, ALU.mult), reads=[kpa, kr] + ggk, writes=[kan + "a"])
                    S.dve(stt(a_[0:M, 1, :], pb[0:M, :], gg[0:M, 1:2], r_[0:M, :], ALU.mult, ALU.mult), reads=[kpb, kr] + ggk, writes=[kan + "b"])
                    t_, kt = t4.next()
                    S.pool(tt(t_[0:M, 0, :], a_[0:M, 0, :], c_[0:M, 0, :], ALU.mult), reads=[kan + "a", kc], writes=[kt + "0"])
                    S.pool(tt(t_[0:M, 1, :], a_[0:M, 1, :], c_[0:M, 1, :], ALU.mult), reads=[kan + "b", kc], writes=[kt + "1"])
                    S.dve(tt(t_[0:M, 2, :], a_[0:M, 1, :], c_[0:M, 0, :], ALU.mult), reads=[kan + "b", kc], writes=[kt + "2"])
                    S.pool(tt(t_[0:M, 3, :], a_[0:M, 0, :], c_[0:M, 1, :], ALU.mult), reads=[kan + "a", kc], writes=[kt + "3"])
                    o_, ko = o12.next()
                    S.pool(tt(o_[0:M, 0, :], t_[0:M, 0, :], t_[0:M, 1, :], ALU.subtract), reads=[kt + "0", kt + "1"], writes=[ko + "0"])
                    S.dve(tt(o_[0:M, 1, :], t_[0:M, 2, :], t_[0:M, 3, :], ALU.add), reads=[kt + "2", kt + "3"], writes=[ko + "1"])
                    for hl in range(M // 32):
                        for half in range(2):
                            if grp < 2:
                                dst = self.QA[grp * 4 + hl, half * 32:(half + 1) * 32, t0:t0 + 512]
                            else:
                                dst = self.KA[hl * 64 + half * 32: hl * 64 + (half + 1) * 32, t0:t0 + 512]
                            S.dma(dst, o_[hl * 32:(hl + 1) * 32, half, :], reads=[ko + str(half)], writes=["QK"], chan=ko + "s%d%d" % (hl, half))
                if jt + 1 < NT:
                    cur = prep(jt + 1, 2)
                mq_, kmq = mqs.next()
                mk_, kmk = mks.next()
                for h in range(4):
                    p, kp = fm(640 + h * 128, 128)
                    S.act(actf(mq_[:, h, :], p[:], AF.Copy), reads=[kp], writes=[kmq])
                for h in range(4):
                    p, kp = fm(1152 + h * 128, 128)
                    S.act(actf(mk_[:, h, :], p[:], AF.Copy, scale=float(128 ** -0.5)), reads=[kp], writes=[kmk])
                S.dma(self.MQ[:, :, t0:t0 + 512].rearrange("h d t -> d h t"), mq_[:], reads=[kmq], writes=["MQ"], chan=kmq + "s")
                S.dma(self.MKs[:, :, t0:t0 + 512].rearrange("h d t -> d h t"), mk_[:], reads=[kmk], writes=["MK"], chan=kmk + "s")
                vt, kv = vts.next()
                mv, kmv = mvs.next()
                sg, ksg = sgs.next()
                gt, kgt = gts.next()
                for sub in range(4):
                    hs = lambda c: h_[:, c, sub * 128:(sub + 1) * 128]
                    p1, k1 = ptm.next()
                    S.pe(mmg([(p1[:, 0:144], hs(c), w[:, c, 1664:1808]) for c in range(8)]), reads=hk + ["wab"], writes=[k1])
                    S.act(actf(vt[:, sub, :, 0:64], p1[:, 0:128].rearrange("p (h d) -> p h d", d=64), AF.Copy), reads=[k1], writes=[kv])
                    S.dve(tt(gt[:, sub, :], p1[:, 128:144], gbias[:], ALU.add), reads=[k1, "gbias"], writes=[kgt])
                    p2, k2 = ptm.next()
                    S.pe(mmg([(p2[:], hs(c), w[:, c, 1808:2320]) for c in range(8)]), reads=hk + ["wab"], writes=[k2])
                    S.dve(cp(mv[:, sub, :, 0:128], p2[:].rearrange("p (h d) -> p h d", d=128)), reads=[k2], writes=[kmv])
                    p3, k3 = ptm.next()
                    S.pe(mmg([(p3[:], hs(c), w[:, c, 2320:2832]) for c in range(8)]), reads=hk + ["wab"], writes=[k3])
                    S.act(actf(sg[:, sub, :], p3[:], AF.Sigmoid), reads=[k3], writes=[ksg])
                S.dma(self.VA[t0:t0 + 512, :].rearrange("(s p) c -> p s c", p=128), vt[:].rearrange("p s h d -> p s (h d)"), reads=[kv], writes=["VA"], chan=kv + "s")
                S.dma(self.MV[t0:t0 + 512, :].rearrange("(s p) c -> p s c", p=128), mv[:].rearrange("p s h d -> p s (h d)"), reads=[kmv], writes=["MV"], chan=kmv + "s")
                S.dma(self.SG[t0:t0 + 512, :].rearrange("(s p) c -> p s c", p=128), sg[:], reads=[ksg], writes=["SG"], chan=ksg + "s")
                S.dma(self.GT[t0:t0 + 512, :].rearrange("(s p) c -> p s c", p=128), gt[:], reads=[kgt], writes=["GT"], chan=kgt + "s")

    def attention(self):
        S, NT, NCH, T, nc = self.S, self.NT, self.NCH, self.T, self.nc
        with self.phase(psum=False):
            ph = self._ph
            ppr = Ring([(ph.enter_context(nc.psum_tensor(self.un("pp"), [128, 1024], F32)), "pp%d" % i) for i in range(2)])
            poa = (ph.enter_context(nc.psum_tensor(self.un("poa"), [128, 512], F32)), "poa")
            pob = (ph.enter_context(nc.psum_tensor(self.un("pob"), [128, 512], F32)), "pob")
            ptb = ph.enter_context(nc.psum_tensor(self.un("ptb"), [128, 1024], BF16))
            K = self.sb("Kall", [128, T], BF16)
            V = self.sb("Vall", [128, NCH, 130], BF16)
            S.dma(K[:], self.KA, writes=["K"], chan="K")
            S.dma(V[:], self.VA.rearrange("(b p) c -> p b c", p=128), writes=["V"], chan="V")
            qr = self.ring("q", 2, [128, 8, 512], BF16)
            for (t_, k_) in qr.items:
                S.dve(mset(t_[:], 0.0), writes=[k_ + "0", k_ + "1"])
            pr = self.ring("p", 3, [128, 1024], BF16)
            rcr = self.ring("rc", 2, [128, 4], F32)
            otr = self.ring("ot", 2, [128, 4, 8, 64], BF16)
            ob = self.ring("ob", 2, [128, 4, 512], BF16)
            items = [(jq, hp, kb) for jq in range(NT) for hp in range(4) for kb in range(NCH)]
            tctx, ictx = {}, {}

            def stage1(i):
                jq, hp, kb = items[i]
                t0 = jq * 512
                if hp == 0 and kb == 0:
                    q_, kq = qr.next()
                    for kvh in range(2):
                        S.dma(q_[kvh * 64:(kvh + 1) * 64, kvh * 4:(kvh + 1) * 4, :], self.QA[kvh * 4:(kvh + 1) * 4, :, t0:t0 + 512].rearrange("h d t -> d h t"),
                              writes=[kq + str(kvh)], chan=kq + str(kvh))
                    tctx[jq] = (q_, kq) + otr.next()
                q_, kq, ot, kot = tctx[jq]
                pb, kpb = ppr.next()

                def st2(e, pb=pb, q_=q_, hp=hp, kb=kb):
                    e.matmul(pb[:, 0:512], lhsT=K[:, kb * 128:(kb + 1) * 128], rhs=q_[:, hp, :], start=True, stop=True)
                    return e.matmul(pb[:, 512:1024], lhsT=K[:, kb * 128:(kb + 1) * 128], rhs=q_[:, 4 + hp, :], start=True, stop=True)
                S.pe(st2, reads=["K", kq + "0", kq + "1"], writes=[kpb])
                p_, kp_ = pr.next()
                S.act(actf(p_[:], pb[:], AF.Exp, bias=self.mb[:, jq, (kb // 4):(kb // 4) + 1]), reads=[kpb, "mb"], writes=[kp_])
                ictx[i] = (p_, kp_)

            def stage2(i):
                jq, hp, kb = items[i]
                t0 = jq * 512
                q_, kq, ot, kot = tctx[jq]
                p_, kp_ = ictx.pop(i)

                def pv8(e, p_=p_, kb=kb):
                    ins = None
                    for hh, (po, _) in enumerate((poa, pob)):
                        for sub in range(4):
                            ins = e.matmul(po[:, sub * 65:(sub + 1) * 65], lhsT=p_[:, hh * 512 + sub * 128: hh * 512 + (sub + 1) * 128],
                                           rhs=V[:, kb, hh * 65:(hh + 1) * 65], start=(kb == 0 and sub == 0), stop=(kb == NCH - 1 and sub == 3))
                    return ins
                S.pe(pv8, reads=["V", kp_], writes=["poa", "pob"])
                if kb != NCH - 1:
                    return
                for hh, (po, kpo) in enumerate((poa, pob)):
                    h = hh * 4 + hp
                    pv = po[:, 0:260].rearrange("p (s c) -> p s c", c=65)
                    r_, kr = rcr.next()
                    S.dve(recip(r_[:], pv[:, :, 64]), reads=[kpo], writes=[kr])
                    S.dve(tt(ot[:, :, h, :], pv[:, :, 0:64], r_[:].unsqueeze(2).to_broadcast([128, 4, 64]), ALU.mult), reads=[kpo, kr], writes=[kot])
                if hp != 3:
                    return
                o_, ko = ob.next()
                for sp in range(2):
                    def tr8(e, ot=ot, sp=sp):
                        ins = None
                        for hp2 in range(4):
                            for s_ in range(2):
                                slot = hp2 * 2 + s_
                                ins = e.transpose(ptb[:, slot * 128:(slot + 1) * 128],
                                                  ot[:, sp * 2 + s_, 2 * hp2:2 * hp2 + 2, :].rearrange("p h d -> p (h d)"), self.ident_b)
                        return ins
                    S.pe(tr8, reads=[kot, "cmb"], writes=["ptb"])
                    S.act(actf(o_[:, :, sp * 256:(sp + 1) * 256].rearrange("p a (s q) -> p a s q", q=128),
                               ptb[:].rearrange("p (a s q) -> p a s q", a=4, s=2), AF.Copy), reads=["ptb"], writes=[ko])
                S.dma(self.MIX[0:512, t0:t0 + 512].rearrange("(a p) t -> p a t", p=128), o_[:], reads=[ko], writes=["MIXa"], chan=ko + "s")

            self.pipeline(len(items), stage1, stage2, 1)

    def mlstm(self, j):
        S, NT, NCH, T, nc = self.S, self.NT, self.NCH, self.T, self.nc
        NG = NCH * 4
        with self.phase():
            G = self.sb("G", [128, NCH, 16], F32)
            S.dma(G[:], self.GT.rearrange("(c p) g -> p c g", p=128), writes=["G"], chan="G")
            G5 = G[:].rearrange("p c (d k h) -> p d c k h", d=2, k=2, h=4)
            gi, gf = G5[:, :, :, 0, :], G5[:, :, :, 1, :]
            shp = [128, 2, NCH, 4]
            mk = lambda n: self.sb(n, shp, F32)
            FL, Bc, A_, AMr, BLr, Mt, mt, WK, THR, KEEP = [mk(n) for n in ("FL", "Bc", "Aa", "AMr", "BLr", "Mt", "mt", "WK", "THR", "KEEP")]
            tmp = mk("tmp")
            fl2 = lambda t, d: t[:, d, :, :].rearrange("p c h -> p (c h)")
            S.act(actf(tmp[:], gf, AF.Abs), reads=["G"], writes=["tmp"])
            S.act(actf(tmp[:], tmp[:], AF.Exp, scale=-1.0), reads=["tmp"], writes=["tmp"])
            S.act(actf(tmp[:], tmp[:], AF.Ln, bias=1.0), reads=["tmp"], writes=["tmp"])
            S.dve(ts(FL[:], gf, 0.0, ALU.min), reads=["G"], writes=["FL"])
            S.dve(tt(FL[:], FL[:], tmp[:], ALU.subtract), reads=["FL", "tmp"], writes=["FL"])
            bw = min(128, NG)
            nblk = NG // bw
            am = self.sb("am", [128, 2, nblk], F32)
            dg = self.sb("dg", [128, 128], F32)
            for d in range(2):
                tri = self.cm[:, C_TRIU, :] if d == 0 else self.cm[:, C_TRIL, :]
                S.pe(mm1(self.ps[0][:, 0:NG], tri, fl2(FL, d)), reads=["FL", "cm"], writes=["ps0"])
                S.act(actf(fl2(Bc, d), self.ps[0][:, 0:NG], AF.Copy), reads=["ps0"], writes=["Bc"])
                S.pe(mm1(self.ps[1][:, 0:NG], self.ones_f, fl2(FL, d)), reads=["FL", "cm"], writes=["ps1"])
                S.act(actf(fl2(BLr, d), self.ps[1][:, 0:NG], AF.Copy), reads=["ps1"], writes=["BLr"])
                S.dve(tt(A_[:, d], gi[:, d], Bc[:, d], ALU.subtract), reads=["G", "Bc"], writes=["Aa"])
                for b in range(nblk):
                    S.pe(tr(self.ps[2][0:bw, 0:128], fl2(A_, d)[:, b * bw:(b + 1) * bw], self.ident_f), reads=["Aa", "cm"], writes=["ps2"])
                    S.dve(lambda e, d=d, b=b, ps2=self.ps[2]: e.tensor_reduce(out=am[0:bw, d, b:b + 1], in_=ps2[0:bw, 0:128], axis=AX.X, op=ALU.max),
                          reads=["ps2"], writes=["am"])
                    S.dve(ts(dg[0:bw, 0:bw], self.ident_f[0:bw, 0:bw], am[0:bw, d, b:b + 1], ALU.mult), reads=["am", "cm"], writes=["dg"])
                    S.pe(mm1(self.ps[3][:, 0:bw], self.ones_f[0:bw, :], dg[0:bw, 0:bw]), reads=["dg", "cm"], writes=["ps3"])
                    S.act(actf(fl2(AMr, d)[:, b * bw:(b + 1) * bw], self.ps[3][:, 0:bw], AF.Copy), reads=["ps3"], writes=["AMr"])
            mrun = self.sb("mrun", [128, 2, 4], F32)
            S.dve(mset(mrun[:], 0.0), writes=["mrun0", "mrun1"])
            for k in range(NCH):
                for d in range(2):
                    eng = "dve"
                    kk = "rec%d" % d
                    c = k if d == 0 else NCH - 1 - k
                    S.add(eng, ts(mt[:, d, c, :], mrun[:, d, :], self.carry[:, d, c:c + 1], ALU.mult), reads=["mrun%d" % d, "carry"], writes=[kk + "mt"])
                    S.add(eng, tt(Mt[:, d, c, :], mt[:, d, c, :], AMr[:, d, c, :], ALU.max), reads=[kk + "mt", "AMr"], writes=[kk + "Mt"])
                    S.add(eng, tt(mrun[:, d, :], Mt[:, d, c, :], BLr[:, d, c, :], ALU.add), reads=[kk + "Mt", "BLr"], writes=["mrun%d" % d])
            rk = ["rec0mt", "rec1mt", "rec0Mt", "rec1Mt"]
            S.dve(tt(KEEP[:], mt[:], Mt[:], ALU.subtract), reads=rk, writes=["KEEP"])
            S.act(actf(KEEP[:], KEEP[:], AF.Exp), reads=["KEEP"], writes=["KEEP"])
            S.dve(tt(KEEP[:].rearrange("p d c h -> p (d c) h"), KEEP[:].rearrange("p d c h -> p (d c) h"),
                     self.carry[:].rearrange("p d c -> p (d c)").unsqueeze(2).to_broadcast([128, 2 * NCH, 4]), ALU.mult), reads=["KEEP", "carry"], writes=["KEEP"])
            S.dve(tt(WK[:], A_[:], Mt[:], ALU.subtract), reads=["Aa"] + rk, writes=["WK"])
            S.act(actf(WK[:], WK[:], AF.Exp), reads=["WK"], writes=["WK"])
            S.dve(tt(THR[:], Bc[:], Mt[:], ALU.add), reads=["Bc"] + rk, writes=["THR"])
            S.act(actf(THR[:], THR[:], AF.Exp, scale=-1.0), reads=["THR"], writes=["THR"])
            maskf = self.cm[:, C_TRIU, :]
            maskb = self.cm[:, C_TRIL, :]
            qr = [self.ring("mq", 2, [128, 4, 512], BF16) for d in range(2)]
            kr = [self.ring("mk", 2, [128, 4, 512], BF16) for d in range(2)]
            vr = [self.ring("mv", 2, [128, 4, 516], BF16) for d in range(2)]
            Cst = [[self.sb("Cst", [128, 129], F32) for h in range(4)] for d in range(2)]
            Cbf = [[self.sb("Cbf", [128, 129], BF16) for h in range(4)] for d in range(2)]
            for d in range(2):
                for h in range(4):
                    S.pool(mset(Cst[d][h][:], 0.0), writes=["C%d%d" % (d, h)])
            atr = self.ring("at", 6, [128, 128], BF16)
            kwr = self.ring("kw", 6, [128, 128], BF16)
            rr_ = self.ring("r", 6, [128, 2], F32)
            hst = [self.ring("hst", 2, [128, 512], F32) for d in range(2)]
            pS = Ring([(self.ps[0], "ps0"), (self.ps[1], "ps1")])
            pH = Ring([(self.ps[2], "ps2"), (self.ps[3], "ps3")])
            pU = Ring([(self.ps[4], "ps4"), (self.ps[5], "ps5")])
            pT = Ring([(self.psb[0], "psb0"), (self.psb[1], "psb1")])
            MQ3 = self.MQ.rearrange("h d t -> d h t")
            MK3 = self.MKs.rearrange("h d t -> d h t")
            HO = [self.HF, self.HB]
            items = [(k, h, d) for k in range(NCH) for h in range(4) for d in range(2)]
            tctx, cctx, ictx = {}, {}, {}

            def geom(it):
                k, h, d = it
                c = k if d == 0 else NCH - 1 - k
                return d, c // 4, c % 4, h, c, (c // 4) * 512

            def ensure(d, jt):
                if (d, jt) in tctx or jt < 0 or jt >= NT:
                    return
                t0 = jt * 512
                q_, kq = qr[d].next()
                k_, kk = kr[d].next()
                v_, kv = vr[d].next()
                S.dma(q_[:], MQ3[:, :, t0:t0 + 512], writes=[kq], chan=kq)
                S.dma(k_[:], MK3[:, :, t0:t0 + 512], writes=[kk], chan=kk)
                S.dma(v_[:], self.MV[t0:t0 + 512, :].rearrange("(s p) c -> p s c", p=128), writes=[kv], chan=kv)
                tctx[(d, jt)] = (q_, kq, k_, kk, v_, kv)

            def stage1(i):
                d, jt, sub, h, c, t0 = geom(items[i])
                mask = maskf if d == 0 else maskb
                if h == 0:
                    ensure(d, jt)
                    if items[i][0] % 4 == 1:
                        ensure(d, jt + (1 if d == 0 else -1))
                    cctx[(d, c)] = hst[d].next()
                q_, kq, k_, kk, v_, kv = tctx[(d, jt)]
                ck = "C%d%d" % (d, h)
                qs = q_[:, h, sub * 128:(sub + 1) * 128]
                ks_ = k_[:, h, sub * 128:(sub + 1) * 128]
                wkc = WK[:, d, c, h:h + 1]
                st_, kst = pS.next()
                S.pe(mm1(st_[:, 0:128], ks_, qs), reads=[kk, kq], writes=[kst])
                at, kat = atr.next()
                S.dve(stt(at[:], st_[:, 0:128], wkc, mask, ALU.mult, ALU.mult), reads=[kst, "WK", "cm"], writes=[kat])
                ptr, kptr = pT.next()
                S.pe(tr(ptr[:, 0:128], ks_, self.ident_b), reads=[kk, "cmb"], writes=[kptr])
                kw, kkw = kwr.next()
                S.act(actf(kw[:], ptr[:, 0:128], AF.Copy, scale=wkc), reads=[kptr, "WK"], writes=[kkw])
                S.act(actf(Cbf[d][h][:], Cst[d][h][:], AF.Copy, scale=KEEP[:, d, c, h:h + 1]), reads=[ck, "KEEP"], writes=[ck + "b"])
                ictx[i] = (at, kat, kw, kkw)

            def stage2(i):
                d, jt, sub, h, c, t0 = geom(items[i])
                q_, kq, k_, kk, v_, kv = tctx[(d, jt)]
                hs_, khs = cctx[(d, c)]
                at, kat, kw, kkw = ictx.pop(i)
                ck = "C%d%d" % (d, h)
                tc0 = t0 + sub * 128
                qs = q_[:, h, sub * 128:(sub + 1) * 128]
                vs = v_[:, sub, h * 129:(h + 1) * 129]
                ph, kph = pH.next()
                S.pe(mmg([(ph[:, 0:129], at[:], vs), (ph[:, 0:129], qs, Cbf[d][h][:])]), reads=[kat, kv, kq, ck + "b"], writes=[kph])
                pu, kpu = pU.next()
                S.pe(mm1(pu[:, 0:129], kw[:], vs), reads=[kkw, kv], writes=[kpu])
                S.dve(stt(Cst[d][h][:], Cst[d][h][:], KEEP[:, d, c, h:h + 1], pu[:, 0:129], ALU.mult, ALU.add), reads=[ck, kpu, "KEEP"], writes=[ck])
                r_, kr_ = rr_.next()
                S.dve(ts(r_[:, 0:1], ph[:, 128:129], -1.0, ALU.mult, THR[:, d, c, h:h + 1], ALU.max), reads=[kph, "THR"], writes=[kr_])
                S.dve(tt(r_[:, 0:1], r_[:, 0:1], ph[:, 128:129], ALU.max), reads=[kr_, kph], writes=[kr_])
                S.dve(recip(r_[:, 1:2], r_[:, 0:1]), reads=[kr_], writes=[kr_])
                S.act(actf(hs_[:, h * 128:(h + 1) * 128], ph[:, 0:128], AF.Copy, scale=r_[:, 1:2]), reads=[kph, kr_], writes=[khs])
                if h == 3:
                    S.dma(HO[d][tc0:tc0 + 128, :], hs_[:], reads=[khs], writes=["HO"], chan=khs + "s")

            self.pipeline(len(items), stage1, stage2, 2)

        with self.phase():
            gain = self.sb("ogain", [128, 512], F32)
            S.dma(gain[:], self.w["mlstm_out_norm"][j:j + 1, :].to_broadcast([128, 512]), writes=["ogain"], chan="ogain")
            hfr = self.ring("hf", 3, [128, 512], F32)
            hbr = self.ring("hb", 3, [128, 512], F32)
            sgr = self.ring("sgl", 3, [128, 512], F32)
            ssq = self.ring("ssq", 3, [128, 8], F32)
            junk = self.ring("junk", 2, [128, 128], F32)
            ymr = self.ring("ym", 3, [128, 512], BF16)
            mxs = self.ring("mxs", 2, [128, 4, 512], BF16)
            pT = Ring([(self.psb[0], "psb0"), (self.psb[1], "psb1")])

            def ld(c):
                hf, khf = hfr.next()
                hb, khb = hbr.next()
                sg, ksg = sgr.next()
                S.dma(hf[:], self.HF[c * 128:(c + 1) * 128, :], writes=[khf], chan=khf)
                S.dma(hb[:], self.HB[c * 128:(c + 1) * 128, :], writes=[khb], chan=khb)
                S.dma(sg[:], self.SG[c * 128:(c + 1) * 128, :], writes=[ksg], chan=ksg)
                return hf, khf, hb, khb, sg, ksg
            pend = [ld(0), ld(1)] if NCH > 1 else [ld(0)]
            for c in range(NCH):
                hf, khf, hb, khb, sg, ksg = pend.pop(0)
                if c + 2 < NCH:
                    pend.append(ld(c + 2))
                sub = c % 4
                if sub == 0:
                    mx, kmx = mxs.next()
                S.dve(tt(hf[:], hf[:], hb[:], ALU.add), reads=[khf, khb], writes=[khf])
                sq_, ksq = ssq.next()
                jk, kjk = junk.next()
                S.dve(mset(sq_[:], 0.0), writes=[ksq + "a", ksq + "b"])
                for hh in range(4):
                    S.act(actf(jk[:], hf[:, hh * 128:(hh + 1) * 128], AF.Square, accum=sq_[:, hh:hh + 1]), reads=[khf], writes=[kjk, ksq + "a"])
                S.act(actf(sq_[:, 4:8], sq_[:, 0:4], AF.Sqrt, bias=EPS, scale=1.0 / 128), reads=[ksq + "a"], writes=[ksq + "b"])
                S.dve(recip(sq_[:, 4:8], sq_[:, 4:8]), reads=[ksq + "b"], writes=[ksq + "b"])
                S.dve(tt(hf[:].rearrange("p (h e) -> p h e", e=128), hf[:].rearrange("p (h e) -> p h e", e=128),
                         sq_[:, 4:8].unsqueeze(2).to_broadcast([128, 4, 128]), ALU.mult), reads=[khf, ksq + "b"], writes=[khf])
                S.pool(tt(sg[:], sg[:], gain[:], ALU.mult), reads=[ksg, "ogain"], writes=[ksg])
                ym, kym = ymr.next()
                S.dve(tt(ym[:], hf[:], sg[:], ALU.mult), reads=[khf, ksg], writes=[kym])
                ptr, kptr = pT.next()

                def tr4(e, ptr=ptr, ym=ym):
                    ins = None
                    for hh in range(4):
                        ins = e.transpose(ptr[:, hh * 128:(hh + 1) * 128], ym[:, hh * 128:(hh + 1) * 128], self.ident_b)
                    return ins
                S.pe(tr4, reads=[kym, "cmb"], writes=[kptr])
                S.act(actf(mx[:, :, sub * 128:(sub + 1) * 128], ptr[:, 0:512].rearrange("p (h e) -> p h e", e=128), AF.Copy), reads=[kptr], writes=[kmx])
                if sub == 3:
                    t0 = (c // 4) * 512
                    S.dma(self.MIX[512:1024, t0:t0 + 512].rearrange("(h p) t -> p h t", p=128), mx[:], reads=[kmx], writes=["MIXb"], chan=kmx + "s")

    def ret_inproj(self, X, j, L):
        S, NT, nc = self.S, self.NT, self.nc
        with self.phase():
            w = self.sb("wret", [128, 8, 6144], BF16)
            g = self.load_gain(self.w["norm_mix"][L], "gmix")
            self.prep_weight(w, "wret", self.w["ret_w_in"][j], 8, [(i * 2048, 2048, i * 2048, 2048, None) for i in range(3)], gain=g, gkey="gmix", reserve=92 * 1024)
            xt, sq, rs, hT = self.norm_bufs()
            X3 = self.x3(X)
            cs = self.ring("cs", 1, [128, 4, 512], F32)
            t4 = self.ring("t4", 1, [128, 4, 512], F32)
            o12 = self.ring("o12", 2, [128, 2, 512], BF16)
            vts = self.ring("rv", 1, [128, 4, 2048], BF16)
            gts = self.ring("rg", 1, [128, 2048], F32)
            pf = Ring([(self.ps[i], self.pk[i]) for i in (1, 2, 3)])
            ptm = Ring([(self.ps[4], "ps4"), (self.ps[5], "ps5")])
            def cs_load(jt_):
                c_, kc = cs.next()
                S.dma(c_[:, 0:2, :], self.ropeR[:, :, jt_ * 512:(jt_ + 1) * 512].rearrange("a p t -> p a t"), writes=[kc, kc + "k"], chan=kc)
                return c_, kc
            prep = self.norm_pipeline(X3, xt, sq, rs, hT)
            cur = prep(0)
            cnxt = cs_load(0)
            for jt in range(NT):
                t0 = jt * 512
                bufs, hk = cur
                h_ = bufs["hT"][0]
                c_, kc = cnxt
                S.act(actf(c_[:, 2:4, :], c_[:, 0:2, :], AF.Copy, scale=1.0 / 16), reads=[kc], writes=[kc + "k"])
                for qk in range(2):
                    if qk == 1 and jt + 1 < NT:
                        cur = prep(jt + 1, 2)
                    co = 0 if qk == 0 else 2
                    ck_ = [kc] if qk == 0 else [kc + "k"]
                    dstT = self.RQ if qk == 0 else self.RK
                    for h in range(4):
                        if qk == 0 and h == 1 and jt + 1 < NT:
                            prep(jt + 1, 1)
                        col = qk * 1024 + h * 256
                        pa, kpa = pf.next()
                        S.pe(mmg([(pa[:], w[:, c, col:col + 128], h_[:, c, :]) for c in range(8)]), reads=hk + self.wkeys("wret", col, 128), writes=[kpa])
                        pb, kpb = pf.next()
                        S.pe(mmg([(pb[:], w[:, c, col + 128:col + 256], h_[:, c, :]) for c in range(8)]), reads=hk + self.wkeys("wret", col + 128, 128), writes=[kpb])
                        t_, kt = t4.next()
                        S.dve(tt(t_[:, 0, :], pa[:], c_[:, co, :], ALU.mult), reads=[kpa] + ck_, writes=[kt + "0"])
                        S.dve(tt(t_[:, 1, :], pb[:], c_[:, co + 1, :], ALU.mult), reads=[kpb] + ck_, writes=[kt + "1"])
                        S.dve(tt(t_[:, 2, :], pb[:], c_[:, co, :], ALU.mult), reads=[kpb] + ck_, writes=[kt + "2"])
                        S.dve(tt(t_[:, 3, :], pa[:], c_[:, co + 1, :], ALU.mult), reads=[kpa] + ck_, writes=[kt + "3"])
                        o_, ko = o12.next()
                        S.pool(tt(o_[:, 0, :], t_[:, 0, :], t_[:, 1, :], ALU.subtract), reads=[kt + "0", kt + "1"], writes=[ko])
                        S.pool(tt(o_[:, 1, :], t_[:, 2, :], t_[:, 3, :], ALU.add), reads=[kt + "2", kt + "3"], writes=[ko])
                        S.dma(dstT[2 * h:2 * h + 2, :, t0:t0 + 512].rearrange("a p t -> p a t"), o_[:], reads=[ko], writes=["RQK"], chan=ko + "s")
                if jt + 1 < NT:
                    cnxt = cs_load(jt + 1)
                vt, kv = vts.next()
                for sub in range(4):
                    hs = lambda c: h_[:, c, sub * 128:(sub + 1) * 128]
                    for blk in range(4):
                        p1, k1 = ptm.next()
                        S.pe(mmg([(p1[:], hs(c), w[:, c, 2048 + blk * 512:2048 + (blk + 1) * 512]) for c in range(8)]), reads=hk + self.wkeys("wret", 2048 + blk * 512, 512), writes=[k1])
                        S.act(actf(vt[:, sub, blk * 512:(blk + 1) * 512], p1[:], AF.Copy), reads=[k1], writes=[kv])
                    gt, kg = gts.next()
                    for blk in range(4):
                        p1, k1 = ptm.next()
                        S.pe(mmg([(p1[:], hs(c), w[:, c, 4096 + blk * 512:4096 + (blk + 1) * 512]) for c in range(8)]), reads=hk + self.wkeys("wret", 4096 + blk * 512, 512), writes=[k1])
                        S.act(actf(gt[:, blk * 512:(blk + 1) * 512], p1[:], AF.Silu), reads=[k1], writes=[kg])
                    S.dma(self.RG[t0 + sub * 128:t0 + (sub + 1) * 128, :], gt[:], reads=[kg], writes=["RG"], chan=kg + "s")
                S.dma(self.RV[t0:t0 + 512, :].rearrange("(s p) c -> p s c", p=128), vt[:], reads=[kv], writes=["RV"], chan=kv + "s")

    def retention(self, j):
        S, NT, NCH, T, nc = self.S, self.NT, self.NCH, self.T, self.nc
        with self.phase():
            lg = self.sb("lg", [128, 8], F32)
            tmp = self.sb("lgt", [128, 8], F32)
            S.dma(lg[:], self.w["ret_decay_logit"][j:j + 1].rearrange("a d h -> a (d h)").to_broadcast([128, 8]), writes=["lg"], chan="lg")
            S.act(actf(tmp[:], lg[:], AF.Abs), reads=["lg"], writes=["lgt"])
            S.act(actf(tmp[:], tmp[:], AF.Exp, scale=-1.0), reads=["lgt"], writes=["lgt"])
            S.act(actf(tmp[:], tmp[:], AF.Ln, bias=1.0), reads=["lgt"], writes=["lgt"])
            S.dve(ts(lg[:], lg[:], 0.0, ALU.min), reads=["lg"], writes=["lg"])
            S.dve(tt(lg[:], lg[:], tmp[:], ALU.subtract), reads=["lg", "lgt"], writes=["lg"])
            DT = self.sb("DT", [128, 8, 128], F32)
            QD = self.sb("QD", [128, 8, 128], F32)
            QDb = self.sb("QDb", [128, 8, 128], BF16)
            KD = self.sb("KD", [128, 8], F32)
            CDC = self.sb("CDC", [128, 8, NCH], F32)
            cdv = self.sb("cdv", [128, 8], F32)
            for hd in range(8):
                d = hd // 4
                S.act(actf(DT[:, hd, :], self.cm[:, C_DFW if d == 0 else C_DBW, :], AF.Exp, scale=lg[:, hd:hd + 1]), reads=["lg", "cm"], writes=["DT"])
                S.dve(tt(DT[:, hd, :], DT[:, hd, :], self.cm[:, C_TRIU if d == 0 else C_SL, :], ALU.mult), reads=["DT", "cm"], writes=["DT"])
                S.act(actf(QD[:, hd, :], self.cm[:, C_QDF if d == 0 else C_QDB, :], AF.Exp, scale=lg[:, hd:hd + 1]), reads=["lg", "cm"], writes=["QD"])
                S.dve(cp(QDb[:, hd, :], QD[:, hd, :]), reads=["QD"], writes=["QDb"])
                S.act(actf(KD[:, hd:hd + 1], self.cm[:, C_KD, d:d + 1], AF.Exp, scale=lg[:, hd:hd + 1]), reads=["lg", "cm"], writes=["KD"])
                S.act(actf(cdv[:, hd:hd + 1], self.cm[:, C_KD, 2:3], AF.Exp, scale=lg[:, hd:hd + 1]), reads=["lg", "cm"], writes=["cdv"])
                S.dve(ts(CDC[:, hd, :], self.carry[:, d, :], cdv[:, hd:hd + 1], ALU.mult), reads=["cdv", "carry"], writes=["CDC"])
            qr = [self.ring("rq", 2, [128, 8, 512], BF16) for d in range(2)]
            kr = [self.ring("rk", 2, [128, 8, 512], BF16) for d in range(2)]
            vr = [self.ring("rv", 3, [128, 2048], BF16) for d in range(2)]
            St = [[[self.sb("St", [128, 512], F32) for c in range(2)] for h in range(4)] for d in range(2)]
            Sb = [[[self.sb("Sb", [128, 512], BF16) for c in range(2)] for h in range(4)] for d in range(2)]
            for d in range(2):
                for h in range(4):
                    for c in range(2):
                        S.pool(mset(St[d][h][c][:], 0.0), writes=["S%d%d%d" % (d, h, c)])
            atr = self.ring("at", 6, [128, 128], BF16)
            kwr = self.ring("kw", 6, [128, 2, 128], BF16)
            qdr = self.ring("qd", 6, [128, 2, 128], BF16)
            yst = [self.ring("yst", 2, [128, 2048], F32) for d in range(2)]
            pS = Ring([(self.ps[0][:, 0:128], "ps0"), (self.ps[5][:, 0:128], "ps5")])
            pO = Ring([(self.ps[1], "ps1"), (self.ps[2], "ps2")])
            pU = Ring([(self.ps[3], "ps3"), (self.ps[4], "ps4")])
            pT = Ring([(self.psb[0], "psb0"), (self.psb[1], "psb1")])
            RQ3 = self.RQ.rearrange("a p t -> p a t")
            RK3 = self.RK.rearrange("a p t -> p a t")
            YO = [self.YF, self.YB]
            items = [(k, h, d) for k in range(NCH) for h in range(4) for d in range(2)]
            tctx, cctx, ictx = {}, {}, {}

            def geom(it):
                k, h, d = it
                c = k if d == 0 else NCH - 1 - k
                return d, c // 4, c % 4, h, c, (c // 4) * 512

            def ensure(d, jt):
                if (d, jt) in tctx or jt < 0 or jt >= NT:
                    return
                t0 = jt * 512
                q_, kq = qr[d].next()
                k_, kk = kr[d].next()
                S.dma(q_[:], RQ3[:, :, t0:t0 + 512], writes=[kq], chan=kq)
                S.dma(k_[:], RK3[:, :, t0:t0 + 512], writes=[kk], chan=kk)
                tctx[(d, jt)] = (q_, kq, k_, kk)

            def ensure_v(d, c):
                if (d, c) in cctx or c < 0 or c >= NCH:
                    return
                v_, kv = vr[d].next()
                S.dma(v_[:], self.RV[c * 128:(c + 1) * 128, :], writes=[kv], chan=kv)
                cctx[(d, c)] = (v_, kv) + yst[d].next()

            def stage1(i):
                d, jt, sub, h, c, t0 = geom(items[i])
                if h == 0:
                    ensure(d, jt)
                    if items[i][0] % 4 == 1:
                        ensure(d, jt + (1 if d == 0 else -1))
                    ensure_v(d, c)
                    ensure_v(d, c + (1 if d == 0 else -1))
                q_, kq, k_, kk = tctx[(d, jt)]
                sl = slice(sub * 128, (sub + 1) * 128)
                hd = d * 4 + h
                st_, kst = pS.next()
                S.pe(mmg([(st_, k_[:, 2 * h + cc, sl], q_[:, 2 * h + cc, sl]) for cc in range(2)]), reads=[kk, kq], writes=[kst])
                at, kat = atr.next()
                S.dve(tt(at[:], st_, DT[:, hd, :], ALU.mult), reads=[kst, "DT"], writes=[kat])
                kw, kkw = kwr.next()
                qd, kqd = qdr.next()
                ptr, kptr = pT.next()

                def tr2(e, ptr=ptr, k_=k_, h=h, sl=sl):
                    ins = None
                    for cc in range(2):
                        ins = e.transpose(ptr[:, cc * 128:(cc + 1) * 128], k_[:, 2 * h + cc, sl], self.ident_b)
                    return ins
                S.pe(tr2, reads=[kk, "cmb"], writes=[kptr])
                S.act(actf(kw[:], ptr[:, 0:256].rearrange("p (c e) -> p c e", e=128), AF.Copy, scale=KD[:, hd:hd + 1]), reads=[kptr, "KD"], writes=[kkw])
                S.pool(tt(qd[:], q_[:, 2 * h:2 * h + 2, sl], QDb[:, hd, :].unsqueeze(1).to_broadcast([128, 2, 128]), ALU.mult), reads=[kq, "QDb"], writes=[kqd])
                for cc in range(2):
                    sk = "S%d%d%d" % (d, h, cc)
                    S.act(actf(Sb[d][h][cc][:], St[d][h][cc][:], AF.Copy, scale=self.carry[:, d, c:c + 1]), reads=[sk, "carry"], writes=[sk + "b"])
                ictx[i] = (at, kat, kw, kkw, qd, kqd)

            def stage2(i):
                d, jt, sub, h, c, t0 = geom(items[i])
                v_, kv, ys, kys = cctx[(d, c)]
                at, kat, kw, kkw, qd, kqd = ictx.pop(i)
                hd = d * 4 + h
                vs = v_[:, h * 512:(h + 1) * 512]
                po, kpo = pO.next()
                S.pe(mmg([(po[:], at[:], vs), (po[:], qd[:, 0, :], Sb[d][h][0][:]), (po[:], qd[:, 1, :], Sb[d][h][1][:])]),
                     reads=[kat, kv, kqd, "S%d%d0b" % (d, h), "S%d%d1b" % (d, h)], writes=[kpo])
                for cc in range(2):
                    pu, kpu = pU.next()
                    sk = "S%d%d%d" % (d, h, cc)
                    S.pe(mm1(pu[:], kw[:, cc, :], vs), reads=[kkw, kv], writes=[kpu])
                    S.dve(stt(St[d][h][cc][:], St[d][h][cc][:], CDC[:, hd, c:c + 1], pu[:], ALU.mult, ALU.add), reads=[sk, kpu, "CDC"], writes=[sk])
                S.dve(cp(ys[:, h * 512:(h + 1) * 512], po[:]), reads=[kpo], writes=[kys])
                if h == 3:
                    S.dma(YO[d][c * 128:(c + 1) * 128, :], ys[:], reads=[kys], writes=["YO"], chan=kys + "s")

            self.pipeline(len(items), stage1, stage2, 2)

        with self.phase():
            gain = self.sb("rgain", [128, 2048], F32)
            S.dma(gain[:], self.w["ret_out_norm"][j:j + 1, :].to_broadcast([128, 2048]), writes=["rgain"], chan="rgain")
            yfr = self.ring("yf", 2, [128, 2048], F32)
            ybr = self.ring("yb", 2, [128, 2048], F32)
            rgr = self.ring("rgl", 2, [128, 2048], F32)
            ssq = self.ring("ssq", 3, [128, 8], F32)
            junk = self.ring("junk", 2, [128, 512], F32)
            ymr = self.ring("ym", 2, [128, 2048], BF16)
            mxs = self.ring("mxs", 2, [128, 16, 512], BF16)
            pT = Ring([(self.psb[0], "psb0"), (self.psb[1], "psb1")])

            def ld(c):
                yf, kyf = yfr.next()
                yb, kyb = ybr.next()
                rg, krg = rgr.next()
                S.dma(yf[:], self.YF[c * 128:(c + 1) * 128, :], writes=[kyf], chan=kyf)
                S.dma(yb[:], self.YB[c * 128:(c + 1) * 128, :], writes=[kyb], chan=kyb)
                S.dma(rg[:], self.RG[c * 128:(c + 1) * 128, :], writes=[krg], chan=krg)
                return yf, kyf, yb, kyb, rg, krg
            nxt = ld(0)
            for c in range(NCH):
                yf, kyf, yb, kyb, rg, krg = nxt
                if c + 1 < NCH:
                    nxt = ld(c + 1)
                sub = c % 4
                sl = slice(sub * 128, (sub + 1) * 128)
                if sub == 0:
                    mx, kmx = mxs.next()
                S.pool(tt(yf[:], yf[:], yb[:], ALU.add), reads=[kyf, kyb], writes=[kyf])
                sq_, ksq = ssq.next()
                jk, kjk = junk.next()
                S.dve(mset(sq_[:], 0.0), writes=[ksq + "a", ksq + "b"])
                for hh in range(4):
                    S.act(actf(jk[:], yf[:, hh * 512:(hh + 1) * 512], AF.Square, accum=sq_[:, hh:hh + 1]), reads=[kyf], writes=[kjk, ksq + "a"])
                S.act(actf(sq_[:, 4:8], sq_[:, 0:4], AF.Sqrt, bias=EPS, scale=1.0 / 512), reads=[ksq + "a"], writes=[ksq + "b"])
                S.dve(recip(sq_[:, 4:8], sq_[:, 4:8]), reads=[ksq + "b"], writes=[ksq + "b"])
                S.dve(tt(yf[:].rearrange("p (h e) -> p h e", e=512), yf[:].rearrange("p (h e) -> p h e", e=512),
                         sq_[:, 4:8].unsqueeze(2).to_broadcast([128, 4, 512]), ALU.mult), reads=[kyf, ksq + "b"], writes=[kyf])
                S.pool(tt(rg[:], rg[:], gain[:], ALU.mult), reads=[krg, "rgain"], writes=[krg])
                ym, kym = ymr.next()
                S.dve(tt(ym[:], yf[:], rg[:], ALU.mult), reads=[kyf, krg], writes=[kym])
                for bg in range(2):
                    ptr, kptr = pT.next()

                    def tr8(e, ptr=ptr, ym=ym, bg=bg):
                        ins = None
                        for b_ in range(8):
                            ins = e.transpose(ptr[:, b_ * 128:(b_ + 1) * 128], ym[:, (bg * 8 + b_) * 128:(bg * 8 + b_ + 1) * 128], self.ident_b)
                        return ins
                    S.pe(tr8, reads=[kym, "cmb"], writes=[kptr])
                    S.act(actf(mx[:, bg * 8:(bg + 1) * 8, sl], ptr[:].rearrange("p (b e) -> p b e", e=128), AF.Copy), reads=[kptr], writes=[kmx])
                if sub == 3:
                    t0 = (c // 4) * 512
                    S.dma(self.MIX[0:2048, t0:t0 + 512].rearrange("(b p) t -> p b t", p=128), mx[:], reads=[kmx], writes=["MIXr"], chan=kmx + "s")


def rope_tables(seglen, head_dim, nseg, reps):
    rows = seglen // 64
    row_idx = np.repeat(np.arange(rows, dtype=np.float32), 64)
    col_idx = np.tile(np.arange(64, dtype=np.float32), rows)
    axis_dim = head_dim // 2
    inv_freq = (np.float32(10000.0) ** (-np.arange(0, axis_dim, 2, dtype=np.float32) / np.float32(axis_dim))).astype(np.float32)
    ang = np.concatenate([row_idx[:, None] * inv_freq, col_idx[:, None] * inv_freq], axis=-1).astype(np.float32)
    cs = np.stack([np.cos(ang), np.sin(ang)], 0).astype(np.float32)
    cs = np.tile(cs, (1, nseg, 1))
    cs = cs.transpose(0, 2, 1)
    return np.ascontiguousarray(np.tile(cs, (1, reps, 1)))


def core_tables(T, nseg):
    NT, NCH = T // 512, T // 128
    seglen = T // nseg
    tps = NT // nseg
    cps = NCH // nseg
    seg_t = np.arange(NT) // tps
    mb = np.where(seg_t[:, None] == seg_t[None, :], 0.0, -30000.0).astype(np.float32)
    carry = np.ones((2, NCH), np.float32)
    carry[0, np.arange(NCH) % cps == 0] = 0.0
    carry[1, np.arange(NCH) % cps == cps - 1] = 0.0
    flb = np.ones((NT, 2), np.float32)
    flb[np.arange(NT) % tps == 0, 0] = 0.0
    flb[np.arange(NT) % tps == tps - 1, 1] = 0.0
    rep = lambda a: np.ascontiguousarray(np.broadcast_to(a.reshape(1, -1), (128, a.size))).astype(np.float32)
    return {
        "maskb": rep(mb), "carry": rep(carry), "flb": rep(flb),
        "ropeA": rope_tables(seglen, 64, nseg, 4), "ropeR": rope_tables(seglen, 256, nseg, 1),
    }


DBG = {}
FULL_STEPS = [("ab", 0, 0), ("ffn", 0), ("ret", 0, 1), ("ffn", 1), ("ab", 1, 2), ("ffn", 2), ("ret", 1, 3), ("ffn", 3), ("final",)]
_CACHE = {}


def run_cores(T, steps, core_x, core_nseg, weights):
    key = (T, tuple(steps))
    if key not in _CACHE:
        _CACHE[key] = MK(T, steps).build()
    nc = _CACHE[key]
    cm = host_cmat()
    tabs = {}
    in_maps = []
    for x, ns in zip(core_x, core_nseg):
        if ns not in tabs:
            tabs[ns] = core_tables(T, ns)
        m = {"xT": np.ascontiguousarray(np.asarray(x, np.float32).T), "cmat": cm}
        m.update(tabs[ns])
        for n, _ in MK.W_SPECS:
            m[n] = weights[n]
        in_maps.append(m)
    res = run_bass_kernel_spmd(nc, in_maps, core_ids=list(range(len(in_maps))))
    return [np.ascontiguousarray(r["yT"].T) for r in res.results]


def kernel(x_prompt, x_sample, **weights):
    weights = {k: np.ascontiguousarray(np.asarray(v, np.float32)) for k, v in weights.items()}
    xp = np.asarray(x_prompt, np.float32)
    xs = np.asarray(x_sample, np.float32)
    T = 8192
    core_x = [xp[0], xp[1]] + [xs[4 * i:4 * i + 4].reshape(T, 1024) for i in range(4)]
    nseg = [1, 1, 4, 4, 4, 4]
    core_x += [core_x[5], core_x[5]]
    nseg += [4, 4]
    outs = run_cores(T, FULL_STEPS, core_x, nseg, weights)
    y_prompt = np.stack([outs[0], outs[1]], 0)
    y_sample = np.concatenate([outs[2 + i].reshape(4, 2048, 1024) for i in range(4)], 0)
    return (y_prompt, y_sample)
```

```python
import contextlib
import numpy as np
import concourse.bass as bass
import concourse.mybir as mybir
from concourse.bass_utils import run_bass_kernel_spmd

F32 = mybir.dt.float32
BF16 = mybir.dt.bfloat16
AF = mybir.ActivationFunctionType
ALU = mybir.AluOpType
AX = mybir.AxisListType
EPS = 1e-6


class _Op:
    __slots__ = ("eng", "fn", "waits", "signal", "idx", "chan", "cidx", "sigval")

    def __init__(self, eng, fn, chan):
        self.eng = eng
        self.fn = fn
        self.chan = chan
        self.waits = []
        self.signal = False
        self.idx = -1
        self.cidx = -1
        self.sigval = 0


class Sched:
    ENG = ("pe", "act", "dve", "pool", "sp")

    def __init__(self, nc):
        self.nc = nc
        self.ops = {e: [] for e in self.ENG}
        self.res = {}
        self.waited = {e: {} for e in self.ENG}
        self.chan_last = {}
        self.chan_n = {}
        self.chan_phase = {}
        self.phase_id = 0

    def _wait(self, op, d):
        eng = op.eng
        if d is op:
            return
        if d.chan is not None:
            ek, val = ("c", d.chan), d.cidx
        else:
            if d.eng == eng and eng == "pe":
                return
            ek, val = d.eng, d.idx
        w = self.waited[eng]
        if w.get(ek, -1) >= val:
            return
        w[ek] = val
        op.waits.append(d)
        d.signal = True

    def add(self, eng, fn, reads=(), writes=(), chan=None):
        if chan is not None:
            chan = (self.phase_id, chan)
        op = _Op(eng, fn, chan)
        deps = []
        res = self.res
        for k in reads:
            st = res.get(k)
            if st is not None and st[0] is not None:
                deps.append(st[0])
        for k in writes:
            st = res.get(k)
            if st is not None:
                if st[0] is not None:
                    deps.append(st[0])
                deps.extend(st[1])
        if chan is not None:
            prev = self.chan_last.get(chan)
            if prev is not None:
                deps.append(prev)
            op.cidx = self.chan_n.get(chan, 0)
            if op.cidx == 0:
                self.chan_phase[chan] = self.phase_id
            assert self.chan_phase[chan] == self.phase_id, chan
            self.chan_n[chan] = op.cidx + 1
            self.chan_last[chan] = op
            op.signal = True
        op.idx = len(self.ops[eng])
        for d in deps:
            self._wait(op, d)
        self.ops[eng].append(op)
        for k in writes:
            res[k] = [op, []]
        for k in reads:
            st = res.get(k)
            if st is None:
                res[k] = [None, [op]]
            elif st[0] is not op:
                st[1].append(op)
        return op

    def barrier(self):
        lasts = []
        for e in self.ENG:
            for o in reversed(self.ops[e]):
                if o.fn is not None and o.chan is None:
                    lasts.append(o)
                    break
        lasts.extend(self.chan_last.values())
        for e in self.ENG:
            op = _Op(e, None, None)
            op.idx = len(self.ops[e])
            for d in lasts:
                self._wait(op, d)
            self.ops[e].append(op)
        self.res = {}
        self.phase_id += 1

    def pe(self, fn, reads=(), writes=()):
        return self.add("pe", fn, reads, writes)

    def act(self, fn, reads=(), writes=()):
        return self.add("act", fn, reads, writes)

    def dve(self, fn, reads=(), writes=()):
        return self.add("dve", fn, reads, writes)

    def pool(self, fn, reads=(), writes=()):
        return self.add("pool", fn, reads, writes)

    def dma(self, out, in_, reads=(), writes=(), chan=None, eng="sp", **kw):
        assert chan is not None
        return self.add(eng, lambda e: e.dma_start(out=out, in_=in_, **kw), reads, writes, chan=chan)

    def emit(self, stack):
        nc = self.nc
        esem = {}
        for e in self.ENG:
            if e != "sp":
                esem[e] = stack.enter_context(nc.semaphore("s_" + e))
        csem = {}
        cbase = {}
        pool = []
        by_phase = {}
        for c, ph in self.chan_phase.items():
            by_phase.setdefault(ph, []).append(c)
        nsem = 0
        for ph in sorted(by_phase):
            used = []
            for c in by_phase[ph]:
                if pool:
                    sv = pool.pop()
                else:
                    sv = [stack.enter_context(nc.semaphore("c%d" % nsem)), 0]
                    nsem += 1
                csem[c] = sv[0]
                cbase[c] = sv[1]
                sv[1] += 16 * self.chan_n[c]
                used.append(sv)
            pool.extend(used)
        self.nsem = nsem
        for e in self.ENG:
            cnt = 0
            for op in self.ops[e]:
                if op.chan is not None:
                    op.sigval = cbase[op.chan] + 16 * (op.cidx + 1)
                elif op.signal:
                    cnt += 1
                    op.sigval = cnt

        def run(e, eng):
            for op in self.ops[e]:
                for d in op.waits:
                    sem = csem[d.chan] if d.chan is not None else esem[d.eng]
                    eng.wait_ge(sem, d.sigval)
                if op.fn is None:
                    continue
                ins = op.fn(eng)
                if op.chan is not None:
                    ins.then_inc(csem[op.chan], 16)
                elif op.signal:
                    ins.then_inc(esem[e], 1)

        with nc.Block() as block:
            @block.sync
            def _(eng):
                run("sp", eng)

            @block.tensor
            def _(eng):
                run("pe", eng)

            @block.scalar
            def _(eng):
                run("act", eng)

            @block.vector
            def _(eng):
                run("dve", eng)

            @block.gpsimd
            def _(eng):
                run("pool", eng)


def mmg(items):
    n = len(items)

    def f(e):
        ins = None
        for i, (o, l, r) in enumerate(items):
            ins = e.matmul(o, lhsT=l, rhs=r, start=(i == 0), stop=(i == n - 1))
        return ins
    return f


def mm1(o, l, r):
    return lambda e: e.matmul(o, lhsT=l, rhs=r, start=True, stop=True)


def tr(o, i, ident):
    return lambda e: e.transpose(o, i, ident)


def actf(o, i, func, bias=None, scale=None, accum=None):
    kw = {}
    if bias is not None:
        kw["bias"] = bias
    if scale is not None:
        kw["scale"] = scale
    if accum is not None:
        kw["accum_out"] = accum
    return lambda e: e.activation(out=o, in_=i, func=func, **kw)


def tt(o, a, b, op):
    return lambda e: e.tensor_tensor(out=o, in0=a, in1=b, op=op)


def ts(o, a, s1, op0, s2=None, op1=None):
    if op1 is None:
        return lambda e: e.tensor_scalar(out=o, in0=a, scalar1=s1, scalar2=None, op0=op0)
    return lambda e: e.tensor_scalar(out=o, in0=a, scalar1=s1, scalar2=s2, op0=op0, op1=op1)


def stt(o, a, s, b, op0, op1):
    return lambda e: e.scalar_tensor_tensor(out=o, in0=a, scalar=s, in1=b, op0=op0, op1=op1)


def cp(o, i):
    return lambda e: e.tensor_copy(out=o, in_=i)


def recip(o, i):
    return lambda e: e.reciprocal(out=o, in_=i)


def mset(o, v):
    return lambda e: e.memset(o, v)


C_ID, C_ONES, C_BD32, C_TRIU, C_TRIL, C_SL, C_DFW, C_DBW, C_QDF, C_QDB, C_SEL, C_KD = range(12)
NCM = 12


def host_cmat():
    s = np.arange(128)[:, None].astype(np.float32)
    l = np.arange(128)[None, :].astype(np.float32)
    m = np.zeros((NCM, 128, 128), np.float32)
    m[C_ID] = np.eye(128)
    m[C_ONES] = 1.0
    m[C_BD32] = (np.arange(128)[:, None] // 32 == np.arange(128)[None, :] // 32)
    m[C_TRIU] = (s <= l)
    m[C_TRIL] = (s >= l)
    m[C_SL] = (s > l)
    m[C_DFW] = np.maximum(l - s, 0)
    m[C_DBW] = np.maximum(s - l, 0)
    m[C_QDF] = np.broadcast_to(l + 1.0, (128, 128))
    m[C_QDB] = np.broadcast_to(128.0 - l, (128, 128))
    m[C_SEL][64, :] = 1.0
    m[C_KD][:, 0] = 127.0 - np.arange(128)
    m[C_KD][:, 1] = np.arange(128)
    m[C_KD][:, 2] = 128.0
    return np.ascontiguousarray(m.transpose(1, 0, 2))


class Ring:
    def __init__(self, items):
        self.items = items
        self.i = 0

    def next(self):
        it = self.items[self.i % len(self.items)]
        self.i += 1
        return it


class MK:
    W_SPECS = [
        ("norm_mix", (4, 1024)), ("norm_ffn", (4, 1024)), ("norm_final", (1024,)),
        ("ab_w_in", (2, 1024, 2832)), ("ab_gate_bias", (2, 16)), ("attn_q_norm", (2, 64)),
        ("attn_k_norm", (2, 64)), ("mlstm_out_norm", (2, 512)), ("ab_w_out", (2, 1024, 1024)),
        ("ret_w_in", (2, 1024, 6144)), ("ret_decay_logit", (2, 2, 4)), ("ret_out_norm", (2, 2048)),
        ("ret_w_out", (2, 2048, 1024)), ("ffn_w_up", (4, 1024, 5632)), ("ffn_conv_w", (4, 3, 2816)),
        ("ffn_conv_b", (4, 2816)), ("ffn_w_down", (4, 2816, 1024)),
    ]

    def __init__(self, T, steps):
        self.T = T
        self.NT = T // 512
        self.NCH = T // 128
        self.steps = steps
        self.nc = nc = bass.Bass("TRN2", target_bir_lowering=False)
        self.S = Sched(nc)
        self._uid = 0
        NT, NCH = self.NT, self.NCH
        di = lambda n, s, dt=F32: nc.dram_tensor(n, list(s), dt, kind="ExternalInput").ap()
        ds = lambda n, s, dt: nc.dram_tensor(n, list(s), dt, kind="Internal").ap()
        self.xT = di("xT", (1024, T))
        self.w = {n: di(n, s) for n, s in self.W_SPECS}
        self.cmat_d = di("cmat", (128, NCM, 128))
        self.ropeA = di("ropeA", (2, 128, T))
        self.ropeR = di("ropeR", (2, 128, T))
        self.mb_d = di("maskb", (128, NT * NT))
        self.carry_d = di("carry", (128, 2 * NCH))
        self.flb_d = di("flb", (128, NT * 2))
        self.yT = nc.dram_tensor("yT", [1024, T], F32, kind="ExternalOutput").ap()
        self.XM = ds("XM", (1024, T), F32)
        self.XR = ds("XR", (1024, T), F32)
        self.ACTS = ds("ACTS", (2816, T), BF16)
        self.MIX = ds("MIX", (2048, T), BF16)
        self.QA = ds("QA", (8, 64, T), BF16)
        self.KA = ds("KA", (128, T), BF16)
        self.VA = ds("VA", (T, 130), BF16)
        self.MQ = ds("MQ", (4, 128, T), BF16)
        self.MKs = ds("MKs", (4, 128, T), BF16)
        self.MV = ds("MV", (T, 516), BF16)
        self.SG = ds("SG", (T, 512), F32)
        self.GT = ds("GT", (T, 16), F32)
        self.HF = ds("HF", (T, 512), F32)
        self.HB = ds("HB", (T, 512), F32)
        self.YB = ds("YB", (T, 2048), F32)
        self.RQ = ds("RQ", (8, 128, T), BF16)
        self.RK = ds("RK", (8, 128, T), BF16)
        self.RV = ds("RV", (T, 2048), BF16)
        self.RG = ds("RG", (T, 2048), F32)
        self.YF = ds("YF", (T, 2048), F32)

    def un(self, n):
        self._uid += 1
        return "%s_%d" % (n, self._uid)

    def sb(self, name, shape, dt):
        return self._ph.enter_context(self.nc.sbuf_tensor(self.un(name), list(shape), dt))

    @contextlib.contextmanager
    def phase(self, psum=True):
        self.S.barrier()
        with contextlib.ExitStack() as ph:
            self._ph = ph
            if psum:
                nc = self.nc
                self.ps = [ph.enter_context(nc.psum_tensor(self.un("ps%d" % i), [128, 512], F32)) for i in range(6)]
                self.psb = [ph.enter_context(nc.psum_tensor(self.un("psb%d" % i), [128, 1024], BF16)) for i in range(2)]
            yield
        self._ph = self._gl

    def ring(self, name, n, shape, dt):
        items = []
        for i in range(n):
            t = self.sb(name, shape, dt)
            items.append((t, self.un(name)))
        return Ring(items)

    def pipeline(self, n, stage1, stage2, la):
        for i in range(n + la):
            if i < n:
                stage1(i)
            if i >= la:
                stage2(i - la)

    def rr(self, engs):
        self._rr = getattr(self, "_rr", 0) + 1
        return engs[self._rr % len(engs)]

    def build(self):
        nc, S = self.nc, self.S
        with contextlib.ExitStack() as gl:
            self._gl = gl
            self._ph = gl
            self.pk = ["ps%d" % i for i in range(6)]
            self.cm = self.sb("cm", [128, NCM, 128], F32)
            self.cmb = self.sb("cmb", [128, 3, 128], BF16)
            S.dma(self.cm[:], self.cmat_d, writes=["cm"], chan="cm")
            S.dve(cp(self.cmb[:], self.cm[:, 0:3, :]), reads=["cm"], writes=["cmb"])
            self.ident_f = self.cm[:, C_ID, :]
            self.ones_f = self.cm[:, C_ONES, :]
            self.ident_b = self.cmb[:, C_ID, :]
            self.ones_b = self.cmb[:, C_ONES, :]
            self.bd32_b = self.cmb[:, C_BD32, :]
            self.carry = self.sb("carry", [128, 2, self.NCH], F32)
            S.dma(self.carry[:], self.carry_d.rearrange("p (d c) -> p d c", d=2), writes=["carry"], chan="carry")
            self.mb = self.sb("mb", [128, self.NT, self.NT], F32)
            S.dma(self.mb[:], self.mb_d.rearrange("p (a b) -> p a b", a=self.NT), writes=["mb"], chan="mb")
            self.flb = self.sb("flb", [128, self.NT * 2], F32)
            S.dma(self.flb[:], self.flb_d, writes=["flb"], chan="flb")
            cur = self.xT
            for st in self.steps:
                kind = st[0]
                if kind == "ab":
                    j, L = st[1], st[2]
                    self.ab_inproj(cur, j, L)
                    self.attention()
                    self.mlstm(j)
                    self.proj_resid(self.MIX, 8, self.w["ab_w_out"][j], cur, self.XM)
                    cur = self.XM
                elif kind == "ret":
                    j, L = st[1], st[2]
                    self.ret_inproj(cur, j, L)
                    if not DBG.get("noscan"):
                        self.retention(j)
                    self.proj_resid(self.MIX, 16, self.w["ret_w_out"][j], cur, self.XM)
                    cur = self.XM
                elif kind == "ffn":
                    L = st[1]
                    self.ffn_up(cur, L)
                    self.proj_resid(self.ACTS, 22, self.w["ffn_w_down"][L], cur, self.XR)
                    cur = self.XR
                elif kind == "final":
                    self.final_norm(cur)
                    cur = None
                elif kind == "copy":
                    self.copy_out(cur)
                    cur = None
            S.barrier()
            S.emit(gl)
        return nc

    def x3(self, X):
        return X.rearrange("(c p) t -> p c t", p=128)

    def load_gain(self, vec_ap, name):
        g = self.sb(name, [128, 8], F32)
        self.S.dma(g[:], vec_ap.rearrange("(c p) -> p c", p=128), writes=[name], chan=name, allow_slow_non_contiguous=True)
        return g

    def wkeys(self, dkey, col0, n, bw=1024):
        return ["%s#%d" % (dkey, b_) for b_ in range(col0 // bw, (col0 + n - 1) // bw + 1)]

    def prep_weight(self, dst, dkey, src, KC, blocks, gain=None, gkey=None, bw=1024, order=None, reserve=0):
        S = self.S
        nstg = 2
        while nstg < 6 and self.nc.sbuf_bytes_remaining >= (nstg + 1) * 4096 + reserve:
            nstg += 1
        stg = self.ring("wstg", nstg, [128, 1024], F32)
        pw = min(bw, 1024)
        pieces = []
        for (d0, nd, s0, ns, vf) in blocks:
            o = 0
            while o < ns:
                n_ = min(pw - (d0 + o) % pw, ns - o)
                pieces.append((d0 + o, n_, s0 + o))
                o += n_
        if order is not None:
            pieces.sort(key=lambda p: (order.index(p[0] // bw) if (p[0] // bw) in order else 999, p[0]))
        for (d0, n_, s0) in pieces:
            bk = "%s#%d" % (dkey, d0 // bw)
            for c in range(KC):
                t, k = stg.next()
                S.dma(t[:, 0:n_], src[c * 128:(c + 1) * 128, s0:s0 + n_], writes=[k], chan=k)
                iv = t[:, 0:n_]
                ov = dst[:, c, d0:d0 + n_]
                eng = self.rr(["dve", "act", "dve", "act", "pool"])
                rd = [k] + ([gkey] if gain is not None else [])
                if gain is None:
                    if eng == "act":
                        S.act(actf(ov, iv, AF.Copy), reads=rd, writes=[bk])
                    else:
                        S.add(eng, cp(ov, iv), reads=rd, writes=[bk])
                else:
                    if eng == "act":
                        S.act(actf(ov, iv, AF.Copy, scale=gain[:, c:c + 1]), reads=rd, writes=[bk])
                    else:
                        S.add(eng, ts(ov, iv, gain[:, c:c + 1], ALU.mult), reads=rd, writes=[bk])

    def norm_load(self, src3, t0, n, ent):
        xt, kx = ent
        self.S.dma(xt[:, :, 0:n], src3[:, :, t0:t0 + n], writes=[kx], chan=kx)

    def norm_tile(self, src3, t0, n, bufs, D_feat=1024, load=True, part=0):
        S = self.S
        xt, kx = bufs["xt"]
        sq, ksq = bufs["sq"]
        rs, krs = bufs["rs"]
        hT, kh = bufs["hT"]
        ssp, kss = bufs["ss"]
        if load:
            S.dma(xt[:, :, 0:n], src3[:, :, t0:t0 + n], writes=[kx], chan=kx)
        if part in (0, 1):
            S.act(actf(sq[:, :, 0:n], xt[:, :, 0:n], AF.Square), reads=[kx], writes=[ksq])
        if part == 1:
            return None
        S.pe(mmg([(ssp[:, 0:n], self.ones_b, sq[:, c, 0:n]) for c in range(8)]), reads=[ksq, "cmb"], writes=[kss])
        S.act(actf(rs[:, 0:n], ssp[:, 0:n], AF.Sqrt, bias=EPS, scale=1.0 / D_feat), reads=[kss], writes=[krs])
        S.dve(recip(rs[:, 0:n], rs[:, 0:n]), reads=[krs], writes=[krs])
        S.dve(tt(hT[:, 0:4, 0:n], xt[:, 0:4, 0:n], rs[:, 0:n].unsqueeze(1).to_broadcast([128, 4, n]), ALU.mult),
              reads=[kx, krs], writes=[kh + "a"])
        S.pool(tt(hT[:, 4:8, 0:n], xt[:, 4:8, 0:n], rs[:, 0:n].unsqueeze(1).to_broadcast([128, 4, n]), ALU.mult),
               reads=[kx, krs], writes=[kh + "b"])
        return [kh + "a", kh + "b"]

    def norm_pipeline(self, X3, xt, sq, rs, hT):
        NT = self.NT
        st = {"nxt": xt.next()}
        self.norm_load(X3, 0, 512, st["nxt"])

        def prep(j, part=0):
            if part in (0, 1):
                st["bufs"] = {"xt": st["nxt"], "sq": sq.next(), "rs": rs.next(), "hT": hT.next(), "ss": (self.ps[0], "ps0")}
            bufs = st["bufs"]
            hk = self.norm_tile(X3, j * 512, 512, bufs, load=False, part=part)
            if part == 1:
                return None
            if j + 1 < NT:
                st["nxt"] = xt.next()
                self.norm_load(X3, (j + 1) * 512, 512, st["nxt"])
            return bufs, hk
        return prep

    def norm_bufs(self, nbuf=1, n=512):
        xt = self.ring("xt", nbuf, [128, 8, n], F32)
        sq = self.ring("sq", 1, [128, 8, n], BF16)
        rs = self.ring("rs", 2, [128, n], F32)
        hT = self.ring("hT", 2, [128, 8, n], BF16)
        return xt, sq, rs, hT

    def proj_resid(self, A, KC, w_d, Xin, Xout):
        S, NT = self.S, self.NT
        with self.phase():
            w = self.sb("wpr", [128, KC, 1024], BF16)
            self.prep_weight(w, "wpr", w_d, KC, [(0, 1024, 0, 1024, None)], bw=512, reserve=(KC * 2 + 32 + 4) * 1024)
            ar = self.ring("a", 2, [128, KC, 512], BF16)
            xr = self.ring("x", 2, [128, 8, 512], F32)
            A3 = A[0:KC * 128, :].rearrange("(c p) t -> p c t", p=128)
            Xi3, Xo3 = self.x3(Xin), self.x3(Xout)
            psr = Ring([(self.ps[i], self.pk[i]) for i in range(4)])
            def pr_load(j):
                at, ka = ar.next()
                xt, kx = xr.next()
                S.dma(at[:], A3[:, :, j * 512:(j + 1) * 512], writes=[ka], chan=ka)
                S.dma(xt[:], Xi3[:, :, j * 512:(j + 1) * 512], writes=[kx], chan=kx)
                return at, ka, xt, kx
            nxt = pr_load(0)
            for j in range(NT):
                t0 = j * 512
                at, ka, xt, kx = nxt
                if j + 1 < NT:
                    nxt = pr_load(j + 1)
                for d in range(8):
                    p, kp = psr.next()
                    S.pe(mmg([(p[:], w[:, c, d * 128:(d + 1) * 128], at[:, c, :]) for c in range(KC)]),
                         reads=[ka] + self.wkeys("wpr", d * 128, 128, 512), writes=[kp])
                    S.dve(tt(xt[:, d, :], p[:], xt[:, d, :], ALU.add), reads=[kp, kx], writes=[kx])
                S.dma(Xo3[:, :, t0:t0 + 512], xt[:], reads=[kx], writes=["Xo"], chan=kx + "s")

    def final_norm(self, X):
        S, NT = self.S, self.NT
        with self.phase():
            g = self.load_gain(self.w["norm_final"], "gfin")
            xt, sq, rs, hT = self.norm_bufs(nbuf=2)
            yr = self.ring("y", 2, [128, 8, 512], F32)
            X3, Y3 = self.x3(X), self.x3(self.yT)
            nxt = xt.next()
            self.norm_load(X3, 0, 512, nxt)
            for j in range(NT):
                t0 = j * 512
                x_, kx = nxt
                if j + 1 < NT:
                    nxt = xt.next()
                    self.norm_load(X3, t0 + 512, 512, nxt)
                q_, kq = sq.next()
                r_, kr = rs.next()
                y_, ky = yr.next()
                S.act(actf(q_[:], x_[:], AF.Square), reads=[kx], writes=[kq])
                S.pe(mmg([(self.ps[0][:], self.ones_b, q_[:, c, :]) for c in range(8)]), reads=[kq, "cmb"], writes=["ps0"])
                S.act(actf(r_[:], self.ps[0][:], AF.Sqrt, bias=EPS, scale=1.0 / 1024), reads=["ps0"], writes=[kr])
                S.dve(recip(r_[:], r_[:]), reads=[kr], writes=[kr])
                for c in range(8):
                    S.dve(stt(y_[:, c, :], x_[:, c, :], g[:, c:c + 1], r_[:], ALU.mult, ALU.mult),
                          reads=[kx, kr, "gfin"], writes=[ky])
                S.dma(Y3[:, :, t0:t0 + 512], y_[:], reads=[ky], writes=["Y"], chan=ky + "s")

    def copy_out(self, X):
        S, NT = self.S, self.NT
        with self.phase():
            xr = self.ring("x", 2, [128, 8, 512], F32)
            X3, Y3 = self.x3(X), self.x3(self.yT)
            for j in range(NT):
                x_, kx = xr.next()
                S.dma(x_[:], X3[:, :, j * 512:(j + 1) * 512], writes=[kx], chan=kx)
                S.dma(Y3[:, :, j * 512:(j + 1) * 512], x_[:], reads=[kx], writes=["Y"], chan=kx + "s")

    def ffn_up(self, X, L):
        S, NT = self.S, self.NT
        nc = self.nc
        NB = 2 * (NT - 1)
        with self.phase():
            w = self.sb("wup", [128, 8, 5632], BF16)
            g = self.load_gain(self.w["norm_ffn"][L], "gffn")
            cw = self.sb("cw", [128, 22, 3], F32)
            cb = self.sb("cb", [128, 22], F32)
            for k3 in range(3):
                S.dma(cw[:, :, k3], self.w["ffn_conv_w"][L, k3].rearrange("(f p) -> p f", p=128), writes=["cw"], chan="cw", allow_slow_non_contiguous=True)
            S.dma(cb[:], self.w["ffn_conv_b"][L].rearrange("(f p) -> p f", p=128), writes=["cb"], chan="cb", allow_slow_non_contiguous=True)
            self.prep_weight(w, "wup", self.w["ffn_w_up"][L], 8,
                             [(i * 2048, min(2048, 5632 - i * 2048), i * 2048, min(2048, 5632 - i * 2048), None) for i in range(3)],
                             gain=g, gkey="gffn", order=[0, 2, 3, 1, 4, 5], reserve=106 * 1024)
            xt, sq, rs, hT = self.norm_bufs()
            X3 = self.x3(X)
            gh = self.sb("gh", [128, 22, NT, 2], F32)
            S.pool(mset(gh[:], 0.0), writes=["gh"])
            if NB > 0:
                xb = self.sb("xb", [128, 8, NT - 1, 2], F32)
                sqb = self.sb("sqb", [128, 8, NB], BF16)
                rsb = self.sb("rsb", [128, NB], F32)
                hb = self.sb("hb", [128, 8, NB], BF16)
                for c in range(8):
                    src = X3[:, c, 511:511 + 512 * (NT - 1)].rearrange("p (b r) -> p b r", r=512)[:, :, 0:2]
                    S.dma(xb[:, c, :, :], src, writes=["xb%d" % c], chan="xb", allow_slow_non_contiguous=True)
                xbf = xb[:].rearrange("p c b e -> p c (b e)")
                xk = ["xb%d" % c for c in range(8)]
                S.act(actf(sqb[:], xbf, AF.Square), reads=xk, writes=["sqb"])
                S.pe(mmg([(self.ps[0][:, 0:NB], self.ones_b, sqb[:, c, :]) for c in range(8)]), reads=["sqb", "cmb"], writes=["ps0"])
                S.act(actf(rsb[:], self.ps[0][:, 0:NB], AF.Sqrt, bias=EPS, scale=1.0 / 1024), reads=["ps0"], writes=["rsb"])
                S.dve(recip(rsb[:], rsb[:]), reads=["rsb"], writes=["rsb"])
                S.dve(tt(hb[:], xbf, rsb[:].unsqueeze(1).to_broadcast([128, 8, NB]), ALU.mult), reads=xk + ["rsb"], writes=["hb"])
                pr = Ring([(self.ps[1], "ps1"), (self.ps[2], "ps2")])
                for f in range(22):
                    p, kp = pr.next()
                    S.pe(mmg([(p[:, 0:NB], w[:, c, 2816 + f * 128:2816 + (f + 1) * 128], hb[:, c, :]) for c in range(8)]),
                         reads=["hb"] + self.wkeys("wup", 2816 + f * 128, 128), writes=[kp])
                    pv = p[:, 0:NB].rearrange("p (b e) -> p b e", e=2)
                    S.dve(cp(gh[:, f, 1:NT, 0], pv[:, :, 0]), reads=[kp], writes=["gh"])
                    S.dve(cp(gh[:, f, 0:NT - 1, 1], pv[:, :, 1]), reads=[kp], writes=["gh"])
            ghf = gh[:].rearrange("p f j e -> p f (j e)")
            S.dve(tt(ghf, ghf, self.flb[:].unsqueeze(1).to_broadcast([128, 22, NT * 2]), ALU.mult), reads=["gh", "flb"], writes=["gh"])
            actr = self.ring("act", 2, [128, 22, 512], BF16)
            gbr = self.ring("gb", 2, [128, 514], F32)
            tr_ = self.ring("tc", 2, [128, 512], F32)
            pu = Ring([(self.ps[1], "ps1"), (self.ps[2], "ps2"), (self.ps[5], "ps5")])
            pg = Ring([(self.ps[3], "ps3"), (self.ps[4], "ps4")])
            A3 = self.ACTS.rearrange("(f p) t -> p f t", p=128)
            prep = self.norm_pipeline(X3, xt, sq, rs, hT)
            cur = prep(0)
            for j in range(NT):
                t0 = j * 512
                bufs, hk = cur
                h_ = bufs["hT"][0]
                a_, ka = actr.next()
                for f in range(22):
                    if f == 1 and j + 1 < NT:
                        prep(j + 1, 1)
                    if f == 6 and j + 1 < NT:
                        cur = prep(j + 1, 2)
                    u, ku = pu.next()
                    gp, kg = pg.next()
                    S.pe(mmg([(u[:], w[:, c, f * 128:(f + 1) * 128], h_[:, c, :]) for c in range(8)]), reads=hk + self.wkeys("wup", f * 128, 128), writes=[ku])
                    S.pe(mmg([(gp[:], w[:, c, 2816 + f * 128:2816 + (f + 1) * 128], h_[:, c, :]) for c in range(8)]), reads=hk + self.wkeys("wup", 2816 + f * 128, 128), writes=[kg])
                    gb, kb = gbr.next()
                    tc, kt = tr_.next()
                    S.act(actf(gb[:, 1:513], gp[:], AF.Copy), reads=[kg], writes=[kb + "m"])
                    S.dve(cp(gb[:, 0:514:513], gh[:, f, j, :]), reads=["gh"], writes=[kb + "h"])
                    S.act(actf(tc[:], gp[:], AF.Identity, bias=cb[:, f:f + 1], scale=cw[:, f, 1:2]), reads=[kg, "cw", "cb"], writes=[kt])
                    S.dve(stt(tc[:], gb[:, 0:512], cw[:, f, 0:1], tc[:], ALU.mult, ALU.add), reads=[kb + "m", kb + "h", kt, "cw"], writes=[kt])
                    S.dve(stt(tc[:], gb[:, 2:514], cw[:, f, 2:3], tc[:], ALU.mult, ALU.add), reads=[kb + "m", kb + "h", kt, "cw"], writes=[kt])
                    S.act(actf(tc[:], tc[:], AF.Gelu), reads=[kt], writes=[kt])
                    S.dve(tt(a_[:, f, :], tc[:], u[:], ALU.mult), reads=[kt, ku], writes=[ka])
                S.dma(A3[:, :, t0:t0 + 512], a_[:], reads=[ka], writes=["ACTS"], chan=ka + "s")

    def ab_inproj(self, X, j, L):
        S, NT, nc = self.S, self.NT, self.nc
        with self.phase():
            w = self.sb("wab", [128, 8, 2832], BF16)
            g = self.load_gain(self.w["norm_mix"][L], "gmix")
            stg = self.ring("wstg", 2, [128, 2832], F32)
            src = self.w["ab_w_in"][j]

            def hsplit(ap, nh, half):
                return ap.rearrange("p (h d) -> p h d", d=64)[:, :, half * 32:(half + 1) * 32]

            def h32(ap, nh):
                return ap.rearrange("p (h d) -> p h d", d=32)

            for c in range(8):
                t, k = stg.next()
                S.dma(t[:], src[c * 128:(c + 1) * 128, :], writes=[k], chan=k)
                gc = g[:, c:c + 1]
                engs = ["dve", "pool"]
                for gi in range(2):
                    for half in range(2):
                        S.add(self.rr(engs), ts(h32(w[:, c, gi * 256 + half * 128: gi * 256 + half * 128 + 128], 4),
                                                hsplit(t[:, gi * 256:(gi + 1) * 256], 4, half), gc, ALU.mult),
                              reads=[k, "gmix"], writes=["wab"])
                for half in range(2):
                    S.add(self.rr(engs), ts(h32(w[:, c, 512 + half * 64: 512 + half * 64 + 64], 2),
                                            hsplit(t[:, 512:640], 2, half), gc, ALU.mult), reads=[k, "gmix"], writes=["wab"])
                for (d0, s0, n) in [(1664, 640, 128), (640, 768, 1024), (1808, 1792, 1024), (1792, 2816, 16)]:
                    S.add(self.rr(engs), ts(w[:, c, d0:d0 + n], t[:, s0:s0 + n], gc, ALU.mult), reads=[k, "gmix"], writes=["wab"])
            gq = self.sb("gq", [128, 2], F32)
            gk = self.sb("gk", [128, 2], F32)
            for r in range(4):
                for half in range(2):
                    S.dma(gq[r * 32:(r + 1) * 32, half:half + 1], self.w["attn_q_norm"][j, half * 32:(half + 1) * 32].unsqueeze(1),
                          writes=["gq%d%d" % (r, half)], chan="gqld", allow_slow_non_contiguous=True)
                    S.dma(gk[r * 32:(r + 1) * 32, half:half + 1], self.w["attn_k_norm"][j, half * 32:(half + 1) * 32].unsqueeze(1),
                          writes=["gk%d%d" % (r, half)], chan="gkld", allow_slow_non_contiguous=True)
            gqk = ["gq%d%d" % (r, h) for r in range(4) for h in range(2)]
            gkk = ["gk%d%d" % (r, h) for r in range(4) for h in range(2)]
            S.dve(ts(gq[:], gq[:], 0.125, ALU.mult), reads=gqk, writes=["gq"])
            gbias = self.sb("gbias", [128, 16], F32)
            S.dma(gbias[:], self.w["ab_gate_bias"][j:j + 1, :].to_broadcast([128, 16]), writes=["gbias"], chan="gbias")
            xt, sq, rs, hT = self.norm_bufs()
            X3 = self.x3(X)
            cs = self.ring("cs", 2, [128, 2, 512], F32)
            sqa = self.ring("sqa", 2, [128, 2, 512], BF16)
            rq = self.ring("rq", 2, [128, 512], F32)
            an = self.ring("an", 2, [128, 2, 512], F32)
            t4 = self.ring("t4", 1, [128, 4, 512], F32)
            o12 = self.ring("o12", 3, [128, 2, 512], BF16)
            mqs = self.ring("mqs", 1, [128, 4, 512], BF16)
            mks = self.ring("mks", 1, [128, 4, 512], BF16)
            vts = self.ring("vt", 2, [128, 4, 2, 65], BF16)
            mvs = self.ring("mvt", 2, [128, 4, 4, 129], BF16)
            sgs = self.ring("sg", 1, [128, 4, 512], F32)
            gts = self.ring("gt", 2, [128, 4, 16], F32)
            for (t, k) in vts.items:
                S.pool(mset(t[:], 1.0), writes=[k])
            for (t, k) in mvs.items:
                S.pool(mset(t[:], 1.0), writes=[k])
            pf = Ring([(self.ps[i], self.pk[i]) for i in (1, 2, 3)])
            ptm = Ring([(self.ps[4], "ps4"), (self.ps[5], "ps5")])
            def cs_load(jt_):
                c_, kc = cs.next()
                S.dma(c_[:], self.ropeA[:, :, jt_ * 512:(jt_ + 1) * 512].rearrange("a p t -> p a t"), writes=[kc], chan=kc)
                return c_, kc
            prep = self.norm_pipeline(X3, xt, sq, rs, hT)
            cur = prep(0)
            cnxt = cs_load(0)
            for jt in range(NT):
                t0 = jt * 512
                bufs, hk = cur
                h_ = bufs["hT"][0]
                c_, kc = cnxt
                if jt + 1 < NT:
                    cnxt = cs_load(jt + 1)

                def fm(col0, M):
                    p, kp = pf.next()
                    S.pe(mmg([(p[0:M, :], w[:, c, col0:col0 + M], h_[:, c, :]) for c in range(8)]), reads=hk + ["wab"], writes=[kp])
                    return p, kp

                for grp in range(3):
                    if grp == 1 and jt + 1 < NT:
                        prep(jt + 1, 1)
                    M = 128 if grp < 2 else 64
                    colA = grp * 256 if grp < 2 else 512
                    colB = colA + M
                    gg, ggk = (gq, ["gq"]) if grp < 2 else (gk, gkk)
                    pa, kpa = fm(colA, M)
                    pb, kpb = fm(colB, M)
                    s_, ks = sqa.next()
                    S.act(actf(s_[0:M, 0, :], pa[0:M, :], AF.Square), reads=[kpa], writes=[ks + "a"])
                    S.act(actf(s_[0:M, 1, :], pb[0:M, :], AF.Square), reads=[kpb], writes=[ks + "b"])
                    S.pe(mmg([(self.ps[0][0:M, :], self.bd32_b[0:M, 0:M], s_[0:M, 0, :]),
                              (self.ps[0][0:M, :], self.bd32_b[0:M, 0:M], s_[0:M, 1, :])]), reads=[ks + "a", ks + "b", "cmb"], writes=["ps0"])
                    r_, kr = rq.next()
                    S.act(actf(r_[0:M, :], self.ps[0][0:M, :], AF.Sqrt, bias=EPS, scale=1.0 / 64), reads=["ps0"], writes=[kr])
                    S.dve(recip(r_[0:M, :], r_[0:M, :]), reads=[kr], writes=[kr])
                    a_, kan = an.next()
                    S.dve(stt(a_[0:M, 0, :], pa[0:M, :], gg[0:M, 0:1], r_[0:M, :], ALU.mult, ALU.mult), reads=[kpa, kr] + ggk, writes=[kan + "a"])
                    S.dve(stt(a_[0:M, 1, :], pb[0:M, :], gg[0:M, 1:2], r_[0:M, :], ALU.mult, ALU.mult), reads=[kpb, kr] + ggk, writes=[kan + "b"])
                    t_, kt = t4.next()
                    S.pool(tt(t_[0:M, 0, :], a_[0:M, 0, :], c_[0:M, 0, :], ALU.mult), reads=[kan + "a", kc], writes=[kt + "0"])
                    S.pool(tt(t_[0:M, 1, :], a_[0:M, 1, :], c_[0:M, 1, :], ALU.mult), reads=[kan + "b", kc], writes=[kt + "1"])
                    S.dve(tt(t_[0:M, 2, :], a_[0:M, 1, :], c_[0:M, 0, :], ALU.mult), reads=[kan + "b", kc], writes=[kt + "2"])
                    S.pool(tt(t_[0:M, 3, :], a_[0:M, 0, :], c_[0:M, 1, :], ALU.mult), reads=[kan + "a", kc], writes=[kt + "3"])
                    o_, ko = o12.next()
                    S.pool(tt(o_[0:M, 0, :], t_[0:M, 0, :], t_[0:M, 1, :], ALU.subtract), reads=[kt + "0", kt + "1"], writes=[ko + "0"])
                    S.dve(tt(o_[0:M, 1, :], t_[0:M, 2, :], t_[0:M, 3, :], ALU.add), reads=[kt + "2", kt + "3"], writes=[ko + "1"])
                    for hl in range(M // 32):
                        for half in range(2):
                            if grp < 2:
                                dst = self.QA[grp * 4 + hl, half * 32:(half + 1) * 32, t0:t0 + 512]
                            else:
                                dst = self.KA[hl * 64 + half * 32: hl * 64 + (half + 1) * 32, t0:t0 + 512]
                            S.dma(dst, o_[hl * 32:(hl + 1) * 32, half, :], reads=[ko + str(half)], writes=["QK"], chan=ko + "s%d%d" % (hl, half))
                if jt + 1 < NT:
                    cur = prep(jt + 1, 2)
                mq_, kmq = mqs.next()
                mk_, kmk = mks.next()
                for h in range(4):
                    p, kp = fm(640 + h * 128, 128)
                    S.act(actf(mq_[:, h, :], p[:], AF.Copy), reads=[kp], writes=[kmq])
                for h in range(4):
                    p, kp = fm(1152 + h * 128, 128)
                    S.act(actf(mk_[:, h, :], p[:], AF.Copy, scale=float(128 ** -0.5)), reads=[kp], writes=[kmk])
                S.dma(self.MQ[:, :, t0:t0 + 512].rearrange("h d t -> d h t"), mq_[:], reads=[kmq], writes=["MQ"], chan=kmq + "s")
                S.dma(self.MKs[:, :, t0:t0 + 512].rearrange("h d t -> d h t"), mk_[:], reads=[kmk], writes=["MK"], chan=kmk + "s")
                vt, kv = vts.next()
                mv, kmv = mvs.next()
                sg, ksg = sgs.next()
                gt, kgt = gts.next()
                for sub in range(4):
                    hs = lambda c: h_[:, c, sub * 128:(sub + 1) * 128]
                    p1, k1 = ptm.next()
                    S.pe(mmg([(p1[:, 0:144], hs(c), w[:, c, 1664:1808]) for c in range(8)]), reads=hk + ["wab"], writes=[k1])
                    S.act(actf(vt[:, sub, :, 0:64], p1[:, 0:128].rearrange("p (h d) -> p h d", d=64), AF.Copy), reads=[k1], writes=[kv])
                    S.dve(tt(gt[:, sub, :], p1[:, 128:144], gbias[:], ALU.add), reads=[k1, "gbias"], writes=[kgt])
                    p2, k2 = ptm.next()
                    S.pe(mmg([(p2[:], hs(c), w[:, c, 1808:2320]) for c in range(8)]), reads=hk + ["wab"], writes=[k2])
                    S.dve(cp(mv[:, sub, :, 0:128], p2[:].rearrange("p (h d) -> p h d", d=128)), reads=[k2], writes=[kmv])
                    p3, k3 = ptm.next()
                    S.pe(mmg([(p3[:], hs(c), w[:, c, 2320:2832]) for c in range(8)]), reads=hk + ["wab"], writes=[k3])
                    S.act(actf(sg[:, sub, :], p3[:], AF.Sigmoid), reads=[k3], writes=[ksg])
                S.dma(self.VA[t0:t0 + 512, :].rearrange("(s p) c -> p s c", p=128), vt[:].rearrange("p s h d -> p s (h d)"), reads=[kv], writes=["VA"], chan=kv + "s")
                S.dma(self.MV[t0:t0 + 512, :].rearrange("(s p) c -> p s c", p=128), mv[:].rearrange("p s h d -> p s (h d)"), reads=[kmv], writes=["MV"], chan=kmv + "s")
                S.dma(self.SG[t0:t0 + 512, :].rearrange("(s p) c -> p s c", p=128), sg[:], reads=[ksg], writes=["SG"], chan=ksg + "s")
                S.dma(self.GT[t0:t0 + 512, :].rearrange("(s p) c -> p s c", p=128), gt[:], reads=[kgt], writes=["GT"], chan=kgt + "s")

    def attention(self):
        S, NT, NCH, T, nc = self.S, self.NT, self.NCH, self.T, self.nc
        with self.phase(psum=False):
            ph = self._ph
            ppr = Ring([(ph.enter_context(nc.psum_tensor(self.un("pp"), [128, 1024], F32)), "pp%d" % i) for i in range(2)])
            poa = (ph.enter_context(nc.psum_tensor(self.un("poa"), [128, 512], F32)), "poa")
            pob = (ph.enter_context(nc.psum_tensor(self.un("pob"), [128, 512], F32)), "pob")
            ptb = ph.enter_context(nc.psum_tensor(self.un("ptb"), [128, 1024], BF16))
            K = self.sb("Kall", [128, T], BF16)
            V = self.sb("Vall", [128, NCH, 130], BF16)
            S.dma(K[:], self.KA, writes=["K"], chan="K")
            S.dma(V[:], self.VA.rearrange("(b p) c -> p b c", p=128), writes=["V"], chan="V")
            qr = self.ring("q", 2, [128, 8, 512], BF16)
            for (t_, k_) in qr.items:
                S.dve(mset(t_[:], 0.0), writes=[k_ + "0", k_ + "1"])
            pr = self.ring("p", 3, [128, 1024], BF16)
            rcr = self.ring("rc", 2, [128, 4], F32)
            otr = self.ring("ot", 2, [128, 4, 8, 64], BF16)
            ob = self.ring("ob", 2, [128, 4, 512], BF16)
            items = [(jq, hp, kb) for jq in range(NT) for hp in range(4) for kb in range(NCH)]
            tctx, ictx = {}, {}

            def stage1(i):
                jq, hp, kb = items[i]
                t0 = jq * 512
                if hp == 0 and kb == 0:
                    q_, kq = qr.next()
                    for kvh in range(2):
                        S.dma(q_[kvh * 64:(kvh + 1) * 64, kvh * 4:(kvh + 1) * 4, :], self.QA[kvh * 4:(kvh + 1) * 4, :, t0:t0 + 512].rearrange("h d t -> d h t"),
                              writes=[kq + str(kvh)], chan=kq + str(kvh))
                    tctx[jq] = (q_, kq) + otr.next()
                q_, kq, ot, kot = tctx[jq]
                pb, kpb = ppr.next()

                def st2(e, pb=pb, q_=q_, hp=hp, kb=kb):
                    e.matmul(pb[:, 0:512], lhsT=K[:, kb * 128:(kb + 1) * 128], rhs=q_[:, hp, :], start=True, stop=True)
                    return e.matmul(pb[:, 512:1024], lhsT=K[:, kb * 128:(kb + 1) * 128], rhs=q_[:, 4 + hp, :], start=True, stop=True)
                S.pe(st2, reads=["K", kq + "0", kq + "1"], writes=[kpb])
                p_, kp_ = pr.next()
                S.act(actf(p_[:], pb[:], AF.Exp, bias=self.mb[:, jq, (kb // 4):(kb // 4) + 1]), reads=[kpb, "mb"], writes=[kp_])
                ictx[i] = (p_, kp_)

            def stage2(i):
                jq, hp, kb = items[i]
                t0 = jq * 512
                q_, kq, ot, kot = tctx[jq]
                p_, kp_ = ictx.pop(i)

                def pv8(e, p_=p_, kb=kb):
                    ins = None
                    for hh, (po, _) in enumerate((poa, pob)):
                        for sub in range(4):
                            ins = e.matmul(po[:, sub * 65:(sub + 1) * 65], lhsT=p_[:, hh * 512 + sub * 128: hh * 512 + (sub + 1) * 128],
                                           rhs=V[:, kb, hh * 65:(hh + 1) * 65], start=(kb == 0 and sub == 0), stop=(kb == NCH - 1 and sub == 3))
                    return ins
                S.pe(pv8, reads=["V", kp_], writes=["poa", "pob"])
                if kb != NCH - 1:
                    return
                for hh, (po, kpo) in enumerate((poa, pob)):
                    h = hh * 4 + hp
                    pv = po[:, 0:260].rearrange("p (s c) -> p s c", c=65)
                    r_, kr = rcr.next()
                    S.dve(recip(r_[:], pv[:, :, 64]), reads=[kpo], writes=[kr])
                    S.dve(tt(ot[:, :, h, :], pv[:, :, 0:64], r_[:].unsqueeze(2).to_broadcast([128, 4, 64]), ALU.mult), reads=[kpo, kr], writes=[kot])
                if hp != 3:
                    return
                o_, ko = ob.next()
                for sp in range(2):
                    def tr8(e, ot=ot, sp=sp):
                        ins = None
                        for hp2 in range(4):
                            for s_ in range(2):
                                slot = hp2 * 2 + s_
                                ins = e.transpose(ptb[:, slot * 128:(slot + 1) * 128],
                                                  ot[:, sp * 2 + s_, 2 * hp2:2 * hp2 + 2, :].rearrange("p h d -> p (h d)"), self.ident_b)
                        return ins
                    S.pe(tr8, reads=[kot, "cmb"], writes=["ptb"])
                    S.act(actf(o_[:, :, sp * 256:(sp + 1) * 256].rearrange("p a (s q) -> p a s q", q=128),
                               ptb[:].rearrange("p (a s q) -> p a s q", a=4, s=2), AF.Copy), reads=["ptb"], writes=[ko])
                S.dma(self.MIX[0:512, t0:t0 + 512].rearrange("(a p) t -> p a t", p=128), o_[:], reads=[ko], writes=["MIXa"], chan=ko + "s")

            self.pipeline(len(items), stage1, stage2, 1)

    def mlstm(self, j):
        S, NT, NCH, T, nc = self.S, self.NT, self.NCH, self.T, self.nc
        NG = NCH * 4
        with self.phase():
            G = self.sb("G", [128, NCH, 16], F32)
            S.dma(G[:], self.GT.rearrange("(c p) g -> p c g", p=128), writes=["G"], chan="G")
            G5 = G[:].rearrange("p c (d k h) -> p d c k h", d=2, k=2, h=4)
            gi, gf = G5[:, :, :, 0, :], G5[:, :, :, 1, :]
            shp = [128, 2, NCH, 4]
            mk = lambda n: self.sb(n, shp, F32)
            FL, Bc, A_, AMr, BLr, Mt, mt, WK, THR, KEEP = [mk(n) for n in ("FL", "Bc", "Aa", "AMr", "BLr", "Mt", "mt", "WK", "THR", "KEEP")]
            tmp = mk("tmp")
            fl2 = lambda t, d: t[:, d, :, :].rearrange("p c h -> p (c h)")
            S.act(actf(tmp[:], gf, AF.Abs), reads=["G"], writes=["tmp"])
            S.act(actf(tmp[:], tmp[:], AF.Exp, scale=-1.0), reads=["tmp"], writes=["tmp"])
            S.act(actf(tmp[:], tmp[:], AF.Ln, bias=1.0), reads=["tmp"], writes=["tmp"])
            S.dve(ts(FL[:], gf, 0.0, ALU.min), reads=["G"], writes=["FL"])
            S.dve(tt(FL[:], FL[:], tmp[:], ALU.subtract), reads=["FL", "tmp"], writes=["FL"])
            bw = min(128, NG)
            nblk = NG // bw
            am = self.sb("am", [128, 2, nblk], F32)
            dg = self.sb("dg", [128, 128], F32)
            for d in range(2):
                tri = self.cm[:, C_TRIU, :] if d == 0 else self.cm[:, C_TRIL, :]
                S.pe(mm1(self.ps[0][:, 0:NG], tri, fl2(FL, d)), reads=["FL", "cm"], writes=["ps0"])
                S.act(actf(fl2(Bc, d), self.ps[0][:, 0:NG], AF.Copy), reads=["ps0"], writes=["Bc"])
                S.pe(mm1(self.ps[1][:, 0:NG], self.ones_f, fl2(FL, d)), reads=["FL", "cm"], writes=["ps1"])
                S.act(actf(fl2(BLr, d), self.ps[1][:, 0:NG], AF.Copy), reads=["ps1"], writes=["BLr"])
                S.dve(tt(A_[:, d], gi[:, d], Bc[:, d], ALU.subtract), reads=["G", "Bc"], writes=["Aa"])
                for b in range(nblk):
                    S.pe(tr(self.ps[2][0:bw, 0:128], fl2(A_, d)[:, b * bw:(b + 1) * bw], self.ident_f), reads=["Aa", "cm"], writes=["ps2"])
                    S.dve(lambda e, d=d, b=b, ps2=self.ps[2]: e.tensor_reduce(out=am[0:bw, d, b:b + 1], in_=ps2[0:bw, 0:128], axis=AX.X, op=ALU.max),
                          reads=["ps2"], writes=["am"])
                    S.dve(ts(dg[0:bw, 0:bw], self.ident_f[0:bw, 0:bw], am[0:bw, d, b:b + 1], ALU.mult), reads=["am", "cm"], writes=["dg"])
                    S.pe(mm1(self.ps[3][:, 0:bw], self.ones_f[0:bw, :], dg[0:bw, 0:bw]), reads=["dg", "cm"], writes=["ps3"])
                    S.act(actf(fl2(AMr, d)[:, b * bw:(b + 1) * bw], self.ps[3][:, 0:bw], AF.Copy), reads=["ps3"], writes=["AMr"])
            mrun = self.sb("mrun", [128, 2, 4], F32)
            S.dve(mset(mrun[:], 0.0), writes=["mrun0", "mrun1"])
            for k in range(NCH):
                for d in range(2):
                    eng = "dve"
                    kk = "rec%d" % d
                    c = k if d == 0 else NCH - 1 - k
                    S.add(eng, ts(mt[:, d, c, :], mrun[:, d, :], self.carry[:, d, c:c + 1], ALU.mult), reads=["mrun%d" % d, "carry"], writes=[kk + "mt"])
                    S.add(eng, tt(Mt[:, d, c, :], mt[:, d, c, :], AMr[:, d, c, :], ALU.max), reads=[kk + "mt", "AMr"], writes=[kk + "Mt"])
                    S.add(eng, tt(mrun[:, d, :], Mt[:, d, c, :], BLr[:, d, c, :], ALU.add), reads=[kk + "Mt", "BLr"], writes=["mrun%d" % d])
            rk = ["rec0mt", "rec1mt", "rec0Mt", "rec1Mt"]
            S.dve(tt(KEEP[:], mt[:], Mt[:], ALU.subtract), reads=rk, writes=["KEEP"])
            S.act(actf(KEEP[:], KEEP[:], AF.Exp), reads=["KEEP"], writes=["KEEP"])
            S.dve(tt(KEEP[:].rearrange("p d c h -> p (d c) h"), KEEP[:].rearrange("p d c h -> p (d c) h"),
                     self.carry[:].rearrange("p d c -> p (d c)").unsqueeze(2).to_broadcast([128, 2 * NCH, 4]), ALU.mult), reads=["KEEP", "carry"], writes=["KEEP"])
            S.dve(tt(WK[:], A_[:], Mt[:], ALU.subtract), reads=["Aa"] + rk, writes=["WK"])
            S.act(actf(WK[:], WK[:], AF.Exp), reads=["WK"], writes=["WK"])
            S.dve(tt(THR[:], Bc[:], Mt[:], ALU.add), reads=["Bc"] + rk, writes=["THR"])
            S.act(actf(THR[:], THR[:], AF.Exp, scale=-1.0), reads=["THR"], writes=["THR"])
            maskf = self.cm[:, C_TRIU, :]
            maskb = self.cm[:, C_TRIL, :]
            qr = [self.ring("mq", 2, [128, 4, 512], BF16) for d in range(2)]
            kr = [self.ring("mk", 2, [128, 4, 512], BF16) for d in range(2)]
            vr = [self.ring("mv", 2, [128, 4, 516], BF16) for d in range(2)]
            Cst = [[self.sb("Cst", [128, 129], F32) for h in range(4)] for d in range(2)]
            Cbf = [[self.sb("Cbf", [128, 129], BF16) for h in range(4)] for d in range(2)]
            for d in range(2):
                for h in range(4):
                    S.pool(mset(Cst[d][h][:], 0.0), writes=["C%d%d" % (d, h)])
            atr = self.ring("at", 6, [128, 128], BF16)
            kwr = self.ring("kw", 6, [128, 128], BF16)
            rr_ = self.ring("r", 6, [128, 2], F32)
            hst = [self.ring("hst", 2, [128, 512], F32) for d in range(2)]
            pS = Ring([(self.ps[0], "ps0"), (self.ps[1], "ps1")])
            pH = Ring([(self.ps[2], "ps2"), (self.ps[3], "ps3")])
            pU = Ring([(self.ps[4], "ps4"), (self.ps[5], "ps5")])
            pT = Ring([(self.psb[0], "psb0"), (self.psb[1], "psb1")])
            MQ3 = self.MQ.rearrange("h d t -> d h t")
            MK3 = self.MKs.rearrange("h d t -> d h t")
            HO = [self.HF, self.HB]
            items = [(k, h, d) for k in range(NCH) for h in range(4) for d in range(2)]
            tctx, cctx, ictx = {}, {}, {}

            def geom(it):
                k, h, d = it
                c = k if d == 0 else NCH - 1 - k
                return d, c // 4, c % 4, h, c, (c // 4) * 512

            def ensure(d, jt):
                if (d, jt) in tctx or jt < 0 or jt >= NT:
                    return
                t0 = jt * 512
                q_, kq = qr[d].next()
                k_, kk = kr[d].next()
                v_, kv = vr[d].next()
                S.dma(q_[:], MQ3[:, :, t0:t0 + 512], writes=[kq], chan=kq)
                S.dma(k_[:], MK3[:, :, t0:t0 + 512], writes=[kk], chan=kk)
                S.dma(v_[:], self.MV[t0:t0 + 512, :].rearrange("(s p) c -> p s c", p=128), writes=[kv], chan=kv)
                tctx[(d, jt)] = (q_, kq, k_, kk, v_, kv)

            def stage1(i):
                d, jt, sub, h, c, t0 = geom(items[i])
                mask = maskf if d == 0 else maskb
                if h == 0:
                    ensure(d, jt)
                    if items[i][0] % 4 == 1:
                        ensure(d, jt + (1 if d == 0 else -1))
                    cctx[(d, c)] = hst[d].next()
                q_, kq, k_, kk, v_, kv = tctx[(d, jt)]
                ck = "C%d%d" % (d, h)
                qs = q_[:, h, sub * 128:(sub + 1) * 128]
                ks_ = k_[:, h, sub * 128:(sub + 1) * 128]
                wkc = WK[:, d, c, h:h + 1]
                st_, kst = pS.next()
                S.pe(mm1(st_[:, 0:128], ks_, qs), reads=[kk, kq], writes=[kst])
                at, kat = atr.next()
                S.dve(stt(at[:], st_[:, 0:128], wkc, mask, ALU.mult, ALU.mult), reads=[kst, "WK", "cm"], writes=[kat])
                ptr, kptr = pT.next()
                S.pe(tr(ptr[:, 0:128], ks_, self.ident_b), reads=[kk, "cmb"], writes=[kptr])
                kw, kkw = kwr.next()
                S.act(actf(kw[:], ptr[:, 0:128], AF.Copy, scale=wkc), reads=[kptr, "WK"], writes=[kkw])
                S.act(actf(Cbf[d][h][:], Cst[d][h][:], AF.Copy, scale=KEEP[:, d, c, h:h + 1]), reads=[ck, "KEEP"], writes=[ck + "b"])
                ictx[i] = (at, kat, kw, kkw)

            def stage2(i):
                d, jt, sub, h, c, t0 = geom(items[i])
                q_, kq, k_, kk, v_, kv = tctx[(d, jt)]
                hs_, khs = cctx[(d, c)]
                at, kat, kw, kkw = ictx.pop(i)
                ck = "C%d%d" % (d, h)
                tc0 = t0 + sub * 128
                qs = q_[:, h, sub * 128:(sub + 1) * 128]
                vs = v_[:, sub, h * 129:(h + 1) * 129]
                ph, kph = pH.next()
                S.pe(mmg([(ph[:, 0:129], at[:], vs), (ph[:, 0:129], qs, Cbf[d][h][:])]), reads=[kat, kv, kq, ck + "b"], writes=[kph])
                pu, kpu = pU.next()
                S.pe(mm1(pu[:, 0:129], kw[:], vs), reads=[kkw, kv], writes=[kpu])
                S.dve(stt(Cst[d][h][:], Cst[d][h][:], KEEP[:, d, c, h:h + 1], pu[:, 0:129], ALU.mult, ALU.add), reads=[ck, kpu, "KEEP"], writes=[ck])
                r_, kr_ = rr_.next()
                S.dve(ts(r_[:, 0:1], ph[:, 128:129], -1.0, ALU.mult, THR[:, d, c, h:h + 1], ALU.max), reads=[kph, "THR"], writes=[kr_])
                S.dve(tt(r_[:, 0:1], r_[:, 0:1], ph[:, 128:129], ALU.max), reads=[kr_, kph], writes=[kr_])
                S.dve(recip(r_[:, 1:2], r_[:, 0:1]), reads=[kr_], writes=[kr_])
                S.act(actf(hs_[:, h * 128:(h + 1) * 128], ph[:, 0:128], AF.Copy, scale=r_[:, 1:2]), reads=[kph, kr_], writes=[khs])
                if h == 3:
                    S.dma(HO[d][tc0:tc0 + 128, :], hs_[:], reads=[khs], writes=["HO"], chan=khs + "s")

            self.pipeline(len(items), stage1, stage2, 2)

        with self.phase():
            gain = self.sb("ogain", [128, 512], F32)
            S.dma(gain[:], self.w["mlstm_out_norm"][j:j + 1, :].to_broadcast([128, 512]), writes=["ogain"], chan="ogain")
            hfr = self.ring("hf", 3, [128, 512], F32)
            hbr = self.ring("hb", 3, [128, 512], F32)
            sgr = self.ring("sgl", 3, [128, 512], F32)
            ssq = self.ring("ssq", 3, [128, 8], F32)
            junk = self.ring("junk", 2, [128, 512], F32)
            ymr = self.ring("ym", 3, [128, 512], BF16)
            mxs = self.ring("mxs", 2, [128, 4, 512], BF16)
            pT = Ring([(self.psb[0], "psb0"), (self.psb[1], "psb1")])

            def ld(c):
                hf, khf = hfr.next()
                hb, khb = hbr.next()
                sg, ksg = sgr.next()
                S.dma(hf[:], self.HF[c * 128:(c + 1) * 128, :], writes=[khf], chan=khf)
                S.dma(hb[:], self.HB[c * 128:(c + 1) * 128, :], writes=[khb], chan=khb)
                S.dma(sg[:], self.SG[c * 128:(c + 1) * 128, :], writes=[ksg], chan=ksg)
                return hf, khf, hb, khb, sg, ksg
            pend = [ld(0), ld(1)] if NCH > 1 else [ld(0)]
            for c in range(NCH):
                hf, khf, hb, khb, sg, ksg = pend.pop(0)
                if c + 2 < NCH:
                    pend.append(ld(c + 2))
                sub = c % 4
                if sub == 0:
                    mx, kmx = mxs.next()
                S.dve(tt(hf[:], hf[:], hb[:], ALU.add), reads=[khf, khb], writes=[khf])
                sq_, ksq = ssq.next()
                jk, kjk = junk.next()
                S.dve(mset(sq_[:], 0.0), writes=[ksq + "a%d" % q_ for q_ in range(4)] + [ksq + "b"])
                for hh in range(4):
                    S.act(actf(jk[:, hh * 128:(hh + 1) * 128], hf[:, hh * 128:(hh + 1) * 128], AF.Square, accum=sq_[:, hh:hh + 1]), reads=[khf], writes=[kjk + str(hh), ksq + "a%d" % hh])
                S.act(actf(sq_[:, 4:8], sq_[:, 0:4], AF.Sqrt, bias=EPS, scale=1.0 / 128), reads=[ksq + "a%d" % q_ for q_ in range(4)], writes=[ksq + "b"])
                S.dve(recip(sq_[:, 4:8], sq_[:, 4:8]), reads=[ksq + "b"], writes=[ksq + "b"])
                S.dve(tt(hf[:].rearrange("p (h e) -> p h e", e=128), hf[:].rearrange("p (h e) -> p h e", e=128),
                         sq_[:, 4:8].unsqueeze(2).to_broadcast([128, 4, 128]), ALU.mult), reads=[khf, ksq + "b"], writes=[khf])
                S.pool(tt(sg[:], sg[:], gain[:], ALU.mult), reads=[ksg, "ogain"], writes=[ksg])
                ym, kym = ymr.next()
                S.dve(tt(ym[:], hf[:], sg[:], ALU.mult), reads=[khf, ksg], writes=[kym])
                ptr, kptr = pT.next()

                def tr4(e, ptr=ptr, ym=ym):
                    ins = None
                    for hh in range(4):
                        ins = e.transpose(ptr[:, hh * 128:(hh + 1) * 128], ym[:, hh * 128:(hh + 1) * 128], self.ident_b)
                    return ins
                S.pe(tr4, reads=[kym, "cmb"], writes=[kptr])
                S.act(actf(mx[:, :, sub * 128:(sub + 1) * 128], ptr[:, 0:512].rearrange("p (h e) -> p h e", e=128), AF.Copy), reads=[kptr], writes=[kmx])
                if sub == 3:
                    t0 = (c // 4) * 512
                    S.dma(self.MIX[512:1024, t0:t0 + 512].rearrange("(h p) t -> p h t", p=128), mx[:], reads=[kmx], writes=["MIXb"], chan=kmx + "s")

    def ret_inproj(self, X, j, L):
        S, NT, nc = self.S, self.NT, self.nc
        with self.phase():
            w = self.sb("wret", [128, 8, 6144], BF16)
            g = self.load_gain(self.w["norm_mix"][L], "gmix")
            self.prep_weight(w, "wret", self.w["ret_w_in"][j], 8, [(i * 2048, 2048, i * 2048, 2048, None) for i in range(3)], gain=g, gkey="gmix", reserve=92 * 1024)
            xt, sq, rs, hT = self.norm_bufs()
            X3 = self.x3(X)
            cs = self.ring("cs", 1, [128, 4, 512], F32)
            t4 = self.ring("t4", 1, [128, 4, 512], F32)
            o12 = self.ring("o12", 2, [128, 2, 512], BF16)
            vts = self.ring("rv", 1, [128, 4, 2048], BF16)
            gts = self.ring("rg", 1, [128, 2048], F32)
            pf = Ring([(self.ps[i], self.pk[i]) for i in (1, 2, 3)])
            ptm = Ring([(self.ps[4], "ps4"), (self.ps[5], "ps5")])
            def cs_load(jt_):
                c_, kc = cs.next()
                S.dma(c_[:, 0:2, :], self.ropeR[:, :, jt_ * 512:(jt_ + 1) * 512].rearrange("a p t -> p a t"), writes=[kc, kc + "k"], chan=kc)
                return c_, kc
            prep = self.norm_pipeline(X3, xt, sq, rs, hT)
            cur = prep(0)
            cnxt = cs_load(0)
            for jt in range(NT):
                t0 = jt * 512
                bufs, hk = cur
                h_ = bufs["hT"][0]
                c_, kc = cnxt
                S.act(actf(c_[:, 2:4, :], c_[:, 0:2, :], AF.Copy, scale=1.0 / 16), reads=[kc], writes=[kc + "k"])
                for qk in range(2):
                    if qk == 1 and jt + 1 < NT:
                        cur = prep(jt + 1, 2)
                    co = 0 if qk == 0 else 2
                    ck_ = [kc] if qk == 0 else [kc + "k"]
                    dstT = self.RQ if qk == 0 else self.RK
                    for h in range(4):
                        if qk == 0 and h == 1 and jt + 1 < NT:
                            prep(jt + 1, 1)
                        col = qk * 1024 + h * 256
                        pa, kpa = pf.next()
                        S.pe(mmg([(pa[:], w[:, c, col:col + 128], h_[:, c, :]) for c in range(8)]), reads=hk + self.wkeys("wret", col, 128), writes=[kpa])
                        pb, kpb = pf.next()
                        S.pe(mmg([(pb[:], w[:, c, col + 128:col + 256], h_[:, c, :]) for c in range(8)]), reads=hk + self.wkeys("wret", col + 128, 128), writes=[kpb])
                        t_, kt = t4.next()
                        S.dve(tt(t_[:, 0, :], pa[:], c_[:, co, :], ALU.mult), reads=[kpa] + ck_, writes=[kt + "0"])
                        S.dve(tt(t_[:, 1, :], pb[:], c_[:, co + 1, :], ALU.mult), reads=[kpb] + ck_, writes=[kt + "1"])
                        S.dve(tt(t_[:, 2, :], pb[:], c_[:, co, :], ALU.mult), reads=[kpb] + ck_, writes=[kt + "2"])
                        S.dve(tt(t_[:, 3, :], pa[:], c_[:, co + 1, :], ALU.mult), reads=[kpa] + ck_, writes=[kt + "3"])
                        o_, ko = o12.next()
                        S.pool(tt(o_[:, 0, :], t_[:, 0, :], t_[:, 1, :], ALU.subtract), reads=[kt + "0", kt + "1"], writes=[ko])
                        S.pool(tt(o_[:, 1, :], t_[:, 2, :], t_[:, 3, :], ALU.add), reads=[kt + "2", kt + "3"], writes=[ko])
                        S.dma(dstT[2 * h:2 * h + 2, :, t0:t0 + 512].rearrange("a p t -> p a t"), o_[:], reads=[ko], writes=["RQK"], chan=ko + "s")
                if jt + 1 < NT:
                    cnxt = cs_load(jt + 1)
                vt, kv = vts.next()
                for sub in range(4):
                    hs = lambda c: h_[:, c, sub * 128:(sub + 1) * 128]
                    for blk in range(4):
                        p1, k1 = ptm.next()
                        S.pe(mmg([(p1[:], hs(c), w[:, c, 2048 + blk * 512:2048 + (blk + 1) * 512]) for c in range(8)]), reads=hk + self.wkeys("wret", 2048 + blk * 512, 512), writes=[k1])
                        S.act(actf(vt[:, sub, blk * 512:(blk + 1) * 512], p1[:], AF.Copy), reads=[k1], writes=[kv])
                    gt, kg = gts.next()
                    for blk in range(4):
                        p1, k1 = ptm.next()
                        S.pe(mmg([(p1[:], hs(c), w[:, c, 4096 + blk * 512:4096 + (blk + 1) * 512]) for c in range(8)]), reads=hk + self.wkeys("wret", 4096 + blk * 512, 512), writes=[k1])
                        S.act(actf(gt[:, blk * 512:(blk + 1) * 512], p1[:], AF.Silu), reads=[k1], writes=[kg])
                    S.dma(self.RG[t0 + sub * 128:t0 + (sub + 1) * 128, :], gt[:], reads=[kg], writes=["RG"], chan=kg + "s")
                S.dma(self.RV[t0:t0 + 512, :].rearrange("(s p) c -> p s c", p=128), vt[:], reads=[kv], writes=["RV"], chan=kv + "s")

    def retention(self, j):
        S, NT, NCH, T, nc = self.S, self.NT, self.NCH, self.T, self.nc
        with self.phase():
            lg = self.sb("lg", [128, 8], F32)
            tmp = self.sb("lgt", [128, 8], F32)
            S.dma(lg[:], self.w["ret_decay_logit"][j:j + 1].rearrange("a d h -> a (d h)").to_broadcast([128, 8]), writes=["lg"], chan="lg")
            S.act(actf(tmp[:], lg[:], AF.Abs), reads=["lg"], writes=["lgt"])
            S.act(actf(tmp[:], tmp[:], AF.Exp, scale=-1.0), reads=["lgt"], writes=["lgt"])
            S.act(actf(tmp[:], tmp[:], AF.Ln, bias=1.0), reads=["lgt"], writes=["lgt"])
            S.dve(ts(lg[:], lg[:], 0.0, ALU.min), reads=["lg"], writes=["lg"])
            S.dve(tt(lg[:], lg[:], tmp[:], ALU.subtract), reads=["lg", "lgt"], writes=["lg"])
            DT = self.sb("DT", [128, 8, 128], F32)
            QD = self.sb("QD", [128, 8, 128], F32)
            QDb = self.sb("QDb", [128, 8, 128], BF16)
            KD = self.sb("KD", [128, 8], F32)
            CDC = self.sb("CDC", [128, 8, NCH], F32)
            cdv = self.sb("cdv", [128, 8], F32)
            for hd in range(8):
                d = hd // 4
                S.act(actf(DT[:, hd, :], self.cm[:, C_DFW if d == 0 else C_DBW, :], AF.Exp, scale=lg[:, hd:hd + 1]), reads=["lg", "cm"], writes=["DT"])
                S.dve(tt(DT[:, hd, :], DT[:, hd, :], self.cm[:, C_TRIU if d == 0 else C_SL, :], ALU.mult), reads=["DT", "cm"], writes=["DT"])
                S.act(actf(QD[:, hd, :], self.cm[:, C_QDF if d == 0 else C_QDB, :], AF.Exp, scale=lg[:, hd:hd + 1]), reads=["lg", "cm"], writes=["QD"])
                S.dve(cp(QDb[:, hd, :], QD[:, hd, :]), reads=["QD"], writes=["QDb"])
                S.act(actf(KD[:, hd:hd + 1], self.cm[:, C_KD, d:d + 1], AF.Exp, scale=lg[:, hd:hd + 1]), reads=["lg", "cm"], writes=["KD"])
                S.act(actf(cdv[:, hd:hd + 1], self.cm[:, C_KD, 2:3], AF.Exp, scale=lg[:, hd:hd + 1]), reads=["lg", "cm"], writes=["cdv"])
                S.dve(ts(CDC[:, hd, :], self.carry[:, d, :], cdv[:, hd:hd + 1], ALU.mult), reads=["cdv", "carry"], writes=["CDC"])
            qr = [self.ring("rq", 2, [128, 8, 512], BF16) for d in range(2)]
            kr = [self.ring("rk", 2, [128, 8, 512], BF16) for d in range(2)]
            vr = [self.ring("rv", 3, [128, 2048], BF16) for d in range(2)]
            St = [[[self.sb("St", [128, 512], F32) for c in range(2)] for h in range(4)] for d in range(2)]
            Sb = [[[self.sb("Sb", [128, 512], BF16) for c in range(2)] for h in range(4)] for d in range(2)]
            for d in range(2):
                for h in range(4):
                    for c in range(2):
                        S.pool(mset(St[d][h][c][:], 0.0), writes=["S%d%d%d" % (d, h, c)])
            atr = self.ring("at", 6, [128, 128], BF16)
            kwr = self.ring("kw", 6, [128, 2, 128], BF16)
            qdr = self.ring("qd", 6, [128, 2, 128], BF16)
            yst = [self.ring("yst", 2, [128, 2048], F32) for d in range(2)]
            pS = Ring([(self.ps[0][:, 0:128], "ps0"), (self.ps[5][:, 0:128], "ps5")])
            pO = Ring([(self.ps[1], "ps1"), (self.ps[2], "ps2")])
            pU = Ring([(self.ps[3], "ps3"), (self.ps[4], "ps4")])
            pT = Ring([(self.psb[0], "psb0"), (self.psb[1], "psb1")])
            RQ3 = self.RQ.rearrange("a p t -> p a t")
            RK3 = self.RK.rearrange("a p t -> p a t")
            YO = [self.YF, self.YB]
            items = [(k, h, d) for k in range(NCH) for h in range(4) for d in range(2)]
            tctx, cctx, ictx = {}, {}, {}

            def geom(it):
                k, h, d = it
                c = k if d == 0 else NCH - 1 - k
                return d, c // 4, c % 4, h, c, (c // 4) * 512

            def ensure(d, jt):
                if (d, jt) in tctx or jt < 0 or jt >= NT:
                    return
                t0 = jt * 512
                q_, kq = qr[d].next()
                k_, kk = kr[d].next()
                S.dma(q_[:], RQ3[:, :, t0:t0 + 512], writes=[kq], chan=kq)
                S.dma(k_[:], RK3[:, :, t0:t0 + 512], writes=[kk], chan=kk)
                tctx[(d, jt)] = (q_, kq, k_, kk)

            def ensure_v(d, c):
                if (d, c) in cctx or c < 0 or c >= NCH:
                    return
                v_, kv = vr[d].next()
                S.dma(v_[:], self.RV[c * 128:(c + 1) * 128, :], writes=[kv], chan=kv)
                cctx[(d, c)] = (v_, kv) + yst[d].next()

            def stage1(i):
                d, jt, sub, h, c, t0 = geom(items[i])
                if h == 0:
                    ensure(d, jt)
                    if items[i][0] % 4 == 1:
                        ensure(d, jt + (1 if d == 0 else -1))
                    ensure_v(d, c)
                    ensure_v(d, c + (1 if d == 0 else -1))
                q_, kq, k_, kk = tctx[(d, jt)]
                sl = slice(sub * 128, (sub + 1) * 128)
                hd = d * 4 + h
                st_, kst = pS.next()
                S.pe(mmg([(st_, k_[:, 2 * h + cc, sl], q_[:, 2 * h + cc, sl]) for cc in range(2)]), reads=[kk, kq], writes=[kst])
                at, kat = atr.next()
                S.dve(tt(at[:], st_, DT[:, hd, :], ALU.mult), reads=[kst, "DT"], writes=[kat])
                kw, kkw = kwr.next()
                qd, kqd = qdr.next()
                ptr, kptr = pT.next()

                def tr2(e, ptr=ptr, k_=k_, h=h, sl=sl):
                    ins = None
                    for cc in range(2):
                        ins = e.transpose(ptr[:, cc * 128:(cc + 1) * 128], k_[:, 2 * h + cc, sl], self.ident_b)
                    return ins
                S.pe(tr2, reads=[kk, "cmb"], writes=[kptr])
                S.act(actf(kw[:], ptr[:, 0:256].rearrange("p (c e) -> p c e", e=128), AF.Copy, scale=KD[:, hd:hd + 1]), reads=[kptr, "KD"], writes=[kkw])
                S.pool(tt(qd[:], q_[:, 2 * h:2 * h + 2, sl], QDb[:, hd, :].unsqueeze(1).to_broadcast([128, 2, 128]), ALU.mult), reads=[kq, "QDb"], writes=[kqd])
                for cc in range(2):
                    sk = "S%d%d%d" % (d, h, cc)
                    S.act(actf(Sb[d][h][cc][:], St[d][h][cc][:], AF.Copy, scale=self.carry[:, d, c:c + 1]), reads=[sk, "carry"], writes=[sk + "b"])
                ictx[i] = (at, kat, kw, kkw, qd, kqd)

            def stage2(i):
                d, jt, sub, h, c, t0 = geom(items[i])
                v_, kv, ys, kys = cctx[(d, c)]
                at, kat, kw, kkw, qd, kqd = ictx.pop(i)
                hd = d * 4 + h
                vs = v_[:, h * 512:(h + 1) * 512]
                po, kpo = pO.next()
                S.pe(mmg([(po[:], at[:], vs), (po[:], qd[:, 0, :], Sb[d][h][0][:]), (po[:], qd[:, 1, :], Sb[d][h][1][:])]),
                     reads=[kat, kv, kqd, "S%d%d0b" % (d, h), "S%d%d1b" % (d, h)], writes=[kpo])
                for cc in range(2):
                    pu, kpu = pU.next()
                    sk = "S%d%d%d" % (d, h, cc)
                    S.pe(mm1(pu[:], kw[:, cc, :], vs), reads=[kkw, kv], writes=[kpu])
                    S.dve(stt(St[d][h][cc][:], St[d][h][cc][:], CDC[:, hd, c:c + 1], pu[:], ALU.mult, ALU.add), reads=[sk, kpu, "CDC"], writes=[sk])
                S.dve(cp(ys[:, h * 512:(h + 1) * 512], po[:]), reads=[kpo], writes=[kys])
                if h == 3:
                    S.dma(YO[d][c * 128:(c + 1) * 128, :], ys[:], reads=[kys], writes=["YO"], chan=kys + "s")

            self.pipeline(len(items), stage1, stage2, 2)

        with self.phase():
            gain = self.sb("rgain", [128, 2048], F32)
            S.dma(gain[:], self.w["ret_out_norm"][j:j + 1, :].to_broadcast([128, 2048]), writes=["rgain"], chan="rgain")
            yfr = self.ring("yf", 2, [128, 2048], F32)
            ybr = self.ring("yb", 2, [128, 2048], F32)
            rgr = self.ring("rgl", 2, [128, 2048], F32)
            ssq = self.ring("ssq", 3, [128, 8], F32)
            junk = self.ring("junk", 2, [128, 2048], F32)
            ymr = self.ring("ym", 2, [128, 2048], BF16)
            mxs = self.ring("mxs", 2, [128, 16, 512], BF16)
            pT = Ring([(self.psb[0], "psb0"), (self.psb[1], "psb1")])

            def ld(c):
                yf, kyf = yfr.next()
                yb, kyb = ybr.next()
                rg, krg = rgr.next()
                S.dma(yf[:], self.YF[c * 128:(c + 1) * 128, :], writes=[kyf], chan=kyf)
                S.dma(yb[:], self.YB[c * 128:(c + 1) * 128, :], writes=[kyb], chan=kyb)
                S.dma(rg[:], self.RG[c * 128:(c + 1) * 128, :], writes=[krg], chan=krg)
                return yf, kyf, yb, kyb, rg, krg
            nxt = ld(0)
            for c in range(NCH):
                yf, kyf, yb, kyb, rg, krg = nxt
                if c + 1 < NCH:
                    nxt = ld(c + 1)
                sub = c % 4
                sl = slice(sub * 128, (sub + 1) * 128)
                if sub == 0:
                    mx, kmx = mxs.next()
                S.pool(tt(yf[:], yf[:], yb[:], ALU.add), reads=[kyf, kyb], writes=[kyf])
                sq_, ksq = ssq.next()
                jk, kjk = junk.next()
                S.dve(mset(sq_[:], 0.0), writes=[ksq + "a%d" % q_ for q_ in range(4)] + [ksq + "b"])
                for hh in range(4):
                    S.act(actf(jk[:, hh * 512:(hh + 1) * 512], yf[:, hh * 512:(hh + 1) * 512], AF.Square, accum=sq_[:, hh:hh + 1]), reads=[kyf], writes=[kjk + str(hh), ksq + "a%d" % hh])
                S.act(actf(sq_[:, 4:8], sq_[:, 0:4], AF.Sqrt, bias=EPS, scale=1.0 / 512), reads=[ksq + "a%d" % q_ for q_ in range(4)], writes=[ksq + "b"])
                S.dve(recip(sq_[:, 4:8], sq_[:, 4:8]), reads=[ksq + "b"], writes=[ksq + "b"])
                S.dve(tt(yf[:].rearrange("p (h e) -> p h e", e=512), yf[:].rearrange("p (h e) -> p h e", e=512),
                         sq_[:, 4:8].unsqueeze(2).to_broadcast([128, 4, 512]), ALU.mult), reads=[kyf, ksq + "b"], writes=[kyf])
                S.pool(tt(rg[:], rg[:], gain[:], ALU.mult), reads=[krg, "rgain"], writes=[krg])
                ym, kym = ymr.next()
                S.dve(tt(ym[:], yf[:], rg[:], ALU.mult), reads=[kyf, krg], writes=[kym])
                for bg in range(2):
                    ptr, kptr = pT.next()

                    def tr8(e, ptr=ptr, ym=ym, bg=bg):
                        ins = None
                        for b_ in range(8):
                            ins = e.transpose(ptr[:, b_ * 128:(b_ + 1) * 128], ym[:, (bg * 8 + b_) * 128:(bg * 8 + b_ + 1) * 128], self.ident_b)
                        return ins
                    S.pe(tr8, reads=[kym, "cmb"], writes=[kptr])
                    S.act(actf(mx[:, bg * 8:(bg + 1) * 8, sl], ptr[:].rearrange("p (b e) -> p b e", e=128), AF.Copy), reads=[kptr], writes=[kmx])
                if sub == 3:
                    t0 = (c // 4) * 512
                    S.dma(self.MIX[0:2048, t0:t0 + 512].rearrange("(b p) t -> p b t", p=128), mx[:], reads=[kmx], writes=["MIXr"], chan=kmx + "s")


def rope_tables(seglen, head_dim, nseg, reps):
    rows = seglen // 64
    row_idx = np.repeat(np.arange(rows, dtype=np.float32), 64)
    col_idx = np.tile(np.arange(64, dtype=np.float32), rows)
    axis_dim = head_dim // 2
    inv_freq = (np.float32(10000.0) ** (-np.arange(0, axis_dim, 2, dtype=np.float32) / np.float32(axis_dim))).astype(np.float32)
    ang = np.concatenate([row_idx[:, None] * inv_freq, col_idx[:, None] * inv_freq], axis=-1).astype(np.float32)
    cs = np.stack([np.cos(ang), np.sin(ang)], 0).astype(np.float32)
    cs = np.tile(cs, (1, nseg, 1))
    cs = cs.transpose(0, 2, 1)
    return np.ascontiguousarray(np.tile(cs, (1, reps, 1)))


def core_tables(T, nseg):
    NT, NCH = T // 512, T // 128
    seglen = T // nseg
    tps = NT // nseg
    cps = NCH // nseg
    seg_t = np.arange(NT) // tps
    mb = np.where(seg_t[:, None] == seg_t[None, :], 0.0, -30000.0).astype(np.float32)
    carry = np.ones((2, NCH), np.float32)
    carry[0, np.arange(NCH) % cps == 0] = 0.0
    carry[1, np.arange(NCH) % cps == cps - 1] = 0.0
    flb = np.ones((NT, 2), np.float32)
    flb[np.arange(NT) % tps == 0, 0] = 0.0
    flb[np.arange(NT) % tps == tps - 1, 1] = 0.0
    rep = lambda a: np.ascontiguousarray(np.broadcast_to(a.reshape(1, -1), (128, a.size))).astype(np.float32)
    return {
        "maskb": rep(mb), "carry": rep(carry), "flb": rep(flb),
        "ropeA": rope_tables(seglen, 64, nseg, 4), "ropeR": rope_tables(seglen, 256, nseg, 1),
    }


DBG = {}
FULL_STEPS = [("ab", 0, 0), ("ffn", 0), ("ret", 0, 1), ("ffn", 1), ("ab", 1, 2), ("ffn", 2), ("ret", 1, 3), ("ffn", 3), ("final",)]
_CACHE = {}


def run_cores(T, steps, core_x, core_nseg, weights):
    key = (T, tuple(steps))
    if key not in _CACHE:
        _CACHE[key] = MK(T, steps).build()
    nc = _CACHE[key]
    cm = host_cmat()
    tabs = {}
    in_maps = []
    for x, ns in zip(core_x, core_nseg):
        if ns not in tabs:
            tabs[ns] = core_tables(T, ns)
        m = {"xT": np.ascontiguousarray(np.asarray(x, np.float32).T), "cmat": cm}
        m.update(tabs[ns])
        for n, _ in MK.W_SPECS:
            m[n] = weights[n]
        in_maps.append(m)
    res = run_bass_kernel_spmd(nc, in_maps, core_ids=list(range(len(in_maps))))
    return [np.ascontiguousarray(r["yT"].T) for r in res.results]


def kernel(x_prompt, x_sample, **weights):
    weights = {k: np.ascontiguousarray(np.asarray(v, np.float32)) for k, v in weights.items()}
    xp = np.asarray(x_prompt, np.float32)
    xs = np.asarray(x_sample, np.float32)
    T = 8192
    core_x = [xp[0], xp[1]] + [xs[4 * i:4 * i + 4].reshape(T, 1024) for i in range(4)]
    nseg = [1, 1, 4, 4, 4, 4]
    core_x += [core_x[5], core_x[5]]
    nseg += [4, 4]
    outs = run_cores(T, FULL_STEPS, core_x, nseg, weights)
    y_prompt = np.stack([outs[0], outs[1]], 0)
    y_sample = np.concatenate([outs[2 + i].reshape(4, 2048, 1024) for i in range(4)], 0)
    return (y_prompt, y_sample)
```
